# Optimizing a Trainium2 kernel written in Bass

```python
import math
import jax
import jax.numpy as jnp
from jax import lax
import numpy as np

D_MODEL = 1024
BATCH = 4
SEQ = 4096
DEPTH = 4
DEC_BATCH = 128
DEC_SEQ = 4
PAST_LEN = 8192
PAGE_SIZE = 128

HEAD_DIM = 64
N_Q_HEADS = 8
N_KV_HEADS = 2
Q_PER_KV = N_Q_HEADS // N_KV_HEADS
ATTN_WIDTH = N_Q_HEADS * HEAD_DIM
KV_WIDTH = N_KV_HEADS * HEAD_DIM
WINDOW = 128
ROPE_THETA = 10000.0
ATTN_SCALE = 1.0 / math.sqrt(HEAD_DIM)
SSM_WIDTH = D_MODEL // 4
SSM_GROUP = 16
N_SSM_GROUPS = SSM_WIDTH // SSM_GROUP
SSM_STATE = 64
CONV_WIDTH = D_MODEL // 4
CONV_K = 31
MIX_WIDTH = ATTN_WIDTH + SSM_WIDTH + CONV_WIDTH
IN_WIDTH = ATTN_WIDTH + 2 * KV_WIDTH + SSM_WIDTH + 2 * CONV_WIDTH
D_FF = -(-8 * D_MODEL // (3 * 256)) * 256
EPS = 1e-6
NEG = -1e30

kernel_name = 'hybrid_s5_conformer_swa_decoder_step'


def _rmsnorm(x, g):
    xf = x.astype(jnp.float32)
    y = xf * lax.rsqrt(jnp.mean(xf * xf, axis=-1, keepdims=True) + EPS)
    return (y * g.astype(jnp.float32)).astype(x.dtype)


def _layernorm(x, g, b):
    xf = x.astype(jnp.float32)
    xc = xf - jnp.mean(xf, axis=-1, keepdims=True)
    var = jnp.mean(xc * xc, axis=-1, keepdims=True)
    y = xc * lax.rsqrt(var + EPS) * g.astype(jnp.float32) + b.astype(jnp.float32)
    return y.astype(x.dtype)


def _rope(x, pos):
    half = HEAD_DIM // 2
    inv_freq = ROPE_THETA ** (-jnp.arange(half, dtype=jnp.float32) / half)
    ang = pos.astype(jnp.float32)[:, None] * inv_freq[None, :]
    cos = jnp.cos(ang)[None, :, None, :]
    sin = jnp.sin(ang)[None, :, None, :]
    xf = x.astype(jnp.float32)
    x1, x2 = xf[..., :half], xf[..., half:]
    return jnp.concatenate([x1 * cos - x2 * sin, x2 * cos + x1 * sin], axis=-1).astype(x.dtype)


def _sink_softmax(s, sinks):
    sk = sinks.astype(jnp.float32).reshape(N_KV_HEADS, Q_PER_KV, 1, 1)
    m = jnp.maximum(jnp.max(s, axis=-1, keepdims=True), sk)
    e = jnp.exp(s - m)
    return e / (jnp.sum(e, axis=-1, keepdims=True) + jnp.exp(sk - m))


def _attn_banded(q, k, v, sinks):
    b, l = q.shape[0], q.shape[1]
    nb = l // WINDOW
    qb = q.reshape(b, nb, WINDOW, N_KV_HEADS, Q_PER_KV, HEAD_DIM)
    kb = k.reshape(b, nb, WINDOW, N_KV_HEADS, HEAD_DIM)
    vb = v.reshape(b, nb, WINDOW, N_KV_HEADS, HEAD_DIM)
    pad = ((0, 0), (1, 0), (0, 0), (0, 0), (0, 0))
    kk = jnp.concatenate([jnp.pad(kb, pad)[:, :-1], kb], axis=2)
    vv = jnp.concatenate([jnp.pad(vb, pad)[:, :-1], vb], axis=2)
    s = jnp.einsum('bnqhgd,bnkhd->bnhgqk', qb, kk, preferred_element_type=jnp.float32) * ATTN_SCALE
    blk = jnp.arange(nb)[:, None, None]
    qi = jnp.arange(WINDOW)[None, :, None]
    kj = jnp.arange(2 * WINDOW)[None, None, :]
    rel = WINDOW + qi - kj
    mask = (rel >= 0) & (rel < WINDOW) & (blk * WINDOW - WINDOW + kj >= 0)
    s = jnp.where(mask[None, :, None, None], s, NEG)
    p = _sink_softmax(s, sinks)
    o = jnp.einsum('bnhgqk,bnkhd->bnqhgd', p.astype(v.dtype), vv)
    return o.reshape(b, l, ATTN_WIDTH)


def _attn_window_cache(q, k, v, k_buf, v_buf, sinks, pos0):
    b, t = q.shape[0], q.shape[1]
    wb = k_buf.shape[1]
    kk = jnp.concatenate([k_buf.astype(k.dtype), k], axis=1)
    vv = jnp.concatenate([v_buf.astype(v.dtype), v], axis=1)
    qg = q.reshape(b, t, N_KV_HEADS, Q_PER_KV, HEAD_DIM)
    s = jnp.einsum('bqhgd,bkhd->bhgqk', qg, kk, preferred_element_type=jnp.float32) * ATTN_SCALE
    qpos = pos0 + jnp.arange(t)
    kpos = pos0 - wb + jnp.arange(wb + t)
    rel = qpos[:, None] - kpos[None, :]
    mask = (rel >= 0) & (rel < WINDOW)
    s = jnp.where(mask, s, NEG)
    p = _sink_softmax(s, sinks)
    o = jnp.einsum('bhgqk,bkhd->bqhgd', p.astype(v.dtype), vv).reshape(b, t, ATTN_WIDTH)
    return o, kk[:, t:], vv[:, t:]


def _complex_affine_combine(e1, e2):
    a1r, a1i, b1r, b1i = e1
    a2r, a2i, b2r, b2i = e2
    ar = a2r * a1r - a2i * a1i
    ai = a2r * a1i + a2i * a1r
    br = a2r * b1r - a2i * b1i + b2r
    bi = a2r * b1i + a2i * b1r + b2i
    return ar, ai, br, bi


def _s5(zu, h0_re, h0_im, lam_re, lam_im, log_dt, b_re, b_im, c_re, c_im, d_skip, w_glu, b_glu):
    f32 = jnp.float32
    bsz, l = zu.shape[0], zu.shape[1]
    u = zu.astype(f32).reshape(bsz, l, N_SSM_GROUPS, SSM_GROUP)
    lr, li = lam_re.astype(f32), lam_im.astype(f32)
    dt = jnp.exp(log_dt.astype(f32))[:, None]
    mag = jnp.exp(lr * dt)
    ab_re = mag * jnp.cos(li * dt)
    ab_im = mag * jnp.sin(li * dt)
    den = lr * lr + li * li
    nr = ab_re - 1.0
    coef_re = (nr * lr + ab_im * li) / den
    coef_im = (ab_im * lr - nr * li) / den
    br, bi = b_re.astype(f32), b_im.astype(f32)
    bb_re = coef_re[..., None] * br - coef_im[..., None] * bi
    bb_im = coef_re[..., None] * bi + coef_im[..., None] * br
    bu_re = jnp.einsum('gpc,blgc->blgp', bb_re, u)
    bu_im = jnp.einsum('gpc,blgc->blgp', bb_im, u)
    a_re = jnp.broadcast_to(ab_re, bu_re.shape)
    a_im = jnp.broadcast_to(ab_im, bu_im.shape)
    acc_r, acc_i, sb_r, sb_i = lax.associative_scan(_complex_affine_combine, (a_re, a_im, bu_re, bu_im), axis=1)
    hr0 = h0_re.astype(f32)[:, None]
    hi0 = h0_im.astype(f32)[:, None]
    h_re = acc_r * hr0 - acc_i * hi0 + sb_r
    h_im = acc_r * hi0 + acc_i * hr0 + sb_i
    y = jnp.einsum('gcp,blgp->blgc', c_re.astype(f32), h_re) - jnp.einsum('gcp,blgp->blgc', c_im.astype(f32), h_im)
    y = (y + d_skip.astype(f32).reshape(N_SSM_GROUPS, SSM_GROUP) * u).reshape(bsz, l, SSM_WIDTH)
    z = jax.nn.gelu(y)
    out = z * jax.nn.sigmoid(z @ w_glu.astype(f32) + b_glu.astype(f32))
    return out.astype(zu.dtype), h_re[:, -1], h_im[:, -1]


def _conv_module(za, zg, buf, conv_w, conv_b, ln_g, ln_b):
    v = za * jax.nn.sigmoid(zg)
    full = jnp.concatenate([buf.astype(v.dtype), v], axis=1)
    y = lax.conv_general_dilated(full, conv_w.astype(v.dtype)[:, None, :], window_strides=(1,), padding='VALID',
                                 dimension_numbers=('NWC', 'WIO', 'NWC'), feature_group_count=CONV_WIDTH)
    y = y + conv_b.astype(v.dtype)
    y = jax.nn.silu(_layernorm(y, ln_g, ln_b))
    return y, full[:, full.shape[1] - (CONV_K - 1):]


def _layer(x, c, pos0, k_buf, v_buf, h0_re, h0_im, conv_buf, w, prompt):
    b, l = x.shape[0], x.shape[1]
    mod = jax.nn.silu(c) @ w['w_mod'] + w['b_mod']
    sh1, sc1, g1, sh2, sc2, g2 = jnp.split(mod[:, None, :], 6, axis=-1)
    h = _rmsnorm(x, w['norm1_g']) * (1 + sc1) + sh1
    z = h @ w['w_in']
    o1 = ATTN_WIDTH
    o2 = o1 + KV_WIDTH
    o3 = o2 + KV_WIDTH
    o4 = o3 + SSM_WIDTH
    o5 = o4 + CONV_WIDTH
    zq, zk, zv, zu, za, zg = jnp.split(z, [o1, o2, o3, o4, o5], axis=-1)
    pos = pos0 + jnp.arange(l)
    q = _rope(zq.reshape(b, l, N_Q_HEADS, HEAD_DIM), pos)
    k = _rope(zk.reshape(b, l, N_KV_HEADS, HEAD_DIM), pos)
    v = zv.reshape(b, l, N_KV_HEADS, HEAD_DIM)
    if prompt:
        o_attn = _attn_banded(q, k, v, w['sinks'])
        n_keep = min(WINDOW, l)
        new_k, new_v = k[:, l - n_keep:], v[:, l - n_keep:]
    else:
        o_attn, new_k, new_v = _attn_window_cache(q, k, v, k_buf, v_buf, w['sinks'], pos0)
    o_ssm, h_re, h_im = _s5(zu, h0_re, h0_im, w['lam_re'], w['lam_im'], w['log_dt'], w['b_re'], w['b_im'],
                            w['c_re'], w['c_im'], w['d_skip'], w['w_glu'], w['b_glu'])
    o_conv, new_conv = _conv_module(za, zg, conv_buf, w['conv_w'], w['conv_b'], w['conv_ln_g'], w['conv_ln_b'])
    mixed = jnp.concatenate([o_attn.astype(x.dtype), o_ssm.astype(x.dtype), o_conv.astype(x.dtype)], axis=-1)
    x = x + g1 * (mixed @ w['w_out'])
    h2 = _rmsnorm(x, w['norm2_g']) * (1 + sc2) + sh2
    ffn = (jax.nn.silu(h2 @ w['w_gate']) * (h2 @ w['w_up'])) @ w['w_down']
    x = x + g2 * ffn
    return x, new_k, new_v, h_re, h_im, new_conv


def setup_inputs(seed: int = 0) -> dict:
    key = jax.random.key(seed)
    ks = jax.random.split(key, 40)
    f32 = jnp.float32

    def nrm(k, shape, s):
        return jax.random.normal(k, shape, f32) * s

    wb = min(WINDOW, PAST_LEN)
    lam_im = jnp.pi * jnp.arange(SSM_STATE, dtype=f32)[None, None, :] + nrm(ks[15], (DEPTH, N_SSM_GROUPS, SSM_STATE), 0.01)
    return {
        'x_prompt': nrm(ks[0], (BATCH, SEQ, D_MODEL), 1.0),
        'x_sample': nrm(ks[1], (DEC_BATCH, DEC_SEQ, D_MODEL), 1.0),
        'c_prompt': nrm(ks[2], (BATCH, D_MODEL), 1.0),
        'c_sample': nrm(ks[3], (DEC_BATCH, D_MODEL), 1.0),
        'cache_k': nrm(ks[4], (DEPTH, DEC_BATCH, wb, N_KV_HEADS, HEAD_DIM), 1.0),
        'cache_v': nrm(ks[5], (DEPTH, DEC_BATCH, wb, N_KV_HEADS, HEAD_DIM), 1.0),
        'state_ssm_re': nrm(ks[6], (DEPTH, DEC_BATCH, N_SSM_GROUPS, SSM_STATE), 0.5),
        'state_ssm_im': nrm(ks[7], (DEPTH, DEC_BATCH, N_SSM_GROUPS, SSM_STATE), 0.5),
        'state_conv': nrm(ks[8], (DEPTH, DEC_BATCH, CONV_K - 1, CONV_WIDTH), 0.5),
        'norm1_g': 1.0 + nrm(ks[9], (DEPTH, D_MODEL), 0.05),
        'norm2_g': 1.0 + nrm(ks[10], (DEPTH, D_MODEL), 0.05),
        'w_mod': nrm(ks[11], (DEPTH, D_MODEL, 6 * D_MODEL), 0.5 * D_MODEL ** -0.5),
        'b_mod': nrm(ks[12], (DEPTH, 6 * D_MODEL), 0.01),
        'w_in': nrm(ks[13], (DEPTH, D_MODEL, IN_WIDTH), D_MODEL ** -0.5),
        'attn_sinks': nrm(ks[14], (DEPTH, N_Q_HEADS), 0.5),
        'ssm_lam_re': -0.5 + nrm(ks[16], (DEPTH, N_SSM_GROUPS, SSM_STATE), 0.01),
        'ssm_lam_im': lam_im,
        'ssm_log_dt': jax.random.uniform(ks[17], (DEPTH, N_SSM_GROUPS), f32, minval=math.log(1e-3), maxval=math.log(1e-1)),
        'ssm_b_re': nrm(ks[18], (DEPTH, N_SSM_GROUPS, SSM_STATE, SSM_GROUP), (2 * SSM_GROUP) ** -0.5),
        'ssm_b_im': nrm(ks[19], (DEPTH, N_SSM_GROUPS, SSM_STATE, SSM_GROUP), (2 * SSM_GROUP) ** -0.5),
        'ssm_c_re': nrm(ks[20], (DEPTH, N_SSM_GROUPS, SSM_GROUP, SSM_STATE), (2 * SSM_STATE) ** -0.5),
        'ssm_c_im': nrm(ks[21], (DEPTH, N_SSM_GROUPS, SSM_GROUP, SSM_STATE), (2 * SSM_STATE) ** -0.5),
        'ssm_d': nrm(ks[22], (DEPTH, SSM_WIDTH), 1.0),
        'ssm_w_glu': nrm(ks[23], (DEPTH, SSM_WIDTH, SSM_WIDTH), SSM_WIDTH ** -0.5),
        'ssm_b_glu': nrm(ks[24], (DEPTH, SSM_WIDTH), 0.01),
        'conv_w': nrm(ks[25], (DEPTH, CONV_K, CONV_WIDTH), CONV_K ** -0.5),
        'conv_b': nrm(ks[26], (DEPTH, CONV_WIDTH), 0.01),
        'conv_ln_g': 1.0 + nrm(ks[27], (DEPTH, CONV_WIDTH), 0.05),
        'conv_ln_b': nrm(ks[28], (DEPTH, CONV_WIDTH), 0.01),
        'w_out': nrm(ks[29], (DEPTH, MIX_WIDTH, D_MODEL), MIX_WIDTH ** -0.5),
        'w_gate': nrm(ks[30], (DEPTH, D_MODEL, D_FF), D_MODEL ** -0.5),
        'w_up': nrm(ks[31], (DEPTH, D_MODEL, D_FF), D_MODEL ** -0.5),
        'w_down': nrm(ks[32], (DEPTH, D_FF, D_MODEL), D_FF ** -0.5),
        'final_norm_g': 1.0 + nrm(ks[33], (D_MODEL,), 0.05),
    }


def reference(x_prompt, x_sample, c_prompt, c_sample, cache_k, cache_v, state_ssm_re, state_ssm_im, state_conv,
              norm1_g, norm2_g, w_mod, b_mod, w_in, attn_sinks, ssm_lam_re, ssm_lam_im, ssm_log_dt,
              ssm_b_re, ssm_b_im, ssm_c_re, ssm_c_im, ssm_d, ssm_w_glu, ssm_b_glu,
              conv_w, conv_b, conv_ln_g, conv_ln_b, w_out, w_gate, w_up, w_down, final_norm_g):
    xp, xs = x_prompt, x_sample
    bp = xp.shape[0]
    zero_h = jnp.zeros((bp, N_SSM_GROUPS, SSM_STATE), jnp.float32)
    zero_conv = jnp.zeros((bp, CONV_K - 1, CONV_WIDTH), xp.dtype)
    pk, pv, pre, pim, pcv = [], [], [], [], []
    sk, sv, sre, sim, scv = [], [], [], [], []
    for l in range(DEPTH):
        w = {
            'norm1_g': norm1_g[l], 'norm2_g': norm2_g[l], 'w_mod': w_mod[l], 'b_mod': b_mod[l],
            'w_in': w_in[l], 'sinks': attn_sinks[l],
            'lam_re': ssm_lam_re[l], 'lam_im': ssm_lam_im[l], 'log_dt': ssm_log_dt[l],
            'b_re': ssm_b_re[l], 'b_im': ssm_b_im[l], 'c_re': ssm_c_re[l], 'c_im': ssm_c_im[l],
            'd_skip': ssm_d[l], 'w_glu': ssm_w_glu[l], 'b_glu': ssm_b_glu[l],
            'conv_w': conv_w[l], 'conv_b': conv_b[l], 'conv_ln_g': conv_ln_g[l], 'conv_ln_b': conv_ln_b[l],
            'w_out': w_out[l], 'w_gate': w_gate[l], 'w_up': w_up[l], 'w_down': w_down[l],
        }
        xp, k1, v1, r1, i1, cv1 = _layer(xp, c_prompt, 0, None, None, zero_h, zero_h, zero_conv, w, True)
        xs, k2, v2, r2, i2, cv2 = _layer(xs, c_sample, PAST_LEN, cache_k[l], cache_v[l], state_ssm_re[l],
                                         state_ssm_im[l], state_conv[l], w, False)
        pk.append(k1); pv.append(v1); pre.append(r1); pim.append(i1); pcv.append(cv1)
        sk.append(k2); sv.append(v2); sre.append(r2); sim.append(i2); scv.append(cv2)
    y_prompt = _rmsnorm(xp, final_norm_g)
    y_sample = _rmsnorm(xs, final_norm_g)
    return (y_prompt, y_sample,
            jnp.stack(pk), jnp.stack(pv), jnp.stack(pre), jnp.stack(pim), jnp.stack(pcv),
            jnp.stack(sk), jnp.stack(sv), jnp.stack(sre), jnp.stack(sim), jnp.stack(scv))
```

```python
import math
import numpy as np
import concourse.bass as bass
import concourse.mybir as mybir
from concourse.bass_utils import run_bass_kernel_spmd

F32 = mybir.dt.float32
BF16 = mybir.dt.bfloat16
ALU = mybir.AluOpType
AF = mybir.ActivationFunctionType

SEG = 30000
NL = 4
D = 1024
KT = 8
NTOK = 4096
T = 256
NCH = NTOK // T
TS = 128
NB = 16
LS = 4
TSM = NB * LS
DFF = 2816
JT = 22
WIN = 2176
PAST = 8192
TWO_PI = 2.0 * math.pi


class Buf:
    __slots__ = ("name", "lw", "rd", "sem", "cnt", "excl")

    def __init__(self, name, excl=False):
        self.name = name
        self.excl = excl
        self.lw = None
        self.rd = {}
        self.sem = None
        self.cnt = 0


class Op:
    __slots__ = ("eng", "fn", "deps", "idx", "dma", "owner", "dcnt", "marked", "ev", "waits")

    def __init__(self, eng, fn, idx):
        self.eng = eng
        self.fn = fn
        self.idx = idx
        self.deps = []
        self.dma = False
        self.owner = None
        self.dcnt = 0
        self.marked = False
        self.ev = None
        self.waits = []


class Prog:
    ENGS = ("pe", "act", "dve", "pool", "sp")

    def __init__(self, nc):
        self.nc = nc
        self.ops = []

    def add(self, eng, fn, reads=(), writes=(), dma_owner=None, extra_deps=()):
        i = len(self.ops)
        op = Op(eng, fn, i)
        if dma_owner is not None:
            op.dma = True
            op.owner = dma_owner
            dma_owner.cnt += 16
            op.dcnt = dma_owner.cnt
        deps = {}
        for b in reads:
            if b.lw is not None:
                deps[b.lw] = "raw"
            if b.excl:
                for r in b.rd.values():
                    if r not in deps:
                        deps[r] = "war"
        for b in writes:
            if b.lw is not None and b.lw not in deps:
                deps[b.lw] = "waw"
            for r in b.rd.values():
                if r not in deps:
                    deps[r] = "war"
        for d in extra_deps:
            deps[d.idx] = "raw"
        deps.pop(i, None)
        for b in reads:
            key = ("d", i) if op.dma else eng
            b.rd[key] = i
        for b in writes:
            b.lw = i
            b.rd = {}
        op.deps = list(deps.items())
        self.ops.append(op)
        return op

    def finalize(self):
        ops = self.ops
        waited = {e: {} for e in self.ENGS}
        for op in ops:
            need = {}
            for d, kind in op.deps:
                p = ops[d]
                if p.dma:
                    key = ("dma", id(p.owner))
                    if need.get(key, (0, None))[0] < p.dcnt:
                        need[key] = (p.dcnt, p)
                else:
                    if p.eng == op.eng and not op.dma:
                        if op.eng == "pe" or kind != "raw":
                            continue
                    key = ("eng", p.eng)
                    if need.get(key, (-1, None))[0] < p.idx:
                        need[key] = (p.idx, p)
            w = waited[op.eng]
            for key, (val, p) in need.items():
                if w.get(key, -1) >= val:
                    continue
                w[key] = val
                op.waits.append(p)
                if not p.dma:
                    p.marked = True
        cnt = {e: 0 for e in self.ENGS}
        for op in ops:
            if not op.dma and op.marked:
                cnt[op.eng] += 1
                op.ev = cnt[op.eng]
        self.evcount = cnt

    def emit(self):
        nc = self.nc
        self.finalize()
        esems = {}
        for e in self.ENGS:
            n = (self.evcount[e] + SEG - 1) // SEG
            esems[e] = [nc.alloc_semaphore(f"ev_{e}_{k}") for k in range(max(n, 1))]
        for op in self.ops:
            if op.dma and op.owner.sem is None:
                op.owner.sem = nc.alloc_semaphore("d_" + op.owner.name)

        def semval(p):
            if p.dma:
                return p.owner.sem, p.dcnt
            k = (p.ev - 1) // SEG
            return esems[p.eng][k], (p.ev - 1) % SEG + 1

        per = {e: [op for op in self.ops if op.eng == e] for e in self.ENGS}

        def run(eng, lst):
            for op in lst:
                for p in op.waits:
                    s, v = semval(p)
                    eng.wait_ge(s, v)
                ins = op.fn(eng)
                if op.dma:
                    ins.then_inc(op.owner.sem, 16)
                elif op.marked:
                    s, _ = semval(op)
                    ins.then_inc(s, 1)

        with nc.Block() as block:
            @block.tensor
            def _(e):
                run(e, per["pe"])

            @block.scalar
            def _(e):
                run(e, per["act"])

            @block.vector
            def _(e):
                run(e, per["dve"])

            @block.gpsimd
            def _(e):
                run(e, per["pool"])

            @block.sync
            def _(e):
                run(e, per["sp"])


def build(nl=NL):
    import os
    KSUB = int(os.environ.get("KSUB", "9"))
    KS2 = int(os.environ.get("KS2", "9"))
    nc = bass.Bass("TRN2", target_bir_lowering=False)
    P = Prog(nc)
    stores = []

    def din(name, shape):
        return nc.dram_tensor(name, list(shape), F32, kind="ExternalInput").ap()

    def dout(name, shape):
        return nc.dram_tensor(name, list(shape), F32, kind="ExternalOutput").ap()

    def sb(name, shape, dt=F32):
        return nc.alloc_sbuf_tensor(name, list(shape), dt)

    xT = din("xT", [D, NTOK])
    xsT = din("xsT", [D, TSM])
    cT = din("cT", [D, 17])
    w_mod = din("w_mod", [NL, D, 6 * D])
    b_modT = din("b_modT", [NL, 128, 48])
    g1T = din("g1T", [NL, 128, KT])
    g2T = din("g2T", [NL, 128, KT])
    gfT = din("gfT", [128, KT])
    w_in2 = din("w_in2", [NL, D, WIN])
    ropeP = din("ropeP", [2, 128, NTOK])
    ropeS = din("ropeS", [2, 128, TSM])
    maskP = din("maskP", [2, 128, 512])
    maskS = din("maskS", [2, 128, 512])
    sinkT = din("sinkT", [NL, 64, 8])
    lamre = din("lamre", [NL, 128, 8])
    lamim = din("lamim", [NL, 128, 8])
    logdt = din("logdt", [NL, 128, 8])
    bre = din("bre", [NL, 128, 8, 16])
    bim = din("bim", [NL, 128, 8, 16])
    cre = din("cre", [NL, 128, 8, 16])
    cim = din("cim", [NL, 128, 8, 16])
    dskipT = din("dskipT", [NL, 128, 2])
    wglu = din("wglu", [NL, 256, 256])
    bgluT = din("bgluT", [NL, 128, 2])
    convwT = din("convwT", [NL, 128, 2, 31])
    convbT = din("convbT", [NL, 128, 2])
    lngT = din("lngT", [NL, 128, 2])
    lnbT = din("lnbT", [NL, 128, 2])
    w_out = din("w_out", [NL, D, D])
    w_gate = din("w_gate", [NL, D, DFF])
    w_up = din("w_up", [NL, D, DFF])
    w_down = din("w_down", [NL, DFF, D])
    kcT = din("kcT", [NL, NB, 128, 128])
    vc = din("vc", [NL, NB, 128, 128])
    kcn = din("kcn", [NL, NB, 128, 128])
    ssmre_in = din("ssmre_in", [NL, 128, 8, NB])
    ssmim_in = din("ssmim_in", [NL, 128, 8, NB])
    sconv_in = din("sconv_in", [NL, 128, 2, NB, 30])
    identd = din("identd", [128, 128])
    jrow = din("jrow", [128, TS + 1])

    yT = dout("yT", [D, NTOK])
    ysT = dout("ysT", [D, TSM])
    nkT = dout("nkT", [NL, 128, 128])
    nv = dout("nv", [NL, 128, 128])
    nre = dout("nre", [NL, 128, 8])
    nim = dout("nim", [NL, 128, 8])
    ncv = dout("ncv", [NL, 128, 2, 30])
    nks_c = dout("nks_c", [NL, NB, 124, 128])
    nvs_c = dout("nvs_c", [NL, NB, 124, 128])
    skT = dout("skT", [NL, 128, TSM])
    svn = dout("svn", [NL, 4, NB, 128])
    sre = dout("sre", [NL, 128, 8, NB])
    sim_o = dout("sim_o", [NL, 128, 8, NB])
    scv = dout("scv", [NL, 128, 2, NB, 30])
    xscr = nc.dram_tensor("xscr", [D, NTOK], F32, kind="Internal").ap()
    wgu_c = nc.dram_tensor("wgu_c", [JT, 128, 2 * KT * 128], BF16, kind="Internal").ap()
    wdn_c = nc.dram_tensor("wdn_c", [16, 128, 11 * 128], BF16, kind="Internal").ap()
    woa_c = nc.dram_tensor("woa_c", [8, 64, 8 * 128], BF16, kind="Internal").ap()
    wor_c = nc.dram_tensor("wor_c", [8, 128, 4 * 128], BF16, kind="Internal").ap()
    bwgu_c = [Buf(f"wguc{j}") for j in range(JT)]
    bwdn_c = [Buf(f"wdnc{j}") for j in range(16)]
    bwo_c = [Buf(f"woc{j}") for j in range(8)]
    DBG = bool(int(os.environ.get("KDBG", "0")))
    if DBG:
        dbg = dout("dbg", [3, 128, 8, TSM])

    PS = [nc.alloc_psum_tensor(f"ps{i}", [128, 512], F32) for i in range(8)]
    bPS = [Buf(f"ps{i}", excl=True) for i in range(8)]
    psrr = [0]

    def nps():
        i = psrr[0]
        psrr[0] = (i + 1) % 6
        return i

    def mm(out, lhsT, rhs, start, stop, r, w):
        return P.add("pe", lambda e: e.matmul(out, lhsT=lhsT, rhs=rhs, start=start, stop=stop), reads=r, writes=w)

    def act(out, in_, func, r, w, bias=None, scale=None):
        kw = {}
        if bias is not None:
            kw["bias"] = bias
        if scale is not None:
            kw["scale"] = scale
        return P.add("act", lambda e: e.activation(out=out, in_=in_, func=func, **kw), reads=r, writes=w)

    def tt(eng, out, in0, in1, op, r, w):
        return P.add(eng, lambda e: e.tensor_tensor(out=out, in0=in0, in1=in1, op=op), reads=r, writes=w)

    def ts(eng, out, in0, s1, s2, op0, op1, r, w):
        if op1 is None:
            return P.add(eng, lambda e: e.tensor_scalar(out=out, in0=in0, scalar1=s1, scalar2=None, op0=op0), reads=r, writes=w)
        return P.add(eng, lambda e: e.tensor_scalar(out=out, in0=in0, scalar1=s1, scalar2=s2, op0=op0, op1=op1), reads=r, writes=w)

    def stt(eng, out, in0, scalar, in1, op0, op1, r, w):
        return P.add(eng, lambda e: e.scalar_tensor_tensor(out=out, in0=in0, scalar=scalar, in1=in1, op0=op0, op1=op1), reads=r, writes=w)

    def cp(eng, out, in_, r, w):
        if eng == "act":
            return P.add("act", lambda e: e.activation(out=out, in_=in_, func=AF.Copy), reads=r, writes=w)
        return P.add(eng, lambda e: e.tensor_copy(out=out, in_=in_), reads=r, writes=w)

    def memset(eng, ap, val, w):
        return P.add(eng, lambda e: e.memset(ap, val), writes=w)

    def ld(out, in_, owner, w, q="sp"):
        return P.add(q, lambda e: e.dma_start(out=out, in_=in_), writes=w, dma_owner=owner)

    def st(out, in_, owner, r):
        o = P.add("sp", lambda e: e.dma_start(out=out, in_=in_), reads=r, dma_owner=owner)
        stores.append(o)
        return o

    ident = sb("ident", [128, 128]); bident = Buf("ident")
    ld(ident[:], identd, bident, [bident])
    ones16 = sb("ones16", [128, 128], BF16); bones = Buf("ones")
    memset("dve", ones16[:], 1.0, [bones])
    jr = sb("jr", [128, TS + 1]); bjr = Buf("jr")
    ld(jr[:], jrow, bjr, [bjr])
    mk = sb("mk", [128, 4, 512], BF16); bmk = Buf("mk")
    ld(mk[:, 0:2, :], maskP.rearrange("a p n -> p a n"), bmk, [bmk], q="pool")
    ld(mk[:, 2:4, :], maskS.rearrange("a p n -> p a n"), bmk, [bmk], q="pool")
    rps = sb("rps", [128, 2, TSM]); brps = Buf("rps")
    ld(rps[:], ropeS.rearrange("a p n -> p a n"), brps, [brps])
    gf = sb("gf", [128, KT]); bgf = Buf("gf")
    ld(gf[:], gfT, bgf, [bgf])
    pat = sb("pat", [128, NB, LS]); bpat = Buf("pat")
    memset("dve", pat[:], 1.0, [bpat])
    memset("dve", pat[:, :, 0:1], 0.0, [bpat])

    modT = sb("modT", [128, NL, 48, 17]); bmod = Buf("modT")
    csb = sb("csb", [128, KT, 17]); bcs = Buf("csb")
    sgc = sb("sgc", [128, KT, 17]); bsgc = Buf("sgc")
    ld(csb[:], cT.rearrange("(k p) n -> p k n", p=128), bcs, [bcs])
    act(sgc[:], csb[:], AF.Sigmoid, [bcs], [bsgc])
    tt("dve", csb[:], csb[:], sgc[:], ALU.mult, [bcs, bsgc], [bcs])
    bmt = sb("bmt", [128, NL, 48]); bbmt = Buf("bmt")
    ld(bmt[:], b_modT.rearrange("l p m -> p l m"), bbmt, [bbmt])
    WMB = 256
    _g0 = nc.sbuf_tensor("wmr0", [128, KT, WMB], F32)
    _g1 = nc.sbuf_tensor("wmr1", [128, KT, WMB], F32)
    wmr = [_g0.__enter__(), _g1.__enter__()]
    bwmr = [Buf(f"wmr{i}") for i in range(2)]
    lastmod = None
    it = 0
    for l in range(nl):
        for blk in range(6 * D // WMB):
            s = it % 2
            it += 1
            ld(wmr[s][:], w_mod[l, :, blk * WMB:(blk + 1) * WMB].rearrange("(k p) n -> p k n", p=128), bwmr[s], [bwmr[s]])
            for mi in range(WMB // 128):
                m = blk * (WMB // 128) + mi
                pi = nps()
                for k in range(KT):
                    mm(PS[pi][:, 0:17], wmr[s][:, k, mi * 128:(mi + 1) * 128], csb[:, k, :], k == 0, k == KT - 1,
                       [bwmr[s], bcs], [bPS[pi]])
                lastmod = ts("dve", modT[:, l, m, :], PS[pi][:, 0:17], bmt[:, l, m:m + 1], None, ALU.add, None, [bPS[pi], bbmt], [bmod])
    _g1.__exit__(None, None, None)
    _g0.__exit__(None, None, None)
    for _e in ("pe", "act", "pool", "sp"):
        P.add(_e, lambda e: e.nop(), extra_deps=[lastmod])

    xs = sb("xs", [128, KT, T]); bx = [Buf(f"x{k}") for k in range(KT)]
    xsm = sb("xsm", [128, KT, TSM]); bxsm = Buf("xsm")
    ld(xsm[:], xsT.rearrange("(k p) n -> p k n", p=128), bxsm, [bxsm])
    h16 = sb("h16", [128, KT, T], BF16); bh = Buf("h16")
    scr16 = sb("scr16", [128, JT, T], BF16); bscr = Buf("scr16")
    rstd = sb("rstd", [128, T]); brstd = Buf("rstd")
    tmpAB = sb("tmpAB", [128, 2, T])
    tmpA = tmpAB[:, 0, :]; btA = Buf("tmpA")
    tmpB = tmpAB[:, 1, :]; btB = Buf("tmpB")
    tmpC = sb("tmpC", [128, T]); btC = Buf("tmpC")
    tmpD = sb("tmpD", [128, T]); btD = Buf("tmpD")
    rp = sb("rp", [128, 2, T]); brp = Buf("rp")
    q16 = sb("q16", [128, 4, T], BF16); bq = Buf("q16")
    kb16 = sb("kb16", [128, 128 + T], BF16); bkb = Buf("kb16")
    k32 = sb("k32", [128, T]); bk32 = Buf("k32")
    v16 = sb("v16", [128, 1 + T // 128, 128], BF16); bv16 = Buf("v16")
    v32 = sb("v32", [128, 128]); bv32 = Buf("v32")
    pown = sb("pown", [128, 512], BF16); bpown = Buf("pown")
    pprev = sb("pprev", [128, 512], BF16); bpprev = Buf("pprev")
    den = tmpAB[0:64].rearrange("p a t -> p (a t)")
    att16 = sb("att16", [64, 8, T], BF16); batt = Buf("att16")
    u32 = sb("u32", [128, 2, T]); bu32 = Buf("u32")
    u16 = sb("u16", [128, 2, T], BF16); bu16 = Buf("u16")
    cb32 = sb("cb32", [128, 2, 30 + T]); bcb32 = Buf("cb32")
    cb16 = sb("cb16", [128, 2, 30 + T], BF16); bcb16 = Buf("cb16")
    cbs32 = sb("cbs32", [128, 2, NB, 34]); bcbs32 = Buf("cbs32")
    cbs16 = sb("cbs16", [128, 2, NB, 34], BF16); bcbs16 = Buf("cbs16")
    ycf = sb("ycf", [128, 2, T]); bycf = Buf("ycf")
    yc16 = scr16[:, 8:12, :]; byc16 = bscr
    oc16 = sb("oc16", [128, 2, T], BF16); boc = Buf("oc16")
    os16 = sb("os16", [128, 2, T], BF16); bos = Buf("os16")
    yss = sb("yss", [128, 2, T]); byss = Buf("yss")
    z32 = sb("z32", [128, 2, T]); bz32 = Buf("z32")
    assert 2 * T == 4 * 8 * 16
    z16 = sb("z16", [128, 2, T], BF16); bz16 = Buf("z16")
    sx = [sb(f"sx{i}", [128, TS]) for i in range(8)]
    bsx = [Buf(f"sx{i}") for i in range(8)]
    sq = [[sb(f"sq{i}{j}", [128, TS]) for j in range(2)] for i in range(2)]
    bsq = [[Buf(f"sq{i}{j}") for j in range(2)] for i in range(2)]
    sh16 = [[sb(f"sh16{i}{j}", [128, TS], BF16) for j in range(2)] for i in range(2)]
    bsh16 = [[Buf(f"sh16{i}{j}") for j in range(2)] for i in range(2)]
    ssi = [0]
    hre16 = sh16[0][0]; bhre = bsh16[0][0]
    him16 = sh16[0][1]; bhim = bsh16[0][1]
    qlre = sb("qlre", [128, 8]); qlim = sb("qlim", [128, 8]); bql = Buf("ql")
    hlre = sb("hlre", [128, 8]); hlim = sb("hlim", [128, 8]); bhl = Buf("hl")
    inre = sb("inre", [128, 8]); inim = sb("inim", [128, 8]); binit = Buf("init")
    sm8 = [sb(f"sm8_{i}", [128, 8]) for i in range(4)]; bsm8 = Buf("sm8")
    h0re = sb("h0re", [128, 8, NB]); h0im = sb("h0im", [128, 8, NB]); bh0 = Buf("h0")
    ahre = sb("ahre", [128, 8, NB]); ahim = sb("ahim", [128, 8, NB]); bah = Buf("ah")
    hsre = sb("hsre", [128, 8, NB]); hsim = sb("hsim", [128, 8, NB]); bhs = Buf("hs")
    st16 = [sb(f"st16_{i}", [128, NB]) for i in range(4)]; bst16 = Buf("st16")
    kc16 = sb("kc16", [128, NB, 128], BF16); bkc = Buf("kc16")
    vc16 = sb("vc16", [128, NB, 128], BF16); bvc = Buf("vc16")
    vn16 = sb("vn16", [4, NB, 128], BF16); bvn16 = Buf("vn16")
    vn32 = sb("vn32", [4, NB, 128]); bvn32 = Buf("vn32")
    pn16 = sb("pn16", [4, 512], BF16); bpn = Buf("pn16")

    win16 = sb("win16", [128, KT, WIN], BF16); bwin = Buf("win16")
    woa = [sb(f"woa{i}", [64, 8, 128], BF16) for i in range(2)]
    wor = [sb(f"wor{i}", [128, 4, 128], BF16) for i in range(2)]
    bwo = [Buf(f"wo{i}") for i in range(2)]
    woi = [0]
    wgl16 = sb("wgl16", [128, 2, 256], BF16); bwgl = Buf("wgl16")
    sp8 = sb("sp8", [128, 12, 8]); bsp8 = Buf("sp8")
    sp2 = sb("sp2", [128, 8, 2]); bsp2 = Buf("sp2")
    cw = sb("cw", [128, 2, 31]); bcw = Buf("cw")
    sk = sb("sk", [64, 8]); bsk = Buf("sk")
    bc32 = yss[:].rearrange("p a (b c) -> p (a b) c", c=16).rearrange("p (a b) c -> p a b c", a=4); bbc = byss
    bbt = z32[:].rearrange("p a (b c) -> p (a b) c", c=16).rearrange("p (a b) c -> p a b c", a=4); bbbt = bz32
    exq = sb("exq", [128, 128]); bexq = Buf("exq")
    LB = sb("LB", [128, 4, 8, 128], BF16); bLB = Buf("LB")
    cosT = sb("cosT", [128, 8, TS + 1]); sinT = sb("sinT", [128, 8, TS + 1]); btab = Buf("tab")
    r4 = sb("r4", [128, 8, NB, LS]); btab4 = Buf("tab4")
    diag16 = sb("diag16", [128, 2, 31, 128], BF16); bdiag = Buf("diag16")
    gsc = sb("gsc", [128, 2, KT, 17]); bgsc = Buf("gsc")
    NG = 2
    wgu = [sb(f"wgu{i}", [128, 2, KT, 128], BF16) for i in range(NG)]
    bwgu = [Buf(f"wgu{i}") for i in range(NG)]
    wdn = [sb(f"wdn{i}", [128, 11, 128], BF16) for i in range(2)]
    bwdn = [Buf(f"wdn{i}") for i in range(2)]
    ffi = [0, 0]

    def S8(i):
        return sp8[:, i, :]

    angt = tmpC[:, 0:TS + 1]; angk = tmpD[:, 0:TS + 1]; bang = btC
    CM = 12582912.0

    def sin_of(out, x, shift, tmp, r, w, wt):
        xs_ = x
        if shift != 0.0:
            ts("dve", out, x, shift, None, ALU.add, None, r, w)
            xs_ = out
        ts("dve", tmp, xs_, 1.0 / TWO_PI, CM, ALU.mult, ALU.add, r + w, wt)
        ts("dve", tmp, tmp, -CM, None, ALU.add, None, r + wt, wt)
        stt("dve", tmp, tmp, -TWO_PI, xs_, ALU.mult, ALU.add, r + w + wt, wt)
        ts("dve", tmp, tmp, math.pi, -math.pi, ALU.min, ALU.max, r + wt, wt)
        act(out, tmp, AF.Sin, r + wt, w)

    def layer_params(l):
        for k in range(KT):
            ld(win16[:, k, :], w_in2[l, k * 128:(k + 1) * 128, :], bwin, [bwin], q="pool")
        ld(wgl16[:], wglu[l].rearrange("(k p) n -> p k n", p=128), bwgl, [bwgl], q="pool")
        ld(sp8[:, 0, :], lamre[l], bsp8, [bsp8])
        ld(sp8[:, 1, :], lamim[l], bsp8, [bsp8])
        ld(sp8[:, 2, :], logdt[l], bsp8, [bsp8])
        ld(sp2[:, 0, :], dskipT[l], bsp2, [bsp2])
        ld(sp2[:, 1, :], bgluT[l], bsp2, [bsp2])
        ld(sp2[:, 2, :], convbT[l], bsp2, [bsp2])
        ld(sp2[:, 3, :], lngT[l], bsp2, [bsp2])
        ld(sp2[:, 4, :], lnbT[l], bsp2, [bsp2])
        ld(cw[:], convwT[l], bcw, [bcw])
        ld(sk[:], sinkT[l], bsk, [bsk])
        act(sk[:], sk[:], AF.Exp, [bsk], [bsk])
        ld(bc32[:, 0], bre[l], bbc, [bbc])
        ld(bc32[:, 1], bim[l], bbc, [bbc])
        ld(bc32[:, 2], cre[l], bbc, [bbc])
        ld(bc32[:, 3], cim[l], bbc, [bbc])
        for a, (gT, off) in enumerate(((g1T, 8), (g2T, 32))):
            ld(sp8[:, 3, :], gT[l], bsp8, [bsp8])
            ts("dve", gsc[:, a], modT[:, l, off:off + 8, :], 1.0, None, ALU.add, None, [bmod], [bgsc])
            tt("dve", gsc[:, a], gsc[:, a], sp8[:, 3, :].unsqueeze(2).to_broadcast([128, KT, 17]), ALU.mult, [bgsc, bsp8], [bgsc])
        R = [bsp8]
        W = [bsp8]
        act(S8(2), S8(2), AF.Exp, R, W)
        tt("dve", S8(3), S8(0), S8(2), ALU.mult, R, W)
        tt("dve", S8(4), S8(1), S8(2), ALU.mult, R, W)
        act(S8(5), S8(3), AF.Exp, R, W)
        sin_of(S8(7), S8(4), 0.0, sm8[0][:], R + [bsm8], W, [bsm8])
        sin_of(S8(6), S8(4), 0.5 * math.pi, sm8[0][:], R + [bsm8], W, [bsm8])
        tt("dve", S8(8), S8(5), S8(6), ALU.mult, R, W)
        tt("dve", S8(9), S8(5), S8(7), ALU.mult, R, W)
        a0, a1, a2, a3 = (sm8[i][:] for i in range(4))
        R2 = [bsp8, bsm8]
        tt("dve", a0, S8(0), S8(0), ALU.mult, R2, [bsm8])
        tt("dve", a1, S8(1), S8(1), ALU.mult, R2, [bsm8])
        tt("dve", a0, a0, a1, ALU.add, R2, [bsm8])
        P.add("dve", lambda e: e.reciprocal(out=a0, in_=a0), reads=R2, writes=[bsm8])
        ts("dve", a1, S8(8), -1.0, None, ALU.add, None, R2, [bsm8])
        tt("dve", a2, a1, S8(0), ALU.mult, R2, [bsm8])
        tt("dve", a3, S8(9), S8(1), ALU.mult, R2, [bsm8])
        tt("dve", a2, a2, a3, ALU.add, R2, [bsm8])
        tt("dve", S8(10), a2, a0, ALU.mult, R2, W)
        tt("dve", a2, S8(9), S8(0), ALU.mult, R2, [bsm8])
        tt("dve", a3, a1, S8(1), ALU.mult, R2, [bsm8])
        tt("dve", a2, a2, a3, ALU.subtract, R2, [bsm8])
        tt("dve", S8(11), a2, a0, ALU.mult, R2, W)
        cre_b = sp8[:, 10, :].unsqueeze(2).to_broadcast([128, 8, 16])
        cim_b = sp8[:, 11, :].unsqueeze(2).to_broadcast([128, 8, 16])
        Rb = [bbc, bsp8, bbbt]
        tt("dve", bbt[:, 0], bc32[:, 0], cre_b, ALU.mult, Rb, [bbbt])
        tt("dve", bbt[:, 2], bc32[:, 1], cim_b, ALU.mult, Rb, [bbbt])
        tt("dve", bbt[:, 0], bbt[:, 0], bbt[:, 2], ALU.subtract, Rb, [bbbt])
        tt("dve", bbt[:, 1], bc32[:, 1], cre_b, ALU.mult, Rb, [bbbt])
        tt("dve", bbt[:, 2], bc32[:, 0], cim_b, ALU.mult, Rb, [bbbt])
        tt("dve", bbt[:, 1], bbt[:, 1], bbt[:, 2], ALU.add, Rb, [bbbt])
        cp("dve", bbt[:, 2], bc32[:, 2], Rb, [bbbt])
        ts("dve", bbt[:, 3], bc32[:, 3], -1.0, None, ALU.mult, None, Rb, [bbbt])
        for mi in range(4):
            for ct in range(8):
                memset("dve", exq[:], 0.0, [bexq])
                for gg in range(2):
                    gp = (2 * ct + gg) % 8
                    cp("dve", exq[64 * gg:64 * gg + 64, 16 * gp:16 * gp + 16], bbt[64 * gg:64 * gg + 64, mi, ct, :], [bbbt], [bexq])
                if mi < 2:
                    pi = nps()
                    P.add("pe", lambda e, pi=pi: e.transpose(out=PS[pi][:, 0:128], in_=exq[:], identity=ident[:]),
                          reads=[bexq, bident], writes=[bPS[pi]])
                    cp("act", LB[:, mi, ct, :], PS[pi][:, 0:128], [bPS[pi]], [bLB])
                else:
                    cp("act", LB[:, mi, ct, :], exq[:], [bexq], [bLB])
        for ct in range(8):
            ts("dve", sx[0][:, 0:TS + 1] if TS + 1 <= TS else angt, jr[:], sp8[:, 4, ct:ct + 1], None, ALU.mult, None, [bjr, bsp8], [bang])
            sin_of(sinT[:, ct, :], angt, 0.0, angk, [bang], [btab], [btD])
            sin_of(cosT[:, ct, :], angt, 0.5 * math.pi, angk, [bang], [btab], [btD])
        for ct in range(8):
            ts("dve", r4[:, ct], pat[:], sp8[:, 5, ct:ct + 1], None, ALU.mult, None, [bpat, bsp8], [btab4])
        for m in range(2):
            for k in range(31):
                ts("pool", diag16[:, m, k, :], ident[:], cw[:, m, k:k + 1], None, ALU.mult, None, [bident, bcw], [bdiag])

    def rmsnorm(xa, bxs, n, gcol, shm, a, l, sample):
        for k in range(KT):
            act(scr16[:, k, 0:n], xa[:, k, :], AF.Square, [bxs[k]], [bscr])
        pi = nps()
        for k in range(KT):
            mm(PS[pi][:, 0:n], ones16[:], scr16[:, k, 0:n], k == 0, k == KT - 1, [bones, bscr], [bPS[pi]])
        ts("dve", rstd[:, 0:n], PS[pi][:, 0:n], 1.0 / D, 1e-6, ALU.mult, ALU.add, [bPS[pi]], [brstd])
        act(rstd[:, 0:n], rstd[:, 0:n], AF.Sqrt, [brstd], [brstd])
        P.add("dve", lambda e: e.reciprocal(out=rstd[:, 0:n], in_=rstd[:, 0:n]), reads=[brstd], writes=[brstd])
        for k in range(KT):
            tA = tmpA if k % 2 == 0 else tmpB
            bA = btA if k % 2 == 0 else btB
            tt("dve", tA[:, 0:n], xa[:, k, :], rstd[:, 0:n], ALU.mult, [bxs[k], brstd], [bA])
            if gcol is None:
                ts("pool", xa[:, k, :], tA[:, 0:n], gf[:, k:k + 1], None, ALU.mult, None, [bA, bgf], [bxs[k]])
            elif not sample:
                if False:
                    pass
                else:
                    act(h16[:, k, 0:n], tA[:, 0:n], AF.Identity, [bA, bgsc, bmod], [bh],
                        bias=modT[:, l, shm + k, 0:1], scale=gsc[:, a, k, 0:1])
            else:
                v3 = tA[:, 0:n].rearrange("p (b t) -> p b t", t=LS)
                if True:
                    tt("pool", v3, v3, gsc[:, a, k, 1:17].unsqueeze(2).to_broadcast([128, NB, LS]), ALU.mult, [bA, bgsc], [bA])
                    tt("pool", h16[:, k, 0:n].rearrange("p (b t) -> p b t", t=LS), v3,
                       modT[:, l, shm + k, 1:17].unsqueeze(2).to_broadcast([128, NB, LS]), ALU.add, [bA, bmod], [bh])

    def resid(xa, bxs, n, mo, pi, l, gm, sample):
        if not sample:
            stt("dve", xa[:, mo, :], PS[pi][:, 0:n], modT[:, l, gm + mo, 0:1], xa[:, mo, :], ALU.mult, ALU.add,
                [bPS[pi], bmod, bxs[mo]], [bxs[mo]])
        else:
            tt("dve", tmpC[:, 0:n].rearrange("p (b t) -> p b t", t=LS), PS[pi][:, 0:n].rearrange("p (b t) -> p b t", t=LS),
               modT[:, l, gm + mo, 1:17].unsqueeze(2).to_broadcast([128, NB, LS]), ALU.mult, [bPS[pi], bmod], [btC])
            tt("dve", xa[:, mo, :], xa[:, mo, :], tmpC[:, 0:n], ALU.add, [btC, bxs[mo]], [bxs[mo]])

    def inproj_tile(m, n):
        pi = nps()
        for k in range(KT):
            mm(PS[pi][:, 0:n], win16[:, k, m * 128:(m + 1) * 128], h16[:, k, 0:n], k == 0, k == KT - 1, [bwin, bh], [bPS[pi]])
        return pi

    def rope_pair(m_a, m_b, cos_ap, sin_ap, brope, out_ap, bout, n, extra32=None):
        pa = inproj_tile(m_a, n)
        pb = inproj_tile(m_b, n)
        tt("dve", tmpA[:, 0:n], PS[pa][:, 0:n], cos_ap, ALU.mult, [bPS[pa], brope], [btA])
        tt("dve", tmpB[:, 0:n], PS[pb][:, 0:n], sin_ap, ALU.mult, [bPS[pb], brope], [btB])
        tt("pool", out_ap, tmpA[:, 0:n], tmpB[:, 0:n], ALU.add, [btA, btB], [bout])
        if extra32 is not None:
            tt("pool", extra32[0], tmpA[:, 0:n], tmpB[:, 0:n], ALU.add, [btA, btB], [extra32[1]])

    def gelu_glu(n):
        for o in range(2):
            tt("pool", tmpC[:, 0:n], yss[:, o, 0:n], yss[:, o, 0:n], ALU.mult, [byss], [btC])
            ts("pool", tmpC[:, 0:n], tmpC[:, 0:n], 0.044715, 1.0, ALU.mult, ALU.add, [btC], [btC])
            tt("pool", tmpC[:, 0:n], tmpC[:, 0:n], yss[:, o, 0:n], ALU.mult, [btC, byss], [btC])
            act(tmpC[:, 0:n], tmpC[:, 0:n], AF.Sigmoid, [btC], [btC], scale=2.0 * math.sqrt(2.0 / math.pi))
            tt("dve", z32[:, o, 0:n], yss[:, o, 0:n], tmpC[:, 0:n], ALU.mult, [byss, btC], [bz32])
            cp("pool", z16[:, o, 0:n], z32[:, o, 0:n], [bz32], [bz16])
        for o in range(2):
            pi = nps()
            for k in range(2):
                mm(PS[pi][:, 0:n], wgl16[:, k, o * 128:(o + 1) * 128], z16[:, k, 0:n], k == 0, k == 1, [bwgl, bz16], [bPS[pi]])
            act(tmpD[:, 0:n], PS[pi][:, 0:n], AF.Sigmoid, [bPS[pi], bsp2], [btD], bias=sp2[:, 1, o:o + 1])
            tt("dve", os16[:, o, 0:n], z32[:, o, 0:n], tmpD[:, 0:n], ALU.mult, [bz32, btD], [bos])

    def conv_ln(rhs_fn, n, view):
        pcs = []
        for m in range(2):
            pi = nps()
            pcs.append(pi)
            for k in range(31):
                mm(view(PS[pi][:, 0:n]), diag16[:, m, k, :], rhs_fn(m, k), k == 0, k == 30, [bdiag, bcb16, bcbs16], [bPS[pi]])
            act(ycf[:, m, 0:n], PS[pi][:, 0:n], AF.Identity, [bPS[pi], bsp2], [bycf], bias=sp2[:, 2, m:m + 1])
            cp("dve", yc16[:, m, 0:n], ycf[:, m, 0:n], [bycf], [byc16])
            act(yc16[:, 2 + m, 0:n], ycf[:, m, 0:n], AF.Square, [bycf], [byc16])
        p1 = nps()
        for m in range(2):
            mm(PS[p1][:, 0:n], ones16[:], yc16[:, m, 0:n], m == 0, m == 1, [bones, byc16], [bPS[p1]])
        p2 = nps()
        for m in range(2):
            mm(PS[p2][:, 0:n], ones16[:], yc16[:, 2 + m, 0:n], m == 0, m == 1, [bones, byc16], [bPS[p2]])
        ts("dve", tmpA[:, 0:n], PS[p1][:, 0:n], 1.0 / 256, None, ALU.mult, None, [bPS[p1]], [btA])
        tt("dve", tmpB[:, 0:n], tmpA[:, 0:n], tmpA[:, 0:n], ALU.mult, [btA], [btB])
        stt("dve", tmpB[:, 0:n], PS[p2][:, 0:n], 1.0 / 256, tmpB[:, 0:n], ALU.mult, ALU.subtract, [bPS[p2], btB], [btB])
        ts("dve", tmpB[:, 0:n], tmpB[:, 0:n], 1e-6, None, ALU.add, None, [btB], [btB])
        act(tmpB[:, 0:n], tmpB[:, 0:n], AF.Sqrt, [btB], [btB])
        P.add("dve", lambda e: e.reciprocal(out=tmpB[:, 0:n], in_=tmpB[:, 0:n]), reads=[btB], writes=[btB])
        for m in range(2):
            tt("dve", tmpC[:, 0:n], ycf[:, m, 0:n], tmpA[:, 0:n], ALU.subtract, [bycf, btA], [btC])
            tt("dve", tmpC[:, 0:n], tmpC[:, 0:n], tmpB[:, 0:n], ALU.mult, [btC, btB], [btC])
            act(tmpD[:, 0:n], tmpC[:, 0:n], AF.Identity, [btC, bsp2], [btD], bias=sp2[:, 4, m:m + 1], scale=sp2[:, 3, m:m + 1])
            act(tmpC[:, 0:n], tmpD[:, 0:n], AF.Sigmoid, [btD], [btC])
            tt("dve", oc16[:, m, 0:n], tmpD[:, 0:n], tmpC[:, 0:n], ALU.mult, [btC, btD], [boc])

    def outproj_ffn(xa, bxs, n, l, sample, first=False):
        for mo in range(8):
            s = woi[0] % 2
            woi[0] += 1
            fa = woa[s][:].rearrange("p h c -> p (h c)")
            fr = wor[s][:].rearrange("p h c -> p (h c)")
            if first:
                ld(woa[s][:], w_out[l, 0:512, mo * 128:(mo + 1) * 128].rearrange("(h d) n -> d h n", d=64), bwo[s], [bwo[s]], q="pool")
                ld(wor[s][:], w_out[l, 512:1024, mo * 128:(mo + 1) * 128].rearrange("(j p) n -> p j n", p=128), bwo[s], [bwo[s]], q="pool")
                P.add("sp", lambda e, fa=fa, mo=mo: e.dma_start(out=woa_c[mo], in_=fa), reads=[bwo[s]], writes=[bwo_c[mo]], dma_owner=bwo[s])
                P.add("sp", lambda e, fr=fr, mo=mo: e.dma_start(out=wor_c[mo], in_=fr), reads=[bwo[s]], writes=[bwo_c[mo]], dma_owner=bwo[s])
            else:
                P.add("sp", lambda e, fa=fa, mo=mo: e.dma_start(out=fa, in_=woa_c[mo]), reads=[bwo_c[mo]], writes=[bwo[s]], dma_owner=bwo[s])
                P.add("sp", lambda e, fr=fr, mo=mo: e.dma_start(out=fr, in_=wor_c[mo]), reads=[bwo_c[mo]], writes=[bwo[s]], dma_owner=bwo[s])
            pi = nps()
            for hq in range(8):
                mm(PS[pi][:, 0:n], woa[s][:, hq, :], att16[:, hq, 0:n], hq == 0, False, [bwo[s], batt], [bPS[pi]])
            for j in range(2):
                mm(PS[pi][:, 0:n], wor[s][:, j, :], os16[:, j, 0:n], False, False, [bwo[s], bos], [bPS[pi]])
            for j in range(2):
                mm(PS[pi][:, 0:n], wor[s][:, 2 + j, :], oc16[:, j, 0:n], False, j == 1, [bwo[s], boc], [bPS[pi]])
            resid(xa, bxs, n, mo, pi, l, 16, sample)
        rmsnorm(xa, bxs, n, 1, 24, 1, l, sample)
        for j in range(JT):
            s = ffi[0] % NG
            ffi[0] += 1
            fg = wgu[s][:].rearrange("p a k c -> p (a k c)")
            if first:
                ld(wgu[s][:, 0], w_gate[l, :, j * 128:(j + 1) * 128].rearrange("(k p) n -> p k n", p=128), bwgu[s], [bwgu[s]], q="pool")
                ld(wgu[s][:, 1], w_up[l, :, j * 128:(j + 1) * 128].rearrange("(k p) n -> p k n", p=128), bwgu[s], [bwgu[s]], q="pool")
                P.add("sp", lambda e, fg=fg, j=j: e.dma_start(out=wgu_c[j], in_=fg), reads=[bwgu[s]], writes=[bwgu_c[j]], dma_owner=bwgu[s])
            else:
                P.add("sp", lambda e, fg=fg, j=j: e.dma_start(out=fg, in_=wgu_c[j]), reads=[bwgu_c[j]], writes=[bwgu[s]], dma_owner=bwgu[s])
            pg = nps()
            for k in range(KT):
                mm(PS[pg][:, 0:n], wgu[s][:, 0, k, :], h16[:, k, 0:n], k == 0, k == KT - 1, [bwgu[s], bh], [bPS[pg]])
            pu = nps()
            for k in range(KT):
                mm(PS[pu][:, 0:n], wgu[s][:, 1, k, :], h16[:, k, 0:n], k == 0, k == KT - 1, [bwgu[s], bh], [bPS[pu]])
            tA = tmpA if j % 2 == 0 else tmpB
            bA = btA if j % 2 == 0 else btB
            act(tA[:, 0:n], PS[pg][:, 0:n], AF.Silu, [bPS[pg]], [bA])
            tt("dve", scr16[:, j, 0:n], tA[:, 0:n], PS[pu][:, 0:n], ALU.mult, [bA, bPS[pu]], [bscr])
        for mo in range(8):
            pi = nps()
            for jh in range(2):
                s = ffi[1] % 2
                ffi[1] += 1
                ci = mo * 2 + jh
                fd = wdn[s][:].rearrange("p j c -> p (j c)")
                if first:
                    ld(wdn[s][:], w_down[l, jh * 1408:(jh + 1) * 1408, mo * 128:(mo + 1) * 128].rearrange("(j p) n -> p j n", p=128), bwdn[s], [bwdn[s]], q="pool")
                    P.add("sp", lambda e, fd=fd, ci=ci: e.dma_start(out=wdn_c[ci], in_=fd), reads=[bwdn[s]], writes=[bwdn_c[ci]], dma_owner=bwdn[s])
                else:
                    P.add("sp", lambda e, fd=fd, ci=ci: e.dma_start(out=fd, in_=wdn_c[ci]), reads=[bwdn_c[ci]], writes=[bwdn[s]], dma_owner=bwdn[s])
                for jj in range(11):
                    j = jh * 11 + jj
                    mm(PS[pi][:, 0:n], wdn[s][:, jj, :], scr16[:, j, 0:n], j == 0, j == JT - 1, [bwdn[s], bscr], [bPS[pi]])
            resid(xa, bxs, n, mo, pi, l, 40, sample)

    def prompt_chunk(l, c):
        n = T
        t0 = c * T
        src = xT if l == 0 else xscr
        xdr = src[:, t0:t0 + T].rearrange("(k p) t -> p k t", p=128)
        ld(xs[:], xdr, bx[0], bx)
        ld(rp[:], ropeP[:, :, t0:t0 + T].rearrange("a p t -> p a t"), brp, [brp])
        if KS2 < 1:
            return
        rmsnorm(xs, bx, n, 1, 0, 0, l, False)
        if KS2 < 2:
            return
        for j in range(4):
            rope_pair(j, 5 + j, rp[:, 0, :], rp[:, 1, :], brp, q16[:, j, :], bq, n)
        rope_pair(4, 9, rp[:, 0, :], rp[:, 1, :], brp, kb16[:, 128:128 + T], bkb, n, extra32=(k32[:, 0:n], bk32))
        if KS2 < 3:
            return
        for o in range(2):
            pi = inproj_tile(10 + o, n)
            cp("act", u32[:, o, :], PS[pi][:, 0:n], [bPS[pi]], [bu32])
            cp("dve", u16[:, o, :], PS[pi][:, 0:n], [bPS[pi]], [bu16])
        for o in range(2):
            pa = inproj_tile(12 + o, n)
            pg = inproj_tile(14 + o, n)
            act(tmpC[:, 0:n], PS[pg][:, 0:n], AF.Sigmoid, [bPS[pg]], [btC])
            tt("dve", cb32[:, o, 30:30 + T], PS[pa][:, 0:n], tmpC[:, 0:n], ALU.mult, [bPS[pa], btC], [bcb32])
            if c == 0:
                memset("pool", cb32[:, o, 0:30], 0.0, [bcb32])
            cp("pool", cb16[:, o, :], cb32[:, o, :], [bcb32], [bcb16])
        if KS2 < 4:
            return
        for tb in range(T // 128):
            pi = nps()
            for k in range(KT):
                mm(PS[pi][:, 0:128], h16[:, k, tb * 128:(tb + 1) * 128], win16[:, k, 2048:2176], k == 0, k == KT - 1, [bh, bwin], [bPS[pi]])
            cp("act", v16[:, 1 + tb, :], PS[pi][:, 0:128], [bPS[pi]], [bv16])
            if c == NCH - 1 and tb == T // 128 - 1:
                cp("dve", v32[:], PS[pi][:, 0:128], [bPS[pi]], [bv32])
                st(nv[l], v32[:], bv32, [bv32])
        if c == NCH - 1:
            st(nkT[l], k32[:, T - 128:T], bk32, [bk32])
        if KSUB < 1:
            return
        for qb in range(T // 128):
            for hh in range(2):
                hs = slice(64 * hh, 64 * hh + 64)
                qrhs = q16[hs, :, qb * 128:(qb + 1) * 128]
                first = (c == 0 and qb == 0)
                po = nps()
                mm(PS[po][:].rearrange("p (g q) -> p g q", g=4), kb16[hs, 128 + qb * 128:128 + (qb + 1) * 128], qrhs, True, True, [bkb, bq], [bPS[po]])
                act(pown[:], PS[po][:], AF.Exp, [bPS[po]], [bpown], scale=0.125)
                tt("pool", pown[:], pown[:], mk[:, 0, :], ALU.mult, [bpown, bmk], [bpown])
                if not first:
                    pp = nps()
                    mm(PS[pp][:].rearrange("p (g q) -> p g q", g=4), kb16[hs, qb * 128:(qb + 1) * 128], qrhs, True, True, [bkb, bq], [bPS[pp]])
                    act(pprev[:], PS[pp][:], AF.Exp, [bPS[pp]], [bpprev], scale=0.125)
                    tt("pool", pprev[:], pprev[:], mk[:, 1, :], ALU.mult, [bpprev, bmk], [bpprev])
                pO = nps()
                pD = nps()
                if not first:
                    mm(PS[pO][0:64, :], v16[:, qb, hs], pprev[:], True, False, [bv16, bpprev], [bPS[pO]])
                    mm(PS[pD][0:64, :], ones16[:, 0:64], pprev[:], True, False, [bones, bpprev], [bPS[pD]])
                mm(PS[pO][0:64, :], v16[:, qb + 1, hs], pown[:], first, True, [bv16, bpown], [bPS[pO]])
                mm(PS[pD][0:64, :], ones16[:, 0:64], pown[:], first, True, [bones, bpown], [bPS[pD]])
                tt("dve", den[:].rearrange("p (g q) -> p g q", g=4), PS[pD][0:64, :].rearrange("p (g q) -> p g q", g=4),
                   sk[:, 4 * hh:4 * hh + 4].unsqueeze(2).to_broadcast([64, 4, 128]), ALU.add, [bPS[pD], bsk], [btA, btB])
                P.add("dve", lambda e: e.reciprocal(out=den[:], in_=den[:]), reads=[btA, btB], writes=[btA, btB])
                tt("dve", att16[:, 4 * hh:4 * hh + 4, qb * 128:(qb + 1) * 128], PS[pO][0:64, :].rearrange("p (g q) -> p g q", g=4),
                   den[:].rearrange("p (g q) -> p g q", g=4), ALU.mult, [bPS[pO], btA, btB], [batt])
        cp("pool", kb16[:, 0:128], kb16[:, T:T + 128], [bkb], [bkb])
        cp("pool", v16[:, 0, :], v16[:, T // 128, :], [bv16], [bv16])
        if KSUB < 2:
            return
        pY = [6, 7]
        for sc in range(T // TS):
            c0 = sc * TS
            firstsub = (c == 0 and sc == 0)
            for ct in range(8):
                uh = ct // 4
                pr = nps()
                pim = nps()
                mm(PS[pr][:, 0:TS], LB[:, 0, ct, :], u16[:, uh, c0:c0 + TS], True, True, [bLB, bu16], [bPS[pr]])
                mm(PS[pim][:, 0:TS], LB[:, 1, ct, :], u16[:, uh, c0:c0 + TS], True, True, [bLB, bu16], [bPS[pim]])
                cs_ = cosT[:, ct, 0:TS]
                sn_ = sinT[:, ct, 0:TS]
                par = ssi[0] % 2
                ssi[0] += 1
                qA, bqA = sq[par][0], bsq[par][0]
                qB, bqB = sq[par][1], bsq[par][1]
                hA, bhA = sh16[par][0], bsh16[par][0]
                hB, bhB = sh16[par][1], bsh16[par][1]
                tt("dve", sx[0][:], PS[pr][:, 0:TS], cs_, ALU.mult, [bPS[pr], btab], [bsx[0]])
                tt("dve", sx[1][:], PS[pim][:, 0:TS], sn_, ALU.mult, [bPS[pim], btab], [bsx[1]])
                tt("dve", sx[0][:], sx[0][:], sx[1][:], ALU.add, [bsx[0], bsx[1]], [bsx[0]])
                tt("dve", sx[2][:], PS[pim][:, 0:TS], cs_, ALU.mult, [bPS[pim], btab], [bsx[2]])
                tt("dve", sx[3][:], PS[pr][:, 0:TS], sn_, ALU.mult, [bPS[pr], btab], [bsx[3]])
                tt("dve", sx[2][:], sx[2][:], sx[3][:], ALU.subtract, [bsx[2], bsx[3]], [bsx[2]])
                rbc = sp8[:, 5, ct:ct + 1].to_broadcast([128, TS])
                ire = 0.0 if firstsub else inre[:, ct:ct + 1]
                iim = 0.0 if firstsub else inim[:, ct:ct + 1]
                P.add("dve", lambda e, ire=ire, rbc=rbc, qA=qA: e.tensor_tensor_scan(out=qA[:], data0=rbc, data1=sx[0][:], initial=ire, op0=ALU.mult, op1=ALU.add),
                      reads=[bsp8, bsx[0], binit], writes=[bqA])
                P.add("dve", lambda e, iim=iim, rbc=rbc, qB=qB: e.tensor_tensor_scan(out=qB[:], data0=rbc, data1=sx[2][:], initial=iim, op0=ALU.mult, op1=ALU.add),
                      reads=[bsp8, bsx[2], binit], writes=[bqB])
                cp("pool", qlre[:, ct:ct + 1], qA[:, TS - 1:TS], [bqA], [bql])
                cp("pool", qlim[:, ct:ct + 1], qB[:, TS - 1:TS], [bqB], [bql])
                tt("pool", sx[6][:], qA[:], cs_, ALU.mult, [bqA, btab], [bsx[6]])
                tt("pool", sx[7][:], qB[:], sn_, ALU.mult, [bqB, btab], [bsx[7]])
                tt("pool", hA[:], sx[6][:], sx[7][:], ALU.subtract, [bsx[6], bsx[7]], [bhA])
                tt("pool", sx[4][:], qA[:], sn_, ALU.mult, [bqA, btab], [bsx[4]])
                tt("pool", sx[5][:], qB[:], cs_, ALU.mult, [bqB, btab], [bsx[5]])
                tt("pool", hB[:], sx[4][:], sx[5][:], ALU.add, [bsx[4], bsx[5]], [bhB])
                ot = ct // 4
                mm(PS[pY[ot]][:, c0:c0 + TS], LB[:, 2, ct, :], hA[:], ct % 4 == 0, False, [bLB, bhA], [bPS[pY[ot]]])
                mm(PS[pY[ot]][:, c0:c0 + TS], LB[:, 3, ct, :], hB[:], False, ct % 4 == 3, [bLB, bhB], [bPS[pY[ot]]])
            cl = cosT[:, :, TS - 1]
            sl = sinT[:, :, TS - 1]
            a0, a1 = sm8[0][:], sm8[1][:]
            tt("dve", a0, qlre[:], cl, ALU.mult, [bql, btab], [bsm8])
            tt("dve", a1, qlim[:], sl, ALU.mult, [bql, btab], [bsm8])
            tt("dve", hlre[:], a0, a1, ALU.subtract, [bsm8], [bhl])
            tt("dve", a0, qlre[:], sl, ALU.mult, [bql, btab], [bsm8])
            tt("dve", a1, qlim[:], cl, ALU.mult, [bql, btab], [bsm8])
            tt("dve", hlim[:], a0, a1, ALU.add, [bsm8], [bhl])
            tt("dve", a0, hlre[:], S8(6), ALU.mult, [bhl, bsp8], [bsm8])
            tt("dve", a1, hlim[:], S8(7), ALU.mult, [bhl, bsp8], [bsm8])
            tt("dve", inre[:], a0, a1, ALU.subtract, [bsm8], [binit])
            tt("dve", a0, hlre[:], S8(7), ALU.mult, [bhl, bsp8], [bsm8])
            tt("dve", a1, hlim[:], S8(6), ALU.mult, [bhl, bsp8], [bsm8])
            tt("dve", inim[:], a0, a1, ALU.add, [bsm8], [binit])
            if c == NCH - 1 and sc == T // TS - 1:
                st(nre[l], hlre[:], bhl, [bhl])
                st(nim[l], hlim[:], bhl, [bhl])
        for o in range(2):
            stt("dve", yss[:, o, 0:n], u32[:, o, 0:n], sp2[:, 0, o:o + 1], PS[pY[o]][:, 0:n], ALU.mult, ALU.add, [bu32, bsp2, bPS[pY[o]]], [byss])
        gelu_glu(n)
        if KSUB < 3:
            return
        conv_ln(lambda m, k: cb16[:, m, k:k + T], n, lambda ap: ap)
        if c == NCH - 1:
            st(ncv[l], cb32[:, :, T:T + 30], bcb32, [bcb32])
        for o in range(2):
            cp("pool", cb32[:, o, 0:30], cb32[:, o, T:T + 30], [bcb32], [bcb32])
        if KSUB < 4:
            return
        outproj_ffn(xs, bx, n, l, False, first=(c == 0))
        if l < nl - 1:
            st(xscr[:, t0:t0 + T].rearrange("(k p) t -> p k t", p=128), xs[:], bx[0], bx)
        else:
            rmsnorm(xs, bx, n, None, 0, 0, l, False)
            st(yT[:, t0:t0 + T].rearrange("(k p) t -> p k t", p=128), xs[:], bx[0], bx)

    def sample_layer(l):
        n = TSM
        bxs = [bxsm] * KT
        ld(kc16[:], kcT[l].rearrange("b f k -> f b k"), bkc, [bkc], q="pool")
        ld(vc16[:], vc[l].rearrange("b k f -> k b f"), bvc, [bvc], q="pool")
        ld(h0re[:], ssmre_in[l], bh0, [bh0])
        ld(h0im[:], ssmim_in[l], bh0, [bh0])
        ld(cbs32[:, :, :, 0:30], sconv_in[l], bcbs32, [bcbs32])
        dd = Buf(f"dd{l}")
        o1 = P.add("sp", lambda e: e.dma_start(out=nks_c[l], in_=kcn[l, :, 4:128, :]), dma_owner=dd)
        stores.append(o1)
        vcn_src = vc[l, :, 4:128, :]
        o2 = P.add("sp", lambda e: e.dma_start(out=nvs_c[l], in_=vcn_src), dma_owner=dd)
        stores.append(o2)
        rmsnorm(xsm, bxs, n, 1, 0, 0, l, True)
        for j in range(4):
            rope_pair(j, 5 + j, rps[:, 0, :], rps[:, 1, :], brps, q16[:, j, 0:n], bq, n)
        rope_pair(4, 9, rps[:, 0, :], rps[:, 1, :], brps, kb16[:, 128:128 + n], bkb, n, extra32=(k32[:, 0:n], bk32))
        st(skT[l], k32[:, 0:n], bk32, [bk32])
        for o in range(2):
            pi = inproj_tile(10 + o, n)
            cp("act", u32[:, o, 0:n], PS[pi][:, 0:n], [bPS[pi]], [bu32])
            cp("dve", u16[:, o, 0:n], PS[pi][:, 0:n], [bPS[pi]], [bu16])
        for o in range(2):
            pa = inproj_tile(12 + o, n)
            pg = inproj_tile(14 + o, n)
            act(tmpC[:, 0:n], PS[pg][:, 0:n], AF.Sigmoid, [bPS[pg]], [btC])
            tt("dve", cbs32[:, o, :, 30:34], PS[pa][:, 0:n].rearrange("p (b t) -> p b t", t=LS),
               tmpC[:, 0:n].rearrange("p (b t) -> p b t", t=LS), ALU.mult, [bPS[pa], btC], [bcbs32])
            cp("pool", cbs16[:, o], cbs32[:, o], [bcbs32], [bcbs16])
        st(scv[l], cbs32[:, :, :, 4:34], bcbs32, [bcbs32])
        pv = nps()
        for b in range(NB):
            for k in range(KT):
                mm(PS[pv][0:4, :].rearrange("p (b f) -> p b f", b=4)[:, b % 4, :] if False else PS[pv][0:4, (b % 4) * 128:(b % 4 + 1) * 128],
                   h16[:, k, 4 * b:4 * b + 4], win16[:, k, 2048:2176], k == 0, k == KT - 1, [bh, bwin], [bPS[pv]])
            if b % 4 == 3:
                g0 = b - 3
                cp("act", vn16[:, g0:g0 + 4, :], PS[pv][0:4, :].rearrange("p (b f) -> p b f", b=4), [bPS[pv]], [bvn16])
                cp("dve", vn32[:, g0:g0 + 4, :], PS[pv][0:4, :].rearrange("p (b f) -> p b f", b=4), [bPS[pv]], [bvn32])
                if b < NB - 1:
                    pv = nps()
        st(svn[l], vn32[:], bvn32, [bvn32])
        pC = nps()
        pN = nps()
        for b in range(NB):
            for hh in range(2):
                hs = slice(64 * hh, 64 * hh + 64)
                col = (b * 2 + hh) * 16
                qrhs = q16[hs, :, 4 * b:4 * b + 4]
                mm(PS[pC][:, col:col + 16].rearrange("p (g t) -> p g t", g=4), kc16[hs, b, :], qrhs, True, True, [bkc, bq], [bPS[pC]])
                mm(PS[pN][0:4, col:col + 16].rearrange("p (g t) -> p g t", g=4), kb16[hs, 128 + 4 * b:128 + 4 * b + 4], qrhs, True, True, [bkb, bq], [bPS[pN]])
        act(pown[:], PS[pC][:], AF.Exp, [bPS[pC]], [bpown], scale=0.125)
        tt("pool", pown[:], pown[:], mk[:, 2, :], ALU.mult, [bpown, bmk], [bpown])
        act(pn16[:], PS[pN][0:4, :], AF.Exp, [bPS[pN]], [bpn], scale=0.125)
        tt("pool", pn16[:], pn16[:], mk[0:4, 3, :], ALU.mult, [bpn, bmk], [bpn])
        pO = nps()
        pD = nps()
        for b in range(NB):
            for hh in range(2):
                hs = slice(64 * hh, 64 * hh + 64)
                col = (b * 2 + hh) * 16
                mm(PS[pO][0:64, col:col + 16], vc16[:, b, hs], pown[:, col:col + 16], True, False, [bvc, bpown], [bPS[pO]])
                mm(PS[pO][0:64, col:col + 16], vn16[0:4, b, hs], pn16[0:4, col:col + 16], False, True, [bvn16, bpn], [bPS[pO]])
                mm(PS[pD][0:64, col:col + 16], ones16[:, 0:64], pown[:, col:col + 16], True, False, [bones, bpown], [bPS[pD]])
                mm(PS[pD][0:64, col:col + 16], ones16[0:4, 0:64], pn16[0:4, col:col + 16], False, True, [bones, bpn], [bPS[pD]])
        tt("dve", den[:].rearrange("p (b h t) -> p b h t", b=NB, t=LS), PS[pD][0:64, :].rearrange("p (b h t) -> p b h t", b=NB, t=LS),
           sk[:, :].unsqueeze(1).unsqueeze(3).to_broadcast([64, NB, 8, LS]), ALU.add, [bPS[pD], bsk], [btA, btB])
        P.add("dve", lambda e: e.reciprocal(out=den[:], in_=den[:]), reads=[btA, btB], writes=[btA, btB])
        tt("dve", att16[:, :, 0:n].rearrange("p h (b t) -> p b h t", t=LS), PS[pO][0:64, :].rearrange("p (b h t) -> p b h t", b=NB, t=LS),
           den[:].rearrange("p (b h t) -> p b h t", b=NB, t=LS), ALU.mult, [bPS[pO], btA, btB], [batt])
        for (dst, x1, y1, x2, y2, op) in ((ahre, 8, h0re, 9, h0im, ALU.subtract), (ahim, 8, h0im, 9, h0re, ALU.add)):
            tt("dve", dst[:], y1[:], sp8[:, x1, :].unsqueeze(2).to_broadcast([128, 8, NB]), ALU.mult, [bh0, bsp8], [bah])
            tt("dve", hsre[:], y2[:], sp8[:, x2, :].unsqueeze(2).to_broadcast([128, 8, NB]), ALU.mult, [bh0, bsp8], [bhs])
            tt("dve", dst[:], dst[:], hsre[:], op, [bah, bhs], [bah])
        pY = [6, 7]
        for ct in range(8):
            uh = ct // 4
            pr = nps()
            pim = nps()
            mm(PS[pr][:, 0:n], LB[:, 0, ct, :], u16[:, uh, 0:n], True, True, [bLB, bu16], [bPS[pr]])
            mm(PS[pim][:, 0:n], LB[:, 1, ct, :], u16[:, uh, 0:n], True, True, [bLB, bu16], [bPS[pim]])
            cs_ = cosT[:, ct, 0:LS].unsqueeze(1).to_broadcast([128, NB, LS])
            sn_ = sinT[:, ct, 0:LS].unsqueeze(1).to_broadcast([128, NB, LS])
            V3 = lambda ap: ap.rearrange("p (b t) -> p b t", t=LS)
            X = [s_[:, 0:n] for s_ in sx]
            tt("dve", V3(X[0]), V3(PS[pr][:, 0:n]), cs_, ALU.mult, [bPS[pr], btab], [bsx[0]])
            tt("dve", V3(X[1]), V3(PS[pim][:, 0:n]), sn_, ALU.mult, [bPS[pim], btab], [bsx[1]])
            tt("pool", X[0], X[0], X[1], ALU.add, [bsx[0], bsx[1]], [bsx[0]])
            tt("dve", V3(X[2]), V3(PS[pim][:, 0:n]), cs_, ALU.mult, [bPS[pim], btab], [bsx[2]])
            tt("dve", V3(X[3]), V3(PS[pr][:, 0:n]), sn_, ALU.mult, [bPS[pr], btab], [bsx[3]])
            tt("pool", X[2], X[2], X[3], ALU.subtract, [bsx[2], bsx[3]], [bsx[2]])
            tt("dve", sx[0][:, 0:n:LS], sx[0][:, 0:n:LS], ahre[:, ct, :], ALU.add, [bsx[0], bah], [bsx[0]])
            tt("dve", sx[2][:, 0:n:LS], sx[2][:, 0:n:LS], ahim[:, ct, :], ALU.add, [bsx[2], bah], [bsx[2]])
            r4v = r4[:, ct].rearrange("p b t -> p (b t)")
            P.add("dve", lambda e, r4v=r4v, X=X: e.tensor_tensor_scan(out=X[4], data0=r4v, data1=X[0], initial=0.0, op0=ALU.mult, op1=ALU.add),
                  reads=[btab4, bsx[0]], writes=[bsx[4]])
            P.add("dve", lambda e, r4v=r4v, X=X: e.tensor_tensor_scan(out=X[5], data0=r4v, data1=X[2], initial=0.0, op0=ALU.mult, op1=ALU.add),
                  reads=[btab4, bsx[2]], writes=[bsx[5]])
            tt("pool", V3(X[6]), V3(X[4]), cs_, ALU.mult, [bsx[4], btab], [bsx[6]])
            tt("pool", V3(X[7]), V3(X[5]), sn_, ALU.mult, [bsx[5], btab], [bsx[7]])
            tt("pool", X[6], X[6], X[7], ALU.subtract, [bsx[6], bsx[7]], [bsx[6]])
            cp("act", hre16[:, 0:n], X[6], [bsx[6]], [bhre])
            cp("act", hsre[:, ct, :], sx[6][:, LS - 1:n:LS], [bsx[6]], [bhs])
            tt("dve", V3(X[1]), V3(X[4]), sn_, ALU.mult, [bsx[4], btab], [bsx[1]])
            tt("dve", V3(X[3]), V3(X[5]), cs_, ALU.mult, [bsx[5], btab], [bsx[3]])
            tt("dve", X[1], X[1], X[3], ALU.add, [bsx[1], bsx[3]], [bsx[1]])
            cp("act", him16[:, 0:n], X[1], [bsx[1]], [bhim])
            cp("act", hsim[:, ct, :], sx[1][:, LS - 1:n:LS], [bsx[1]], [bhs])
            ot = ct // 4
            mm(PS[pY[ot]][:, 0:n], LB[:, 2, ct, :], hre16[:, 0:n], ct % 4 == 0, False, [bLB, bhre], [bPS[pY[ot]]])
            mm(PS[pY[ot]][:, 0:n], LB[:, 3, ct, :], him16[:, 0:n], False, ct % 4 == 3, [bLB, bhim], [bPS[pY[ot]]])
        st(sre[l], hsre[:], bhs, [bhs])
        st(sim_o[l], hsim[:], bhs, [bhs])
        for o in range(2):
            stt("dve", yss[:, o, 0:n], u32[:, o, 0:n], sp2[:, 0, o:o + 1], PS[pY[o]][:, 0:n], ALU.mult, ALU.add, [bu32, bsp2, bPS[pY[o]]], [byss])
        gelu_glu(n)
        conv_ln(lambda m, k: cbs16[:, m, :, k:k + LS], n, lambda ap: ap.rearrange("p (b t) -> p b t", t=LS))
        if DBG and l == 0:
            dsb = xs[:, 0:6, :].rearrange("p (a k) (h t) -> p a (k h) t", k=2, t=TSM); bdsb = bx[0]
            memset("dve", dsb, 0.0, bx)
            cp("dve", dsb[0:64, 0, :, :], att16[:, :, 0:n], [batt, bdsb], [bdsb])
            cp("dve", dsb[:, 1, 0:2, :], os16[:, :, 0:n], [bos, bdsb], [bdsb])
            cp("dve", dsb[:, 2, 0:2, :], oc16[:, :, 0:n], [boc, bdsb], [bdsb])
            st(dbg.rearrange("a p h t -> p a h t"), dsb, bdsb, [bdsb])
        outproj_ffn(xsm, bxs, n, l, True)
        if l == nl - 1:
            rmsnorm(xsm, bxs, n, None, 0, 0, l, True)
            st(ysT.rearrange("(k p) t -> p k t", p=128), xsm[:], bxsm, [bxsm])

    STG = int(os.environ.get("KSTAGE", "9"))
    for l in range(nl):
        if STG >= 1:
            layer_params(l)
        for c in range(NCH):
            if STG >= 3 or (STG == 2 and c == 0):
                prompt_chunk(l, c)
        if STG >= 4:
            sample_layer(l)

    P.add("sp", lambda e: e.nop(), extra_deps=stores)
    P.emit()
    return nc


def _perm_win(w):
    q = w[:, 0:512].reshape(D, 8, 64)
    k = w[:, 512:640].reshape(D, 2, 64)
    v = w[:, 640:768]
    u = w[:, 768:1024]
    a = w[:, 1024:1280]
    g = w[:, 1280:1536]

    def swap(t):
        return np.concatenate([t[..., 32:], t[..., :32]], axis=-1)
    order = [0, 4, 1, 5, 2, 6, 3, 7]
    qt = q[:, order, :].reshape(D, 512)
    qs = swap(q)[:, order, :].reshape(D, 512)
    kt = k.reshape(D, 128)
    ks = swap(k).reshape(D, 128)
    return np.ascontiguousarray(np.concatenate([qt, kt, qs, ks, u, a, g, v], axis=1))


def _rope_tab(pos):
    half = 32
    inv = (np.float32(10000.0) ** (-(np.arange(half, dtype=np.float32) / np.float32(half)))).astype(np.float32)
    ang = (pos.astype(np.float32)[None, :] * inv[:, None]).astype(np.float32)
    c = np.cos(ang.astype(np.float64)).astype(np.float32)
    s = np.sin(ang.astype(np.float64)).astype(np.float32)
    cos = np.concatenate([c, c, c, c], axis=0)
    sins = np.concatenate([-s, s, -s, s], axis=0)
    return np.ascontiguousarray(np.stack([cos, sins], axis=0))


_NC_CACHE = {}


def kernel(**inp):
    f = lambda a: np.ascontiguousarray(np.asarray(a, dtype=np.float32))
    I = {k: np.asarray(v) for k, v in inp.items()}
    nlr = _NC_CACHE.get("nl", NL)
    if "nc" not in _NC_CACHE:
        _NC_CACHE["nc"] = build(nlr)
    nc = _NC_CACHE["nc"]

    def pk(a):
        return f(a.reshape(NL, 8, 128).transpose(0, 2, 1))

    def p2(a):
        return f(a.reshape(NL, 2, 128).transpose(0, 2, 1))

    shared = {
        "w_mod": f(I["w_mod"]),
        "b_modT": f(I["b_mod"].reshape(NL, 48, 128).transpose(0, 2, 1)),
        "g1T": pk(I["norm1_g"]), "g2T": pk(I["norm2_g"]),
        "gfT": f(I["final_norm_g"].reshape(8, 128).T),
        "w_in2": f(np.stack([_perm_win(I["w_in"][l]) for l in range(NL)])),
        "ropeP": _rope_tab(np.arange(NTOK)),
        "ropeS": _rope_tab(PAST + np.tile(np.arange(LS), NB)),
        "sinkT": f(np.broadcast_to(I["attn_sinks"][:, None, :], (NL, 64, 8))),
        "lamre": f(I["ssm_lam_re"].reshape(NL, 8, 128).transpose(0, 2, 1)),
        "lamim": f(I["ssm_lam_im"].reshape(NL, 8, 128).transpose(0, 2, 1)),
        "logdt": f(np.repeat(I["ssm_log_dt"], 64, axis=1).reshape(NL, 8, 128).transpose(0, 2, 1)),
        "bre": f(I["ssm_b_re"].reshape(NL, 8, 128, 16).transpose(0, 2, 1, 3)),
        "bim": f(I["ssm_b_im"].reshape(NL, 8, 128, 16).transpose(0, 2, 1, 3)),
        "cre": f(I["ssm_c_re"].reshape(NL, 8, 2, 16, 64).transpose(0, 2, 4, 1, 3).reshape(NL, 128, 8, 16)),
        "cim": f(I["ssm_c_im"].reshape(NL, 8, 2, 16, 64).transpose(0, 2, 4, 1, 3).reshape(NL, 128, 8, 16)),
        "dskipT": p2(I["ssm_d"]), "wglu": f(I["ssm_w_glu"]), "bgluT": p2(I["ssm_b_glu"]),
        "convwT": f(I["conv_w"].reshape(NL, 31, 2, 128).transpose(0, 3, 2, 1)),
        "convbT": p2(I["conv_b"]), "lngT": p2(I["conv_ln_g"]), "lnbT": p2(I["conv_ln_b"]),
        "w_out": f(I["w_out"]), "w_gate": f(I["w_gate"]), "w_up": f(I["w_up"]), "w_down": f(I["w_down"]),
        "identd": np.eye(128, dtype=np.float32),
        "jrow": f(np.broadcast_to(np.arange(TS + 1, dtype=np.float32)[None, :], (128, TS + 1))),
    }
    kk = np.arange(128)[:, None]
    qq = np.tile(np.arange(128), 4)[None, :]
    m_own = (qq >= kk).astype(np.float32)
    m_prev = (kk > qq).astype(np.float32)
    shared["maskP"] = f(np.stack([m_own, m_prev]))
    tq = np.tile(np.arange(LS), 128)[None, :]
    m_c = (kk > tq).astype(np.float32)
    m_n = (kk <= tq).astype(np.float32)
    shared["maskS"] = f(np.stack([m_c, m_n]))

    in_maps = []
    for c in range(8):
        b = c % 4
        sbs = slice(NB * c, NB * (c + 1))
        m = dict(shared)
        m["xT"] = f(I["x_prompt"][b].T)
        m["xsT"] = f(I["x_sample"][sbs].reshape(TSM, D).T)
        m["cT"] = f(np.concatenate([I["c_prompt"][b][None, :], I["c_sample"][sbs]], axis=0).T)
        ck = I["cache_k"][:, sbs].reshape(NL, NB, 128, 128)
        cv = I["cache_v"][:, sbs].reshape(NL, NB, 128, 128)
        m["kcT"] = f(ck.transpose(0, 1, 3, 2))
        m["kcn"] = f(ck)
        m["vc"] = f(cv)
        m["ssmre_in"] = f(I["state_ssm_re"][:, sbs].reshape(NL, NB, 8, 128).transpose(0, 3, 2, 1))
        m["ssmim_in"] = f(I["state_ssm_im"][:, sbs].reshape(NL, NB, 8, 128).transpose(0, 3, 2, 1))
        m["sconv_in"] = f(I["state_conv"][:, sbs].reshape(NL, NB, 30, 2, 128).transpose(0, 4, 3, 1, 2))
        in_maps.append(m)

    import os
    ncr = int(os.environ.get("KCORES", "8"))
    res = run_bass_kernel_spmd(nc, in_maps[:ncr], core_ids=list(range(ncr)))
    R = list(res.results)
    while len(R) < 8:
        R.append(R[0])

    y_prompt = np.stack([R[b]["yT"].T for b in range(4)])
    y_sample = np.concatenate([R[c]["ysT"].T.reshape(NB, LS, D) for c in range(8)], axis=0)
    nk_p = np.stack([R[b]["nkT"].transpose(0, 2, 1).reshape(NL, 128, 2, 64) for b in range(4)], axis=1)
    nv_p = np.stack([R[b]["nv"].reshape(NL, 128, 2, 64) for b in range(4)], axis=1)

    def unst(a):
        return a.transpose(0, 2, 1).reshape(NL, 16, 64)
    re_p = np.stack([unst(R[b]["nre"]) for b in range(4)], axis=1)
    im_p = np.stack([unst(R[b]["nim"]) for b in range(4)], axis=1)
    cv_p = np.stack([R[b]["ncv"].transpose(0, 3, 2, 1).reshape(NL, 30, 256) for b in range(4)], axis=1)
    nk_s, nv_s, re_s, im_s, cv_s = [], [], [], [], []
    for c in range(8):
        r = R[c]
        knew = r["skT"].transpose(0, 2, 1).reshape(NL, NB, LS, 128)
        nk_s.append(np.concatenate([r["nks_c"], knew], axis=2).reshape(NL, NB, 128, 2, 64))
        vnew = r["svn"].transpose(0, 2, 1, 3)
        nv_s.append(np.concatenate([r["nvs_c"], vnew], axis=2).reshape(NL, NB, 128, 2, 64))
        re_s.append(r["sre"].transpose(0, 3, 2, 1).reshape(NL, NB, 16, 64))
        im_s.append(r["sim_o"].transpose(0, 3, 2, 1).reshape(NL, NB, 16, 64))
        cv_s.append(r["scv"].transpose(0, 3, 4, 2, 1).reshape(NL, NB, 30, 256))
    cat = lambda xs_: np.ascontiguousarray(np.concatenate(xs_, axis=1).astype(np.float32))
    if "dbg" in R[0]:
        _NC_CACHE["dbg"] = R[0]["dbg"]
    outs = (y_prompt, y_sample, nk_p, nv_p, re_p, im_p, cv_p, cat(nk_s), cat(nv_s), cat(re_s), cat(im_s), cat(cv_s))
    return tuple(np.ascontiguousarray(o.astype(np.float32)) for o in outs)
```

```python
import math
import numpy as np
import concourse.bass as bass
import concourse.mybir as mybir
from concourse.bass_utils import run_bass_kernel_spmd

F32 = mybir.dt.float32
BF16 = mybir.dt.bfloat16
ALU = mybir.AluOpType
AF = mybir.ActivationFunctionType

SEG = 30000
NL = 4
D = 1024
KT = 8
NTOK = 4096
T = 256
NCH = NTOK // T
TS = 128
NB = 16
LS = 4
TSM = NB * LS
DFF = 2816
JT = 22
WIN = 2176
PAST = 8192
TWO_PI = 2.0 * math.pi


class Buf:
    __slots__ = ("name", "lw", "rd", "sem", "cnt", "excl")

    def __init__(self, name, excl=False):
        self.name = name
        self.excl = excl
        self.lw = None
        self.rd = {}
        self.sem = None
        self.cnt = 0


class Op:
    __slots__ = ("eng", "fn", "deps", "idx", "dma", "owner", "dcnt", "marked", "ev", "waits")

    def __init__(self, eng, fn, idx):
        self.eng = eng
        self.fn = fn
        self.idx = idx
        self.deps = []
        self.dma = False
        self.owner = None
        self.dcnt = 0
        self.marked = False
        self.ev = None
        self.waits = []


class Prog:
    ENGS = ("pe", "act", "dve", "pool", "sp")

    def __init__(self, nc):
        self.nc = nc
        self.ops = []

    def add(self, eng, fn, reads=(), writes=(), dma_owner=None, extra_deps=()):
        i = len(self.ops)
        op = Op(eng, fn, i)
        if dma_owner is not None:
            op.dma = True
            op.owner = dma_owner
            dma_owner.cnt += 16
            op.dcnt = dma_owner.cnt
        deps = {}
        for b in reads:
            if b.lw is not None:
                deps[b.lw] = "raw"
            if b.excl:
                for r in b.rd.values():
                    if r not in deps:
                        deps[r] = "war"
        for b in writes:
            if b.lw is not None and b.lw not in deps:
                deps[b.lw] = "waw"
            for r in b.rd.values():
                if r not in deps:
                    deps[r] = "war"
        for d in extra_deps:
            deps[d.idx] = "raw"
        deps.pop(i, None)
        for b in reads:
            key = ("d", i) if op.dma else eng
            b.rd[key] = i
        for b in writes:
            b.lw = i
            b.rd = {}
        op.deps = list(deps.items())
        self.ops.append(op)
        return op

    def finalize(self):
        ops = self.ops
        waited = {e: {} for e in self.ENGS}
        for op in ops:
            need = {}
            for d, kind in op.deps:
                p = ops[d]
                if p.dma:
                    key = ("dma", id(p.owner))
                    if need.get(key, (0, None))[0] < p.dcnt:
                        need[key] = (p.dcnt, p)
                else:
                    if p.eng == op.eng and not op.dma:
                        if op.eng == "pe" or kind != "raw":
                            continue
                    key = ("eng", p.eng)
                    if need.get(key, (-1, None))[0] < p.idx:
                        need[key] = (p.idx, p)
            w = waited[op.eng]
            for key, (val, p) in need.items():
                if w.get(key, -1) >= val:
                    continue
                w[key] = val
                op.waits.append(p)
                if not p.dma:
                    p.marked = True
        cnt = {e: 0 for e in self.ENGS}
        for op in ops:
            if not op.dma and op.marked:
                cnt[op.eng] += 1
                op.ev = cnt[op.eng]
        self.evcount = cnt

    def emit(self):
        nc = self.nc
        self.finalize()
        esems = {}
        for e in self.ENGS:
            n = (self.evcount[e] + SEG - 1) // SEG
            esems[e] = [nc.alloc_semaphore(f"ev_{e}_{k}") for k in range(max(n, 1))]
        for op in self.ops:
            if op.dma and op.owner.sem is None:
                op.owner.sem = nc.alloc_semaphore("d_" + op.owner.name)

        def semval(p):
            if p.dma:
                return p.owner.sem, p.dcnt
            k = (p.ev - 1) // SEG
            return esems[p.eng][k], (p.ev - 1) % SEG + 1

        per = {e: [op for op in self.ops if op.eng == e] for e in self.ENGS}

        def run(eng, lst):
            for op in lst:
                for p in op.waits:
                    s, v = semval(p)
                    eng.wait_ge(s, v)
                ins = op.fn(eng)
                if op.dma:
                    ins.then_inc(op.owner.sem, 16)
                elif op.marked:
                    s, _ = semval(op)
                    ins.then_inc(s, 1)

        with nc.Block() as block:
            @block.tensor
            def _(e):
                run(e, per["pe"])

            @block.scalar
            def _(e):
                run(e, per["act"])

            @block.vector
            def _(e):
                run(e, per["dve"])

            @block.gpsimd
            def _(e):
                run(e, per["pool"])

            @block.sync
            def _(e):
                run(e, per["sp"])


def build(nl=NL):
    import os
    KSUB = int(os.environ.get("KSUB", "9"))
    KS2 = int(os.environ.get("KS2", "9"))
    nc = bass.Bass("TRN2", target_bir_lowering=False)
    P = Prog(nc)
    stores = []

    def din(name, shape):
        return nc.dram_tensor(name, list(shape), F32, kind="ExternalInput").ap()

    def dout(name, shape):
        return nc.dram_tensor(name, list(shape), F32, kind="ExternalOutput").ap()

    def sb(name, shape, dt=F32):
        return nc.alloc_sbuf_tensor(name, list(shape), dt)

    xT = din("xT", [D, NTOK])
    xsT = din("xsT", [D, TSM])
    cT = din("cT", [D, 17])
    w_mod = din("w_mod", [NL, D, 6 * D])
    b_modT = din("b_modT", [NL, 128, 48])
    g1T = din("g1T", [NL, 128, KT])
    g2T = din("g2T", [NL, 128, KT])
    gfT = din("gfT", [128, KT])
    w_in2 = din("w_in2", [NL, D, WIN])
    ropeP = din("ropeP", [2, 128, NTOK])
    ropeS = din("ropeS", [2, 128, TSM])
    maskP = din("maskP", [2, 128, 512])
    maskS = din("maskS", [2, 128, 512])
    sinkT = din("sinkT", [NL, 64, 8])
    lamre = din("lamre", [NL, 128, 8])
    lamim = din("lamim", [NL, 128, 8])
    logdt = din("logdt", [NL, 128, 8])
    bre = din("bre", [NL, 128, 8, 16])
    bim = din("bim", [NL, 128, 8, 16])
    cre = din("cre", [NL, 128, 8, 16])
    cim = din("cim", [NL, 128, 8, 16])
    dskipT = din("dskipT", [NL, 128, 2])
    wglu = din("wglu", [NL, 256, 256])
    bgluT = din("bgluT", [NL, 128, 2])
    convwT = din("convwT", [NL, 128, 2, 31])
    convbT = din("convbT", [NL, 128, 2])
    lngT = din("lngT", [NL, 128, 2])
    lnbT = din("lnbT", [NL, 128, 2])
    w_out = din("w_out", [NL, D, D])
    w_gate = din("w_gate", [NL, D, DFF])
    w_up = din("w_up", [NL, D, DFF])
    w_down = din("w_down", [NL, DFF, D])
    kcT = din("kcT", [NL, NB, 128, 128])
    vc = din("vc", [NL, NB, 128, 128])
    kcn = din("kcn", [NL, NB, 128, 128])
    ssmre_in = din("ssmre_in", [NL, 128, 8, NB])
    ssmim_in = din("ssmim_in", [NL, 128, 8, NB])
    sconv_in = din("sconv_in", [NL, 128, 2, NB, 30])
    identd = din("identd", [128, 128])
    jrow = din("jrow", [128, TS + 1])

    yT = dout("yT", [D, NTOK])
    ysT = dout("ysT", [D, TSM])
    nkT = dout("nkT", [NL, 128, 128])
    nv = dout("nv", [NL, 128, 128])
    nre = dout("nre", [NL, 128, 8])
    nim = dout("nim", [NL, 128, 8])
    ncv = dout("ncv", [NL, 128, 2, 30])
    nks_c = dout("nks_c", [NL, NB, 124, 128])
    nvs_c = dout("nvs_c", [NL, NB, 124, 128])
    skT = dout("skT", [NL, 128, TSM])
    svn = dout("svn", [NL, 4, NB, 128])
    sre = dout("sre", [NL, 128, 8, NB])
    sim_o = dout("sim_o", [NL, 128, 8, NB])
    scv = dout("scv", [NL, 128, 2, NB, 30])
    xscr = nc.dram_tensor("xscr", [D, NTOK], F32, kind="Internal").ap()
    wgu_c = nc.dram_tensor("wgu_c", [JT, 128, 2 * KT * 128], BF16, kind="Internal").ap()
    wdn_c = nc.dram_tensor("wdn_c", [16, 128, 11 * 128], BF16, kind="Internal").ap()
    woa_c = nc.dram_tensor("woa_c", [8, 64, 8 * 128], BF16, kind="Internal").ap()
    wor_c = nc.dram_tensor("wor_c", [8, 128, 4 * 128], BF16, kind="Internal").ap()
    bwgu_c = [Buf(f"wguc{j}") for j in range(JT)]
    bwdn_c = [Buf(f"wdnc{j}") for j in range(16)]
    bwo_c = [Buf(f"woc{j}") for j in range(8)]
    DBG = bool(int(os.environ.get("KDBG", "0")))
    if DBG:
        dbg = dout("dbg", [3, 128, 8, TSM])

    PS = [nc.alloc_psum_tensor(f"ps{i}", [128, 512], F32) for i in range(8)]
    bPS = [Buf(f"ps{i}", excl=True) for i in range(8)]
    psrr = [0]

    def nps():
        i = psrr[0]
        psrr[0] = (i + 1) % 6
        return i

    def mm(out, lhsT, rhs, start, stop, r, w):
        return P.add("pe", lambda e: e.matmul(out, lhsT=lhsT, rhs=rhs, start=start, stop=stop), reads=r, writes=w)

    def act(out, in_, func, r, w, bias=None, scale=None):
        kw = {}
        if bias is not None:
            kw["bias"] = bias
        if scale is not None:
            kw["scale"] = scale
        return P.add("act", lambda e: e.activation(out=out, in_=in_, func=func, **kw), reads=r, writes=w)

    def tt(eng, out, in0, in1, op, r, w):
        return P.add(eng, lambda e: e.tensor_tensor(out=out, in0=in0, in1=in1, op=op), reads=r, writes=w)

    def ts(eng, out, in0, s1, s2, op0, op1, r, w):
        if op1 is None:
            return P.add(eng, lambda e: e.tensor_scalar(out=out, in0=in0, scalar1=s1, scalar2=None, op0=op0), reads=r, writes=w)
        return P.add(eng, lambda e: e.tensor_scalar(out=out, in0=in0, scalar1=s1, scalar2=s2, op0=op0, op1=op1), reads=r, writes=w)

    def stt(eng, out, in0, scalar, in1, op0, op1, r, w):
        return P.add(eng, lambda e: e.scalar_tensor_tensor(out=out, in0=in0, scalar=scalar, in1=in1, op0=op0, op1=op1), reads=r, writes=w)

    def cp(eng, out, in_, r, w):
        if eng == "act":
            return P.add("act", lambda e: e.activation(out=out, in_=in_, func=AF.Copy), reads=r, writes=w)
        return P.add(eng, lambda e: e.tensor_copy(out=out, in_=in_), reads=r, writes=w)

    def memset(eng, ap, val, w):
        return P.add(eng, lambda e: e.memset(ap, val), writes=w)

    def ld(out, in_, owner, w, q="sp"):
        return P.add(q, lambda e: e.dma_start(out=out, in_=in_), writes=w, dma_owner=owner)

    def st(out, in_, owner, r):
        o = P.add("sp", lambda e: e.dma_start(out=out, in_=in_), reads=r, dma_owner=owner)
        stores.append(o)
        return o

    ident = sb("ident", [128, 128]); bident = Buf("ident")
    ld(ident[:], identd, bident, [bident])
    ones16 = sb("ones16", [128, 128], BF16); bones = Buf("ones")
    memset("dve", ones16[:], 1.0, [bones])
    jr = sb("jr", [128, TS + 1]); bjr = Buf("jr")
    ld(jr[:], jrow, bjr, [bjr])
    mk = sb("mk", [128, 4, 512], BF16); bmk = Buf("mk")
    mkn = mk[:, 0:2, :]; bmkn = bmk
    ld(mk[:, 0:2, :], maskP.rearrange("a p n -> p a n"), bmk, [bmk], q="pool")
    ld(mk[:, 2:4, :], maskS.rearrange("a p n -> p a n"), bmk, [bmk], q="pool")
    ident16 = sb("ident16", [128, 128], BF16); bident16 = Buf("ident16")
    cp("dve", ident16[:], ident[:], [bident], [bident16])
    rps = sb("rps", [128, 2, TSM]); brps = Buf("rps")
    ld(rps[:], ropeS.rearrange("a p n -> p a n"), brps, [brps])
    gf = sb("gf", [128, KT]); bgf = Buf("gf")
    ld(gf[:], gfT, bgf, [bgf])
    pat = sb("pat", [128, NB, LS]); bpat = Buf("pat")
    memset("dve", pat[:], 1.0, [bpat])
    memset("dve", pat[:, :, 0:1], 0.0, [bpat])

    modT = sb("modT", [128, NL, 48, 17]); bmod = Buf("modT")
    csb = sb("csb", [128, KT, 17]); bcs = Buf("csb")
    sgc = sb("sgc", [128, KT, 17]); bsgc = Buf("sgc")
    ld(csb[:], cT.rearrange("(k p) n -> p k n", p=128), bcs, [bcs])
    act(sgc[:], csb[:], AF.Sigmoid, [bcs], [bsgc])
    tt("dve", csb[:], csb[:], sgc[:], ALU.mult, [bcs, bsgc], [bcs])
    bmt = sb("bmt", [128, NL, 48]); bbmt = Buf("bmt")
    ld(bmt[:], b_modT.rearrange("l p m -> p l m"), bbmt, [bbmt])
    WMB = 256
    _g0 = nc.sbuf_tensor("wmr0", [128, KT, WMB], F32)
    _g1 = nc.sbuf_tensor("wmr1", [128, KT, WMB], F32)
    wmr = [_g0.__enter__(), _g1.__enter__()]
    bwmr = [Buf(f"wmr{i}") for i in range(2)]
    lastmod = None
    it = 0
    for l in range(nl):
        for blk in range(6 * D // WMB):
            s = it % 2
            it += 1
            ld(wmr[s][:], w_mod[l, :, blk * WMB:(blk + 1) * WMB].rearrange("(k p) n -> p k n", p=128), bwmr[s], [bwmr[s]])
            for mi in range(WMB // 128):
                m = blk * (WMB // 128) + mi
                pi = nps()
                for k in range(KT):
                    mm(PS[pi][:, 0:17], wmr[s][:, k, mi * 128:(mi + 1) * 128], csb[:, k, :], k == 0, k == KT - 1,
                       [bwmr[s], bcs], [bPS[pi]])
                lastmod = ts("dve", modT[:, l, m, :], PS[pi][:, 0:17], bmt[:, l, m:m + 1], None, ALU.add, None, [bPS[pi], bbmt], [bmod])
    _g1.__exit__(None, None, None)
    _g0.__exit__(None, None, None)
    for _e in ("pe", "act", "pool", "sp"):
        P.add(_e, lambda e: e.nop(), extra_deps=[lastmod])

    xs = sb("xs", [128, KT, T]); bx = [Buf(f"x{k}") for k in range(KT)]
    xsm = sb("xsm", [128, KT, TSM]); bxsm = Buf("xsm")
    ld(xsm[:], xsT.rearrange("(k p) n -> p k n", p=128), bxsm, [bxsm])
    h16 = sb("h16", [128, KT, T], BF16); bh = Buf("h16")
    scr16 = sb("scr16", [128, JT, T], BF16); bscr = Buf("scr16")
    rstd = sb("rstd", [128, T]); brstd = Buf("rstd")
    tmpAB = sb("tmpAB", [128, 2, T])
    tmpA = tmpAB[:, 0, :]; btA = Buf("tmpA")
    tmpB = tmpAB[:, 1, :]; btB = Buf("tmpB")
    tmpC = sb("tmpC", [128, T]); btC = Buf("tmpC")
    tmpD = sb("tmpD", [128, T]); btD = Buf("tmpD")
    rp = sb("rp", [128, 2, T]); brp = Buf("rp")
    q16 = sb("q16", [128, 4, T], BF16); bq = Buf("q16")
    kb16 = sb("kb16", [128, 128 + T], BF16); bkb = Buf("kb16")
    k32 = sb("k32", [128, T]); bk32 = Buf("k32")
    v16 = sb("v16", [128, 1 + T // 128, 128], BF16); bv16 = Buf("v16")
    v32 = sb("v32", [128, 128]); bv32 = Buf("v32")
    pown = sb("pown", [128, 512], BF16); bpown = Buf("pown")
    pprev = sb("pprev", [128, 512], BF16); bpprev = Buf("pprev")
    den = tmpAB[0:64].rearrange("p a t -> p (a t)")
    att16 = sb("att16", [64, 8, T], BF16); batt = Buf("att16")
    u32 = sb("u32", [128, 2, T]); bu32 = Buf("u32")
    u16 = sb("u16", [128, 2, T], BF16); bu16 = Buf("u16")
    cb32 = sb("cb32", [128, 2, 30 + T]); bcb32 = Buf("cb32")
    cb16 = sb("cb16", [128, 2, 30 + T], BF16); bcb16 = Buf("cb16")
    cbs32 = sb("cbs32", [128, 2, NB, 34]); bcbs32 = Buf("cbs32")
    cbs16 = sb("cbs16", [128, 2, NB, 34], BF16); bcbs16 = Buf("cbs16")
    ycf = sb("ycf", [128, 2, T]); bycf = Buf("ycf")
    yc16 = scr16[:, 8:12, :]; byc16 = bscr
    oc16 = sb("oc16", [128, 2, T], BF16); boc = Buf("oc16")
    os16 = sb("os16", [128, 2, T], BF16); bos = Buf("os16")
    yss = sb("yss", [128, 2, T]); byss = Buf("yss")
    z32 = sb("z32", [128, 2, T]); bz32 = Buf("z32")
    assert 2 * T == 4 * 8 * 16
    z16 = sb("z16", [128, 2, T], BF16); bz16 = Buf("z16")
    sx = [sb(f"sx{i}", [128, TS]) for i in range(8)]
    bsx = [Buf(f"sx{i}") for i in range(8)]
    sq = [[sb(f"sq{i}{j}", [128, TS]) for j in range(2)] for i in range(2)]
    bsq = [[Buf(f"sq{i}{j}") for j in range(2)] for i in range(2)]
    sh16 = [[sb(f"sh16{i}{j}", [128, TS], BF16) for j in range(2)] for i in range(2)]
    bsh16 = [[Buf(f"sh16{i}{j}") for j in range(2)] for i in range(2)]
    ssi = [0]
    hre16 = sh16[0][0]; bhre = bsh16[0][0]
    him16 = sh16[0][1]; bhim = bsh16[0][1]
    qlre = sb("qlre", [128, 8]); qlim = sb("qlim", [128, 8]); bql = Buf("ql")
    hlre = sb("hlre", [128, 8]); hlim = sb("hlim", [128, 8]); bhl = Buf("hl")
    inre = sb("inre", [128, 8]); inim = sb("inim", [128, 8]); binit = Buf("init")
    sm8 = [sb(f"sm8_{i}", [128, 8]) for i in range(4)]; bsm8 = Buf("sm8")
    h0re = sb("h0re", [128, 8, NB]); h0im = sb("h0im", [128, 8, NB]); bh0 = Buf("h0")
    ahre = sb("ahre", [128, 8, NB]); ahim = sb("ahim", [128, 8, NB]); bah = Buf("ah")
    hsre = sb("hsre", [128, 8, NB]); hsim = sb("hsim", [128, 8, NB]); bhs = Buf("hs")
    st16 = [sb(f"st16_{i}", [128, NB]) for i in range(4)]; bst16 = Buf("st16")
    r_kc = sb("r_kc", [128, 2048], BF16); bkc = Buf("kc16")
    r_vc = sb("r_vc", [128, 2048], BF16); bvc = Buf("vc16")
    kc16 = r_kc[:].rearrange("p (b k) -> p b k", k=128)
    vc16 = r_vc[:].rearrange("p (b k) -> p b k", k=128)
    vn16 = sb("vn16", [4, NB, 128], BF16); bvn16 = Buf("vn16")
    vn32 = sb("vn32", [4, NB, 128]); bvn32 = Buf("vn32")
    pn16 = sb("pn16", [4, 512], BF16); bpn = Buf("pn16")

    win16 = sb("win16", [128, KT, WIN], BF16); bwin = Buf("win16")
    woa = [sb(f"woa{i}", [64, 8, 128], BF16) for i in range(2)]
    wor = [sb(f"wor{i}", [128, 4, 128], BF16) for i in range(2)]
    bwo = [Buf(f"wo{i}") for i in range(2)]
    woi = [0]
    wgl16 = sb("wgl16", [128, 2, 256], BF16); bwgl = Buf("wgl16")
    sp8 = sb("sp8", [128, 12, 8]); bsp8 = Buf("sp8")
    sp2 = sb("sp2", [128, 8, 2]); bsp2 = Buf("sp2")
    cw = sb("cw", [128, 2, 31]); bcw = Buf("cw")
    sk = sb("sk", [64, 8]); bsk = Buf("sk")
    bc32 = yss[:].rearrange("p a (b c) -> p (a b) c", c=16).rearrange("p (a b) c -> p a b c", a=4); bbc = byss
    bbt = z32[:].rearrange("p a (b c) -> p (a b) c", c=16).rearrange("p (a b) c -> p a b c", a=4); bbbt = bz32
    exq = sb("exq", [128, 128]); bexq = Buf("exq")
    LB = sb("LB", [128, 4, 8, 128], BF16); bLB = Buf("LB")
    cosT = sb("cosT", [128, 8, TS + 1]); sinT = sb("sinT", [128, 8, TS + 1]); btab = Buf("tab")
    r4 = sb("r4", [128, 8, NB, LS]); btab4 = Buf("tab4")
    diag16 = sb("diag16", [128, 2, 31, 128], BF16); bdiag = Buf("diag16")
    gsc = sb("gsc", [128, 2, KT, 17]); bgsc = Buf("gsc")
    NG = 5
    _raw = [sb("wr0", [128, 2048], BF16), sb("wr1", [128, 2048], BF16), r_kc, r_vc, sb("wr4", [128, 2048], BF16)]
    bwgu = [Buf("wr0"), Buf("wr1"), bkc, bvc, Buf("wr4")]
    wfl = [r[:] for r in _raw]
    wgu = [r[:].rearrange("p (a k c) -> p a k c", a=2, k=KT) for r in _raw]
    wdn = [r[:, 0:1408].rearrange("p (j c) -> p j c", c=128) for r in _raw]
    bwdn = bwgu
    ffi = [0, 0]

    def S8(i):
        return sp8[:, i, :]

    angt = tmpC[:, 0:TS + 1]; angk = tmpD[:, 0:TS + 1]; bang = btC
    CM = 12582912.0

    def sin_of(out, x, shift, tmp, r, w, wt):
        xs_ = x
        if shift != 0.0:
            ts("dve", out, x, shift, None, ALU.add, None, r, w)
            xs_ = out
        ts("dve", tmp, xs_, 1.0 / TWO_PI, CM, ALU.mult, ALU.add, r + w, wt)
        ts("dve", tmp, tmp, -CM, None, ALU.add, None, r + wt, wt)
        stt("dve", tmp, tmp, -TWO_PI, xs_, ALU.mult, ALU.add, r + w + wt, wt)
        ts("dve", tmp, tmp, math.pi, -math.pi, ALU.min, ALU.max, r + wt, wt)
        act(out, tmp, AF.Sin, r + wt, w)

    def layer_params(l):
        for k in range(KT):
            ld(win16[:, k, :], w_in2[l, k * 128:(k + 1) * 128, :], bwin, [bwin], q="pool")
        ld(wgl16[:], wglu[l].rearrange("(k p) n -> p k n", p=128), bwgl, [bwgl], q="pool")
        ld(sp8[:, 0, :], lamre[l], bsp8, [bsp8])
        ld(sp8[:, 1, :], lamim[l], bsp8, [bsp8])
        ld(sp8[:, 2, :], logdt[l], bsp8, [bsp8])
        ld(sp2[:, 0, :], dskipT[l], bsp2, [bsp2])
        ld(sp2[:, 1, :], bgluT[l], bsp2, [bsp2])
        ld(sp2[:, 2, :], convbT[l], bsp2, [bsp2])
        ld(sp2[:, 3, :], lngT[l], bsp2, [bsp2])
        ld(sp2[:, 4, :], lnbT[l], bsp2, [bsp2])
        ld(cw[:], convwT[l], bcw, [bcw])
        ld(sk[:], sinkT[l], bsk, [bsk])
        act(sk[:], sk[:], AF.Exp, [bsk], [bsk])
        ld(bc32[:, 0], bre[l], bbc, [bbc])
        ld(bc32[:, 1], bim[l], bbc, [bbc])
        ld(bc32[:, 2], cre[l], bbc, [bbc])
        ld(bc32[:, 3], cim[l], bbc, [bbc])
        for a, (gT, off) in enumerate(((g1T, 8), (g2T, 32))):
            ld(sp8[:, 3, :], gT[l], bsp8, [bsp8])
            ts("dve", gsc[:, a], modT[:, l, off:off + 8, :], 1.0, None, ALU.add, None, [bmod], [bgsc])
            tt("dve", gsc[:, a], gsc[:, a], sp8[:, 3, :].unsqueeze(2).to_broadcast([128, KT, 17]), ALU.mult, [bgsc, bsp8], [bgsc])
        R = [bsp8]
        W = [bsp8]
        act(S8(2), S8(2), AF.Exp, R, W)
        tt("dve", S8(3), S8(0), S8(2), ALU.mult, R, W)
        tt("dve", S8(4), S8(1), S8(2), ALU.mult, R, W)
        act(S8(5), S8(3), AF.Exp, R, W)
        sin_of(S8(7), S8(4), 0.0, sm8[0][:], R + [bsm8], W, [bsm8])
        sin_of(S8(6), S8(4), 0.5 * math.pi, sm8[0][:], R + [bsm8], W, [bsm8])
        tt("dve", S8(8), S8(5), S8(6), ALU.mult, R, W)
        tt("dve", S8(9), S8(5), S8(7), ALU.mult, R, W)
        a0, a1, a2, a3 = (sm8[i][:] for i in range(4))
        R2 = [bsp8, bsm8]
        tt("dve", a0, S8(0), S8(0), ALU.mult, R2, [bsm8])
        tt("dve", a1, S8(1), S8(1), ALU.mult, R2, [bsm8])
        tt("dve", a0, a0, a1, ALU.add, R2, [bsm8])
        P.add("dve", lambda e: e.reciprocal(out=a0, in_=a0), reads=R2, writes=[bsm8])
        ts("dve", a1, S8(8), -1.0, None, ALU.add, None, R2, [bsm8])
        tt("dve", a2, a1, S8(0), ALU.mult, R2, [bsm8])
        tt("dve", a3, S8(9), S8(1), ALU.mult, R2, [bsm8])
        tt("dve", a2, a2, a3, ALU.add, R2, [bsm8])
        tt("dve", S8(10), a2, a0, ALU.mult, R2, W)
        tt("dve", a2, S8(9), S8(0), ALU.mult, R2, [bsm8])
        tt("dve", a3, a1, S8(1), ALU.mult, R2, [bsm8])
        tt("dve", a2, a2, a3, ALU.subtract, R2, [bsm8])
        tt("dve", S8(11), a2, a0, ALU.mult, R2, W)
        cre_b = sp8[:, 10, :].unsqueeze(2).to_broadcast([128, 8, 16])
        cim_b = sp8[:, 11, :].unsqueeze(2).to_broadcast([128, 8, 16])
        Rb = [bbc, bsp8, bbbt]
        tt("dve", bbt[:, 0], bc32[:, 0], cre_b, ALU.mult, Rb, [bbbt])
        tt("dve", bbt[:, 2], bc32[:, 1], cim_b, ALU.mult, Rb, [bbbt])
        tt("dve", bbt[:, 0], bbt[:, 0], bbt[:, 2], ALU.subtract, Rb, [bbbt])
        tt("dve", bbt[:, 1], bc32[:, 1], cre_b, ALU.mult, Rb, [bbbt])
        tt("dve", bbt[:, 2], bc32[:, 0], cim_b, ALU.mult, Rb, [bbbt])
        tt("dve", bbt[:, 1], bbt[:, 1], bbt[:, 2], ALU.add, Rb, [bbbt])
        cp("dve", bbt[:, 2], bc32[:, 2], Rb, [bbbt])
        ts("dve", bbt[:, 3], bc32[:, 3], -1.0, None, ALU.mult, None, Rb, [bbbt])
        for mi in range(4):
            for ct in range(8):
                memset("dve", exq[:], 0.0, [bexq])
                for gg in range(2):
                    gp = (2 * ct + gg) % 8
                    cp("dve", exq[64 * gg:64 * gg + 64, 16 * gp:16 * gp + 16], bbt[64 * gg:64 * gg + 64, mi, ct, :], [bbbt], [bexq])
                if mi < 2:
                    pi = nps()
                    P.add("pe", lambda e, pi=pi: e.transpose(out=PS[pi][:, 0:128], in_=exq[:], identity=ident[:]),
                          reads=[bexq, bident], writes=[bPS[pi]])
                    cp("act", LB[:, mi, ct, :], PS[pi][:, 0:128], [bPS[pi]], [bLB])
                else:
                    cp("act", LB[:, mi, ct, :], exq[:], [bexq], [bLB])
        for ct in range(8):
            ts("dve", sx[0][:, 0:TS + 1] if TS + 1 <= TS else angt, jr[:], sp8[:, 4, ct:ct + 1], None, ALU.mult, None, [bjr, bsp8], [bang])
            sin_of(sinT[:, ct, :], angt, 0.0, angk, [bang], [btab], [btD])
            sin_of(cosT[:, ct, :], angt, 0.5 * math.pi, angk, [bang], [btab], [btD])
        for ct in range(8):
            ts("dve", r4[:, ct], pat[:], sp8[:, 5, ct:ct + 1], None, ALU.mult, None, [bpat, bsp8], [btab4])
        for m in range(2):
            for k in range(31):
                ts("pool", diag16[:, m, k, :], ident[:], cw[:, m, k:k + 1], None, ALU.mult, None, [bident, bcw], [bdiag])

    def rmsnorm(xa, bxs, n, gcol, shm, a, l, sample):
        for k in range(KT):
            act(scr16[:, k, 0:n], xa[:, k, :], AF.Square, [bxs[k]], [bscr])
        pi = nps()
        for k in range(KT):
            mm(PS[pi][:, 0:n], ones16[:], scr16[:, k, 0:n], k == 0, k == KT - 1, [bones, bscr], [bPS[pi]])
        ts("dve", rstd[:, 0:n], PS[pi][:, 0:n], 1.0 / D, 1e-6, ALU.mult, ALU.add, [bPS[pi]], [brstd])
        act(rstd[:, 0:n], rstd[:, 0:n], AF.Sqrt, [brstd], [brstd])
        P.add("dve", lambda e: e.reciprocal(out=rstd[:, 0:n], in_=rstd[:, 0:n]), reads=[brstd], writes=[brstd])
        for k in range(KT):
            tA = tmpA if k % 2 == 0 else tmpB
            bA = btA if k % 2 == 0 else btB
            tt("dve", tA[:, 0:n], xa[:, k, :], rstd[:, 0:n], ALU.mult, [bxs[k], brstd], [bA])
            if gcol is None:
                ts("pool", xa[:, k, :], tA[:, 0:n], gf[:, k:k + 1], None, ALU.mult, None, [bA, bgf], [bxs[k]])
            elif not sample:
                if False:
                    pass
                else:
                    act(h16[:, k, 0:n], tA[:, 0:n], AF.Identity, [bA, bgsc, bmod], [bh],
                        bias=modT[:, l, shm + k, 0:1], scale=gsc[:, a, k, 0:1])
            else:
                v3 = tA[:, 0:n].rearrange("p (b t) -> p b t", t=LS)
                if True:
                    tt("pool", v3, v3, gsc[:, a, k, 1:17].unsqueeze(2).to_broadcast([128, NB, LS]), ALU.mult, [bA, bgsc], [bA])
                    tt("pool", h16[:, k, 0:n].rearrange("p (b t) -> p b t", t=LS), v3,
                       modT[:, l, shm + k, 1:17].unsqueeze(2).to_broadcast([128, NB, LS]), ALU.add, [bA, bmod], [bh])

    def resid(xa, bxs, n, mo, pi, l, gm, sample):
        if not sample:
            stt("dve", xa[:, mo, :], PS[pi][:, 0:n], modT[:, l, gm + mo, 0:1], xa[:, mo, :], ALU.mult, ALU.add,
                [bPS[pi], bmod, bxs[mo]], [bxs[mo]])
        else:
            tt("dve", tmpC[:, 0:n].rearrange("p (b t) -> p b t", t=LS), PS[pi][:, 0:n].rearrange("p (b t) -> p b t", t=LS),
               modT[:, l, gm + mo, 1:17].unsqueeze(2).to_broadcast([128, NB, LS]), ALU.mult, [bPS[pi], bmod], [btC])
            tt("dve", xa[:, mo, :], xa[:, mo, :], tmpC[:, 0:n], ALU.add, [btC, bxs[mo]], [bxs[mo]])

    def inproj_tile(m, n):
        pi = nps()
        for k in range(KT):
            mm(PS[pi][:, 0:n], win16[:, k, m * 128:(m + 1) * 128], h16[:, k, 0:n], k == 0, k == KT - 1, [bwin, bh], [bPS[pi]])
        return pi

    def rope_pair(m_a, m_b, cos_ap, sin_ap, brope, out_ap, bout, n, extra32=None):
        pa = inproj_tile(m_a, n)
        pb = inproj_tile(m_b, n)
        tt("dve", tmpA[:, 0:n], PS[pa][:, 0:n], cos_ap, ALU.mult, [bPS[pa], brope], [btA])
        tt("dve", tmpB[:, 0:n], PS[pb][:, 0:n], sin_ap, ALU.mult, [bPS[pb], brope], [btB])
        tt("pool", out_ap, tmpA[:, 0:n], tmpB[:, 0:n], ALU.add, [btA, btB], [bout])
        if extra32 is not None:
            tt("pool", extra32[0], tmpA[:, 0:n], tmpB[:, 0:n], ALU.add, [btA, btB], [extra32[1]])

    def gelu_glu(n):
        for o in range(2):
            tt("pool", tmpC[:, 0:n], yss[:, o, 0:n], yss[:, o, 0:n], ALU.mult, [byss], [btC])
            ts("pool", tmpC[:, 0:n], tmpC[:, 0:n], 0.044715, 1.0, ALU.mult, ALU.add, [btC], [btC])
            tt("pool", tmpC[:, 0:n], tmpC[:, 0:n], yss[:, o, 0:n], ALU.mult, [btC, byss], [btC])
            act(tmpC[:, 0:n], tmpC[:, 0:n], AF.Sigmoid, [btC], [btC], scale=2.0 * math.sqrt(2.0 / math.pi))
            tt("dve", z32[:, o, 0:n], yss[:, o, 0:n], tmpC[:, 0:n], ALU.mult, [byss, btC], [bz32])
            cp("pool", z16[:, o, 0:n], z32[:, o, 0:n], [bz32], [bz16])
        for o in range(2):
            pi = nps()
            for k in range(2):
                mm(PS[pi][:, 0:n], wgl16[:, k, o * 128:(o + 1) * 128], z16[:, k, 0:n], k == 0, k == 1, [bwgl, bz16], [bPS[pi]])
            act(tmpD[:, 0:n], PS[pi][:, 0:n], AF.Sigmoid, [bPS[pi], bsp2], [btD], bias=sp2[:, 1, o:o + 1])
            tt("dve", os16[:, o, 0:n], z32[:, o, 0:n], tmpD[:, 0:n], ALU.mult, [bz32, btD], [bos])

    def conv_ln(rhs_fn, n, view):
        pcs = []
        for m in range(2):
            pi = nps()
            pcs.append(pi)
            for k in range(31):
                mm(view(PS[pi][:, 0:n]), diag16[:, m, k, :], rhs_fn(m, k), k == 0, k == 30, [bdiag, bcb16, bcbs16], [bPS[pi]])
            act(ycf[:, m, 0:n], PS[pi][:, 0:n], AF.Identity, [bPS[pi], bsp2], [bycf], bias=sp2[:, 2, m:m + 1])
            cp("dve", yc16[:, m, 0:n], ycf[:, m, 0:n], [bycf], [byc16])
            act(yc16[:, 2 + m, 0:n], ycf[:, m, 0:n], AF.Square, [bycf], [byc16])
        p1 = nps()
        for m in range(2):
            mm(PS[p1][:, 0:n], ones16[:], yc16[:, m, 0:n], m == 0, m == 1, [bones, byc16], [bPS[p1]])
        p2 = nps()
        for m in range(2):
            mm(PS[p2][:, 0:n], ones16[:], yc16[:, 2 + m, 0:n], m == 0, m == 1, [bones, byc16], [bPS[p2]])
        ts("dve", tmpA[:, 0:n], PS[p1][:, 0:n], 1.0 / 256, None, ALU.mult, None, [bPS[p1]], [btA])
        tt("dve", tmpB[:, 0:n], tmpA[:, 0:n], tmpA[:, 0:n], ALU.mult, [btA], [btB])
        stt("dve", tmpB[:, 0:n], PS[p2][:, 0:n], 1.0 / 256, tmpB[:, 0:n], ALU.mult, ALU.subtract, [bPS[p2], btB], [btB])
        ts("dve", tmpB[:, 0:n], tmpB[:, 0:n], 1e-6, None, ALU.add, None, [btB], [btB])
        act(tmpB[:, 0:n], tmpB[:, 0:n], AF.Sqrt, [btB], [btB])
        P.add("dve", lambda e: e.reciprocal(out=tmpB[:, 0:n], in_=tmpB[:, 0:n]), reads=[btB], writes=[btB])
        for m in range(2):
            tt("dve", tmpC[:, 0:n], ycf[:, m, 0:n], tmpA[:, 0:n], ALU.subtract, [bycf, btA], [btC])
            tt("dve", tmpC[:, 0:n], tmpC[:, 0:n], tmpB[:, 0:n], ALU.mult, [btC, btB], [btC])
            act(tmpD[:, 0:n], tmpC[:, 0:n], AF.Identity, [btC, bsp2], [btD], bias=sp2[:, 4, m:m + 1], scale=sp2[:, 3, m:m + 1])
            act(tmpC[:, 0:n], tmpD[:, 0:n], AF.Sigmoid, [btD], [btC])
            tt("dve", oc16[:, m, 0:n], tmpD[:, 0:n], tmpC[:, 0:n], ALU.mult, [btC, btD], [boc])

    def outproj_ffn(xa, bxs, n, l, sample, first=False):
        for mo in range(8):
            s = woi[0] % 2
            woi[0] += 1
            fa = woa[s][:].rearrange("p h c -> p (h c)")
            fr = wor[s][:].rearrange("p h c -> p (h c)")
            if first:
                ld(woa[s][:], w_out[l, 0:512, mo * 128:(mo + 1) * 128].rearrange("(h d) n -> d h n", d=64), bwo[s], [bwo[s]], q="pool")
                ld(wor[s][:], w_out[l, 512:1024, mo * 128:(mo + 1) * 128].rearrange("(j p) n -> p j n", p=128), bwo[s], [bwo[s]], q="pool")
                P.add("sp", lambda e, fa=fa, mo=mo: e.dma_start(out=woa_c[mo], in_=fa), reads=[bwo[s]], writes=[bwo_c[mo]], dma_owner=bwo[s])
                P.add("sp", lambda e, fr=fr, mo=mo: e.dma_start(out=wor_c[mo], in_=fr), reads=[bwo[s]], writes=[bwo_c[mo]], dma_owner=bwo[s])
            else:
                P.add("sp", lambda e, fa=fa, mo=mo: e.dma_start(out=fa, in_=woa_c[mo]), reads=[bwo_c[mo]], writes=[bwo[s]], dma_owner=bwo[s])
                P.add("sp", lambda e, fr=fr, mo=mo: e.dma_start(out=fr, in_=wor_c[mo]), reads=[bwo_c[mo]], writes=[bwo[s]], dma_owner=bwo[s])
            pi = nps()
            for hq in range(8):
                mm(PS[pi][:, 0:n], woa[s][:, hq, :], att16[:, hq, 0:n], hq == 0, False, [bwo[s], batt], [bPS[pi]])
            for j in range(2):
                mm(PS[pi][:, 0:n], wor[s][:, j, :], os16[:, j, 0:n], False, False, [bwo[s], bos], [bPS[pi]])
            for j in range(2):
                mm(PS[pi][:, 0:n], wor[s][:, 2 + j, :], oc16[:, j, 0:n], False, j == 1, [bwo[s], boc], [bPS[pi]])
            resid(xa, bxs, n, mo, pi, l, 16, sample)
        rmsnorm(xa, bxs, n, 1, 24, 1, l, sample)
        for j in range(JT):
            s = ffi[0] % NG
            ffi[0] += 1
            fg = wfl[s]
            if first:
                ld(wgu[s][:, 0], w_gate[l, :, j * 128:(j + 1) * 128].rearrange("(k p) n -> p k n", p=128), bwgu[s], [bwgu[s]], q="pool")
                ld(wgu[s][:, 1], w_up[l, :, j * 128:(j + 1) * 128].rearrange("(k p) n -> p k n", p=128), bwgu[s], [bwgu[s]], q="pool")
                P.add("sp", lambda e, fg=fg, j=j: e.dma_start(out=wgu_c[j], in_=fg), reads=[bwgu[s]], writes=[bwgu_c[j]], dma_owner=bwgu[s])
            else:
                P.add("sp", lambda e, fg=fg, j=j: e.dma_start(out=fg, in_=wgu_c[j]), reads=[bwgu_c[j]], writes=[bwgu[s]], dma_owner=bwgu[s])
            pg = nps()
            for k in range(KT):
                mm(PS[pg][:, 0:n], wgu[s][:, 0, k, :], h16[:, k, 0:n], k == 0, k == KT - 1, [bwgu[s], bh], [bPS[pg]])
            pu = nps()
            for k in range(KT):
                mm(PS[pu][:, 0:n], wgu[s][:, 1, k, :], h16[:, k, 0:n], k == 0, k == KT - 1, [bwgu[s], bh], [bPS[pu]])
            tA = tmpA if j % 2 == 0 else tmpB
            bA = btA if j % 2 == 0 else btB
            act(tA[:, 0:n], PS[pg][:, 0:n], AF.Silu, [bPS[pg]], [bA])
            tt("dve", scr16[:, j, 0:n], tA[:, 0:n], PS[pu][:, 0:n], ALU.mult, [bA, bPS[pu]], [bscr])
        for mo in range(8):
            pi = nps()
            for jh in range(2):
                s = ffi[0] % NG
                ffi[0] += 1
                ci = mo * 2 + jh
                fd = wfl[s][:, 0:1408]
                if first:
                    ld(wdn[s], w_down[l, jh * 1408:(jh + 1) * 1408, mo * 128:(mo + 1) * 128].rearrange("(j p) n -> p j n", p=128), bwdn[s], [bwdn[s]], q="pool")
                    P.add("sp", lambda e, fd=fd, ci=ci: e.dma_start(out=wdn_c[ci], in_=fd), reads=[bwdn[s]], writes=[bwdn_c[ci]], dma_owner=bwdn[s])
                else:
                    P.add("sp", lambda e, fd=fd, ci=ci: e.dma_start(out=fd, in_=wdn_c[ci]), reads=[bwdn_c[ci]], writes=[bwdn[s]], dma_owner=bwdn[s])
                for jj in range(11):
                    j = jh * 11 + jj
                    mm(PS[pi][:, 0:n], wdn[s][:, jj, :], scr16[:, j, 0:n], j == 0, j == JT - 1, [bwdn[s], bscr], [bPS[pi]])
            resid(xa, bxs, n, mo, pi, l, 40, sample)

    def prompt_chunk(l, c):
        n = T
        t0 = c * T
        src = xT if l == 0 else xscr
        xdr = src[:, t0:t0 + T].rearrange("(k p) t -> p k t", p=128)
        ld(xs[:], xdr, bx[0], bx)
        ld(rp[:], ropeP[:, :, t0:t0 + T].rearrange("a p t -> p a t"), brp, [brp])
        if KS2 < 1:
            return
        rmsnorm(xs, bx, n, 1, 0, 0, l, False)
        if KS2 < 2:
            return
        for j in range(4):
            rope_pair(j, 5 + j, rp[:, 0, :], rp[:, 1, :], brp, q16[:, j, :], bq, n)
        rope_pair(4, 9, rp[:, 0, :], rp[:, 1, :], brp, kb16[:, 128:128 + T], bkb, n, extra32=(k32[:, 0:n], bk32))
        if KS2 < 3:
            return
        for o in range(2):
            pi = inproj_tile(10 + o, n)
            cp("act", u32[:, o, :], PS[pi][:, 0:n], [bPS[pi]], [bu32])
            cp("dve", u16[:, o, :], PS[pi][:, 0:n], [bPS[pi]], [bu16])
        for o in range(2):
            pa = inproj_tile(12 + o, n)
            pg = inproj_tile(14 + o, n)
            act(tmpC[:, 0:n], PS[pg][:, 0:n], AF.Sigmoid, [bPS[pg]], [btC])
            tt("dve", cb32[:, o, 30:30 + T], PS[pa][:, 0:n], tmpC[:, 0:n], ALU.mult, [bPS[pa], btC], [bcb32])
            if c == 0:
                memset("pool", cb32[:, o, 0:30], 0.0, [bcb32])
            cp("pool", cb16[:, o, :], cb32[:, o, :], [bcb32], [bcb16])
        if KS2 < 4:
            return
        for tb in range(T // 128):
            pi = nps()
            for k in range(KT):
                mm(PS[pi][:, 0:128], h16[:, k, tb * 128:(tb + 1) * 128], win16[:, k, 2048:2176], k == 0, k == KT - 1, [bh, bwin], [bPS[pi]])
            cp("act", v16[:, 1 + tb, :], PS[pi][:, 0:128], [bPS[pi]], [bv16])
            if c == NCH - 1 and tb == T // 128 - 1:
                cp("dve", v32[:], PS[pi][:, 0:128], [bPS[pi]], [bv32])
                st(nv[l], v32[:], bv32, [bv32])
        if c == NCH - 1:
            st(nkT[l], k32[:, T - 128:T], bk32, [bk32])
        if KSUB < 1:
            return
        blocks = [(qb_, hh_) for qb_ in range(T // 128) for hh_ in range(2)]

        def emit_scores(qb_, hh_):
            hs_ = slice(64 * hh_, 64 * hh_ + 64)
            qrhs_ = q16[hs_, :, qb_ * 128:(qb_ + 1) * 128]
            first_ = (c == 0 and qb_ == 0)
            po_ = nps()
            mm(PS[po_][:].rearrange("p (g q) -> p g q", g=4), kb16[hs_, 128 + qb_ * 128:128 + (qb_ + 1) * 128], qrhs_, True, False, [bkb, bq], [bPS[po_]])
            mm(PS[po_][:], ident16[:], mkn[:, 0, :], False, True, [bident16, bmkn], [bPS[po_]])
            pp_ = None
            if not first_:
                pp_ = nps()
                mm(PS[pp_][:].rearrange("p (g q) -> p g q", g=4), kb16[hs_, qb_ * 128:(qb_ + 1) * 128], qrhs_, True, False, [bkb, bq], [bPS[pp_]])
                mm(PS[pp_][:], ident16[:], mkn[:, 1, :], False, True, [bident16, bmkn], [bPS[pp_]])
            return po_, pp_
        pend = emit_scores(*blocks[0])
        for bi, (qb, hh) in enumerate(blocks):
            hs = slice(64 * hh, 64 * hh + 64)
            first = (c == 0 and qb == 0)
            po, pp = pend
            act(pown[:], PS[po][:], AF.Exp, [bPS[po]], [bpown], scale=0.125)
            if not first:
                act(pprev[:], PS[pp][:], AF.Exp, [bPS[pp]], [bpprev], scale=0.125)
            if bi + 1 < len(blocks):
                pend = emit_scores(*blocks[bi + 1])
            pO = 6
            pD = 7
            if not first:
                mm(PS[pO][0:64, :], v16[:, qb, hs], pprev[:], True, False, [bv16, bpprev], [bPS[pO]])
                mm(PS[pD][0:64, :], ones16[:, 0:64], pprev[:], True, False, [bones, bpprev], [bPS[pD]])
            mm(PS[pO][0:64, :], v16[:, qb + 1, hs], pown[:], first, True, [bv16, bpown], [bPS[pO]])
            mm(PS[pD][0:64, :], ones16[:, 0:64], pown[:], first, True, [bones, bpown], [bPS[pD]])
            tt("dve", den[:].rearrange("p (g q) -> p g q", g=4), PS[pD][0:64, :].rearrange("p (g q) -> p g q", g=4),
               sk[:, 4 * hh:4 * hh + 4].unsqueeze(2).to_broadcast([64, 4, 128]), ALU.add, [bPS[pD], bsk], [btA, btB])
            P.add("dve", lambda e: e.reciprocal(out=den[:], in_=den[:]), reads=[btA, btB], writes=[btA, btB])
            tt("dve", att16[:, 4 * hh:4 * hh + 4, qb * 128:(qb + 1) * 128], PS[pO][0:64, :].rearrange("p (g q) -> p g q", g=4),
               den[:].rearrange("p (g q) -> p g q", g=4), ALU.mult, [bPS[pO], btA, btB], [batt])
        cp("pool", kb16[:, 0:128], kb16[:, T:T + 128], [bkb], [bkb])
        cp("pool", v16[:, 0, :], v16[:, T // 128, :], [bv16], [bv16])
        if KSUB < 2:
            return
        pY = [6, 7]
        def emit_bu(sc_, ct_):
            pr_ = nps()
            pim_ = nps()
            mm(PS[pr_][:, 0:TS], LB[:, 0, ct_, :], u16[:, ct_ // 4, sc_ * TS:(sc_ + 1) * TS], True, True, [bLB, bu16], [bPS[pr_]])
            mm(PS[pim_][:, 0:TS], LB[:, 1, ct_, :], u16[:, ct_ // 4, sc_ * TS:(sc_ + 1) * TS], True, True, [bLB, bu16], [bPS[pim_]])
            return pr_, pim_
        iters = [(sc_, ct_) for sc_ in range(T // TS) for ct_ in range(8)]
        pend = emit_bu(*iters[0])
        for sc in range(T // TS):
            c0 = sc * TS
            firstsub = (c == 0 and sc == 0)
            for ct in range(8):
                uh = ct // 4
                pr, pim = pend
                nxt = sc * 8 + ct + 1
                if nxt < len(iters):
                    pend = emit_bu(*iters[nxt])
                cs_ = cosT[:, ct, 0:TS]
                sn_ = sinT[:, ct, 0:TS]
                par = ssi[0] % 2
                ssi[0] += 1
                qA, bqA = sq[par][0], bsq[par][0]
                qB, bqB = sq[par][1], bsq[par][1]
                hA, bhA = sh16[par][0], bsh16[par][0]
                hB, bhB = sh16[par][1], bsh16[par][1]
                tt("dve", sx[0][:], PS[pr][:, 0:TS], cs_, ALU.mult, [bPS[pr], btab], [bsx[0]])
                tt("dve", sx[1][:], PS[pim][:, 0:TS], sn_, ALU.mult, [bPS[pim], btab], [bsx[1]])
                tt("dve", sx[0][:], sx[0][:], sx[1][:], ALU.add, [bsx[0], bsx[1]], [bsx[0]])
                tt("dve", sx[2][:], PS[pim][:, 0:TS], cs_, ALU.mult, [bPS[pim], btab], [bsx[2]])
                tt("dve", sx[3][:], PS[pr][:, 0:TS], sn_, ALU.mult, [bPS[pr], btab], [bsx[3]])
                tt("dve", sx[2][:], sx[2][:], sx[3][:], ALU.subtract, [bsx[2], bsx[3]], [bsx[2]])
                rbc = sp8[:, 5, ct:ct + 1].to_broadcast([128, TS])
                ire = 0.0 if firstsub else inre[:, ct:ct + 1]
                iim = 0.0 if firstsub else inim[:, ct:ct + 1]
                P.add("dve", lambda e, ire=ire, rbc=rbc, qA=qA: e.tensor_tensor_scan(out=qA[:], data0=rbc, data1=sx[0][:], initial=ire, op0=ALU.mult, op1=ALU.add),
                      reads=[bsp8, bsx[0], binit], writes=[bqA])
                P.add("dve", lambda e, iim=iim, rbc=rbc, qB=qB: e.tensor_tensor_scan(out=qB[:], data0=rbc, data1=sx[2][:], initial=iim, op0=ALU.mult, op1=ALU.add),
                      reads=[bsp8, bsx[2], binit], writes=[bqB])
                cp("pool", qlre[:, ct:ct + 1], qA[:, TS - 1:TS], [bqA], [bql])
                cp("pool", qlim[:, ct:ct + 1], qB[:, TS - 1:TS], [bqB], [bql])
                tt("pool", sx[6][:], qA[:], cs_, ALU.mult, [bqA, btab], [bsx[6]])
                tt("pool", sx[7][:], qB[:], sn_, ALU.mult, [bqB, btab], [bsx[7]])
                tt("pool", hA[:], sx[6][:], sx[7][:], ALU.subtract, [bsx[6], bsx[7]], [bhA])
                tt("pool", sx[4][:], qA[:], sn_, ALU.mult, [bqA, btab], [bsx[4]])
                tt("pool", sx[5][:], qB[:], cs_, ALU.mult, [bqB, btab], [bsx[5]])
                tt("pool", hB[:], sx[4][:], sx[5][:], ALU.add, [bsx[4], bsx[5]], [bhB])
                ot = ct // 4
                mm(PS[pY[ot]][:, c0:c0 + TS], LB[:, 2, ct, :], hA[:], ct % 4 == 0, False, [bLB, bhA], [bPS[pY[ot]]])
                mm(PS[pY[ot]][:, c0:c0 + TS], LB[:, 3, ct, :], hB[:], False, ct % 4 == 3, [bLB, bhB], [bPS[pY[ot]]])
            cl = cosT[:, :, TS - 1]
            sl = sinT[:, :, TS - 1]
            a0, a1 = sm8[0][:], sm8[1][:]
            tt("dve", a0, qlre[:], cl, ALU.mult, [bql, btab], [bsm8])
            tt("dve", a1, qlim[:], sl, ALU.mult, [bql, btab], [bsm8])
            tt("dve", hlre[:], a0, a1, ALU.subtract, [bsm8], [bhl])
            tt("dve", a0, qlre[:], sl, ALU.mult, [bql, btab], [bsm8])
            tt("dve", a1, qlim[:], cl, ALU.mult, [bql, btab], [bsm8])
            tt("dve", hlim[:], a0, a1, ALU.add, [bsm8], [bhl])
            tt("dve", a0, hlre[:], S8(6), ALU.mult, [bhl, bsp8], [bsm8])
            tt("dve", a1, hlim[:], S8(7), ALU.mult, [bhl, bsp8], [bsm8])
            tt("dve", inre[:], a0, a1, ALU.subtract, [bsm8], [binit])
            tt("dve", a0, hlre[:], S8(7), ALU.mult, [bhl, bsp8], [bsm8])
            tt("dve", a1, hlim[:], S8(6), ALU.mult, [bhl, bsp8], [bsm8])
            tt("dve", inim[:], a0, a1, ALU.add, [bsm8], [binit])
            if c == NCH - 1 and sc == T // TS - 1:
                st(nre[l], hlre[:], bhl, [bhl])
                st(nim[l], hlim[:], bhl, [bhl])
        for o in range(2):
            stt("dve", yss[:, o, 0:n], u32[:, o, 0:n], sp2[:, 0, o:o + 1], PS[pY[o]][:, 0:n], ALU.mult, ALU.add, [bu32, bsp2, bPS[pY[o]]], [byss])
        gelu_glu(n)
        if KSUB < 3:
            return
        conv_ln(lambda m, k: cb16[:, m, k:k + T], n, lambda ap: ap)
        if c == NCH - 1:
            st(ncv[l], cb32[:, :, T:T + 30], bcb32, [bcb32])
        for o in range(2):
            cp("pool", cb32[:, o, 0:30], cb32[:, o, T:T + 30], [bcb32], [bcb32])
        if KSUB < 4:
            return
        outproj_ffn(xs, bx, n, l, False, first=(c == 0))
        if l < nl - 1:
            st(xscr[:, t0:t0 + T].rearrange("(k p) t -> p k t", p=128), xs[:], bx[0], bx)
        else:
            rmsnorm(xs, bx, n, None, 0, 0, l, False)
            st(yT[:, t0:t0 + T].rearrange("(k p) t -> p k t", p=128), xs[:], bx[0], bx)

    def sample_layer(l):
        n = TSM
        bxs = [bxsm] * KT
        ld(kc16, kcT[l].rearrange("b f k -> f b k"), bkc, [bkc], q="pool")
        ld(vc16, vc[l].rearrange("b k f -> k b f"), bvc, [bvc], q="pool")
        ld(h0re[:], ssmre_in[l], bh0, [bh0])
        ld(h0im[:], ssmim_in[l], bh0, [bh0])
        ld(cbs32[:, :, :, 0:30], sconv_in[l], bcbs32, [bcbs32])
        dd = Buf(f"dd{l}")
        o1 = P.add("sp", lambda e: e.dma_start(out=nks_c[l], in_=kcn[l, :, 4:128, :]), dma_owner=dd)
        stores.append(o1)
        vcn_src = vc[l, :, 4:128, :]
        o2 = P.add("sp", lambda e: e.dma_start(out=nvs_c[l], in_=vcn_src), dma_owner=dd)
        stores.append(o2)
        rmsnorm(xsm, bxs, n, 1, 0, 0, l, True)
        for j in range(4):
            rope_pair(j, 5 + j, rps[:, 0, :], rps[:, 1, :], brps, q16[:, j, 0:n], bq, n)
        rope_pair(4, 9, rps[:, 0, :], rps[:, 1, :], brps, kb16[:, 128:128 + n], bkb, n, extra32=(k32[:, 0:n], bk32))
        st(skT[l], k32[:, 0:n], bk32, [bk32])
        for o in range(2):
            pi = inproj_tile(10 + o, n)
            cp("act", u32[:, o, 0:n], PS[pi][:, 0:n], [bPS[pi]], [bu32])
            cp("dve", u16[:, o, 0:n], PS[pi][:, 0:n], [bPS[pi]], [bu16])
        for o in range(2):
            pa = inproj_tile(12 + o, n)
            pg = inproj_tile(14 + o, n)
            act(tmpC[:, 0:n], PS[pg][:, 0:n], AF.Sigmoid, [bPS[pg]], [btC])
            tt("dve", cbs32[:, o, :, 30:34], PS[pa][:, 0:n].rearrange("p (b t) -> p b t", t=LS),
               tmpC[:, 0:n].rearrange("p (b t) -> p b t", t=LS), ALU.mult, [bPS[pa], btC], [bcbs32])
            cp("pool", cbs16[:, o], cbs32[:, o], [bcbs32], [bcbs16])
        st(scv[l], cbs32[:, :, :, 4:34], bcbs32, [bcbs32])
        pv = nps()
        for b in range(NB):
            for k in range(KT):
                mm(PS[pv][0:4, :].rearrange("p (b f) -> p b f", b=4)[:, b % 4, :] if False else PS[pv][0:4, (b % 4) * 128:(b % 4 + 1) * 128],
                   h16[:, k, 4 * b:4 * b + 4], win16[:, k, 2048:2176], k == 0, k == KT - 1, [bh, bwin], [bPS[pv]])
            if b % 4 == 3:
                g0 = b - 3
                cp("act", vn16[:, g0:g0 + 4, :], PS[pv][0:4, :].rearrange("p (b f) -> p b f", b=4), [bPS[pv]], [bvn16])
                cp("dve", vn32[:, g0:g0 + 4, :], PS[pv][0:4, :].rearrange("p (b f) -> p b f", b=4), [bPS[pv]], [bvn32])
                if b < NB - 1:
                    pv = nps()
        st(svn[l], vn32[:], bvn32, [bvn32])
        pC = nps()
        pN = nps()
        for b in range(NB):
            for hh in range(2):
                hs = slice(64 * hh, 64 * hh + 64)
                col = (b * 2 + hh) * 16
                qrhs = q16[hs, :, 4 * b:4 * b + 4]
                mm(PS[pC][:, col:col + 16].rearrange("p (g t) -> p g t", g=4), kc16[hs, b, :], qrhs, True, True, [bkc, bq], [bPS[pC]])
                mm(PS[pN][0:4, col:col + 16].rearrange("p (g t) -> p g t", g=4), kb16[hs, 128 + 4 * b:128 + 4 * b + 4], qrhs, True, True, [bkb, bq], [bPS[pN]])
        act(pown[:], PS[pC][:], AF.Exp, [bPS[pC]], [bpown], scale=0.125)
        tt("pool", pown[:], pown[:], mk[:, 2, :], ALU.mult, [bpown, bmk], [bpown])
        act(pn16[:], PS[pN][0:4, :], AF.Exp, [bPS[pN]], [bpn], scale=0.125)
        tt("pool", pn16[:], pn16[:], mk[0:4, 3, :], ALU.mult, [bpn, bmk], [bpn])
        pO = nps()
        pD = nps()
        for b in range(NB):
            for hh in range(2):
                hs = slice(64 * hh, 64 * hh + 64)
                col = (b * 2 + hh) * 16
                mm(PS[pO][0:64, col:col + 16], vc16[:, b, hs], pown[:, col:col + 16], True, False, [bvc, bpown], [bPS[pO]])
                mm(PS[pO][0:64, col:col + 16], vn16[0:4, b, hs], pn16[0:4, col:col + 16], False, True, [bvn16, bpn], [bPS[pO]])
                mm(PS[pD][0:64, col:col + 16], ones16[:, 0:64], pown[:, col:col + 16], True, False, [bones, bpown], [bPS[pD]])
                mm(PS[pD][0:64, col:col + 16], ones16[0:4, 0:64], pn16[0:4, col:col + 16], False, True, [bones, bpn], [bPS[pD]])
        tt("dve", den[:].rearrange("p (b h t) -> p b h t", b=NB, t=LS), PS[pD][0:64, :].rearrange("p (b h t) -> p b h t", b=NB, t=LS),
           sk[:, :].unsqueeze(1).unsqueeze(3).to_broadcast([64, NB, 8, LS]), ALU.add, [bPS[pD], bsk], [btA, btB])
        P.add("dve", lambda e: e.reciprocal(out=den[:], in_=den[:]), reads=[btA, btB], writes=[btA, btB])
        tt("dve", att16[:, :, 0:n].rearrange("p h (b t) -> p b h t", t=LS), PS[pO][0:64, :].rearrange("p (b h t) -> p b h t", b=NB, t=LS),
           den[:].rearrange("p (b h t) -> p b h t", b=NB, t=LS), ALU.mult, [bPS[pO], btA, btB], [batt])
        for (dst, x1, y1, x2, y2, op) in ((ahre, 8, h0re, 9, h0im, ALU.subtract), (ahim, 8, h0im, 9, h0re, ALU.add)):
            tt("dve", dst[:], y1[:], sp8[:, x1, :].unsqueeze(2).to_broadcast([128, 8, NB]), ALU.mult, [bh0, bsp8], [bah])
            tt("dve", hsre[:], y2[:], sp8[:, x2, :].unsqueeze(2).to_broadcast([128, 8, NB]), ALU.mult, [bh0, bsp8], [bhs])
            tt("dve", dst[:], dst[:], hsre[:], op, [bah, bhs], [bah])
        pY = [6, 7]
        for ct in range(8):
            uh = ct // 4
            pr = nps()
            pim = nps()
            mm(PS[pr][:, 0:n], LB[:, 0, ct, :], u16[:, uh, 0:n], True, True, [bLB, bu16], [bPS[pr]])
            mm(PS[pim][:, 0:n], LB[:, 1, ct, :], u16[:, uh, 0:n], True, True, [bLB, bu16], [bPS[pim]])
            cs_ = cosT[:, ct, 0:LS].unsqueeze(1).to_broadcast([128, NB, LS])
            sn_ = sinT[:, ct, 0:LS].unsqueeze(1).to_broadcast([128, NB, LS])
            V3 = lambda ap: ap.rearrange("p (b t) -> p b t", t=LS)
            X = [s_[:, 0:n] for s_ in sx]
            tt("dve", V3(X[0]), V3(PS[pr][:, 0:n]), cs_, ALU.mult, [bPS[pr], btab], [bsx[0]])
            tt("dve", V3(X[1]), V3(PS[pim][:, 0:n]), sn_, ALU.mult, [bPS[pim], btab], [bsx[1]])
            tt("pool", X[0], X[0], X[1], ALU.add, [bsx[0], bsx[1]], [bsx[0]])
            tt("dve", V3(X[2]), V3(PS[pim][:, 0:n]), cs_, ALU.mult, [bPS[pim], btab], [bsx[2]])
            tt("dve", V3(X[3]), V3(PS[pr][:, 0:n]), sn_, ALU.mult, [bPS[pr], btab], [bsx[3]])
            tt("pool", X[2], X[2], X[3], ALU.subtract, [bsx[2], bsx[3]], [bsx[2]])
            tt("dve", sx[0][:, 0:n:LS], sx[0][:, 0:n:LS], ahre[:, ct, :], ALU.add, [bsx[0], bah], [bsx[0]])
            tt("dve", sx[2][:, 0:n:LS], sx[2][:, 0:n:LS], ahim[:, ct, :], ALU.add, [bsx[2], bah], [bsx[2]])
            r4v = r4[:, ct].rearrange("p b t -> p (b t)")
            P.add("dve", lambda e, r4v=r4v, X=X: e.tensor_tensor_scan(out=X[4], data0=r4v, data1=X[0], initial=0.0, op0=ALU.mult, op1=ALU.add),
                  reads=[btab4, bsx[0]], writes=[bsx[4]])
            P.add("dve", lambda e, r4v=r4v, X=X: e.tensor_tensor_scan(out=X[5], data0=r4v, data1=X[2], initial=0.0, op0=ALU.mult, op1=ALU.add),
                  reads=[btab4, bsx[2]], writes=[bsx[5]])
            tt("pool", V3(X[6]), V3(X[4]), cs_, ALU.mult, [bsx[4], btab], [bsx[6]])
            tt("pool", V3(X[7]), V3(X[5]), sn_, ALU.mult, [bsx[5], btab], [bsx[7]])
            tt("pool", X[6], X[6], X[7], ALU.subtract, [bsx[6], bsx[7]], [bsx[6]])
            cp("act", hre16[:, 0:n], X[6], [bsx[6]], [bhre])
            cp("act", hsre[:, ct, :], sx[6][:, LS - 1:n:LS], [bsx[6]], [bhs])
            tt("dve", V3(X[1]), V3(X[4]), sn_, ALU.mult, [bsx[4], btab], [bsx[1]])
            tt("dve", V3(X[3]), V3(X[5]), cs_, ALU.mult, [bsx[5], btab], [bsx[3]])
            tt("dve", X[1], X[1], X[3], ALU.add, [bsx[1], bsx[3]], [bsx[1]])
            cp("act", him16[:, 0:n], X[1], [bsx[1]], [bhim])
            cp("act", hsim[:, ct, :], sx[1][:, LS - 1:n:LS], [bsx[1]], [bhs])
            ot = ct // 4
            mm(PS[pY[ot]][:, 0:n], LB[:, 2, ct, :], hre16[:, 0:n], ct % 4 == 0, False, [bLB, bhre], [bPS[pY[ot]]])
            mm(PS[pY[ot]][:, 0:n], LB[:, 3, ct, :], him16[:, 0:n], False, ct % 4 == 3, [bLB, bhim], [bPS[pY[ot]]])
        st(sre[l], hsre[:], bhs, [bhs])
        st(sim_o[l], hsim[:], bhs, [bhs])
        for o in range(2):
            stt("dve", yss[:, o, 0:n], u32[:, o, 0:n], sp2[:, 0, o:o + 1], PS[pY[o]][:, 0:n], ALU.mult, ALU.add, [bu32, bsp2, bPS[pY[o]]], [byss])
        gelu_glu(n)
        conv_ln(lambda m, k: cbs16[:, m, :, k:k + LS], n, lambda ap: ap.rearrange("p (b t) -> p b t", t=LS))
        if DBG and l == 0:
            dsb = xs[:, 0:6, :].rearrange("p (a k) (h t) -> p a (k h) t", k=2, t=TSM); bdsb = bx[0]
            memset("dve", dsb, 0.0, bx)
            cp("dve", dsb[0:64, 0, :, :], att16[:, :, 0:n], [batt, bdsb], [bdsb])
            cp("dve", dsb[:, 1, 0:2, :], os16[:, :, 0:n], [bos, bdsb], [bdsb])
            cp("dve", dsb[:, 2, 0:2, :], oc16[:, :, 0:n], [boc, bdsb], [bdsb])
            st(dbg.rearrange("a p h t -> p a h t"), dsb, bdsb, [bdsb])
        outproj_ffn(xsm, bxs, n, l, True)
        if l == nl - 1:
            rmsnorm(xsm, bxs, n, None, 0, 0, l, True)
            st(ysT.rearrange("(k p) t -> p k t", p=128), xsm[:], bxsm, [bxsm])

    STG = int(os.environ.get("KSTAGE", "9"))
    for l in range(nl):
        if STG >= 1:
            layer_params(l)
        for c in range(NCH):
            if STG >= 3 or (STG == 2 and c == 0):
                prompt_chunk(l, c)
        if STG >= 4:
            sample_layer(l)

    P.add("sp", lambda e: e.nop(), extra_deps=stores)
    P.emit()
    return nc


def _perm_win(w):
    q = w[:, 0:512].reshape(D, 8, 64)
    k = w[:, 512:640].reshape(D, 2, 64)
    v = w[:, 640:768]
    u = w[:, 768:1024]
    a = w[:, 1024:1280]
    g = w[:, 1280:1536]

    def swap(t):
        return np.concatenate([t[..., 32:], t[..., :32]], axis=-1)
    order = [0, 4, 1, 5, 2, 6, 3, 7]
    qt = q[:, order, :].reshape(D, 512)
    qs = swap(q)[:, order, :].reshape(D, 512)
    kt = k.reshape(D, 128)
    ks = swap(k).reshape(D, 128)
    return np.ascontiguousarray(np.concatenate([qt, kt, qs, ks, u, a, g, v], axis=1))


def _rope_tab(pos):
    half = 32
    inv = (np.float32(10000.0) ** (-(np.arange(half, dtype=np.float32) / np.float32(half)))).astype(np.float32)
    ang = (pos.astype(np.float32)[None, :] * inv[:, None]).astype(np.float32)
    c = np.cos(ang.astype(np.float64)).astype(np.float32)
    s = np.sin(ang.astype(np.float64)).astype(np.float32)
    cos = np.concatenate([c, c, c, c], axis=0)
    sins = np.concatenate([-s, s, -s, s], axis=0)
    return np.ascontiguousarray(np.stack([cos, sins], axis=0))


_NC_CACHE = {}


def kernel(**inp):
    f = lambda a: np.ascontiguousarray(np.asarray(a, dtype=np.float32))
    I = {k: np.asarray(v) for k, v in inp.items()}
    nlr = _NC_CACHE.get("nl", NL)
    if "nc" not in _NC_CACHE:
        _NC_CACHE["nc"] = build(nlr)
    nc = _NC_CACHE["nc"]

    def pk(a):
        return f(a.reshape(NL, 8, 128).transpose(0, 2, 1))

    def p2(a):
        return f(a.reshape(NL, 2, 128).transpose(0, 2, 1))

    shared = {
        "w_mod": f(I["w_mod"]),
        "b_modT": f(I["b_mod"].reshape(NL, 48, 128).transpose(0, 2, 1)),
        "g1T": pk(I["norm1_g"]), "g2T": pk(I["norm2_g"]),
        "gfT": f(I["final_norm_g"].reshape(8, 128).T),
        "w_in2": f(np.stack([_perm_win(I["w_in"][l]) for l in range(NL)])),
        "ropeP": _rope_tab(np.arange(NTOK)),
        "ropeS": _rope_tab(PAST + np.tile(np.arange(LS), NB)),
        "sinkT": f(np.broadcast_to(I["attn_sinks"][:, None, :], (NL, 64, 8))),
        "lamre": f(I["ssm_lam_re"].reshape(NL, 8, 128).transpose(0, 2, 1)),
        "lamim": f(I["ssm_lam_im"].reshape(NL, 8, 128).transpose(0, 2, 1)),
        "logdt": f(np.repeat(I["ssm_log_dt"], 64, axis=1).reshape(NL, 8, 128).transpose(0, 2, 1)),
        "bre": f(I["ssm_b_re"].reshape(NL, 8, 128, 16).transpose(0, 2, 1, 3)),
        "bim": f(I["ssm_b_im"].reshape(NL, 8, 128, 16).transpose(0, 2, 1, 3)),
        "cre": f(I["ssm_c_re"].reshape(NL, 8, 2, 16, 64).transpose(0, 2, 4, 1, 3).reshape(NL, 128, 8, 16)),
        "cim": f(I["ssm_c_im"].reshape(NL, 8, 2, 16, 64).transpose(0, 2, 4, 1, 3).reshape(NL, 128, 8, 16)),
        "dskipT": p2(I["ssm_d"]), "wglu": f(I["ssm_w_glu"]), "bgluT": p2(I["ssm_b_glu"]),
        "convwT": f(I["conv_w"].reshape(NL, 31, 2, 128).transpose(0, 3, 2, 1)),
        "convbT": p2(I["conv_b"]), "lngT": p2(I["conv_ln_g"]), "lnbT": p2(I["conv_ln_b"]),
        "w_out": f(I["w_out"]), "w_gate": f(I["w_gate"]), "w_up": f(I["w_up"]), "w_down": f(I["w_down"]),
        "identd": np.eye(128, dtype=np.float32),
        "jrow": f(np.broadcast_to(np.arange(TS + 1, dtype=np.float32)[None, :], (128, TS + 1))),
    }
    kk = np.arange(128)[:, None]
    qq = np.tile(np.arange(128), 4)[None, :]
    m_own = np.where(qq >= kk, 0.0, -30000.0).astype(np.float32)
    m_prev = np.where(kk > qq, 0.0, -30000.0).astype(np.float32)
    shared["maskP"] = f(np.stack([m_own, m_prev]))
    tq = np.tile(np.arange(LS), 128)[None, :]
    m_c = (kk > tq).astype(np.float32)
    m_n = (kk <= tq).astype(np.float32)
    shared["maskS"] = f(np.stack([m_c, m_n]))

    in_maps = []
    for c in range(8):
        b = c % 4
        sbs = slice(NB * c, NB * (c + 1))
        m = dict(shared)
        m["xT"] = f(I["x_prompt"][b].T)
        m["xsT"] = f(I["x_sample"][sbs].reshape(TSM, D).T)
        m["cT"] = f(np.concatenate([I["c_prompt"][b][None, :], I["c_sample"][sbs]], axis=0).T)
        ck = I["cache_k"][:, sbs].reshape(NL, NB, 128, 128)
        cv = I["cache_v"][:, sbs].reshape(NL, NB, 128, 128)
        m["kcT"] = f(ck.transpose(0, 1, 3, 2))
        m["kcn"] = f(ck)
        m["vc"] = f(cv)
        m["ssmre_in"] = f(I["state_ssm_re"][:, sbs].reshape(NL, NB, 8, 128).transpose(0, 3, 2, 1))
        m["ssmim_in"] = f(I["state_ssm_im"][:, sbs].reshape(NL, NB, 8, 128).transpose(0, 3, 2, 1))
        m["sconv_in"] = f(I["state_conv"][:, sbs].reshape(NL, NB, 30, 2, 128).transpose(0, 4, 3, 1, 2))
        in_maps.append(m)

    import os
    ncr = int(os.environ.get("KCORES", "8"))
    res = run_bass_kernel_spmd(nc, in_maps[:ncr], core_ids=list(range(ncr)))
    R = list(res.results)
    while len(R) < 8:
        R.append(R[0])

    y_prompt = np.stack([R[b]["yT"].T for b in range(4)])
    y_sample = np.concatenate([R[c]["ysT"].T.reshape(NB, LS, D) for c in range(8)], axis=0)
    nk_p = np.stack([R[b]["nkT"].transpose(0, 2, 1).reshape(NL, 128, 2, 64) for b in range(4)], axis=1)
    nv_p = np.stack([R[b]["nv"].reshape(NL, 128, 2, 64) for b in range(4)], axis=1)

    def unst(a):
        return a.transpose(0, 2, 1).reshape(NL, 16, 64)
    re_p = np.stack([unst(R[b]["nre"]) for b in range(4)], axis=1)
    im_p = np.stack([unst(R[b]["nim"]) for b in range(4)], axis=1)
    cv_p = np.stack([R[b]["ncv"].transpose(0, 3, 2, 1).reshape(NL, 30, 256) for b in range(4)], axis=1)
    nk_s, nv_s, re_s, im_s, cv_s = [], [], [], [], []
    for c in range(8):
        r = R[c]
        knew = r["skT"].transpose(0, 2, 1).reshape(NL, NB, LS, 128)
        nk_s.append(np.concatenate([r["nks_c"], knew], axis=2).reshape(NL, NB, 128, 2, 64))
        vnew = r["svn"].transpose(0, 2, 1, 3)
        nv_s.append(np.concatenate([r["nvs_c"], vnew], axis=2).reshape(NL, NB, 128, 2, 64))
        re_s.append(r["sre"].transpose(0, 3, 2, 1).reshape(NL, NB, 16, 64))
        im_s.append(r["sim_o"].transpose(0, 3, 2, 1).reshape(NL, NB, 16, 64))
        cv_s.append(r["scv"].transpose(0, 3, 4, 2, 1).reshape(NL, NB, 30, 256))
    cat = lambda xs_: np.ascontiguousarray(np.concatenate(xs_, axis=1).astype(np.float32))
    if "dbg" in R[0]:
        _NC_CACHE["dbg"] = R[0]["dbg"]
    outs = (y_prompt, y_sample, nk_p, nv_p, re_p, im_p, cv_p, cat(nk_s), cat(nv_s), cat(re_s), cat(im_s), cat(cv_s))
    return tuple(np.ascontiguousarray(o.astype(np.float32)) for o in outs)
```

```python
import math
import numpy as np
import concourse.bass as bass
import concourse.mybir as mybir
from concourse.bass_utils import run_bass_kernel_spmd

F32 = mybir.dt.float32
BF16 = mybir.dt.bfloat16
ALU = mybir.AluOpType
AF = mybir.ActivationFunctionType

SEG = 30000
NL = 4
D = 1024
KT = 8
NTOK = 4096
T = 256
NCH = NTOK // T
TS = 128
NB = 16
LS = 4
TSM = NB * LS
DFF = 2816
JT = 22
WIN = 2176
PAST = 8192
TWO_PI = 2.0 * math.pi


class Buf:
    __slots__ = ("name", "lw", "rd", "sem", "cnt", "excl")

    def __init__(self, name, excl=False):
        self.name = name
        self.excl = excl
        self.lw = None
        self.rd = {}
        self.sem = None
        self.cnt = 0


class Op:
    __slots__ = ("eng", "fn", "deps", "idx", "dma", "owner", "dcnt", "marked", "ev", "waits")

    def __init__(self, eng, fn, idx):
        self.eng = eng
        self.fn = fn
        self.idx = idx
        self.deps = []
        self.dma = False
        self.owner = None
        self.dcnt = 0
        self.marked = False
        self.ev = None
        self.waits = []


class Prog:
    ENGS = ("pe", "act", "dve", "pool", "sp")

    def __init__(self, nc):
        self.nc = nc
        self.ops = []

    def add(self, eng, fn, reads=(), writes=(), dma_owner=None, extra_deps=()):
        i = len(self.ops)
        op = Op(eng, fn, i)
        if dma_owner is not None:
            op.dma = True
            op.owner = dma_owner
            dma_owner.cnt += 16
            op.dcnt = dma_owner.cnt
        deps = {}
        for b in reads:
            if b.lw is not None:
                deps[b.lw] = "raw"
            if b.excl:
                for r in b.rd.values():
                    if r not in deps:
                        deps[r] = "war"
        for b in writes:
            if b.lw is not None and b.lw not in deps:
                deps[b.lw] = "waw"
            for r in b.rd.values():
                if r not in deps:
                    deps[r] = "war"
        for d in extra_deps:
            deps[d.idx] = "raw"
        deps.pop(i, None)
        for b in reads:
            key = ("d", i) if op.dma else eng
            b.rd[key] = i
        for b in writes:
            b.lw = i
            b.rd = {}
        op.deps = list(deps.items())
        self.ops.append(op)
        return op

    def finalize(self):
        ops = self.ops
        waited = {e: {} for e in self.ENGS}
        for op in ops:
            need = {}
            for d, kind in op.deps:
                p = ops[d]
                if p.dma:
                    key = ("dma", id(p.owner))
                    if need.get(key, (0, None))[0] < p.dcnt:
                        need[key] = (p.dcnt, p)
                else:
                    if p.eng == op.eng and not op.dma:
                        if op.eng == "pe" or kind != "raw":
                            continue
                    key = ("eng", p.eng)
                    if need.get(key, (-1, None))[0] < p.idx:
                        need[key] = (p.idx, p)
            w = waited[op.eng]
            for key, (val, p) in need.items():
                if w.get(key, -1) >= val:
                    continue
                w[key] = val
                op.waits.append(p)
                if not p.dma:
                    p.marked = True
        cnt = {e: 0 for e in self.ENGS}
        for op in ops:
            if not op.dma and op.marked:
                cnt[op.eng] += 1
                op.ev = cnt[op.eng]
        self.evcount = cnt

    def emit(self):
        nc = self.nc
        self.finalize()
        esems = {}
        for e in self.ENGS:
            n = (self.evcount[e] + SEG - 1) // SEG
            esems[e] = [nc.alloc_semaphore(f"ev_{e}_{k}") for k in range(max(n, 1))]
        for op in self.ops:
            if op.dma and op.owner.sem is None:
                op.owner.sem = nc.alloc_semaphore("d_" + op.owner.name)

        def semval(p):
            if p.dma:
                return p.owner.sem, p.dcnt
            k = (p.ev - 1) // SEG
            return esems[p.eng][k], (p.ev - 1) % SEG + 1

        per = {e: [op for op in self.ops if op.eng == e] for e in self.ENGS}

        def run(eng, lst):
            for op in lst:
                for p in op.waits:
                    s, v = semval(p)
                    eng.wait_ge(s, v)
                ins = op.fn(eng)
                if op.dma:
                    ins.then_inc(op.owner.sem, 16)
                elif op.marked:
                    s, _ = semval(op)
                    ins.then_inc(s, 1)

        with nc.Block() as block:
            @block.tensor
            def _(e):
                run(e, per["pe"])

            @block.scalar
            def _(e):
                run(e, per["act"])

            @block.vector
            def _(e):
                run(e, per["dve"])

            @block.gpsimd
            def _(e):
                run(e, per["pool"])

            @block.sync
            def _(e):
                run(e, per["sp"])


def build(nl=NL):
    import os
    KSUB = int(os.environ.get("KSUB", "9"))
    KS2 = int(os.environ.get("KS2", "9"))
    nc = bass.Bass("TRN2", target_bir_lowering=False)
    P = Prog(nc)
    stores = []

    def din(name, shape):
        return nc.dram_tensor(name, list(shape), F32, kind="ExternalInput").ap()

    def dout(name, shape):
        return nc.dram_tensor(name, list(shape), F32, kind="ExternalOutput").ap()

    def sb(name, shape, dt=F32):
        return nc.alloc_sbuf_tensor(name, list(shape), dt)

    xT = din("xT", [D, NTOK])
    xsT = din("xsT", [D, TSM])
    cT = din("cT", [D, 17])
    w_mod = din("w_mod", [NL, D, 6 * D])
    b_modT = din("b_modT", [NL, 128, 48])
    g1T = din("g1T", [NL, 128, KT])
    g2T = din("g2T", [NL, 128, KT])
    gfT = din("gfT", [128, KT])
    w_in2 = din("w_in2", [NL, D, WIN])
    ropeP = din("ropeP", [2, 128, NTOK])
    ropeS = din("ropeS", [2, 128, TSM])
    maskP = din("maskP", [2, 128, 512])
    maskS = din("maskS", [2, 128, 512])
    sinkT = din("sinkT", [NL, 64, 8])
    lamre = din("lamre", [NL, 128, 8])
    lamim = din("lamim", [NL, 128, 8])
    logdt = din("logdt", [NL, 128, 8])
    bre = din("bre", [NL, 128, 8, 16])
    bim = din("bim", [NL, 128, 8, 16])
    cre = din("cre", [NL, 128, 8, 16])
    cim = din("cim", [NL, 128, 8, 16])
    dskipT = din("dskipT", [NL, 128, 2])
    wglu = din("wglu", [NL, 256, 256])
    bgluT = din("bgluT", [NL, 128, 2])
    convwT = din("convwT", [NL, 128, 2, 31])
    convbT = din("convbT", [NL, 128, 2])
    lngT = din("lngT", [NL, 128, 2])
    lnbT = din("lnbT", [NL, 128, 2])
    w_out = din("w_out", [NL, D, D])
    w_gate = din("w_gate", [NL, D, DFF])
    w_up = din("w_up", [NL, D, DFF])
    w_down = din("w_down", [NL, DFF, D])
    kcT = din("kcT", [NL, NB, 128, 128])
    vc = din("vc", [NL, NB, 128, 128])
    kcn = din("kcn", [NL, NB, 128, 128])
    ssmre_in = din("ssmre_in", [NL, 128, 8, NB])
    ssmim_in = din("ssmim_in", [NL, 128, 8, NB])
    sconv_in = din("sconv_in", [NL, 128, 2, NB, 30])
    identd = din("identd", [128, 128])
    jrow = din("jrow", [128, TS + 1])

    yT = dout("yT", [D, NTOK])
    ysT = dout("ysT", [D, TSM])
    nkT = dout("nkT", [NL, 128, 128])
    nv = dout("nv", [NL, 128, 128])
    nre = dout("nre", [NL, 128, 8])
    nim = dout("nim", [NL, 128, 8])
    ncv = dout("ncv", [NL, 128, 2, 30])
    nks_c = dout("nks_c", [NL, NB, 124, 128])
    nvs_c = dout("nvs_c", [NL, NB, 124, 128])
    skT = dout("skT", [NL, 128, TSM])
    svn = dout("svn", [NL, 4, NB, 128])
    sre = dout("sre", [NL, 128, 8, NB])
    sim_o = dout("sim_o", [NL, 128, 8, NB])
    scv = dout("scv", [NL, 128, 2, NB, 30])
    xscr = nc.dram_tensor("xscr", [D, NTOK], F32, kind="Internal").ap()
    wgu_c = nc.dram_tensor("wgu_c", [JT, 128, 2 * KT * 128], BF16, kind="Internal").ap()
    wdn_c = nc.dram_tensor("wdn_c", [16, 128, 11 * 128], BF16, kind="Internal").ap()
    woa_c = nc.dram_tensor("woa_c", [8, 64, 8 * 128], BF16, kind="Internal").ap()
    wor_c = nc.dram_tensor("wor_c", [8, 128, 4 * 128], BF16, kind="Internal").ap()
    bwgu_c = [Buf(f"wguc{j}") for j in range(JT)]
    bwdn_c = [Buf(f"wdnc{j}") for j in range(16)]
    bwo_c = [Buf(f"woc{j}") for j in range(8)]
    DBG = bool(int(os.environ.get("KDBG", "0")))
    if DBG:
        dbg = dout("dbg", [3, 128, 8, TSM])

    PS = [nc.alloc_psum_tensor(f"ps{i}", [128, 512], F32) for i in range(8)]
    bPS = [Buf(f"ps{i}", excl=True) for i in range(8)]
    psrr = [0]

    def nps():
        i = psrr[0]
        psrr[0] = (i + 1) % 6
        return i

    def mm(out, lhsT, rhs, start, stop, r, w):
        return P.add("pe", lambda e: e.matmul(out, lhsT=lhsT, rhs=rhs, start=start, stop=stop), reads=r, writes=w)

    def act(out, in_, func, r, w, bias=None, scale=None):
        kw = {}
        if bias is not None:
            kw["bias"] = bias
        if scale is not None:
            kw["scale"] = scale
        return P.add("act", lambda e: e.activation(out=out, in_=in_, func=func, **kw), reads=r, writes=w)

    def tt(eng, out, in0, in1, op, r, w):
        return P.add(eng, lambda e: e.tensor_tensor(out=out, in0=in0, in1=in1, op=op), reads=r, writes=w)

    def ts(eng, out, in0, s1, s2, op0, op1, r, w):
        if op1 is None:
            return P.add(eng, lambda e: e.tensor_scalar(out=out, in0=in0, scalar1=s1, scalar2=None, op0=op0), reads=r, writes=w)
        return P.add(eng, lambda e: e.tensor_scalar(out=out, in0=in0, scalar1=s1, scalar2=s2, op0=op0, op1=op1), reads=r, writes=w)

    def stt(eng, out, in0, scalar, in1, op0, op1, r, w):
        return P.add(eng, lambda e: e.scalar_tensor_tensor(out=out, in0=in0, scalar=scalar, in1=in1, op0=op0, op1=op1), reads=r, writes=w)

    def cp(eng, out, in_, r, w):
        if eng == "act":
            return P.add("act", lambda e: e.activation(out=out, in_=in_, func=AF.Copy), reads=r, writes=w)
        return P.add(eng, lambda e: e.tensor_copy(out=out, in_=in_), reads=r, writes=w)

    def memset(eng, ap, val, w):
        return P.add(eng, lambda e: e.memset(ap, val), writes=w)

    def ld(out, in_, owner, w, q="sp"):
        return P.add(q, lambda e: e.dma_start(out=out, in_=in_), writes=w, dma_owner=owner)

    def st(out, in_, owner, r):
        o = P.add("sp", lambda e: e.dma_start(out=out, in_=in_), reads=r, dma_owner=owner)
        stores.append(o)
        return o

    ident = sb("ident", [128, 128]); bident = Buf("ident")
    ld(ident[:], identd, bident, [bident])
    ones16 = sb("ones16", [128, 128], BF16); bones = Buf("ones")
    memset("dve", ones16[:], 1.0, [bones])
    jr = sb("jr", [128, TS + 1]); bjr = Buf("jr")
    ld(jr[:], jrow, bjr, [bjr])
    mk = sb("mk", [128, 4, 512], BF16); bmk = Buf("mk")
    mkn = mk[:, 0:2, :]; bmkn = bmk
    ld(mk[:, 0:2, :], maskP.rearrange("a p n -> p a n"), bmk, [bmk], q="pool")
    ld(mk[:, 2:4, :], maskS.rearrange("a p n -> p a n"), bmk, [bmk], q="pool")
    ident16 = sb("ident16", [128, 128], BF16); bident16 = Buf("ident16")
    cp("dve", ident16[:], ident[:], [bident], [bident16])
    rps = sb("rps", [128, 2, TSM]); brps = Buf("rps")
    ld(rps[:], ropeS.rearrange("a p n -> p a n"), brps, [brps])
    gf = sb("gf", [128, KT]); bgf = Buf("gf")
    ld(gf[:], gfT, bgf, [bgf])
    pat = sb("pat", [128, NB, LS]); bpat = Buf("pat")
    memset("dve", pat[:], 1.0, [bpat])
    memset("dve", pat[:, :, 0:1], 0.0, [bpat])

    modT = sb("modT", [128, NL, 48, 17]); bmod = Buf("modT")
    csb = sb("csb", [128, KT, 17]); bcs = Buf("csb")
    sgc = sb("sgc", [128, KT, 17]); bsgc = Buf("sgc")
    ld(csb[:], cT.rearrange("(k p) n -> p k n", p=128), bcs, [bcs])
    act(sgc[:], csb[:], AF.Sigmoid, [bcs], [bsgc])
    tt("dve", csb[:], csb[:], sgc[:], ALU.mult, [bcs, bsgc], [bcs])
    bmt = sb("bmt", [128, NL, 48]); bbmt = Buf("bmt")
    ld(bmt[:], b_modT.rearrange("l p m -> p l m"), bbmt, [bbmt])
    WMB = 256
    _g0 = nc.sbuf_tensor("wmr0", [128, KT, WMB], F32)
    _g1 = nc.sbuf_tensor("wmr1", [128, KT, WMB], F32)
    wmr = [_g0.__enter__(), _g1.__enter__()]
    bwmr = [Buf(f"wmr{i}") for i in range(2)]
    lastmod = None
    it = 0
    for l in range(nl):
        for blk in range(6 * D // WMB):
            s = it % 2
            it += 1
            ld(wmr[s][:], w_mod[l, :, blk * WMB:(blk + 1) * WMB].rearrange("(k p) n -> p k n", p=128), bwmr[s], [bwmr[s]])
            for mi in range(WMB // 128):
                m = blk * (WMB // 128) + mi
                pi = nps()
                for k in range(KT):
                    mm(PS[pi][:, 0:17], wmr[s][:, k, mi * 128:(mi + 1) * 128], csb[:, k, :], k == 0, k == KT - 1,
                       [bwmr[s], bcs], [bPS[pi]])
                lastmod = ts("dve", modT[:, l, m, :], PS[pi][:, 0:17], bmt[:, l, m:m + 1], None, ALU.add, None, [bPS[pi], bbmt], [bmod])
    _g1.__exit__(None, None, None)
    _g0.__exit__(None, None, None)
    for _e in ("pe", "act", "pool", "sp"):
        P.add(_e, lambda e: e.nop(), extra_deps=[lastmod])

    xs = sb("xs", [128, KT, T]); bx = [Buf(f"x{k}") for k in range(KT)]
    xsm = sb("xsm", [128, KT, TSM]); bxsm = Buf("xsm")
    ld(xsm[:], xsT.rearrange("(k p) n -> p k n", p=128), bxsm, [bxsm])
    h16 = sb("h16", [128, KT, T], BF16); bh = Buf("h16")
    scr16 = sb("scr16", [128, JT, T], BF16); bscr = Buf("scr16")
    rstd = sb("rstd", [128, T]); brstd = Buf("rstd")
    tmpAB = sb("tmpAB", [128, 2, T])
    tmpA = tmpAB[:, 0, :]; btA = Buf("tmpA")
    tmpB = tmpAB[:, 1, :]; btB = Buf("tmpB")
    tmpC = sb("tmpC", [128, T]); btC = Buf("tmpC")
    tmpD = sb("tmpD", [128, T]); btD = Buf("tmpD")
    rp = sb("rp", [128, 2, T]); brp = Buf("rp")
    q16 = sb("q16", [128, 4, T], BF16); bq = Buf("q16")
    kb16 = sb("kb16", [128, 128 + T], BF16); bkb = Buf("kb16")
    k32 = sb("k32", [128, T]); bk32 = Buf("k32")
    v16 = sb("v16", [128, 1 + T // 128, 128], BF16); bv16 = Buf("v16")
    v32 = sb("v32", [128, 128]); bv32 = Buf("v32")
    pown = sb("pown", [128, 512], BF16); bpown = Buf("pown")
    pprev = sb("pprev", [128, 512], BF16); bpprev = Buf("pprev")
    den = tmpAB[0:64].rearrange("p a t -> p (a t)")
    att16 = sb("att16", [64, 8, T], BF16); batt = Buf("att16")
    u32 = sb("u32", [128, 2, T]); bu32 = Buf("u32")
    u16 = sb("u16", [128, 2, T], BF16); bu16 = Buf("u16")
    cb32 = sb("cb32", [128, 2, 30 + T]); bcb32 = Buf("cb32")
    cb16 = sb("cb16", [128, 2, 30 + T], BF16); bcb16 = Buf("cb16")
    cbs32 = sb("cbs32", [128, 2, NB, 34]); bcbs32 = Buf("cbs32")
    cbs16 = sb("cbs16", [128, 2, NB, 34], BF16); bcbs16 = Buf("cbs16")
    ycf = sb("ycf", [128, 2, T]); bycf = Buf("ycf")
    cva = sb("cva", [128, T]); bcva = Buf("cva")
    cvb = sb("cvb", [128, T]); bcvb = Buf("cvb")
    cvc = sb("cvc", [128, T]); bcvc = Buf("cvc")
    yc16 = scr16[:, 8:12, :]; byc16 = bscr
    oc16 = sb("oc16", [128, 2, T], BF16); boc = Buf("oc16")
    os16 = sb("os16", [128, 2, T], BF16); bos = Buf("os16")
    yss = sb("yss", [128, 2, T]); byss = Buf("yss")
    z32 = sb("z32", [128, 2, T]); bz32 = Buf("z32")
    assert 2 * T == 4 * 8 * 16
    z16 = sb("z16", [128, 2, T], BF16); bz16 = Buf("z16")
    sx = [sb(f"sx{i}", [128, TS]) for i in range(8)]
    bsx = [Buf(f"sx{i}") for i in range(8)]
    sq = [[sb(f"sq{i}{j}", [128, TS]) for j in range(2)] for i in range(2)]
    bsq = [[Buf(f"sq{i}{j}") for j in range(2)] for i in range(2)]
    sh16 = [[sb(f"sh16{i}{j}", [128, TS], BF16) for j in range(2)] for i in range(2)]
    bsh16 = [[Buf(f"sh16{i}{j}") for j in range(2)] for i in range(2)]
    ssi = [0]
    hre16 = sh16[0][0]; bhre = bsh16[0][0]
    him16 = sh16[0][1]; bhim = bsh16[0][1]
    qlre = sb("qlre", [128, 8]); qlim = sb("qlim", [128, 8]); bql = Buf("ql")
    hlre = sb("hlre", [128, 8]); hlim = sb("hlim", [128, 8]); bhl = Buf("hl")
    inre = sb("inre", [128, 8]); inim = sb("inim", [128, 8]); binit = Buf("init")
    sm8 = [sb(f"sm8_{i}", [128, 8]) for i in range(4)]; bsm8 = Buf("sm8")
    h0re = sb("h0re", [128, 8, NB]); h0im = sb("h0im", [128, 8, NB]); bh0 = Buf("h0")
    ahre = sb("ahre", [128, 8, NB]); ahim = sb("ahim", [128, 8, NB]); bah = Buf("ah")
    hsre = sb("hsre", [128, 8, NB]); hsim = sb("hsim", [128, 8, NB]); bhs = Buf("hs")
    st16 = [sb(f"st16_{i}", [128, NB]) for i in range(4)]; bst16 = Buf("st16")
    r_kc = sb("r_kc", [128, 2048], BF16); bkc = Buf("kc16")
    r_vc = sb("r_vc", [128, 2048], BF16); bvc = Buf("vc16")
    kc16 = r_kc[:].rearrange("p (b k) -> p b k", k=128)
    vc16 = r_vc[:].rearrange("p (b k) -> p b k", k=128)
    vn16 = sb("vn16", [4, NB, 128], BF16); bvn16 = Buf("vn16")
    vn32 = sb("vn32", [4, NB, 128]); bvn32 = Buf("vn32")
    pn16 = sb("pn16", [4, 512], BF16); bpn = Buf("pn16")

    win16 = sb("win16", [128, KT, WIN], BF16); bwin = Buf("win16")
    wgl16 = sb("wgl16", [128, 2, 256], BF16); bwgl = Buf("wgl16")
    sp8 = sb("sp8", [128, 12, 8]); bsp8 = Buf("sp8")
    sp2 = sb("sp2", [128, 8, 2]); bsp2 = Buf("sp2")
    cw = sb("cw", [128, 2, 31]); bcw = Buf("cw")
    sk = sb("sk", [64, 8]); bsk = Buf("sk")
    bc32 = yss[:].rearrange("p a (b c) -> p (a b) c", c=16).rearrange("p (a b) c -> p a b c", a=4); bbc = byss
    bbt = z32[:].rearrange("p a (b c) -> p (a b) c", c=16).rearrange("p (a b) c -> p a b c", a=4); bbbt = bz32
    exq = sb("exq", [128, 128]); bexq = Buf("exq")
    LB = sb("LB", [128, 4, 8, 128], BF16); bLB = Buf("LB")
    cosT = sb("cosT", [128, 8, TS + 1]); sinT = sb("sinT", [128, 8, TS + 1]); btab = Buf("tab")
    r4 = sb("r4", [128, 8, NB, LS]); btab4 = Buf("tab4")
    diag16 = sb("diag16", [128, 2, 31, 128], BF16); bdiag = Buf("diag16")
    gsc = sb("gsc", [128, 2, KT, 17]); bgsc = Buf("gsc")
    NG = 6
    _raw = [sb("wr0", [128, 2048], BF16), sb("wr1", [128, 2048], BF16), r_kc, r_vc, sb("wr4", [128, 2048], BF16), sb("wr5", [128, 2048], BF16)]
    bwgu = [Buf("wr0"), Buf("wr1"), bkc, bvc, Buf("wr4"), Buf("wr5")]
    wfl = [r[:] for r in _raw]
    wgu = [r[:].rearrange("p (a k c) -> p a k c", a=2, k=KT) for r in _raw]
    wdn = [r[:, 0:1408].rearrange("p (j c) -> p j c", c=128) for r in _raw]
    bwdn = bwgu
    ffi = [0, 0]

    def S8(i):
        return sp8[:, i, :]

    angt = tmpC[:, 0:TS + 1]; angk = tmpD[:, 0:TS + 1]; bang = btC
    CM = 12582912.0

    def sin_of(out, x, shift, tmp, r, w, wt):
        xs_ = x
        if shift != 0.0:
            ts("dve", out, x, shift, None, ALU.add, None, r, w)
            xs_ = out
        ts("dve", tmp, xs_, 1.0 / TWO_PI, CM, ALU.mult, ALU.add, r + w, wt)
        ts("dve", tmp, tmp, -CM, None, ALU.add, None, r + wt, wt)
        stt("dve", tmp, tmp, -TWO_PI, xs_, ALU.mult, ALU.add, r + w + wt, wt)
        ts("dve", tmp, tmp, math.pi, -math.pi, ALU.min, ALU.max, r + wt, wt)
        act(out, tmp, AF.Sin, r + wt, w)

    def layer_params(l):
        for k in range(KT):
            ld(win16[:, k, :], w_in2[l, k * 128:(k + 1) * 128, :], bwin, [bwin], q="pool")
        ld(wgl16[:], wglu[l].rearrange("(k p) n -> p k n", p=128), bwgl, [bwgl], q="pool")
        ld(sp8[:, 0, :], lamre[l], bsp8, [bsp8])
        ld(sp8[:, 1, :], lamim[l], bsp8, [bsp8])
        ld(sp8[:, 2, :], logdt[l], bsp8, [bsp8])
        ld(sp2[:, 0, :], dskipT[l], bsp2, [bsp2])
        ld(sp2[:, 1, :], bgluT[l], bsp2, [bsp2])
        ld(sp2[:, 2, :], convbT[l], bsp2, [bsp2])
        ld(sp2[:, 3, :], lngT[l], bsp2, [bsp2])
        ld(sp2[:, 4, :], lnbT[l], bsp2, [bsp2])
        ld(cw[:], convwT[l], bcw, [bcw])
        ld(sk[:], sinkT[l], bsk, [bsk])
        act(sk[:], sk[:], AF.Exp, [bsk], [bsk])
        ld(bc32[:, 0], bre[l], bbc, [bbc])
        ld(bc32[:, 1], bim[l], bbc, [bbc])
        ld(bc32[:, 2], cre[l], bbc, [bbc])
        ld(bc32[:, 3], cim[l], bbc, [bbc])
        for a, (gT, off) in enumerate(((g1T, 8), (g2T, 32))):
            ld(sp8[:, 3, :], gT[l], bsp8, [bsp8])
            ts("dve", gsc[:, a], modT[:, l, off:off + 8, :], 1.0, None, ALU.add, None, [bmod], [bgsc])
            tt("dve", gsc[:, a], gsc[:, a], sp8[:, 3, :].unsqueeze(2).to_broadcast([128, KT, 17]), ALU.mult, [bgsc, bsp8], [bgsc])
        R = [bsp8]
        W = [bsp8]
        act(S8(2), S8(2), AF.Exp, R, W)
        tt("dve", S8(3), S8(0), S8(2), ALU.mult, R, W)
        tt("dve", S8(4), S8(1), S8(2), ALU.mult, R, W)
        act(S8(5), S8(3), AF.Exp, R, W)
        sin_of(S8(7), S8(4), 0.0, sm8[0][:], R + [bsm8], W, [bsm8])
        sin_of(S8(6), S8(4), 0.5 * math.pi, sm8[0][:], R + [bsm8], W, [bsm8])
        tt("dve", S8(8), S8(5), S8(6), ALU.mult, R, W)
        tt("dve", S8(9), S8(5), S8(7), ALU.mult, R, W)
        a0, a1, a2, a3 = (sm8[i][:] for i in range(4))
        R2 = [bsp8, bsm8]
        tt("dve", a0, S8(0), S8(0), ALU.mult, R2, [bsm8])
        tt("dve", a1, S8(1), S8(1), ALU.mult, R2, [bsm8])
        tt("dve", a0, a0, a1, ALU.add, R2, [bsm8])
        P.add("dve", lambda e: e.reciprocal(out=a0, in_=a0), reads=R2, writes=[bsm8])
        ts("dve", a1, S8(8), -1.0, None, ALU.add, None, R2, [bsm8])
        tt("dve", a2, a1, S8(0), ALU.mult, R2, [bsm8])
        tt("dve", a3, S8(9), S8(1), ALU.mult, R2, [bsm8])
        tt("dve", a2, a2, a3, ALU.add, R2, [bsm8])
        tt("dve", S8(10), a2, a0, ALU.mult, R2, W)
        tt("dve", a2, S8(9), S8(0), ALU.mult, R2, [bsm8])
        tt("dve", a3, a1, S8(1), ALU.mult, R2, [bsm8])
        tt("dve", a2, a2, a3, ALU.subtract, R2, [bsm8])
        tt("dve", S8(11), a2, a0, ALU.mult, R2, W)
        cre_b = sp8[:, 10, :].unsqueeze(2).to_broadcast([128, 8, 16])
        cim_b = sp8[:, 11, :].unsqueeze(2).to_broadcast([128, 8, 16])
        Rb = [bbc, bsp8, bbbt]
        tt("dve", bbt[:, 0], bc32[:, 0], cre_b, ALU.mult, Rb, [bbbt])
        tt("dve", bbt[:, 2], bc32[:, 1], cim_b, ALU.mult, Rb, [bbbt])
        tt("dve", bbt[:, 0], bbt[:, 0], bbt[:, 2], ALU.subtract, Rb, [bbbt])
        tt("dve", bbt[:, 1], bc32[:, 1], cre_b, ALU.mult, Rb, [bbbt])
        tt("dve", bbt[:, 2], bc32[:, 0], cim_b, ALU.mult, Rb, [bbbt])
        tt("dve", bbt[:, 1], bbt[:, 1], bbt[:, 2], ALU.add, Rb, [bbbt])
        cp("dve", bbt[:, 2], bc32[:, 2], Rb, [bbbt])
        ts("dve", bbt[:, 3], bc32[:, 3], -1.0, None, ALU.mult, None, Rb, [bbbt])
        for mi in range(4):
            for ct in range(8):
                memset("dve", exq[:], 0.0, [bexq])
                for gg in range(2):
                    gp = (2 * ct + gg) % 8
                    cp("dve", exq[64 * gg:64 * gg + 64, 16 * gp:16 * gp + 16], bbt[64 * gg:64 * gg + 64, mi, ct, :], [bbbt], [bexq])
                if mi < 2:
                    pi = nps()
                    P.add("pe", lambda e, pi=pi: e.transpose(out=PS[pi][:, 0:128], in_=exq[:], identity=ident[:]),
                          reads=[bexq, bident], writes=[bPS[pi]])
                    cp("act", LB[:, mi, ct, :], PS[pi][:, 0:128], [bPS[pi]], [bLB])
                else:
                    cp("act", LB[:, mi, ct, :], exq[:], [bexq], [bLB])
        for ct in range(8):
            ts("dve", sx[0][:, 0:TS + 1] if TS + 1 <= TS else angt, jr[:], sp8[:, 4, ct:ct + 1], None, ALU.mult, None, [bjr, bsp8], [bang])
            sin_of(sinT[:, ct, :], angt, 0.0, angk, [bang], [btab], [btD])
            sin_of(cosT[:, ct, :], angt, 0.5 * math.pi, angk, [bang], [btab], [btD])
        for ct in range(8):
            ts("dve", r4[:, ct], pat[:], sp8[:, 5, ct:ct + 1], None, ALU.mult, None, [bpat, bsp8], [btab4])
        for m in range(2):
            for k in range(31):
                ts("pool", diag16[:, m, k, :], ident[:], cw[:, m, k:k + 1], None, ALU.mult, None, [bident, bcw], [bdiag])

    def rmsnorm(xa, bxs, n, gcol, shm, a, l, sample):
        for k in range(KT):
            act(scr16[:, k, 0:n], xa[:, k, :], AF.Square, [bxs[k]], [bscr])
        pi = nps()
        for k in range(KT):
            mm(PS[pi][:, 0:n], ones16[:], scr16[:, k, 0:n], k == 0, k == KT - 1, [bones, bscr], [bPS[pi]])
        ts("dve", rstd[:, 0:n], PS[pi][:, 0:n], 1.0 / D, 1e-6, ALU.mult, ALU.add, [bPS[pi]], [brstd])
        act(rstd[:, 0:n], rstd[:, 0:n], AF.Sqrt, [brstd], [brstd])
        P.add("dve", lambda e: e.reciprocal(out=rstd[:, 0:n], in_=rstd[:, 0:n]), reads=[brstd], writes=[brstd])
        for k in range(KT):
            tA = tmpA if k % 2 == 0 else tmpB
            bA = btA if k % 2 == 0 else btB
            tt("dve", tA[:, 0:n], xa[:, k, :], rstd[:, 0:n], ALU.mult, [bxs[k], brstd], [bA])
            if gcol is None:
                ts("pool", xa[:, k, :], tA[:, 0:n], gf[:, k:k + 1], None, ALU.mult, None, [bA, bgf], [bxs[k]])
            elif not sample:
                if False:
                    pass
                else:
                    act(h16[:, k, 0:n], tA[:, 0:n], AF.Identity, [bA, bgsc, bmod], [bh],
                        bias=modT[:, l, shm + k, 0:1], scale=gsc[:, a, k, 0:1])
            else:
                v3 = tA[:, 0:n].rearrange("p (b t) -> p b t", t=LS)
                if True:
                    tt("pool", v3, v3, gsc[:, a, k, 1:17].unsqueeze(2).to_broadcast([128, NB, LS]), ALU.mult, [bA, bgsc], [bA])
                    tt("pool", h16[:, k, 0:n].rearrange("p (b t) -> p b t", t=LS), v3,
                       modT[:, l, shm + k, 1:17].unsqueeze(2).to_broadcast([128, NB, LS]), ALU.add, [bA, bmod], [bh])

    def resid(xa, bxs, n, mo, pi, l, gm, sample):
        if not sample:
            stt("dve", xa[:, mo, :], PS[pi][:, 0:n], modT[:, l, gm + mo, 0:1], xa[:, mo, :], ALU.mult, ALU.add,
                [bPS[pi], bmod, bxs[mo]], [bxs[mo]])
        else:
            tt("dve", tmpC[:, 0:n].rearrange("p (b t) -> p b t", t=LS), PS[pi][:, 0:n].rearrange("p (b t) -> p b t", t=LS),
               modT[:, l, gm + mo, 1:17].unsqueeze(2).to_broadcast([128, NB, LS]), ALU.mult, [bPS[pi], bmod], [btC])
            tt("dve", xa[:, mo, :], xa[:, mo, :], tmpC[:, 0:n], ALU.add, [btC, bxs[mo]], [bxs[mo]])

    def inproj_tile(m, n):
        pi = nps()
        for k in range(KT):
            mm(PS[pi][:, 0:n], win16[:, k, m * 128:(m + 1) * 128], h16[:, k, 0:n], k == 0, k == KT - 1, [bwin, bh], [bPS[pi]])
        return pi

    def rope_pair(m_a, m_b, cos_ap, sin_ap, brope, out_ap, bout, n, extra32=None):
        pa = inproj_tile(m_a, n)
        pb = inproj_tile(m_b, n)
        tt("dve", tmpA[:, 0:n], PS[pa][:, 0:n], cos_ap, ALU.mult, [bPS[pa], brope], [btA])
        tt("dve", tmpB[:, 0:n], PS[pb][:, 0:n], sin_ap, ALU.mult, [bPS[pb], brope], [btB])
        tt("pool", out_ap, tmpA[:, 0:n], tmpB[:, 0:n], ALU.add, [btA, btB], [bout])
        if extra32 is not None:
            tt("pool", extra32[0], tmpA[:, 0:n], tmpB[:, 0:n], ALU.add, [btA, btB], [extra32[1]])

    def gelu_glu(n):
        for o in range(2):
            tt("pool", tmpC[:, 0:n], yss[:, o, 0:n], yss[:, o, 0:n], ALU.mult, [byss], [btC])
            ts("pool", tmpC[:, 0:n], tmpC[:, 0:n], 0.044715, 1.0, ALU.mult, ALU.add, [btC], [btC])
            tt("pool", tmpC[:, 0:n], tmpC[:, 0:n], yss[:, o, 0:n], ALU.mult, [btC, byss], [btC])
            act(tmpC[:, 0:n], tmpC[:, 0:n], AF.Sigmoid, [btC], [btC], scale=2.0 * math.sqrt(2.0 / math.pi))
            tt("dve", z32[:, o, 0:n], yss[:, o, 0:n], tmpC[:, 0:n], ALU.mult, [byss, btC], [bz32])
            cp("pool", z16[:, o, 0:n], z32[:, o, 0:n], [bz32], [bz16])
        for o in range(2):
            pi = nps()
            for k in range(2):
                mm(PS[pi][:, 0:n], wgl16[:, k, o * 128:(o + 1) * 128], z16[:, k, 0:n], k == 0, k == 1, [bwgl, bz16], [bPS[pi]])
            act(tmpD[:, 0:n], PS[pi][:, 0:n], AF.Sigmoid, [bPS[pi], bsp2], [btD], bias=sp2[:, 1, o:o + 1])
            tt("dve", os16[:, o, 0:n], z32[:, o, 0:n], tmpD[:, 0:n], ALU.mult, [bz32, btD], [bos])

    def conv_ln_stages(rhs_fn, n, view):
        def st_conv(m):
            def f():
                pi = nps()
                for k in range(31):
                    mm(view(PS[pi][:, 0:n]), diag16[:, m, k, :], rhs_fn(m, k), k == 0, k == 30, [bdiag, bcb16, bcbs16], [bPS[pi]])
                act(ycf[:, m, 0:n], PS[pi][:, 0:n], AF.Identity, [bPS[pi], bsp2], [bycf], bias=sp2[:, 2, m:m + 1])
                cp("dve", yc16[:, m, 0:n], ycf[:, m, 0:n], [bycf], [byc16])
                act(yc16[:, 2 + m, 0:n], ycf[:, m, 0:n], AF.Square, [bycf], [byc16])
            return f

        def st_stats():
            p1 = nps()
            for m in range(2):
                mm(PS[p1][:, 0:n], ones16[:], yc16[:, m, 0:n], m == 0, m == 1, [bones, byc16], [bPS[p1]])
            p2 = nps()
            for m in range(2):
                mm(PS[p2][:, 0:n], ones16[:], yc16[:, 2 + m, 0:n], m == 0, m == 1, [bones, byc16], [bPS[p2]])
            ts("dve", cva[:, 0:n], PS[p1][:, 0:n], 1.0 / 256, None, ALU.mult, None, [bPS[p1]], [bcva])
            tt("dve", cvb[:, 0:n], cva[:, 0:n], cva[:, 0:n], ALU.mult, [bcva], [bcvb])
            stt("dve", cvb[:, 0:n], PS[p2][:, 0:n], 1.0 / 256, cvb[:, 0:n], ALU.mult, ALU.subtract, [bPS[p2], bcvb], [bcvb])
            ts("dve", cvb[:, 0:n], cvb[:, 0:n], 1e-6, None, ALU.add, None, [bcvb], [bcvb])
            act(cvb[:, 0:n], cvb[:, 0:n], AF.Sqrt, [bcvb], [bcvb])
            P.add("dve", lambda e: e.reciprocal(out=cvb[:, 0:n], in_=cvb[:, 0:n]), reads=[bcvb], writes=[bcvb])

        def st_apply(m):
            def f():
                tt("dve", ycf[:, m, 0:n], ycf[:, m, 0:n], cva[:, 0:n], ALU.subtract, [bycf, bcva], [bycf])
                tt("dve", ycf[:, m, 0:n], ycf[:, m, 0:n], cvb[:, 0:n], ALU.mult, [bycf, bcvb], [bycf])
                act(ycf[:, m, 0:n], ycf[:, m, 0:n], AF.Identity, [bycf, bsp2], [bycf], bias=sp2[:, 4, m:m + 1], scale=sp2[:, 3, m:m + 1])
                act(cvc[:, 0:n], ycf[:, m, 0:n], AF.Sigmoid, [bycf], [bcvc])
                tt("dve", oc16[:, m, 0:n], ycf[:, m, 0:n], cvc[:, 0:n], ALU.mult, [bcvc, bycf], [boc])
            return f
        return [st_conv(0), st_conv(1), st_stats, st_apply(0), st_apply(1)]

    def conv_ln(rhs_fn, n, view):
        for f in conv_ln_stages(rhs_fn, n, view):
            f()

    def outproj_ffn(xa, bxs, n, l, sample, first=False):
        for mo in range(8):
            s = ffi[0] % NG
            ffi[0] += 1
            wa_v = wfl[s][0:64, 0:1024].rearrange("p (h c) -> p h c", c=128)
            wr_v = wfl[s][:, 1024:1536].rearrange("p (h c) -> p h c", c=128)
            fa = wfl[s][0:64, 0:1024]
            fr = wfl[s][:, 1024:1536]
            bw = bwgu[s]
            if first:
                ld(wa_v, w_out[l, 0:512, mo * 128:(mo + 1) * 128].rearrange("(h d) n -> d h n", d=64), bw, [bw], q="pool")
                ld(wr_v, w_out[l, 512:1024, mo * 128:(mo + 1) * 128].rearrange("(j p) n -> p j n", p=128), bw, [bw], q="pool")
                P.add("sp", lambda e, fa=fa, mo=mo: e.dma_start(out=woa_c[mo], in_=fa), reads=[bw], writes=[bwo_c[mo]], dma_owner=bw)
                P.add("sp", lambda e, fr=fr, mo=mo: e.dma_start(out=wor_c[mo], in_=fr), reads=[bw], writes=[bwo_c[mo]], dma_owner=bw)
            else:
                P.add("sp", lambda e, fa=fa, mo=mo: e.dma_start(out=fa, in_=woa_c[mo]), reads=[bwo_c[mo]], writes=[bw], dma_owner=bw)
                P.add("sp", lambda e, fr=fr, mo=mo: e.dma_start(out=fr, in_=wor_c[mo]), reads=[bwo_c[mo]], writes=[bw], dma_owner=bw)
            pi = nps()
            for hq in range(8):
                mm(PS[pi][:, 0:n], wa_v[:, hq, :], att16[:, hq, 0:n], hq == 0, False, [bw, batt], [bPS[pi]])
            for j in range(2):
                mm(PS[pi][:, 0:n], wr_v[:, j, :], os16[:, j, 0:n], False, False, [bw, bos], [bPS[pi]])
            for j in range(2):
                mm(PS[pi][:, 0:n], wr_v[:, 2 + j, :], oc16[:, j, 0:n], False, j == 1, [bw, boc], [bPS[pi]])
            resid(xa, bxs, n, mo, pi, l, 16, sample)
        rmsnorm(xa, bxs, n, 1, 24, 1, l, sample)
        for j in range(JT):
            s = ffi[0] % NG
            ffi[0] += 1
            fg = wfl[s]
            if first:
                ld(wgu[s][:, 0], w_gate[l, :, j * 128:(j + 1) * 128].rearrange("(k p) n -> p k n", p=128), bwgu[s], [bwgu[s]], q="pool")
                ld(wgu[s][:, 1], w_up[l, :, j * 128:(j + 1) * 128].rearrange("(k p) n -> p k n", p=128), bwgu[s], [bwgu[s]], q="pool")
                P.add("sp", lambda e, fg=fg, j=j: e.dma_start(out=wgu_c[j], in_=fg), reads=[bwgu[s]], writes=[bwgu_c[j]], dma_owner=bwgu[s])
            else:
                P.add("sp", lambda e, fg=fg, j=j: e.dma_start(out=fg, in_=wgu_c[j]), reads=[bwgu_c[j]], writes=[bwgu[s]], dma_owner=bwgu[s])
            pg = nps()
            for k in range(KT):
                mm(PS[pg][:, 0:n], wgu[s][:, 0, k, :], h16[:, k, 0:n], k == 0, k == KT - 1, [bwgu[s], bh], [bPS[pg]])
            pu = nps()
            for k in range(KT):
                mm(PS[pu][:, 0:n], wgu[s][:, 1, k, :], h16[:, k, 0:n], k == 0, k == KT - 1, [bwgu[s], bh], [bPS[pu]])
            tA = tmpA if j % 2 == 0 else tmpB
            bA = btA if j % 2 == 0 else btB
            act(tA[:, 0:n], PS[pg][:, 0:n], AF.Silu, [bPS[pg]], [bA])
            tt("dve", scr16[:, j, 0:n], tA[:, 0:n], PS[pu][:, 0:n], ALU.mult, [bA, bPS[pu]], [bscr])
        for mo in range(8):
            pi = nps()
            for jh in range(2):
                s = ffi[0] % NG
                ffi[0] += 1
                ci = mo * 2 + jh
                fd = wfl[s][:, 0:1408]
                if first:
                    ld(wdn[s], w_down[l, jh * 1408:(jh + 1) * 1408, mo * 128:(mo + 1) * 128].rearrange("(j p) n -> p j n", p=128), bwdn[s], [bwdn[s]], q="pool")
                    P.add("sp", lambda e, fd=fd, ci=ci: e.dma_start(out=wdn_c[ci], in_=fd), reads=[bwdn[s]], writes=[bwdn_c[ci]], dma_owner=bwdn[s])
                else:
                    P.add("sp", lambda e, fd=fd, ci=ci: e.dma_start(out=fd, in_=wdn_c[ci]), reads=[bwdn_c[ci]], writes=[bwdn[s]], dma_owner=bwdn[s])
                for jj in range(11):
                    j = jh * 11 + jj
                    mm(PS[pi][:, 0:n], wdn[s][:, jj, :], scr16[:, j, 0:n], j == 0, j == JT - 1, [bwdn[s], bscr], [bPS[pi]])
            resid(xa, bxs, n, mo, pi, l, 40, sample)

    def prompt_chunk(l, c):
        n = T
        t0 = c * T
        src = xT if l == 0 else xscr
        xdr = src[:, t0:t0 + T].rearrange("(k p) t -> p k t", p=128)
        ld(xs[:], xdr, bx[0], bx)
        ld(rp[:], ropeP[:, :, t0:t0 + T].rearrange("a p t -> p a t"), brp, [brp])
        if KS2 < 1:
            return
        rmsnorm(xs, bx, n, 1, 0, 0, l, False)
        if KS2 < 2:
            return
        for j in range(4):
            rope_pair(j, 5 + j, rp[:, 0, :], rp[:, 1, :], brp, q16[:, j, :], bq, n)
        rope_pair(4, 9, rp[:, 0, :], rp[:, 1, :], brp, kb16[:, 128:128 + T], bkb, n, extra32=(k32[:, 0:n], bk32))
        if KS2 < 3:
            return
        for o in range(2):
            pi = inproj_tile(10 + o, n)
            cp("act", u32[:, o, :], PS[pi][:, 0:n], [bPS[pi]], [bu32])
            cp("dve", u16[:, o, :], PS[pi][:, 0:n], [bPS[pi]], [bu16])
        for o in range(2):
            pa = inproj_tile(12 + o, n)
            pg = inproj_tile(14 + o, n)
            act(tmpC[:, 0:n], PS[pg][:, 0:n], AF.Sigmoid, [bPS[pg]], [btC])
            tt("dve", cb32[:, o, 30:30 + T], PS[pa][:, 0:n], tmpC[:, 0:n], ALU.mult, [bPS[pa], btC], [bcb32])
            if c == 0:
                memset("pool", cb32[:, o, 0:30], 0.0, [bcb32])
            cp("pool", cb16[:, o, :], cb32[:, o, :], [bcb32], [bcb16])
        if KS2 < 4:
            return
        for tb in range(T // 128):
            pi = nps()
            for k in range(KT):
                mm(PS[pi][:, 0:128], h16[:, k, tb * 128:(tb + 1) * 128], win16[:, k, 2048:2176], k == 0, k == KT - 1, [bh, bwin], [bPS[pi]])
            cp("act", v16[:, 1 + tb, :], PS[pi][:, 0:128], [bPS[pi]], [bv16])
            if c == NCH - 1 and tb == T // 128 - 1:
                cp("dve", v32[:], PS[pi][:, 0:128], [bPS[pi]], [bv32])
                st(nv[l], v32[:], bv32, [bv32])
        if c == NCH - 1:
            st(nkT[l], k32[:, T - 128:T], bk32, [bk32])
        if KSUB < 1:
            return
        blocks = [(qb_, hh_) for qb_ in range(T // 128) for hh_ in range(2)]

        def emit_scores(qb_, hh_):
            hs_ = slice(64 * hh_, 64 * hh_ + 64)
            qrhs_ = q16[hs_, :, qb_ * 128:(qb_ + 1) * 128]
            first_ = (c == 0 and qb_ == 0)
            po_ = nps()
            mm(PS[po_][:].rearrange("p (g q) -> p g q", g=4), kb16[hs_, 128 + qb_ * 128:128 + (qb_ + 1) * 128], qrhs_, True, False, [bkb, bq], [bPS[po_]])
            mm(PS[po_][:], ident16[:], mkn[:, 0, :], False, True, [bident16, bmkn], [bPS[po_]])
            pp_ = None
            if not first_:
                pp_ = nps()
                mm(PS[pp_][:].rearrange("p (g q) -> p g q", g=4), kb16[hs_, qb_ * 128:(qb_ + 1) * 128], qrhs_, True, False, [bkb, bq], [bPS[pp_]])
                mm(PS[pp_][:], ident16[:], mkn[:, 1, :], False, True, [bident16, bmkn], [bPS[pp_]])
            return po_, pp_
        pend = emit_scores(*blocks[0])
        for bi, (qb, hh) in enumerate(blocks):
            hs = slice(64 * hh, 64 * hh + 64)
            first = (c == 0 and qb == 0)
            po, pp = pend
            act(pown[:], PS[po][:], AF.Exp, [bPS[po]], [bpown], scale=0.125)
            if not first:
                act(pprev[:], PS[pp][:], AF.Exp, [bPS[pp]], [bpprev], scale=0.125)
            if bi + 1 < len(blocks):
                pend = emit_scores(*blocks[bi + 1])
            pO = 6
            pD = 7
            if not first:
                mm(PS[pO][0:64, :], v16[:, qb, hs], pprev[:], True, False, [bv16, bpprev], [bPS[pO]])
                mm(PS[pD][0:64, :], ones16[:, 0:64], pprev[:], True, False, [bones, bpprev], [bPS[pD]])
            mm(PS[pO][0:64, :], v16[:, qb + 1, hs], pown[:], first, True, [bv16, bpown], [bPS[pO]])
            mm(PS[pD][0:64, :], ones16[:, 0:64], pown[:], first, True, [bones, bpown], [bPS[pD]])
            tt("dve", den[:].rearrange("p (g q) -> p g q", g=4), PS[pD][0:64, :].rearrange("p (g q) -> p g q", g=4),
               sk[:, 4 * hh:4 * hh + 4].unsqueeze(2).to_broadcast([64, 4, 128]), ALU.add, [bPS[pD], bsk], [btA, btB])
            P.add("dve", lambda e: e.reciprocal(out=den[:], in_=den[:]), reads=[btA, btB], writes=[btA, btB])
            tt("dve", att16[:, 4 * hh:4 * hh + 4, qb * 128:(qb + 1) * 128], PS[pO][0:64, :].rearrange("p (g q) -> p g q", g=4),
               den[:].rearrange("p (g q) -> p g q", g=4), ALU.mult, [bPS[pO], btA, btB], [batt])
        cp("pool", kb16[:, 0:128], kb16[:, T:T + 128], [bkb], [bkb])
        cp("pool", v16[:, 0, :], v16[:, T // 128, :], [bv16], [bv16])
        if KSUB < 2:
            return
        pY = [6, 7]
        def emit_bu(sc_, ct_):
            pr_ = nps()
            pim_ = nps()
            mm(PS[pr_][:, 0:TS], LB[:, 0, ct_, :], u16[:, ct_ // 4, sc_ * TS:(sc_ + 1) * TS], True, True, [bLB, bu16], [bPS[pr_]])
            mm(PS[pim_][:, 0:TS], LB[:, 1, ct_, :], u16[:, ct_ // 4, sc_ * TS:(sc_ + 1) * TS], True, True, [bLB, bu16], [bPS[pim_]])
            return pr_, pim_
        iters = [(sc_, ct_) for sc_ in range(T // TS) for ct_ in range(8)]
        cstages = conv_ln_stages(lambda m, k: cb16[:, m, k:k + T], n, lambda ap: ap)
        csched = {1: 0, 3: 1, 6: 2, 9: 3, 12: 4}
        pend = emit_bu(*iters[0])
        for sc in range(T // TS):
            c0 = sc * TS
            firstsub = (c == 0 and sc == 0)
            for ct in range(8):
                uh = ct // 4
                pr, pim = pend
                nxt = sc * 8 + ct + 1
                if nxt < len(iters):
                    pend = emit_bu(*iters[nxt])
                cs_ = cosT[:, ct, 0:TS]
                sn_ = sinT[:, ct, 0:TS]
                par = ssi[0] % 2
                ssi[0] += 1
                qA, bqA = sq[par][0], bsq[par][0]
                qB, bqB = sq[par][1], bsq[par][1]
                hA, bhA = sh16[par][0], bsh16[par][0]
                hB, bhB = sh16[par][1], bsh16[par][1]
                tt("dve", sx[0][:], PS[pr][:, 0:TS], cs_, ALU.mult, [bPS[pr], btab], [bsx[0]])
                tt("dve", sx[1][:], PS[pim][:, 0:TS], sn_, ALU.mult, [bPS[pim], btab], [bsx[1]])
                tt("dve", sx[0][:], sx[0][:], sx[1][:], ALU.add, [bsx[0], bsx[1]], [bsx[0]])
                tt("dve", sx[2][:], PS[pim][:, 0:TS], cs_, ALU.mult, [bPS[pim], btab], [bsx[2]])
                tt("dve", sx[3][:], PS[pr][:, 0:TS], sn_, ALU.mult, [bPS[pr], btab], [bsx[3]])
                tt("dve", sx[2][:], sx[2][:], sx[3][:], ALU.subtract, [bsx[2], bsx[3]], [bsx[2]])
                rbc = sp8[:, 5, ct:ct + 1].to_broadcast([128, TS])
                ire = 0.0 if firstsub else inre[:, ct:ct + 1]
                iim = 0.0 if firstsub else inim[:, ct:ct + 1]
                P.add("dve", lambda e, ire=ire, rbc=rbc, qA=qA: e.tensor_tensor_scan(out=qA[:], data0=rbc, data1=sx[0][:], initial=ire, op0=ALU.mult, op1=ALU.add),
                      reads=[bsp8, bsx[0], binit], writes=[bqA])
                P.add("dve", lambda e, iim=iim, rbc=rbc, qB=qB: e.tensor_tensor_scan(out=qB[:], data0=rbc, data1=sx[2][:], initial=iim, op0=ALU.mult, op1=ALU.add),
                      reads=[bsp8, bsx[2], binit], writes=[bqB])
                cp("pool", qlre[:, ct:ct + 1], qA[:, TS - 1:TS], [bqA], [bql])
                cp("pool", qlim[:, ct:ct + 1], qB[:, TS - 1:TS], [bqB], [bql])
                tt("pool", sx[6][:], qA[:], cs_, ALU.mult, [bqA, btab], [bsx[6]])
                tt("pool", sx[7][:], qB[:], sn_, ALU.mult, [bqB, btab], [bsx[7]])
                tt("pool", hA[:], sx[6][:], sx[7][:], ALU.subtract, [bsx[6], bsx[7]], [bhA])
                tt("pool", sx[4][:], qA[:], sn_, ALU.mult, [bqA, btab], [bsx[4]])
                tt("pool", sx[5][:], qB[:], cs_, ALU.mult, [bqB, btab], [bsx[5]])
                tt("pool", hB[:], sx[4][:], sx[5][:], ALU.add, [bsx[4], bsx[5]], [bhB])
                ot = ct // 4
                mm(PS[pY[ot]][:, c0:c0 + TS], LB[:, 2, ct, :], hA[:], ct % 4 == 0, False, [bLB, bhA], [bPS[pY[ot]]])
                mm(PS[pY[ot]][:, c0:c0 + TS], LB[:, 3, ct, :], hB[:], False, ct % 4 == 3, [bLB, bhB], [bPS[pY[ot]]])
                if (sc * 8 + ct) in csched:
                    cstages[csched[sc * 8 + ct]]()
            cl = cosT[:, :, TS - 1]
            sl = sinT[:, :, TS - 1]
            a0, a1 = sm8[0][:], sm8[1][:]
            tt("dve", a0, qlre[:], cl, ALU.mult, [bql, btab], [bsm8])
            tt("dve", a1, qlim[:], sl, ALU.mult, [bql, btab], [bsm8])
            tt("dve", hlre[:], a0, a1, ALU.subtract, [bsm8], [bhl])
            tt("dve", a0, qlre[:], sl, ALU.mult, [bql, btab], [bsm8])
            tt("dve", a1, qlim[:], cl, ALU.mult, [bql, btab], [bsm8])
            tt("dve", hlim[:], a0, a1, ALU.add, [bsm8], [bhl])
            tt("dve", a0, hlre[:], S8(6), ALU.mult, [bhl, bsp8], [bsm8])
            tt("dve", a1, hlim[:], S8(7), ALU.mult, [bhl, bsp8], [bsm8])
            tt("dve", inre[:], a0, a1, ALU.subtract, [bsm8], [binit])
            tt("dve", a0, hlre[:], S8(7), ALU.mult, [bhl, bsp8], [bsm8])
            tt("dve", a1, hlim[:], S8(6), ALU.mult, [bhl, bsp8], [bsm8])
            tt("dve", inim[:], a0, a1, ALU.add, [bsm8], [binit])
            if c == NCH - 1 and sc == T // TS - 1:
                st(nre[l], hlre[:], bhl, [bhl])
                st(nim[l], hlim[:], bhl, [bhl])
        for o in range(2):
            stt("dve", yss[:, o, 0:n], u32[:, o, 0:n], sp2[:, 0, o:o + 1], PS[pY[o]][:, 0:n], ALU.mult, ALU.add, [bu32, bsp2, bPS[pY[o]]], [byss])
        gelu_glu(n)
        if KSUB < 3:
            return
        if c == NCH - 1:
            st(ncv[l], cb32[:, :, T:T + 30], bcb32, [bcb32])
        for o in range(2):
            cp("pool", cb32[:, o, 0:30], cb32[:, o, T:T + 30], [bcb32], [bcb32])
        if KSUB < 4:
            return
        outproj_ffn(xs, bx, n, l, False, first=(c == 0))
        if l < nl - 1:
            st(xscr[:, t0:t0 + T].rearrange("(k p) t -> p k t", p=128), xs[:], bx[0], bx)
        else:
            rmsnorm(xs, bx, n, None, 0, 0, l, False)
            st(yT[:, t0:t0 + T].rearrange("(k p) t -> p k t", p=128), xs[:], bx[0], bx)

    def sample_layer(l):
        n = TSM
        bxs = [bxsm] * KT
        ld(kc16, kcT[l].rearrange("b f k -> f b k"), bkc, [bkc], q="pool")
        ld(vc16, vc[l].rearrange("b k f -> k b f"), bvc, [bvc], q="pool")
        ld(h0re[:], ssmre_in[l], bh0, [bh0])
        ld(h0im[:], ssmim_in[l], bh0, [bh0])
        ld(cbs32[:, :, :, 0:30], sconv_in[l], bcbs32, [bcbs32])
        dd = Buf(f"dd{l}")
        o1 = P.add("sp", lambda e: e.dma_start(out=nks_c[l], in_=kcn[l, :, 4:128, :]), dma_owner=dd)
        stores.append(o1)
        vcn_src = vc[l, :, 4:128, :]
        o2 = P.add("sp", lambda e: e.dma_start(out=nvs_c[l], in_=vcn_src), dma_owner=dd)
        stores.append(o2)
        rmsnorm(xsm, bxs, n, 1, 0, 0, l, True)
        for j in range(4):
            rope_pair(j, 5 + j, rps[:, 0, :], rps[:, 1, :], brps, q16[:, j, 0:n], bq, n)
        rope_pair(4, 9, rps[:, 0, :], rps[:, 1, :], brps, kb16[:, 128:128 + n], bkb, n, extra32=(k32[:, 0:n], bk32))
        st(skT[l], k32[:, 0:n], bk32, [bk32])
        for o in range(2):
            pi = inproj_tile(10 + o, n)
            cp("act", u32[:, o, 0:n], PS[pi][:, 0:n], [bPS[pi]], [bu32])
            cp("dve", u16[:, o, 0:n], PS[pi][:, 0:n], [bPS[pi]], [bu16])
        for o in range(2):
            pa = inproj_tile(12 + o, n)
            pg = inproj_tile(14 + o, n)
            act(tmpC[:, 0:n], PS[pg][:, 0:n], AF.Sigmoid, [bPS[pg]], [btC])
            tt("dve", cbs32[:, o, :, 30:34], PS[pa][:, 0:n].rearrange("p (b t) -> p b t", t=LS),
               tmpC[:, 0:n].rearrange("p (b t) -> p b t", t=LS), ALU.mult, [bPS[pa], btC], [bcbs32])
            cp("pool", cbs16[:, o], cbs32[:, o], [bcbs32], [bcbs16])
        st(scv[l], cbs32[:, :, :, 4:34], bcbs32, [bcbs32])
        pv = nps()
        for b in range(NB):
            for k in range(KT):
                mm(PS[pv][0:4, :].rearrange("p (b f) -> p b f", b=4)[:, b % 4, :] if False else PS[pv][0:4, (b % 4) * 128:(b % 4 + 1) * 128],
                   h16[:, k, 4 * b:4 * b + 4], win16[:, k, 2048:2176], k == 0, k == KT - 1, [bh, bwin], [bPS[pv]])
            if b % 4 == 3:
                g0 = b - 3
                cp("act", vn16[:, g0:g0 + 4, :], PS[pv][0:4, :].rearrange("p (b f) -> p b f", b=4), [bPS[pv]], [bvn16])
                cp("dve", vn32[:, g0:g0 + 4, :], PS[pv][0:4, :].rearrange("p (b f) -> p b f", b=4), [bPS[pv]], [bvn32])
                if b < NB - 1:
                    pv = nps()
        st(svn[l], vn32[:], bvn32, [bvn32])
        pC = nps()
        pN = nps()
        for b in range(NB):
            for hh in range(2):
                hs = slice(64 * hh, 64 * hh + 64)
                col = (b * 2 + hh) * 16
                qrhs = q16[hs, :, 4 * b:4 * b + 4]
                mm(PS[pC][:, col:col + 16].rearrange("p (g t) -> p g t", g=4), kc16[hs, b, :], qrhs, True, True, [bkc, bq], [bPS[pC]])
                mm(PS[pN][0:4, col:col + 16].rearrange("p (g t) -> p g t", g=4), kb16[hs, 128 + 4 * b:128 + 4 * b + 4], qrhs, True, True, [bkb, bq], [bPS[pN]])
        act(pown[:], PS[pC][:], AF.Exp, [bPS[pC]], [bpown], scale=0.125)
        tt("pool", pown[:], pown[:], mk[:, 2, :], ALU.mult, [bpown, bmk], [bpown])
        act(pn16[:], PS[pN][0:4, :], AF.Exp, [bPS[pN]], [bpn], scale=0.125)
        tt("pool", pn16[:], pn16[:], mk[0:4, 3, :], ALU.mult, [bpn, bmk], [bpn])
        pO = nps()
        pD = nps()
        for b in range(NB):
            for hh in range(2):
                hs = slice(64 * hh, 64 * hh + 64)
                col = (b * 2 + hh) * 16
                mm(PS[pO][0:64, col:col + 16], vc16[:, b, hs], pown[:, col:col + 16], True, False, [bvc, bpown], [bPS[pO]])
                mm(PS[pO][0:64, col:col + 16], vn16[0:4, b, hs], pn16[0:4, col:col + 16], False, True, [bvn16, bpn], [bPS[pO]])
                mm(PS[pD][0:64, col:col + 16], ones16[:, 0:64], pown[:, col:col + 16], True, False, [bones, bpown], [bPS[pD]])
                mm(PS[pD][0:64, col:col + 16], ones16[0:4, 0:64], pn16[0:4, col:col + 16], False, True, [bones, bpn], [bPS[pD]])
        tt("dve", den[:].rearrange("p (b h t) -> p b h t", b=NB, t=LS), PS[pD][0:64, :].rearrange("p (b h t) -> p b h t", b=NB, t=LS),
           sk[:, :].unsqueeze(1).unsqueeze(3).to_broadcast([64, NB, 8, LS]), ALU.add, [bPS[pD], bsk], [btA, btB])
        P.add("dve", lambda e: e.reciprocal(out=den[:], in_=den[:]), reads=[btA, btB], writes=[btA, btB])
        tt("dve", att16[:, :, 0:n].rearrange("p h (b t) -> p b h t", t=LS), PS[pO][0:64, :].rearrange("p (b h t) -> p b h t", b=NB, t=LS),
           den[:].rearrange("p (b h t) -> p b h t", b=NB, t=LS), ALU.mult, [bPS[pO], btA, btB], [batt])
        for (dst, x1, y1, x2, y2, op) in ((ahre, 8, h0re, 9, h0im, ALU.subtract), (ahim, 8, h0im, 9, h0re, ALU.add)):
            tt("dve", dst[:], y1[:], sp8[:, x1, :].unsqueeze(2).to_broadcast([128, 8, NB]), ALU.mult, [bh0, bsp8], [bah])
            tt("dve", hsre[:], y2[:], sp8[:, x2, :].unsqueeze(2).to_broadcast([128, 8, NB]), ALU.mult, [bh0, bsp8], [bhs])
            tt("dve", dst[:], dst[:], hsre[:], op, [bah, bhs], [bah])
        pY = [6, 7]
        for ct in range(8):
            uh = ct // 4
            pr = nps()
            pim = nps()
            mm(PS[pr][:, 0:n], LB[:, 0, ct, :], u16[:, uh, 0:n], True, True, [bLB, bu16], [bPS[pr]])
            mm(PS[pim][:, 0:n], LB[:, 1, ct, :], u16[:, uh, 0:n], True, True, [bLB, bu16], [bPS[pim]])
            cs_ = cosT[:, ct, 0:LS].unsqueeze(1).to_broadcast([128, NB, LS])
            sn_ = sinT[:, ct, 0:LS].unsqueeze(1).to_broadcast([128, NB, LS])
            V3 = lambda ap: ap.rearrange("p (b t) -> p b t", t=LS)
            X = [s_[:, 0:n] for s_ in sx]
            tt("dve", V3(X[0]), V3(PS[pr][:, 0:n]), cs_, ALU.mult, [bPS[pr], btab], [bsx[0]])
            tt("dve", V3(X[1]), V3(PS[pim][:, 0:n]), sn_, ALU.mult, [bPS[pim], btab], [bsx[1]])
            tt("pool", X[0], X[0], X[1], ALU.add, [bsx[0], bsx[1]], [bsx[0]])
            tt("dve", V3(X[2]), V3(PS[pim][:, 0:n]), cs_, ALU.mult, [bPS[pim], btab], [bsx[2]])
            tt("dve", V3(X[3]), V3(PS[pr][:, 0:n]), sn_, ALU.mult, [bPS[pr], btab], [bsx[3]])
            tt("pool", X[2], X[2], X[3], ALU.subtract, [bsx[2], bsx[3]], [bsx[2]])
            tt("dve", sx[0][:, 0:n:LS], sx[0][:, 0:n:LS], ahre[:, ct, :], ALU.add, [bsx[0], bah], [bsx[0]])
            tt("dve", sx[2][:, 0:n:LS], sx[2][:, 0:n:LS], ahim[:, ct, :], ALU.add, [bsx[2], bah], [bsx[2]])
            r4v = r4[:, ct].rearrange("p b t -> p (b t)")
            P.add("dve", lambda e, r4v=r4v, X=X: e.tensor_tensor_scan(out=X[4], data0=r4v, data1=X[0], initial=0.0, op0=ALU.mult, op1=ALU.add),
                  reads=[btab4, bsx[0]], writes=[bsx[4]])
            P.add("dve", lambda e, r4v=r4v, X=X: e.tensor_tensor_scan(out=X[5], data0=r4v, data1=X[2], initial=0.0, op0=ALU.mult, op1=ALU.add),
                  reads=[btab4, bsx[2]], writes=[bsx[5]])
            tt("pool", V3(X[6]), V3(X[4]), cs_, ALU.mult, [bsx[4], btab], [bsx[6]])
            tt("pool", V3(X[7]), V3(X[5]), sn_, ALU.mult, [bsx[5], btab], [bsx[7]])
            tt("pool", X[6], X[6], X[7], ALU.subtract, [bsx[6], bsx[7]], [bsx[6]])
            cp("act", hre16[:, 0:n], X[6], [bsx[6]], [bhre])
            cp("act", hsre[:, ct, :], sx[6][:, LS - 1:n:LS], [bsx[6]], [bhs])
            tt("dve", V3(X[1]), V3(X[4]), sn_, ALU.mult, [bsx[4], btab], [bsx[1]])
            tt("dve", V3(X[3]), V3(X[5]), cs_, ALU.mult, [bsx[5], btab], [bsx[3]])
            tt("dve", X[1], X[1], X[3], ALU.add, [bsx[1], bsx[3]], [bsx[1]])
            cp("act", him16[:, 0:n], X[1], [bsx[1]], [bhim])
            cp("act", hsim[:, ct, :], sx[1][:, LS - 1:n:LS], [bsx[1]], [bhs])
            ot = ct // 4
            mm(PS[pY[ot]][:, 0:n], LB[:, 2, ct, :], hre16[:, 0:n], ct % 4 == 0, False, [bLB, bhre], [bPS[pY[ot]]])
            mm(PS[pY[ot]][:, 0:n], LB[:, 3, ct, :], him16[:, 0:n], False, ct % 4 == 3, [bLB, bhim], [bPS[pY[ot]]])
        st(sre[l], hsre[:], bhs, [bhs])
        st(sim_o[l], hsim[:], bhs, [bhs])
        for o in range(2):
            stt("dve", yss[:, o, 0:n], u32[:, o, 0:n], sp2[:, 0, o:o + 1], PS[pY[o]][:, 0:n], ALU.mult, ALU.add, [bu32, bsp2, bPS[pY[o]]], [byss])
        gelu_glu(n)
        conv_ln(lambda m, k: cbs16[:, m, :, k:k + LS], n, lambda ap: ap.rearrange("p (b t) -> p b t", t=LS))
        if DBG and l == 0:
            dsb = xs[:, 0:6, :].rearrange("p (a k) (h t) -> p a (k h) t", k=2, t=TSM); bdsb = bx[0]
            memset("dve", dsb, 0.0, bx)
            cp("dve", dsb[0:64, 0, :, :], att16[:, :, 0:n], [batt, bdsb], [bdsb])
            cp("dve", dsb[:, 1, 0:2, :], os16[:, :, 0:n], [bos, bdsb], [bdsb])
            cp("dve", dsb[:, 2, 0:2, :], oc16[:, :, 0:n], [boc, bdsb], [bdsb])
            st(dbg.rearrange("a p h t -> p a h t"), dsb, bdsb, [bdsb])
        outproj_ffn(xsm, bxs, n, l, True)
        if l == nl - 1:
            rmsnorm(xsm, bxs, n, None, 0, 0, l, True)
            st(ysT.rearrange("(k p) t -> p k t", p=128), xsm[:], bxsm, [bxsm])

    STG = int(os.environ.get("KSTAGE", "9"))
    for l in range(nl):
        if STG >= 1:
            layer_params(l)
        for c in range(NCH):
            if STG >= 3 or (STG == 2 and c == 0):
                prompt_chunk(l, c)
        if STG >= 4:
            sample_layer(l)

    P.add("sp", lambda e: e.nop(), extra_deps=stores)
    P.emit()
    return nc


def _perm_win(w):
    q = w[:, 0:512].reshape(D, 8, 64)
    k = w[:, 512:640].reshape(D, 2, 64)
    v = w[:, 640:768]
    u = w[:, 768:1024]
    a = w[:, 1024:1280]
    g = w[:, 1280:1536]

    def swap(t):
        return np.concatenate([t[..., 32:], t[..., :32]], axis=-1)
    order = [0, 4, 1, 5, 2, 6, 3, 7]
    qt = q[:, order, :].reshape(D, 512)
    qs = swap(q)[:, order, :].reshape(D, 512)
    kt = k.reshape(D, 128)
    ks = swap(k).reshape(D, 128)
    return np.ascontiguousarray(np.concatenate([qt, kt, qs, ks, u, a, g, v], axis=1))


def _rope_tab(pos):
    half = 32
    inv = (np.float32(10000.0) ** (-(np.arange(half, dtype=np.float32) / np.float32(half)))).astype(np.float32)
    ang = (pos.astype(np.float32)[None, :] * inv[:, None]).astype(np.float32)
    c = np.cos(ang.astype(np.float64)).astype(np.float32)
    s = np.sin(ang.astype(np.float64)).astype(np.float32)
    cos = np.concatenate([c, c, c, c], axis=0)
    sins = np.concatenate([-s, s, -s, s], axis=0)
    return np.ascontiguousarray(np.stack([cos, sins], axis=0))


_NC_CACHE = {}


def kernel(**inp):
    f = lambda a: np.ascontiguousarray(np.asarray(a, dtype=np.float32))
    I = {k: np.asarray(v) for k, v in inp.items()}
    nlr = _NC_CACHE.get("nl", NL)
    if "nc" not in _NC_CACHE:
        _NC_CACHE["nc"] = build(nlr)
    nc = _NC_CACHE["nc"]

    def pk(a):
        return f(a.reshape(NL, 8, 128).transpose(0, 2, 1))

    def p2(a):
        return f(a.reshape(NL, 2, 128).transpose(0, 2, 1))

    shared = {
        "w_mod": f(I["w_mod"]),
        "b_modT": f(I["b_mod"].reshape(NL, 48, 128).transpose(0, 2, 1)),
        "g1T": pk(I["norm1_g"]), "g2T": pk(I["norm2_g"]),
        "gfT": f(I["final_norm_g"].reshape(8, 128).T),
        "w_in2": f(np.stack([_perm_win(I["w_in"][l]) for l in range(NL)])),
        "ropeP": _rope_tab(np.arange(NTOK)),
        "ropeS": _rope_tab(PAST + np.tile(np.arange(LS), NB)),
        "sinkT": f(np.broadcast_to(I["attn_sinks"][:, None, :], (NL, 64, 8))),
        "lamre": f(I["ssm_lam_re"].reshape(NL, 8, 128).transpose(0, 2, 1)),
        "lamim": f(I["ssm_lam_im"].reshape(NL, 8, 128).transpose(0, 2, 1)),
        "logdt": f(np.repeat(I["ssm_log_dt"], 64, axis=1).reshape(NL, 8, 128).transpose(0, 2, 1)),
        "bre": f(I["ssm_b_re"].reshape(NL, 8, 128, 16).transpose(0, 2, 1, 3)),
        "bim": f(I["ssm_b_im"].reshape(NL, 8, 128, 16).transpose(0, 2, 1, 3)),
        "cre": f(I["ssm_c_re"].reshape(NL, 8, 2, 16, 64).transpose(0, 2, 4, 1, 3).reshape(NL, 128, 8, 16)),
        "cim": f(I["ssm_c_im"].reshape(NL, 8, 2, 16, 64).transpose(0, 2, 4, 1, 3).reshape(NL, 128, 8, 16)),
        "dskipT": p2(I["ssm_d"]), "wglu": f(I["ssm_w_glu"]), "bgluT": p2(I["ssm_b_glu"]),
        "convwT": f(I["conv_w"].reshape(NL, 31, 2, 128).transpose(0, 3, 2, 1)),
        "convbT": p2(I["conv_b"]), "lngT": p2(I["conv_ln_g"]), "lnbT": p2(I["conv_ln_b"]),
        "w_out": f(I["w_out"]), "w_gate": f(I["w_gate"]), "w_up": f(I["w_up"]), "w_down": f(I["w_down"]),
        "identd": np.eye(128, dtype=np.float32),
        "jrow": f(np.broadcast_to(np.arange(TS + 1, dtype=np.float32)[None, :], (128, TS + 1))),
    }
    kk = np.arange(128)[:, None]
    qq = np.tile(np.arange(128), 4)[None, :]
    m_own = np.where(qq >= kk, 0.0, -30000.0).astype(np.float32)
    m_prev = np.where(kk > qq, 0.0, -30000.0).astype(np.float32)
    shared["maskP"] = f(np.stack([m_own, m_prev]))
    tq = np.tile(np.arange(LS), 128)[None, :]
    m_c = (kk > tq).astype(np.float32)
    m_n = (kk <= tq).astype(np.float32)
    shared["maskS"] = f(np.stack([m_c, m_n]))

    in_maps = []
    for c in range(8):
        b = c % 4
        sbs = slice(NB * c, NB * (c + 1))
        m = dict(shared)
        m["xT"] = f(I["x_prompt"][b].T)
        m["xsT"] = f(I["x_sample"][sbs].reshape(TSM, D).T)
        m["cT"] = f(np.concatenate([I["c_prompt"][b][None, :], I["c_sample"][sbs]], axis=0).T)
        ck = I["cache_k"][:, sbs].reshape(NL, NB, 128, 128)
        cv = I["cache_v"][:, sbs].reshape(NL, NB, 128, 128)
        m["kcT"] = f(ck.transpose(0, 1, 3, 2))
        m["kcn"] = f(ck)
        m["vc"] = f(cv)
        m["ssmre_in"] = f(I["state_ssm_re"][:, sbs].reshape(NL, NB, 8, 128).transpose(0, 3, 2, 1))
        m["ssmim_in"] = f(I["state_ssm_im"][:, sbs].reshape(NL, NB, 8, 128).transpose(0, 3, 2, 1))
        m["sconv_in"] = f(I["state_conv"][:, sbs].reshape(NL, NB, 30, 2, 128).transpose(0, 4, 3, 1, 2))
        in_maps.append(m)

    import os
    ncr = int(os.environ.get("KCORES", "8"))
    res = run_bass_kernel_spmd(nc, in_maps[:ncr], core_ids=list(range(ncr)))
    R = list(res.results)
    while len(R) < 8:
        R.append(R[0])

    y_prompt = np.stack([R[b]["yT"].T for b in range(4)])
    y_sample = np.concatenate([R[c]["ysT"].T.reshape(NB, LS, D) for c in range(8)], axis=0)
    nk_p = np.stack([R[b]["nkT"].transpose(0, 2, 1).reshape(NL, 128, 2, 64) for b in range(4)], axis=1)
    nv_p = np.stack([R[b]["nv"].reshape(NL, 128, 2, 64) for b in range(4)], axis=1)

    def unst(a):
        return a.transpose(0, 2, 1).reshape(NL, 16, 64)
    re_p = np.stack([unst(R[b]["nre"]) for b in range(4)], axis=1)
    im_p = np.stack([unst(R[b]["nim"]) for b in range(4)], axis=1)
    cv_p = np.stack([R[b]["ncv"].transpose(0, 3, 2, 1).reshape(NL, 30, 256) for b in range(4)], axis=1)
    nk_s, nv_s, re_s, im_s, cv_s = [], [], [], [], []
    for c in range(8):
        r = R[c]
        knew = r["skT"].transpose(0, 2, 1).reshape(NL, NB, LS, 128)
        nk_s.append(np.concatenate([r["nks_c"], knew], axis=2).reshape(NL, NB, 128, 2, 64))
        vnew = r["svn"].transpose(0, 2, 1, 3)
        nv_s.append(np.concatenate([r["nvs_c"], vnew], axis=2).reshape(NL, NB, 128, 2, 64))
        re_s.append(r["sre"].transpose(0, 3, 2, 1).reshape(NL, NB, 16, 64))
        im_s.append(r["sim_o"].transpose(0, 3, 2, 1).reshape(NL, NB, 16, 64))
        cv_s.append(r["scv"].transpose(0, 3, 4, 2, 1).reshape(NL, NB, 30, 256))
    cat = lambda xs_: np.ascontiguousarray(np.concatenate(xs_, axis=1).astype(np.float32))
    if "dbg" in R[0]:
        _NC_CACHE["dbg"] = R[0]["dbg"]
    outs = (y_prompt, y_sample, nk_p, nv_p, re_p, im_p, cv_p, cat(nk_s), cat(nv_s), cat(re_s), cat(im_s), cat(cv_s))
    return tuple(np.ascontiguousarray(o.astype(np.float32)) for o in outs)
```

```python
import math
import numpy as np
import concourse.bass as bass
import concourse.mybir as mybir
from concourse.bass_utils import run_bass_kernel_spmd

F32 = mybir.dt.float32
BF16 = mybir.dt.bfloat16
ALU = mybir.AluOpType
AF = mybir.ActivationFunctionType

SEG = 30000
NL = 4
NS = 5
HF = 332
D = 1024
KT = 8
NTOK = 2048
T = 256
NCH = NTOK // T
TS = 128
NB = 16
LS = 4
TSM = NB * LS
DFF = 2816
JT = 22
WIN = 2176
PAST = 8192
TWO_PI = 2.0 * math.pi


class Buf:
    __slots__ = ("name", "lw", "rd", "sem", "cnt", "excl")

    def __init__(self, name, excl=False):
        self.name = name
        self.excl = excl
        self.lw = None
        self.rd = {}
        self.sem = None
        self.cnt = 0


class Op:
    __slots__ = ("eng", "fn", "deps", "idx", "dma", "owner", "dcnt", "marked", "ev", "waits", "dinc")

    def __init__(self, eng, fn, idx):
        self.eng = eng
        self.fn = fn
        self.idx = idx
        self.deps = []
        self.dma = False
        self.owner = None
        self.dcnt = 0
        self.marked = False
        self.ev = None
        self.waits = []


class Prog:
    ENGS = ("pe", "act", "dve", "pool", "sp")

    def __init__(self, nc):
        self.nc = nc
        self.ops = []

    def add(self, eng, fn, reads=(), writes=(), dma_owner=None, extra_deps=(), dinc=16):
        i = len(self.ops)
        op = Op(eng, fn, i)
        if dma_owner is not None:
            op.dma = True
            op.owner = dma_owner
            op.dinc = dinc
            dma_owner.cnt += dinc
            op.dcnt = dma_owner.cnt
        deps = {}
        for b in reads:
            if b.lw is not None:
                deps[b.lw] = "raw"
            if b.excl:
                for r in b.rd.values():
                    if r not in deps:
                        deps[r] = "war"
        for b in writes:
            if b.lw is not None and b.lw not in deps:
                deps[b.lw] = "waw"
            for r in b.rd.values():
                if r not in deps:
                    deps[r] = "war"
        for d in extra_deps:
            deps[d.idx] = "raw"
        deps.pop(i, None)
        for b in reads:
            key = ("d", i) if op.dma else eng
            b.rd[key] = i
        for b in writes:
            b.lw = i
            b.rd = {}
        op.deps = list(deps.items())
        self.ops.append(op)
        return op

    def finalize(self):
        ops = self.ops
        waited = {e: {} for e in self.ENGS}
        for op in ops:
            need = {}
            for d, kind in op.deps:
                p = ops[d]
                if p.dma:
                    key = ("dma", id(p.owner))
                    if need.get(key, (0, None))[0] < p.dcnt:
                        need[key] = (p.dcnt, p)
                else:
                    if p.eng == op.eng and not op.dma:
                        if op.eng == "pe" or kind != "raw":
                            continue
                    key = ("eng", p.eng)
                    if need.get(key, (-1, None))[0] < p.idx:
                        need[key] = (p.idx, p)
            w = waited[op.eng]
            for key, (val, p) in need.items():
                if w.get(key, -1) >= val:
                    continue
                w[key] = val
                op.waits.append(p)
                if not p.dma:
                    p.marked = True
        cnt = {e: 0 for e in self.ENGS}
        for op in ops:
            if not op.dma and op.marked:
                cnt[op.eng] += 1
                op.ev = cnt[op.eng]
        self.evcount = cnt

    def emit(self):
        nc = self.nc
        self.finalize()
        esems = {}
        for e in self.ENGS:
            n = (self.evcount[e] + SEG - 1) // SEG
            esems[e] = [nc.alloc_semaphore(f"ev_{e}_{k}") for k in range(max(n, 1))]
        for op in self.ops:
            if op.dma and op.owner.sem is None:
                op.owner.sem = nc.alloc_semaphore("d_" + op.owner.name)

        def semval(p):
            if p.dma:
                return p.owner.sem, p.dcnt
            k = (p.ev - 1) // SEG
            return esems[p.eng][k], (p.ev - 1) % SEG + 1

        per = {e: [op for op in self.ops if op.eng == e] for e in self.ENGS}

        def run(eng, lst):
            for op in lst:
                for p in op.waits:
                    s, v = semval(p)
                    eng.wait_ge(s, v)
                ins = op.fn(eng)
                if op.dma:
                    ins.then_inc(op.owner.sem, op.dinc)
                elif op.marked:
                    s, _ = semval(op)
                    ins.then_inc(s, 1)

        with nc.Block() as block:
            @block.tensor
            def _(e):
                run(e, per["pe"])

            @block.scalar
            def _(e):
                run(e, per["act"])

            @block.vector
            def _(e):
                run(e, per["dve"])

            @block.gpsimd
            def _(e):
                run(e, per["pool"])

            @block.sync
            def _(e):
                run(e, per["sp"])


def build(nl=NS):
    import os
    KSUB = int(os.environ.get("KSUB", "9"))
    KS2 = int(os.environ.get("KS2", "9"))
    nc = bass.Bass("TRN2", target_bir_lowering=False)
    P = Prog(nc)
    stores = []

    def din(name, shape):
        return nc.dram_tensor(name, list(shape), F32, kind="ExternalInput").ap()

    def dout(name, shape):
        return nc.dram_tensor(name, list(shape), F32, kind="ExternalOutput").ap()

    def sb(name, shape, dt=F32):
        return nc.alloc_sbuf_tensor(name, list(shape), dt)

    xT = din("xT", [D, NTOK])
    xsT = din("xsT", [D, TSM])
    cT = din("cT", [D, 17])
    w_mod = din("w_mod", [NS, D, 6 * D])
    b_modT = din("b_modT", [NS, 128, 48])
    g1T = din("g1T", [NS, 128, KT])
    g2T = din("g2T", [NS, 128, KT])
    gfT = din("gfT", [128, KT])
    w_in2 = din("w_in2", [NS, D, WIN])
    ropeP = din("ropeP", [2, 128, NTOK])
    ropeS = din("ropeS", [2, 128, TSM])
    maskP = din("maskP", [2, 128, 512])
    maskS = din("maskS", [2, 128, 512])
    sinkT = din("sinkT", [NS, 64, 8])
    lamre = din("lamre", [NS, 128, 8])
    lamim = din("lamim", [NS, 128, 8])
    logdt = din("logdt", [NS, 128, 8])
    bre = din("bre", [NS, 128, 8, 16])
    bim = din("bim", [NS, 128, 8, 16])
    cre = din("cre", [NS, 128, 8, 16])
    cim = din("cim", [NS, 128, 8, 16])
    dskipT = din("dskipT", [NS, 128, 2])
    wglu = din("wglu", [NS, 256, 256])
    bgluT = din("bgluT", [NS, 128, 2])
    convwT = din("convwT", [NS, 128, 2, 31])
    convbT = din("convbT", [NS, 128, 2])
    lngT = din("lngT", [NS, 128, 2])
    lnbT = din("lnbT", [NS, 128, 2])
    w_out = din("w_out", [NS, D, D])
    w_gate = din("w_gate", [NS, D, DFF])
    w_up = din("w_up", [NS, D, DFF])
    w_down = din("w_down", [NS, DFF, D])
    kcT = din("kcT", [NS, NB, 128, 128])
    vc = din("vc", [NS, NB, 128, 128])
    kcn = din("kcn", [NS, NB, 128, 128])
    ssmre_in = din("ssmre_in", [NS, 128, 8, NB])
    ssmim_in = din("ssmim_in", [NS, 128, 8, NB])
    sconv_in = din("sconv_in", [NS, 128, 2, NB, 30])
    identd = din("identd", [128, 128])
    hprev = din("hprev", [128, 2])
    hin = nc.dram_tensor("hin", [128, HF], F32)
    hall = nc.dram_tensor("hall", [256, HF], F32)
    bhin = Buf("hin"); bhall = Buf("hall")
    jrow = din("jrow", [128, TS + 1])

    yT = dout("yT", [D, NTOK])
    ysT = dout("ysT", [D, TSM])
    nkT = dout("nkT", [NS, 128, 128])
    nv = dout("nv", [NS, 128, 128])
    nre = dout("nre", [NS, 128, 8])
    nim = dout("nim", [NS, 128, 8])
    ncv = dout("ncv", [NS, 128, 2, 30])
    nks_c = dout("nks_c", [NS, NB, 124, 128])
    nvs_c = dout("nvs_c", [NS, NB, 124, 128])
    skT = dout("skT", [NS, 128, TSM])
    svn = dout("svn", [NS, 4, NB, 128])
    sre = dout("sre", [NS, 128, 8, NB])
    sim_o = dout("sim_o", [NS, 128, 8, NB])
    scv = dout("scv", [NS, 128, 2, NB, 30])
    xscr = nc.dram_tensor("xscr", [D, NTOK], F32, kind="Internal").ap()
    wgu_c = nc.dram_tensor("wgu_c", [JT, 128, 2 * KT * 128], BF16, kind="Internal").ap()
    wdn_c = nc.dram_tensor("wdn_c", [16, 128, 11 * 128], BF16, kind="Internal").ap()
    woa_c = nc.dram_tensor("woa_c", [8, 64, 8 * 128], BF16, kind="Internal").ap()
    wor_c = nc.dram_tensor("wor_c", [8, 128, 4 * 128], BF16, kind="Internal").ap()
    bwgu_c = [Buf(f"wguc{j}") for j in range(JT)]
    bwdn_c = [Buf(f"wdnc{j}") for j in range(16)]
    bwo_c = [Buf(f"woc{j}") for j in range(8)]
    DBG = bool(int(os.environ.get("KDBG", "0")))
    if DBG:
        dbg = dout("dbg", [3, 128, 8, TSM])

    PS = [nc.alloc_psum_tensor(f"ps{i}", [128, 512], F32) for i in range(8)]
    bPS = [Buf(f"ps{i}", excl=True) for i in range(8)]
    psrr = [0]

    def nps():
        i = psrr[0]
        psrr[0] = (i + 1) % 6
        return i

    def mm(out, lhsT, rhs, start, stop, r, w):
        return P.add("pe", lambda e: e.matmul(out, lhsT=lhsT, rhs=rhs, start=start, stop=stop), reads=r, writes=w)

    def act(out, in_, func, r, w, bias=None, scale=None):
        kw = {}
        if bias is not None:
            kw["bias"] = bias
        if scale is not None:
            kw["scale"] = scale
        return P.add("act", lambda e: e.activation(out=out, in_=in_, func=func, **kw), reads=r, writes=w)

    def tt(eng, out, in0, in1, op, r, w):
        return P.add(eng, lambda e: e.tensor_tensor(out=out, in0=in0, in1=in1, op=op), reads=r, writes=w)

    def ts(eng, out, in0, s1, s2, op0, op1, r, w):
        if op1 is None:
            return P.add(eng, lambda e: e.tensor_scalar(out=out, in0=in0, scalar1=s1, scalar2=None, op0=op0), reads=r, writes=w)
        return P.add(eng, lambda e: e.tensor_scalar(out=out, in0=in0, scalar1=s1, scalar2=s2, op0=op0, op1=op1), reads=r, writes=w)

    def stt(eng, out, in0, scalar, in1, op0, op1, r, w):
        return P.add(eng, lambda e: e.scalar_tensor_tensor(out=out, in0=in0, scalar=scalar, in1=in1, op0=op0, op1=op1), reads=r, writes=w)

    def cp(eng, out, in_, r, w):
        if eng == "act":
            return P.add("act", lambda e: e.activation(out=out, in_=in_, func=AF.Copy), reads=r, writes=w)
        return P.add(eng, lambda e: e.tensor_copy(out=out, in_=in_), reads=r, writes=w)

    def memset(eng, ap, val, w):
        return P.add(eng, lambda e: e.memset(ap, val), writes=w)

    def ld(out, in_, owner, w, q="sp"):
        return P.add(q, lambda e: e.dma_start(out=out, in_=in_), writes=w, dma_owner=owner)

    def st(out, in_, owner, r):
        o = P.add("sp", lambda e: e.dma_start(out=out, in_=in_), reads=r, dma_owner=owner)
        stores.append(o)
        return o

    ident = sb("ident", [128, 128]); bident = Buf("ident")
    ld(ident[:], identd, bident, [bident])
    ones16 = sb("ones16", [128, 128], BF16); bones = Buf("ones")
    memset("dve", ones16[:], 1.0, [bones])
    jr = sb("jr", [128, TS + 1]); bjr = Buf("jr")
    ld(jr[:], jrow, bjr, [bjr])
    mk = sb("mk", [128, 4, 512], BF16); bmk = Buf("mk")
    mkn = mk[:, 0:2, :]; bmkn = bmk
    ld(mk[:, 0:2, :], maskP.rearrange("a p n -> p a n"), bmk, [bmk], q="pool")
    ld(mk[:, 2:4, :], maskS.rearrange("a p n -> p a n"), bmk, [bmk], q="pool")
    ident16 = sb("ident16", [128, 128], BF16); bident16 = Buf("ident16")
    cp("dve", ident16[:], ident[:], [bident], [bident16])
    rps = sb("rps", [128, 2, TSM]); brps = Buf("rps")
    ld(rps[:], ropeS.rearrange("a p n -> p a n"), brps, [brps])
    gf = sb("gf", [128, KT]); bgf = Buf("gf")
    ld(gf[:], gfT, bgf, [bgf])
    pat = sb("pat", [128, NB, LS]); bpat = Buf("pat")
    memset("dve", pat[:], 1.0, [bpat])
    memset("dve", pat[:, :, 0:1], 0.0, [bpat])

    modT = sb("modT", [128, NS, 48, 17]); bmod = Buf("modT")
    csb = sb("csb", [128, KT, 17]); bcs = Buf("csb")
    sgc = sb("sgc", [128, KT, 17]); bsgc = Buf("sgc")
    ld(csb[:], cT.rearrange("(k p) n -> p k n", p=128), bcs, [bcs])
    act(sgc[:], csb[:], AF.Sigmoid, [bcs], [bsgc])
    tt("dve", csb[:], csb[:], sgc[:], ALU.mult, [bcs, bsgc], [bcs])
    bmt = sb("bmt", [128, NS, 48]); bbmt = Buf("bmt")
    ld(bmt[:], b_modT.rearrange("l p m -> p l m"), bbmt, [bbmt])
    WMB = 256
    _g0 = nc.sbuf_tensor("wmr0", [128, KT, WMB], F32)
    _g1 = nc.sbuf_tensor("wmr1", [128, KT, WMB], F32)
    wmr = [_g0.__enter__(), _g1.__enter__()]
    bwmr = [Buf(f"wmr{i}") for i in range(2)]
    lastmod = None
    it = 0
    for l in range(nl):
        for blk in range(6 * D // WMB):
            s = it % 2
            it += 1
            ld(wmr[s][:], w_mod[l, :, blk * WMB:(blk + 1) * WMB].rearrange("(k p) n -> p k n", p=128), bwmr[s], [bwmr[s]])
            for mi in range(WMB // 128):
                m = blk * (WMB // 128) + mi
                pi = nps()
                for k in range(KT):
                    mm(PS[pi][:, 0:17], wmr[s][:, k, mi * 128:(mi + 1) * 128], csb[:, k, :], k == 0, k == KT - 1,
                       [bwmr[s], bcs], [bPS[pi]])
                lastmod = ts("dve", modT[:, l, m, :], PS[pi][:, 0:17], bmt[:, l, m:m + 1], None, ALU.add, None, [bPS[pi], bbmt], [bmod])
    _g1.__exit__(None, None, None)
    _g0.__exit__(None, None, None)
    for _e in ("pe", "act", "pool", "sp"):
        P.add(_e, lambda e: e.nop(), extra_deps=[lastmod])

    xs = sb("xs", [128, KT, T]); bx = [Buf(f"x{k}") for k in range(KT)]
    xsm = sb("xsm", [128, KT, TSM]); bxsm = Buf("xsm")
    ld(xsm[:], xsT.rearrange("(k p) n -> p k n", p=128), bxsm, [bxsm])
    h16 = sb("h16", [128, KT, T], BF16); bh = Buf("h16")
    scr16 = sb("scr16", [128, JT, T], BF16); bscr = Buf("scr16")
    rstd = sb("rstd", [128, T]); brstd = Buf("rstd")
    tmpAB = sb("tmpAB", [128, 2, T])
    tmpA = tmpAB[:, 0, :]; btA = Buf("tmpA")
    tmpB = tmpAB[:, 1, :]; btB = Buf("tmpB")
    tmpC = sb("tmpC", [128, T]); btC = Buf("tmpC")
    tmpD = sb("tmpD", [128, T]); btD = Buf("tmpD")
    rp = sb("rp", [128, 2, T]); brp = Buf("rp")
    q16 = sb("q16", [128, 4, T], BF16); bq = Buf("q16")
    kb16 = sb("kb16", [128, 128 + T], BF16); bkb = Buf("kb16")
    k32 = sb("k32", [128, T]); bk32 = Buf("k32")
    v16 = sb("v16", [128, 1 + T // 128, 128], BF16); bv16 = Buf("v16")
    v32 = sb("v32", [128, 128]); bv32 = Buf("v32")
    pown = sb("pown", [128, 512], BF16); bpown = Buf("pown")
    pprev = sb("pprev", [128, 512], BF16); bpprev = Buf("pprev")
    den = tmpAB[0:64].rearrange("p a t -> p (a t)")
    att16 = sb("att16", [64, 8, T], BF16); batt = Buf("att16")
    u32 = sb("u32", [128, 2, T]); bu32 = Buf("u32")
    u16 = sb("u16", [128, 2, T], BF16); bu16 = Buf("u16")
    cb32 = sb("cb32", [128, 2, 30 + T]); bcb32 = Buf("cb32")
    cb16 = sb("cb16", [128, 2, 30 + T], BF16); bcb16 = Buf("cb16")
    cbs32 = sb("cbs32", [128, 2, NB, 34]); bcbs32 = Buf("cbs32")
    cbs16 = sb("cbs16", [128, 2, NB, 34], BF16); bcbs16 = Buf("cbs16")
    ycf = sb("ycf", [128, 2, T]); bycf = Buf("ycf")
    cva = sb("cva", [128, T]); bcva = Buf("cva")
    cvb = sb("cvb", [128, T]); bcvb = Buf("cvb")
    cvc = sb("cvc", [128, 2, T]); bcvc = Buf("cvc")
    yc16 = scr16[:, 8:12, :]; byc16 = bscr
    oc16 = sb("oc16", [128, 2, T], BF16); boc = Buf("oc16")
    os16 = sb("os16", [128, 2, T], BF16); bos = Buf("os16")
    yss = sb("yss", [128, 2, T]); byss = Buf("yss")
    z32 = sb("z32", [128, 2, T]); bz32 = Buf("z32")
    assert 2 * T == 4 * 8 * 16
    z16 = sb("z16", [128, 2, T], BF16); bz16 = Buf("z16")
    sx = [sb(f"sx{i}", [128, TS]) for i in range(8)]
    bsx = [Buf(f"sx{i}") for i in range(8)]
    sq = [[sb(f"sq{i}{j}", [128, TS]) for j in range(2)] for i in range(2)]
    bsq = [[Buf(f"sq{i}{j}") for j in range(2)] for i in range(2)]
    sh16 = [[sb(f"sh16{i}{j}", [128, TS], BF16) for j in range(2)] for i in range(2)]
    bsh16 = [[Buf(f"sh16{i}{j}") for j in range(2)] for i in range(2)]
    ssi = [0]
    hre16 = sh16[0][0]; bhre = bsh16[0][0]
    him16 = sh16[0][1]; bhim = bsh16[0][1]
    qlre = sb("qlre", [128, 8]); qlim = sb("qlim", [128, 8]); bql = Buf("ql")
    hlre = sb("hlre", [128, 8]); hlim = sb("hlim", [128, 8]); bhl = Buf("hl")
    inre = sb("inre", [128, 8]); inim = sb("inim", [128, 8]); binit = Buf("init")
    sm8 = [sb(f"sm8_{i}", [128, 8]) for i in range(4)]; bsm8 = Buf("sm8")
    h0re = sb("h0re", [128, 8, NB]); h0im = sb("h0im", [128, 8, NB]); bh0 = Buf("h0")
    ahre = sb("ahre", [128, 8, NB]); ahim = sb("ahim", [128, 8, NB]); bah = Buf("ah")
    hsre = sb("hsre", [128, 8, NB]); hsim = sb("hsim", [128, 8, NB]); bhs = Buf("hs")
    st16 = [sb(f"st16_{i}", [128, NB]) for i in range(4)]; bst16 = Buf("st16")
    r_kc = sb("r_kc", [128, 2048], BF16); bkc = Buf("kc16")
    r_vc = sb("r_vc", [128, 2048], BF16); bvc = Buf("vc16")
    kc16 = r_kc[:].rearrange("p (b k) -> p b k", k=128)
    vc16 = r_vc[:].rearrange("p (b k) -> p b k", k=128)
    vn16 = sb("vn16", [4, NB, 128], BF16); bvn16 = Buf("vn16")
    vn32 = sb("vn32", [4, NB, 128]); bvn32 = Buf("vn32")
    pn16 = sb("pn16", [4, 512], BF16); bpn = Buf("pn16")

    win16 = sb("win16", [128, KT, WIN], BF16); bwin = Buf("win16")
    wgl16 = sb("wgl16", [128, 2, 256], BF16); bwgl = Buf("wgl16")
    sp8 = sb("sp8", [128, 12, 8]); bsp8 = Buf("sp8")
    sp2 = sb("sp2", [128, 8, 2]); bsp2 = Buf("sp2")
    cw = sb("cw", [128, 2, 31]); bcw = Buf("cw")
    sk = sb("sk", [64, 8]); bsk = Buf("sk")
    bc32 = yss[:].rearrange("p a (b c) -> p (a b) c", c=16).rearrange("p (a b) c -> p a b c", a=4); bbc = byss
    bbt = z32[:].rearrange("p a (b c) -> p (a b) c", c=16).rearrange("p (a b) c -> p a b c", a=4); bbbt = bz32
    exq = sb("exq", [128, 128]); bexq = Buf("exq")
    LB = sb("LB", [128, 4, 8, 128], BF16); bLB = Buf("LB")
    cosT = sb("cosT", [128, 8, TS + 1]); sinT = sb("sinT", [128, 8, TS + 1]); btab = Buf("tab")
    r4 = sb("r4", [128, 8, NB, LS]); btab4 = Buf("tab4")
    diag16 = sb("diag16", [128, 2, 31, 128], BF16); bdiag = Buf("diag16")
    gsc = sb("gsc", [128, 2, KT, 17]); bgsc = Buf("gsc")
    NG = 5
    _raw = [sb("wr0", [128, 2048], BF16), sb("wr1", [128, 2048], BF16), r_kc, r_vc, sb("wr4", [128, 2048], BF16)]
    bwgu = [Buf("wr0"), Buf("wr1"), bkc, bvc, Buf("wr4")]
    wfl = [r[:] for r in _raw]
    wgu = [r[:].rearrange("p (a k c) -> p a k c", a=2, k=KT) for r in _raw]
    wdn = [r[:, 0:1408].rearrange("p (j c) -> p j c", c=128) for r in _raw]
    bwdn = bwgu
    ffi = [0, 0]

    def S8(i):
        return sp8[:, i, :]

    angt = tmpC[:, 0:TS + 1]; angk = tmpD[:, 0:TS + 1]; bang = btC
    CM = 12582912.0

    def sin_of(out, x, shift, tmp, r, w, wt):
        xs_ = x
        if shift != 0.0:
            ts("dve", out, x, shift, None, ALU.add, None, r, w)
            xs_ = out
        ts("dve", tmp, xs_, 1.0 / TWO_PI, CM, ALU.mult, ALU.add, r + w, wt)
        ts("dve", tmp, tmp, -CM, None, ALU.add, None, r + wt, wt)
        stt("dve", tmp, tmp, -TWO_PI, xs_, ALU.mult, ALU.add, r + w + wt, wt)
        ts("dve", tmp, tmp, math.pi, -math.pi, ALU.min, ALU.max, r + wt, wt)
        act(out, tmp, AF.Sin, r + wt, w)

    def layer_params(l):
        for k in range(KT):
            ld(win16[:, k, :], w_in2[l, k * 128:(k + 1) * 128, :], bwin, [bwin], q="pool")
        ld(wgl16[:], wglu[l].rearrange("(k p) n -> p k n", p=128), bwgl, [bwgl], q="pool")
        ld(sp8[:, 0, :], lamre[l], bsp8, [bsp8])
        ld(sp8[:, 1, :], lamim[l], bsp8, [bsp8])
        ld(sp8[:, 2, :], logdt[l], bsp8, [bsp8])
        ld(sp2[:, 0, :], dskipT[l], bsp2, [bsp2])
        ld(sp2[:, 1, :], bgluT[l], bsp2, [bsp2])
        ld(sp2[:, 2, :], convbT[l], bsp2, [bsp2])
        ld(sp2[:, 3, :], lngT[l], bsp2, [bsp2])
        ld(sp2[:, 4, :], lnbT[l], bsp2, [bsp2])
        ld(cw[:], convwT[l], bcw, [bcw])
        ld(sk[:], sinkT[l], bsk, [bsk])
        act(sk[:], sk[:], AF.Exp, [bsk], [bsk])
        ld(bc32[:, 0], bre[l], bbc, [bbc])
        ld(bc32[:, 1], bim[l], bbc, [bbc])
        ld(bc32[:, 2], cre[l], bbc, [bbc])
        ld(bc32[:, 3], cim[l], bbc, [bbc])
        for a, (gT, off) in enumerate(((g1T, 8), (g2T, 32))):
            ld(sp8[:, 3, :], gT[l], bsp8, [bsp8])
            ts("dve", gsc[:, a], modT[:, l, off:off + 8, :], 1.0, None, ALU.add, None, [bmod], [bgsc])
            tt("dve", gsc[:, a], gsc[:, a], sp8[:, 3, :].unsqueeze(2).to_broadcast([128, KT, 17]), ALU.mult, [bgsc, bsp8], [bgsc])
        R = [bsp8]
        W = [bsp8]
        act(S8(2), S8(2), AF.Exp, R, W)
        tt("dve", S8(3), S8(0), S8(2), ALU.mult, R, W)
        tt("dve", S8(4), S8(1), S8(2), ALU.mult, R, W)
        act(S8(5), S8(3), AF.Exp, R, W)
        sin_of(S8(7), S8(4), 0.0, sm8[0][:], R + [bsm8], W, [bsm8])
        sin_of(S8(6), S8(4), 0.5 * math.pi, sm8[0][:], R + [bsm8], W, [bsm8])
        tt("dve", S8(8), S8(5), S8(6), ALU.mult, R, W)
        tt("dve", S8(9), S8(5), S8(7), ALU.mult, R, W)
        a0, a1, a2, a3 = (sm8[i][:] for i in range(4))
        R2 = [bsp8, bsm8]
        tt("dve", a0, S8(0), S8(0), ALU.mult, R2, [bsm8])
        tt("dve", a1, S8(1), S8(1), ALU.mult, R2, [bsm8])
        tt("dve", a0, a0, a1, ALU.add, R2, [bsm8])
        P.add("dve", lambda e: e.reciprocal(out=a0, in_=a0), reads=R2, writes=[bsm8])
        ts("dve", a1, S8(8), -1.0, None, ALU.add, None, R2, [bsm8])
        tt("dve", a2, a1, S8(0), ALU.mult, R2, [bsm8])
        tt("dve", a3, S8(9), S8(1), ALU.mult, R2, [bsm8])
        tt("dve", a2, a2, a3, ALU.add, R2, [bsm8])
        tt("dve", S8(10), a2, a0, ALU.mult, R2, W)
        tt("dve", a2, S8(9), S8(0), ALU.mult, R2, [bsm8])
        tt("dve", a3, a1, S8(1), ALU.mult, R2, [bsm8])
        tt("dve", a2, a2, a3, ALU.subtract, R2, [bsm8])
        tt("dve", S8(11), a2, a0, ALU.mult, R2, W)
        cre_b = sp8[:, 10, :].unsqueeze(2).to_broadcast([128, 8, 16])
        cim_b = sp8[:, 11, :].unsqueeze(2).to_broadcast([128, 8, 16])
        Rb = [bbc, bsp8, bbbt]
        tt("dve", bbt[:, 0], bc32[:, 0], cre_b, ALU.mult, Rb, [bbbt])
        tt("dve", bbt[:, 2], bc32[:, 1], cim_b, ALU.mult, Rb, [bbbt])
        tt("dve", bbt[:, 0], bbt[:, 0], bbt[:, 2], ALU.subtract, Rb, [bbbt])
        tt("dve", bbt[:, 1], bc32[:, 1], cre_b, ALU.mult, Rb, [bbbt])
        tt("dve", bbt[:, 2], bc32[:, 0], cim_b, ALU.mult, Rb, [bbbt])
        tt("dve", bbt[:, 1], bbt[:, 1], bbt[:, 2], ALU.add, Rb, [bbbt])
        cp("dve", bbt[:, 2], bc32[:, 2], Rb, [bbbt])
        ts("dve", bbt[:, 3], bc32[:, 3], -1.0, None, ALU.mult, None, Rb, [bbbt])
        xsf = xs[:].rearrange("p k t -> p (k t)")
        Eb = [xsf[:, 0:1024].rearrange("p (c n) -> p c n", n=128), xsf[:, 1024:2048].rearrange("p (c n) -> p c n", n=128)]
        bEb = [bx[0:4], bx[4:8]]
        for mi in range(4):
            E = Eb[mi % 2]
            bE = bEb[mi % 2]
            memset("dve", E, 0.0, bE)
            for ct in range(8):
                for gg in range(2):
                    gp = (2 * ct + gg) % 8
                    cp("dve", E[64 * gg:64 * gg + 64, ct, 16 * gp:16 * gp + 16], bbt[64 * gg:64 * gg + 64, mi, ct, :], [bbbt], bE)
            if mi < 2:
                for ct in range(8):
                    pi = nps()
                    P.add("pe", lambda e, pi=pi, E=E, ct=ct: e.transpose(out=PS[pi][:, 0:128], in_=E[:, ct, :], identity=ident[:]),
                          reads=bE + [bident], writes=[bPS[pi]])
                    cp("act", LB[:, mi, ct, :], PS[pi][:, 0:128], [bPS[pi]], [bLB])
            else:
                cp("act", LB[:, mi], E, bE, [bLB])
        angt3 = xsf[:, 0:8 * (TS + 1)].rearrange("p (c j) -> p c j", j=TS + 1)
        angk3 = scr16[:, 0:9, :].rearrange("p a b -> p (a b)").bitcast(F32)[:, 0:8 * (TS + 1)].rearrange("p (c j) -> p c j", j=TS + 1)
        tt("dve", angt3, jr[:].unsqueeze(1).to_broadcast([128, 8, TS + 1]), sp8[:, 4, :].unsqueeze(2).to_broadcast([128, 8, TS + 1]),
           ALU.mult, [bjr, bsp8], bx)
        sin_of(sinT[:], angt3, 0.0, angk3, bx, [btab], [bscr])
        sin_of(cosT[:], angt3, 0.5 * math.pi, angk3, bx, [btab], [bscr])
        for ct in range(8):
            ts("dve", r4[:, ct], pat[:], sp8[:, 5, ct:ct + 1], None, ALU.mult, None, [bpat, bsp8], [btab4])
        for m in range(2):
            for k in range(31):
                act(diag16[:, m, k, :], ident[:], AF.Identity, [bident, bcw], [bdiag], scale=cw[:, m, k:k + 1])

    def rmsnorm(xa, bxs, n, gcol, shm, a, l, sample):
        for k in range(KT):
            act(scr16[:, k, 0:n], xa[:, k, :], AF.Square, [bxs[k]], [bscr])
        pi = nps()
        for k in range(KT):
            mm(PS[pi][:, 0:n], ones16[:], scr16[:, k, 0:n], k == 0, k == KT - 1, [bones, bscr], [bPS[pi]])
        ts("dve", rstd[:, 0:n], PS[pi][:, 0:n], 1.0 / D, 1e-6, ALU.mult, ALU.add, [bPS[pi]], [brstd])
        act(rstd[:, 0:n], rstd[:, 0:n], AF.Sqrt, [brstd], [brstd])
        P.add("dve", lambda e: e.reciprocal(out=rstd[:, 0:n], in_=rstd[:, 0:n]), reads=[brstd], writes=[brstd])
        for k in range(KT):
            tA = tmpA if k % 2 == 0 else tmpB
            bA = btA if k % 2 == 0 else btB
            tt("dve", tA[:, 0:n], xa[:, k, :], rstd[:, 0:n], ALU.mult, [bxs[k], brstd], [bA])
            if gcol is None:
                ts("pool", xa[:, k, :], tA[:, 0:n], gf[:, k:k + 1], None, ALU.mult, None, [bA, bgf], [bxs[k]])
            elif not sample:
                if False:
                    pass
                else:
                    act(h16[:, k, 0:n], tA[:, 0:n], AF.Identity, [bA, bgsc, bmod], [bh],
                        bias=modT[:, l, shm + k, 0:1], scale=gsc[:, a, k, 0:1])
            else:
                v3 = tA[:, 0:n].rearrange("p (b t) -> p b t", t=LS)
                if True:
                    tt("pool", v3, v3, gsc[:, a, k, 1:17].unsqueeze(2).to_broadcast([128, NB, LS]), ALU.mult, [bA, bgsc], [bA])
                    tt("pool", h16[:, k, 0:n].rearrange("p (b t) -> p b t", t=LS), v3,
                       modT[:, l, shm + k, 1:17].unsqueeze(2).to_broadcast([128, NB, LS]), ALU.add, [bA, bmod], [bh])

    def resid(xa, bxs, n, mo, pi, l, gm, sample):
        if not sample:
            stt("dve", xa[:, mo, :], PS[pi][:, 0:n], modT[:, l, gm + mo, 0:1], xa[:, mo, :], ALU.mult, ALU.add,
                [bPS[pi], bmod, bxs[mo]], [bxs[mo]])
        else:
            tt("dve", tmpC[:, 0:n].rearrange("p (b t) -> p b t", t=LS), PS[pi][:, 0:n].rearrange("p (b t) -> p b t", t=LS),
               modT[:, l, gm + mo, 1:17].unsqueeze(2).to_broadcast([128, NB, LS]), ALU.mult, [bPS[pi], bmod], [btC])
            tt("dve", xa[:, mo, :], xa[:, mo, :], tmpC[:, 0:n], ALU.add, [btC, bxs[mo]], [bxs[mo]])

    def inproj_tile(m, n):
        pi = nps()
        for k in range(KT):
            mm(PS[pi][:, 0:n], win16[:, k, m * 128:(m + 1) * 128], h16[:, k, 0:n], k == 0, k == KT - 1, [bwin, bh], [bPS[pi]])
        return pi

    def rope_pair(m_a, m_b, cos_ap, sin_ap, brope, out_ap, bout, n, extra32=None):
        pa = inproj_tile(m_a, n)
        pb = inproj_tile(m_b, n)
        tt("dve", tmpA[:, 0:n], PS[pa][:, 0:n], cos_ap, ALU.mult, [bPS[pa], brope], [btA])
        tt("dve", tmpB[:, 0:n], PS[pb][:, 0:n], sin_ap, ALU.mult, [bPS[pb], brope], [btB])
        tt("pool", out_ap, tmpA[:, 0:n], tmpB[:, 0:n], ALU.add, [btA, btB], [bout])
        if extra32 is not None:
            tt("pool", extra32[0], tmpA[:, 0:n], tmpB[:, 0:n], ALU.add, [btA, btB], [extra32[1]])

    def gelu_glu(n):
        for o in range(2):
            tt("pool", tmpC[:, 0:n], yss[:, o, 0:n], yss[:, o, 0:n], ALU.mult, [byss], [btC])
            ts("pool", tmpC[:, 0:n], tmpC[:, 0:n], 0.044715, 1.0, ALU.mult, ALU.add, [btC], [btC])
            tt("pool", tmpC[:, 0:n], tmpC[:, 0:n], yss[:, o, 0:n], ALU.mult, [btC, byss], [btC])
            act(tmpC[:, 0:n], tmpC[:, 0:n], AF.Sigmoid, [btC], [btC], scale=2.0 * math.sqrt(2.0 / math.pi))
            tt("dve", z32[:, o, 0:n], yss[:, o, 0:n], tmpC[:, 0:n], ALU.mult, [byss, btC], [bz32])
            cp("pool", z16[:, o, 0:n], z32[:, o, 0:n], [bz32], [bz16])
        for o in range(2):
            pi = nps()
            for k in range(2):
                mm(PS[pi][:, 0:n], wgl16[:, k, o * 128:(o + 1) * 128], z16[:, k, 0:n], k == 0, k == 1, [bwgl, bz16], [bPS[pi]])
            act(tmpD[:, 0:n], PS[pi][:, 0:n], AF.Sigmoid, [bPS[pi], bsp2], [btD], bias=sp2[:, 1, o:o + 1])
            tt("dve", os16[:, o, 0:n], z32[:, o, 0:n], tmpD[:, 0:n], ALU.mult, [bz32, btD], [bos])

    def conv_ln_stages(rhs_fn, n, view):
        pcs = {}

        def st_mm(m):
            def f():
                pi = nps()
                pcs[m] = pi
                for k in range(31):
                    mm(view(PS[pi][:, 0:n]), diag16[:, m, k, :], rhs_fn(m, k), k == 0, k == 30, [bdiag, bcb16, bcbs16], [bPS[pi]])
            return f

        def st_evac(m):
            def f():
                pi = pcs[m]
                act(ycf[:, m, 0:n], PS[pi][:, 0:n], AF.Identity, [bPS[pi], bsp2], [bycf], bias=sp2[:, 2, m:m + 1])
                act(yc16[:, 2 + m, 0:n], PS[pi][:, 0:n], AF.Square, [bPS[pi], bsp2], [byc16], bias=sp2[:, 2, m:m + 1])
            return f

        def st_cast(m):
            def f():
                cp("dve", yc16[:, m, 0:n], ycf[:, m, 0:n], [bycf], [byc16])
            return f

        def st_statmm():
            p1 = nps()
            for m in range(2):
                mm(PS[p1][:, 0:n], ones16[:], yc16[:, m, 0:n], m == 0, m == 1, [bones, byc16], [bPS[p1]])
            p2 = nps()
            for m in range(2):
                mm(PS[p2][:, 0:n], ones16[:], yc16[:, 2 + m, 0:n], m == 0, m == 1, [bones, byc16], [bPS[p2]])
            pcs["p1"] = p1
            pcs["p2"] = p2

        def st_stat1():
            p1, p2 = pcs["p1"], pcs["p2"]
            ts("dve", cva[:, 0:n], PS[p1][:, 0:n], 1.0 / 256, None, ALU.mult, None, [bPS[p1]], [bcva])
            tt("dve", cvb[:, 0:n], cva[:, 0:n], cva[:, 0:n], ALU.mult, [bcva], [bcvb])
            stt("dve", cvb[:, 0:n], PS[p2][:, 0:n], 1.0 / 256, cvb[:, 0:n], ALU.mult, ALU.subtract, [bPS[p2], bcvb], [bcvb])
            ts("dve", cvb[:, 0:n], cvb[:, 0:n], 1e-6, None, ALU.add, None, [bcvb], [bcvb])
            act(cvb[:, 0:n], cvb[:, 0:n], AF.Sqrt, [bcvb], [bcvb])

        def st_stat2():
            P.add("dve", lambda e: e.reciprocal(out=cvb[:, 0:n], in_=cvb[:, 0:n]), reads=[bcvb], writes=[bcvb])

        def st_apply1(m):
            def f():
                tt("dve", ycf[:, m, 0:n], ycf[:, m, 0:n], cva[:, 0:n], ALU.subtract, [bycf, bcva], [bycf])
                tt("dve", ycf[:, m, 0:n], ycf[:, m, 0:n], cvb[:, 0:n], ALU.mult, [bycf, bcvb], [bycf])
                act(ycf[:, m, 0:n], ycf[:, m, 0:n], AF.Identity, [bycf, bsp2], [bycf], bias=sp2[:, 4, m:m + 1], scale=sp2[:, 3, m:m + 1])
                act(cvc[:, m, 0:n], ycf[:, m, 0:n], AF.Sigmoid, [bycf], [bcvc])
            return f

        def st_apply2(m):
            def f():
                tt("dve", oc16[:, m, 0:n], ycf[:, m, 0:n], cvc[:, m, 0:n], ALU.mult, [bcvc, bycf], [boc])
            return f
        return [st_mm(0), st_mm(1), st_evac(0), st_evac(1), st_cast(0), st_cast(1), st_statmm, st_stat1, st_stat2,
                st_apply1(0), st_apply1(1), st_apply2(0), st_apply2(1)]

    def conv_ln(rhs_fn, n, view):
        for f in conv_ln_stages(rhs_fn, n, view):
            f()

    def outproj_ffn(xa, bxs, n, l, sample, first=False):
        for mo in range(8):
            s = ffi[0] % NG
            ffi[0] += 1
            wa_v = wfl[s][0:64, 0:1024].rearrange("p (h c) -> p h c", c=128)
            wr_v = wfl[s][:, 1024:1536].rearrange("p (h c) -> p h c", c=128)
            fa = wfl[s][0:64, 0:1024]
            fr = wfl[s][:, 1024:1536]
            bw = bwgu[s]
            if first:
                ld(wa_v, w_out[l, 0:512, mo * 128:(mo + 1) * 128].rearrange("(h d) n -> d h n", d=64), bw, [bw], q="pool")
                ld(wr_v, w_out[l, 512:1024, mo * 128:(mo + 1) * 128].rearrange("(j p) n -> p j n", p=128), bw, [bw], q="pool")
                P.add("sp", lambda e, fa=fa, mo=mo: e.dma_start(out=woa_c[mo], in_=fa), reads=[bw], writes=[bwo_c[mo]], dma_owner=bw)
                P.add("sp", lambda e, fr=fr, mo=mo: e.dma_start(out=wor_c[mo], in_=fr), reads=[bw], writes=[bwo_c[mo]], dma_owner=bw)
            else:
                P.add("sp", lambda e, fa=fa, mo=mo: e.dma_start(out=fa, in_=woa_c[mo]), reads=[bwo_c[mo]], writes=[bw], dma_owner=bw)
                P.add("sp", lambda e, fr=fr, mo=mo: e.dma_start(out=fr, in_=wor_c[mo]), reads=[bwo_c[mo]], writes=[bw], dma_owner=bw)
            pi = nps()
            for hq in range(8):
                mm(PS[pi][:, 0:n], wa_v[:, hq, :], att16[:, hq, 0:n], hq == 0, False, [bw, batt], [bPS[pi]])
            for j in range(2):
                mm(PS[pi][:, 0:n], wr_v[:, j, :], os16[:, j, 0:n], False, False, [bw, bos], [bPS[pi]])
            for j in range(2):
                mm(PS[pi][:, 0:n], wr_v[:, 2 + j, :], oc16[:, j, 0:n], False, j == 1, [bw, boc], [bPS[pi]])
            resid(xa, bxs, n, mo, pi, l, 16, sample)
        rmsnorm(xa, bxs, n, 1, 24, 1, l, sample)
        for j in range(JT):
            s = ffi[0] % NG
            ffi[0] += 1
            fg = wfl[s]
            if first:
                ld(wgu[s][:, 0], w_gate[l, :, j * 128:(j + 1) * 128].rearrange("(k p) n -> p k n", p=128), bwgu[s], [bwgu[s]], q="pool")
                ld(wgu[s][:, 1], w_up[l, :, j * 128:(j + 1) * 128].rearrange("(k p) n -> p k n", p=128), bwgu[s], [bwgu[s]], q="pool")
                P.add("sp", lambda e, fg=fg, j=j: e.dma_start(out=wgu_c[j], in_=fg), reads=[bwgu[s]], writes=[bwgu_c[j]], dma_owner=bwgu[s])
            else:
                P.add("sp", lambda e, fg=fg, j=j: e.dma_start(out=fg, in_=wgu_c[j]), reads=[bwgu_c[j]], writes=[bwgu[s]], dma_owner=bwgu[s])
            pg = nps()
            for k in range(KT):
                mm(PS[pg][:, 0:n], wgu[s][:, 0, k, :], h16[:, k, 0:n], k == 0, k == KT - 1, [bwgu[s], bh], [bPS[pg]])
            pu = nps()
            for k in range(KT):
                mm(PS[pu][:, 0:n], wgu[s][:, 1, k, :], h16[:, k, 0:n], k == 0, k == KT - 1, [bwgu[s], bh], [bPS[pu]])
            tA = tmpA if j % 2 == 0 else tmpB
            bA = btA if j % 2 == 0 else btB
            act(tA[:, 0:n], PS[pg][:, 0:n], AF.Silu, [bPS[pg]], [bA])
            tt("dve", scr16[:, j, 0:n], tA[:, 0:n], PS[pu][:, 0:n], ALU.mult, [bA, bPS[pu]], [bscr])
        for mo in range(8):
            pi = nps()
            for jh in range(2):
                s = ffi[0] % NG
                ffi[0] += 1
                ci = mo * 2 + jh
                fd = wfl[s][:, 0:1408]
                if first:
                    ld(wdn[s], w_down[l, jh * 1408:(jh + 1) * 1408, mo * 128:(mo + 1) * 128].rearrange("(j p) n -> p j n", p=128), bwdn[s], [bwdn[s]], q="pool")
                    P.add("sp", lambda e, fd=fd, ci=ci: e.dma_start(out=wdn_c[ci], in_=fd), reads=[bwdn[s]], writes=[bwdn_c[ci]], dma_owner=bwdn[s])
                else:
                    P.add("sp", lambda e, fd=fd, ci=ci: e.dma_start(out=fd, in_=wdn_c[ci]), reads=[bwdn_c[ci]], writes=[bwdn[s]], dma_owner=bwdn[s])
                for jj in range(11):
                    j = jh * 11 + jj
                    mm(PS[pi][:, 0:n], wdn[s][:, jj, :], scr16[:, j, 0:n], j == 0, j == JT - 1, [bwdn[s], bscr], [bPS[pi]])
            resid(xa, bxs, n, mo, pi, l, 40, sample)

    hp = sb("hp", [128, 2]); bhp = Buf("hp")
    ld(hp[:], hprev, bhp, [bhp])
    hst = ycf[:].rearrange("p a t -> p (a t)")[:, 0:HF]
    GROUPS = [[0, 4], [1, 5], [2, 6], [3, 7]]

    def slot_begin(l):
        if l == 0:
            memset("dve", kb16[:, 0:128], 0.0, [bkb])
            memset("dve", v16[:, 0, :], 0.0, [bv16])
            memset("dve", cb32[:, :, 0:30], 0.0, [bcb32])
            memset("dve", inre[:], 0.0, [binit])
            memset("dve", inim[:], 0.0, [binit])
            return
        P.add("sp", lambda e: e.dma_start(out=hst, in_=hall.ap()[0:128, :]), reads=[bhall], writes=[bycf], dma_owner=bycf)
        ts("dve", hst, hst, hp[:, 0:1], None, ALU.mult, None, [bycf, bhp], [bycf])
        cp("dve", kb16[:, 0:128], hst[:, 0:128], [bycf], [bkb])
        cp("dve", v16[:, 0, :], hst[:, 128:256], [bycf], [bv16])
        cp("dve", cb32[:, :, 0:30], hst[:, 256:316].rearrange("p (a r) -> p a r", a=2), [bycf], [bcb32])
        cp("dve", inre[:], hst[:, 316:324], [bycf], [binit])
        cp("dve", inim[:], hst[:, 324:332], [bycf], [binit])

    def slot_end(l):
        cp("dve", hst[:, 0:128], kb16[:, 0:128], [bkb], [bycf])
        cp("dve", hst[:, 128:256], v16[:, 0, :], [bv16], [bycf])
        cp("dve", hst[:, 256:316].rearrange("p (a r) -> p a r", a=2), cb32[:, :, 0:30], [bcb32], [bycf])
        cp("dve", hst[:, 316:324], inre[:], [binit], [bycf])
        cp("dve", hst[:, 324:332], inim[:], [binit], [bycf])
        P.add("sp", lambda e: e.dma_start(out=hin.ap(), in_=hst), reads=[bycf], writes=[bhin], dma_owner=bycf)
        P.add("pool", lambda e: e.collective_compute("AllGather", ALU.bypass, replica_groups=GROUPS,
                                                     ins=[hin.ap().opt()], outs=[hall.ap().opt()]),
              reads=[bhin], writes=[bhall], dma_owner=bhall, dinc=1)

    def prompt_chunk(l, c):
        n = T
        t0 = c * T
        src = xT if l == 0 else xscr
        xdr = src[:, t0:t0 + T].rearrange("(k p) t -> p k t", p=128)
        ld(xs[:], xdr, bx[0], bx)
        ld(rp[:], ropeP[:, :, t0:t0 + T].rearrange("a p t -> p a t"), brp, [brp])
        if KS2 < 1:
            return
        rmsnorm(xs, bx, n, 1, 0, 0, l, False)
        if KS2 < 2:
            return
        for j in range(4):
            rope_pair(j, 5 + j, rp[:, 0, :], rp[:, 1, :], brp, q16[:, j, :], bq, n)
        rope_pair(4, 9, rp[:, 0, :], rp[:, 1, :], brp, kb16[:, 128:128 + T], bkb, n, extra32=(k32[:, 0:n], bk32))
        if KS2 < 3:
            return
        for o in range(2):
            pi = inproj_tile(10 + o, n)
            cp("act", u32[:, o, :], PS[pi][:, 0:n], [bPS[pi]], [bu32])
            cp("dve", u16[:, o, :], PS[pi][:, 0:n], [bPS[pi]], [bu16])
        for o in range(2):
            pa = inproj_tile(12 + o, n)
            pg = inproj_tile(14 + o, n)
            act(tmpC[:, 0:n], PS[pg][:, 0:n], AF.Sigmoid, [bPS[pg]], [btC])
            tt("dve", cb32[:, o, 30:30 + T], PS[pa][:, 0:n], tmpC[:, 0:n], ALU.mult, [bPS[pa], btC], [bcb32])
            cp("pool", cb16[:, o, :], cb32[:, o, :], [bcb32], [bcb16])
        if KS2 < 4:
            return
        for tb in range(T // 128):
            pi = nps()
            for k in range(KT):
                mm(PS[pi][:, 0:128], h16[:, k, tb * 128:(tb + 1) * 128], win16[:, k, 2048:2176], k == 0, k == KT - 1, [bh, bwin], [bPS[pi]])
            cp("act", v16[:, 1 + tb, :], PS[pi][:, 0:128], [bPS[pi]], [bv16])
            if c == NCH - 1 and tb == T // 128 - 1:
                cp("dve", v32[:], PS[pi][:, 0:128], [bPS[pi]], [bv32])
                st(nv[l], v32[:], bv32, [bv32])
        if c == NCH - 1:
            st(nkT[l], k32[:, T - 128:T], bk32, [bk32])
        if KSUB < 1:
            return
        blocks = [(qb_, hh_) for qb_ in range(T // 128) for hh_ in range(2)]

        def emit_scores(qb_, hh_):
            hs_ = slice(64 * hh_, 64 * hh_ + 64)
            qrhs_ = q16[hs_, :, qb_ * 128:(qb_ + 1) * 128]
            first_ = False
            po_ = nps()
            mm(PS[po_][:].rearrange("p (g q) -> p g q", g=4), kb16[hs_, 128 + qb_ * 128:128 + (qb_ + 1) * 128], qrhs_, True, False, [bkb, bq], [bPS[po_]])
            mm(PS[po_][:], ident16[:], mkn[:, 0, :], False, True, [bident16, bmkn], [bPS[po_]])
            pp_ = None
            if not first_:
                pp_ = nps()
                mm(PS[pp_][:].rearrange("p (g q) -> p g q", g=4), kb16[hs_, qb_ * 128:(qb_ + 1) * 128], qrhs_, True, False, [bkb, bq], [bPS[pp_]])
                mm(PS[pp_][:], ident16[:], mkn[:, 1, :], False, True, [bident16, bmkn], [bPS[pp_]])
            return po_, pp_
        pend = emit_scores(*blocks[0])
        for bi, (qb, hh) in enumerate(blocks):
            hs = slice(64 * hh, 64 * hh + 64)
            first = False
            po, pp = pend
            act(pown[:], PS[po][:], AF.Exp, [bPS[po]], [bpown], scale=0.125)
            if c == 0 and qb == 0:
                act(pprev[:], PS[pp][:], AF.Exp, [bPS[pp], bhp], [bpprev], scale=0.125, bias=hp[:, 1:2])
            else:
                act(pprev[:], PS[pp][:], AF.Exp, [bPS[pp]], [bpprev], scale=0.125)
            if bi + 1 < len(blocks):
                pend = emit_scores(*blocks[bi + 1])
            pO = 6
            pD = 7
            if not first:
                mm(PS[pO][0:64, :], v16[:, qb, hs], pprev[:], True, False, [bv16, bpprev], [bPS[pO]])
                mm(PS[pD][0:64, :], ones16[:, 0:64], pprev[:], True, False, [bones, bpprev], [bPS[pD]])
            mm(PS[pO][0:64, :], v16[:, qb + 1, hs], pown[:], first, True, [bv16, bpown], [bPS[pO]])
            mm(PS[pD][0:64, :], ones16[:, 0:64], pown[:], first, True, [bones, bpown], [bPS[pD]])
            tt("dve", den[:].rearrange("p (g q) -> p g q", g=4), PS[pD][0:64, :].rearrange("p (g q) -> p g q", g=4),
               sk[:, 4 * hh:4 * hh + 4].unsqueeze(2).to_broadcast([64, 4, 128]), ALU.add, [bPS[pD], bsk], [btA, btB])
            P.add("dve", lambda e: e.reciprocal(out=den[:], in_=den[:]), reads=[btA, btB], writes=[btA, btB])
            tt("dve", att16[:, 4 * hh:4 * hh + 4, qb * 128:(qb + 1) * 128], PS[pO][0:64, :].rearrange("p (g q) -> p g q", g=4),
               den[:].rearrange("p (g q) -> p g q", g=4), ALU.mult, [bPS[pO], btA, btB], [batt])
        cp("pool", kb16[:, 0:128], kb16[:, T:T + 128], [bkb], [bkb])
        cp("pool", v16[:, 0, :], v16[:, T // 128, :], [bv16], [bv16])
        if KSUB < 2:
            return
        pY = [6, 7]
        def emit_bu(sc_, ct_):
            pr_ = nps()
            pim_ = nps()
            mm(PS[pr_][:, 0:TS], LB[:, 0, ct_, :], u16[:, ct_ // 4, sc_ * TS:(sc_ + 1) * TS], True, True, [bLB, bu16], [bPS[pr_]])
            mm(PS[pim_][:, 0:TS], LB[:, 1, ct_, :], u16[:, ct_ // 4, sc_ * TS:(sc_ + 1) * TS], True, True, [bLB, bu16], [bPS[pim_]])
            return pr_, pim_
        iters = [(sc_, ct_) for sc_ in range(T // TS) for ct_ in range(8)]
        cstages = conv_ln_stages(lambda m, k: cb16[:, m, k:k + T], n, lambda ap: ap)
        csched = {0: [0], 1: [1], 2: [2], 3: [3], 4: [4], 5: [5], 6: [6], 8: [7], 9: [8], 10: [9], 11: [10], 13: [11], 14: [12]}
        pend = emit_bu(*iters[0])
        for sc in range(T // TS):
            c0 = sc * TS
            firstsub = (c == 0 and sc == 0)
            for ct in range(8):
                uh = ct // 4
                pr, pim = pend
                nxt = sc * 8 + ct + 1
                if nxt < len(iters):
                    pend = emit_bu(*iters[nxt])
                cs_ = cosT[:, ct, 0:TS]
                sn_ = sinT[:, ct, 0:TS]
                par = ssi[0] % 2
                ssi[0] += 1
                qA, bqA = sq[par][0], bsq[par][0]
                qB, bqB = sq[par][1], bsq[par][1]
                hA, bhA = sh16[par][0], bsh16[par][0]
                hB, bhB = sh16[par][1], bsh16[par][1]
                tt("dve", sx[0][:], PS[pr][:, 0:TS], cs_, ALU.mult, [bPS[pr], btab], [bsx[0]])
                tt("dve", sx[1][:], PS[pim][:, 0:TS], sn_, ALU.mult, [bPS[pim], btab], [bsx[1]])
                tt("dve", sx[0][:], sx[0][:], sx[1][:], ALU.add, [bsx[0], bsx[1]], [bsx[0]])
                tt("dve", sx[2][:], PS[pim][:, 0:TS], cs_, ALU.mult, [bPS[pim], btab], [bsx[2]])
                tt("dve", sx[3][:], PS[pr][:, 0:TS], sn_, ALU.mult, [bPS[pr], btab], [bsx[3]])
                tt("dve", sx[2][:], sx[2][:], sx[3][:], ALU.subtract, [bsx[2], bsx[3]], [bsx[2]])
                rbc = sp8[:, 5, ct:ct + 1].to_broadcast([128, TS])
                ire = inre[:, ct:ct + 1]
                iim = inim[:, ct:ct + 1]
                P.add("dve", lambda e, ire=ire, rbc=rbc, qA=qA: e.tensor_tensor_scan(out=qA[:], data0=rbc, data1=sx[0][:], initial=ire, op0=ALU.mult, op1=ALU.add),
                      reads=[bsp8, bsx[0], binit], writes=[bqA])
                P.add("dve", lambda e, iim=iim, rbc=rbc, qB=qB: e.tensor_tensor_scan(out=qB[:], data0=rbc, data1=sx[2][:], initial=iim, op0=ALU.mult, op1=ALU.add),
                      reads=[bsp8, bsx[2], binit], writes=[bqB])
                cp("pool", qlre[:, ct:ct + 1], qA[:, TS - 1:TS], [bqA], [bql])
                cp("pool", qlim[:, ct:ct + 1], qB[:, TS - 1:TS], [bqB], [bql])
                tt("pool", sx[6][:], qA[:], cs_, ALU.mult, [bqA, btab], [bsx[6]])
                tt("pool", sx[7][:], qB[:], sn_, ALU.mult, [bqB, btab], [bsx[7]])
                tt("pool", hA[:], sx[6][:], sx[7][:], ALU.subtract, [bsx[6], bsx[7]], [bhA])
                tt("pool", sx[4][:], qA[:], sn_, ALU.mult, [bqA, btab], [bsx[4]])
                tt("pool", sx[5][:], qB[:], cs_, ALU.mult, [bqB, btab], [bsx[5]])
                tt("pool", hB[:], sx[4][:], sx[5][:], ALU.add, [bsx[4], bsx[5]], [bhB])
                ot = ct // 4
                mm(PS[pY[ot]][:, c0:c0 + TS], LB[:, 2, ct, :], hA[:], ct % 4 == 0, False, [bLB, bhA], [bPS[pY[ot]]])
                mm(PS[pY[ot]][:, c0:c0 + TS], LB[:, 3, ct, :], hB[:], False, ct % 4 == 3, [bLB, bhB], [bPS[pY[ot]]])
                for si_ in csched.get(sc * 8 + ct, []):
                    cstages[si_]()
            cl = cosT[:, :, TS - 1]
            sl = sinT[:, :, TS - 1]
            a0, a1 = sm8[0][:], sm8[1][:]
            tt("dve", a0, qlre[:], cl, ALU.mult, [bql, btab], [bsm8])
            tt("dve", a1, qlim[:], sl, ALU.mult, [bql, btab], [bsm8])
            tt("dve", hlre[:], a0, a1, ALU.subtract, [bsm8], [bhl])
            tt("dve", a0, qlre[:], sl, ALU.mult, [bql, btab], [bsm8])
            tt("dve", a1, qlim[:], cl, ALU.mult, [bql, btab], [bsm8])
            tt("dve", hlim[:], a0, a1, ALU.add, [bsm8], [bhl])
            tt("dve", a0, hlre[:], S8(6), ALU.mult, [bhl, bsp8], [bsm8])
            tt("dve", a1, hlim[:], S8(7), ALU.mult, [bhl, bsp8], [bsm8])
            tt("dve", inre[:], a0, a1, ALU.subtract, [bsm8], [binit])
            tt("dve", a0, hlre[:], S8(7), ALU.mult, [bhl, bsp8], [bsm8])
            tt("dve", a1, hlim[:], S8(6), ALU.mult, [bhl, bsp8], [bsm8])
            tt("dve", inim[:], a0, a1, ALU.add, [bsm8], [binit])
            if c == NCH - 1 and sc == T // TS - 1:
                st(nre[l], hlre[:], bhl, [bhl])
                st(nim[l], hlim[:], bhl, [bhl])
        for o in range(2):
            stt("dve", yss[:, o, 0:n], u32[:, o, 0:n], sp2[:, 0, o:o + 1], PS[pY[o]][:, 0:n], ALU.mult, ALU.add, [bu32, bsp2, bPS[pY[o]]], [byss])
        gelu_glu(n)
        if KSUB < 3:
            return
        if c == NCH - 1:
            st(ncv[l], cb32[:, :, T:T + 30], bcb32, [bcb32])
        for o in range(2):
            cp("pool", cb32[:, o, 0:30], cb32[:, o, T:T + 30], [bcb32], [bcb32])
        if KSUB < 4:
            return
        outproj_ffn(xs, bx, n, l, False, first=(c == 0))
        if l < nl - 1:
            st(xscr[:, t0:t0 + T].rearrange("(k p) t -> p k t", p=128), xs[:], bx[0], bx)
        else:
            rmsnorm(xs, bx, n, None, 0, 0, l, False)
            st(yT[:, t0:t0 + T].rearrange("(k p) t -> p k t", p=128), xs[:], bx[0], bx)

    def sample_layer(l):
        n = TSM
        bxs = [bxsm] * KT
        ld(kc16, kcT[l].rearrange("b f k -> f b k"), bkc, [bkc], q="pool")
        ld(vc16, vc[l].rearrange("b k f -> k b f"), bvc, [bvc], q="pool")
        ld(h0re[:], ssmre_in[l], bh0, [bh0])
        ld(h0im[:], ssmim_in[l], bh0, [bh0])
        ld(cbs32[:, :, :, 0:30], sconv_in[l], bcbs32, [bcbs32])
        dd = Buf(f"dd{l}")
        o1 = P.add("sp", lambda e: e.dma_start(out=nks_c[l], in_=kcn[l, :, 4:128, :]), dma_owner=dd)
        stores.append(o1)
        vcn_src = vc[l, :, 4:128, :]
        o2 = P.add("sp", lambda e: e.dma_start(out=nvs_c[l], in_=vcn_src), dma_owner=dd)
        stores.append(o2)
        rmsnorm(xsm, bxs, n, 1, 0, 0, l, True)
        for j in range(4):
            rope_pair(j, 5 + j, rps[:, 0, :], rps[:, 1, :], brps, q16[:, j, 0:n], bq, n)
        rope_pair(4, 9, rps[:, 0, :], rps[:, 1, :], brps, kb16[:, 128:128 + n], bkb, n, extra32=(k32[:, 0:n], bk32))
        st(skT[l], k32[:, 0:n], bk32, [bk32])
        for o in range(2):
            pi = inproj_tile(10 + o, n)
            cp("act", u32[:, o, 0:n], PS[pi][:, 0:n], [bPS[pi]], [bu32])
            cp("dve", u16[:, o, 0:n], PS[pi][:, 0:n], [bPS[pi]], [bu16])
        for o in range(2):
            pa = inproj_tile(12 + o, n)
            pg = inproj_tile(14 + o, n)
            act(tmpC[:, 0:n], PS[pg][:, 0:n], AF.Sigmoid, [bPS[pg]], [btC])
            tt("dve", cbs32[:, o, :, 30:34], PS[pa][:, 0:n].rearrange("p (b t) -> p b t", t=LS),
               tmpC[:, 0:n].rearrange("p (b t) -> p b t", t=LS), ALU.mult, [bPS[pa], btC], [bcbs32])
            cp("pool", cbs16[:, o], cbs32[:, o], [bcbs32], [bcbs16])
        st(scv[l], cbs32[:, :, :, 4:34], bcbs32, [bcbs32])
        pv = nps()
        for b in range(NB):
            for k in range(KT):
                mm(PS[pv][0:4, :].rearrange("p (b f) -> p b f", b=4)[:, b % 4, :] if False else PS[pv][0:4, (b % 4) * 128:(b % 4 + 1) * 128],
                   h16[:, k, 4 * b:4 * b + 4], win16[:, k, 2048:2176], k == 0, k == KT - 1, [bh, bwin], [bPS[pv]])
            if b % 4 == 3:
                g0 = b - 3
                cp("act", vn16[:, g0:g0 + 4, :], PS[pv][0:4, :].rearrange("p (b f) -> p b f", b=4), [bPS[pv]], [bvn16])
                cp("dve", vn32[:, g0:g0 + 4, :], PS[pv][0:4, :].rearrange("p (b f) -> p b f", b=4), [bPS[pv]], [bvn32])
                if b < NB - 1:
                    pv = nps()
        st(svn[l], vn32[:], bvn32, [bvn32])
        pC = nps()
        pN = nps()
        for b in range(NB):
            for hh in range(2):
                hs = slice(64 * hh, 64 * hh + 64)
                col = (b * 2 + hh) * 16
                qrhs = q16[hs, :, 4 * b:4 * b + 4]
                mm(PS[pC][:, col:col + 16].rearrange("p (g t) -> p g t", g=4), kc16[hs, b, :], qrhs, True, True, [bkc, bq], [bPS[pC]])
                mm(PS[pN][0:4, col:col + 16].rearrange("p (g t) -> p g t", g=4), kb16[hs, 128 + 4 * b:128 + 4 * b + 4], qrhs, True, True, [bkb, bq], [bPS[pN]])
        act(pown[:], PS[pC][:], AF.Exp, [bPS[pC]], [bpown], scale=0.125)
        tt("pool", pown[:], pown[:], mk[:, 2, :], ALU.mult, [bpown, bmk], [bpown])
        act(pn16[:], PS[pN][0:4, :], AF.Exp, [bPS[pN]], [bpn], scale=0.125)
        tt("pool", pn16[:], pn16[:], mk[0:4, 3, :], ALU.mult, [bpn, bmk], [bpn])
        pO = nps()
        pD = nps()
        for b in range(NB):
            for hh in range(2):
                hs = slice(64 * hh, 64 * hh + 64)
                col = (b * 2 + hh) * 16
                mm(PS[pO][0:64, col:col + 16], vc16[:, b, hs], pown[:, col:col + 16], True, False, [bvc, bpown], [bPS[pO]])
                mm(PS[pO][0:64, col:col + 16], vn16[0:4, b, hs], pn16[0:4, col:col + 16], False, True, [bvn16, bpn], [bPS[pO]])
                mm(PS[pD][0:64, col:col + 16], ones16[:, 0:64], pown[:, col:col + 16], True, False, [bones, bpown], [bPS[pD]])
                mm(PS[pD][0:64, col:col + 16], ones16[0:4, 0:64], pn16[0:4, col:col + 16], False, True, [bones, bpn], [bPS[pD]])
        tt("dve", den[:].rearrange("p (b h t) -> p b h t", b=NB, t=LS), PS[pD][0:64, :].rearrange("p (b h t) -> p b h t", b=NB, t=LS),
           sk[:, :].unsqueeze(1).unsqueeze(3).to_broadcast([64, NB, 8, LS]), ALU.add, [bPS[pD], bsk], [btA, btB])
        P.add("dve", lambda e: e.reciprocal(out=den[:], in_=den[:]), reads=[btA, btB], writes=[btA, btB])
        tt("dve", att16[:, :, 0:n].rearrange("p h (b t) -> p b h t", t=LS), PS[pO][0:64, :].rearrange("p (b h t) -> p b h t", b=NB, t=LS),
           den[:].rearrange("p (b h t) -> p b h t", b=NB, t=LS), ALU.mult, [bPS[pO], btA, btB], [batt])
        for (dst, x1, y1, x2, y2, op) in ((ahre, 8, h0re, 9, h0im, ALU.subtract), (ahim, 8, h0im, 9, h0re, ALU.add)):
            tt("dve", dst[:], y1[:], sp8[:, x1, :].unsqueeze(2).to_broadcast([128, 8, NB]), ALU.mult, [bh0, bsp8], [bah])
            tt("dve", hsre[:], y2[:], sp8[:, x2, :].unsqueeze(2).to_broadcast([128, 8, NB]), ALU.mult, [bh0, bsp8], [bhs])
            tt("dve", dst[:], dst[:], hsre[:], op, [bah, bhs], [bah])
        pY = [6, 7]
        for ct in range(8):
            uh = ct // 4
            pr = nps()
            pim = nps()
            mm(PS[pr][:, 0:n], LB[:, 0, ct, :], u16[:, uh, 0:n], True, True, [bLB, bu16], [bPS[pr]])
            mm(PS[pim][:, 0:n], LB[:, 1, ct, :], u16[:, uh, 0:n], True, True, [bLB, bu16], [bPS[pim]])
            cs_ = cosT[:, ct, 0:LS].unsqueeze(1).to_broadcast([128, NB, LS])
            sn_ = sinT[:, ct, 0:LS].unsqueeze(1).to_broadcast([128, NB, LS])
            V3 = lambda ap: ap.rearrange("p (b t) -> p b t", t=LS)
            X = [s_[:, 0:n] for s_ in sx]
            tt("dve", V3(X[0]), V3(PS[pr][:, 0:n]), cs_, ALU.mult, [bPS[pr], btab], [bsx[0]])
            tt("dve", V3(X[1]), V3(PS[pim][:, 0:n]), sn_, ALU.mult, [bPS[pim], btab], [bsx[1]])
            tt("pool", X[0], X[0], X[1], ALU.add, [bsx[0], bsx[1]], [bsx[0]])
            tt("dve", V3(X[2]), V3(PS[pim][:, 0:n]), cs_, ALU.mult, [bPS[pim], btab], [bsx[2]])
            tt("dve", V3(X[3]), V3(PS[pr][:, 0:n]), sn_, ALU.mult, [bPS[pr], btab], [bsx[3]])
            tt("pool", X[2], X[2], X[3], ALU.subtract, [bsx[2], bsx[3]], [bsx[2]])
            tt("dve", sx[0][:, 0:n:LS], sx[0][:, 0:n:LS], ahre[:, ct, :], ALU.add, [bsx[0], bah], [bsx[0]])
            tt("dve", sx[2][:, 0:n:LS], sx[2][:, 0:n:LS], ahim[:, ct, :], ALU.add, [bsx[2], bah], [bsx[2]])
            r4v = r4[:, ct].rearrange("p b t -> p (b t)")
            P.add("dve", lambda e, r4v=r4v, X=X: e.tensor_tensor_scan(out=X[4], data0=r4v, data1=X[0], initial=0.0, op0=ALU.mult, op1=ALU.add),
                  reads=[btab4, bsx[0]], writes=[bsx[4]])
            P.add("dve", lambda e, r4v=r4v, X=X: e.tensor_tensor_scan(out=X[5], data0=r4v, data1=X[2], initial=0.0, op0=ALU.mult, op1=ALU.add),
                  reads=[btab4, bsx[2]], writes=[bsx[5]])
            tt("pool", V3(X[6]), V3(X[4]), cs_, ALU.mult, [bsx[4], btab], [bsx[6]])
            tt("pool", V3(X[7]), V3(X[5]), sn_, ALU.mult, [bsx[5], btab], [bsx[7]])
            tt("pool", X[6], X[6], X[7], ALU.subtract, [bsx[6], bsx[7]], [bsx[6]])
            cp("act", hre16[:, 0:n], X[6], [bsx[6]], [bhre])
            cp("act", hsre[:, ct, :], sx[6][:, LS - 1:n:LS], [bsx[6]], [bhs])
            tt("dve", V3(X[1]), V3(X[4]), sn_, ALU.mult, [bsx[4], btab], [bsx[1]])
            tt("dve", V3(X[3]), V3(X[5]), cs_, ALU.mult, [bsx[5], btab], [bsx[3]])
            tt("dve", X[1], X[1], X[3], ALU.add, [bsx[1], bsx[3]], [bsx[1]])
            cp("act", him16[:, 0:n], X[1], [bsx[1]], [bhim])
            cp("act", hsim[:, ct, :], sx[1][:, LS - 1:n:LS], [bsx[1]], [bhs])
            ot = ct // 4
            mm(PS[pY[ot]][:, 0:n], LB[:, 2, ct, :], hre16[:, 0:n], ct % 4 == 0, False, [bLB, bhre], [bPS[pY[ot]]])
            mm(PS[pY[ot]][:, 0:n], LB[:, 3, ct, :], him16[:, 0:n], False, ct % 4 == 3, [bLB, bhim], [bPS[pY[ot]]])
        st(sre[l], hsre[:], bhs, [bhs])
        st(sim_o[l], hsim[:], bhs, [bhs])
        for o in range(2):
            stt("dve", yss[:, o, 0:n], u32[:, o, 0:n], sp2[:, 0, o:o + 1], PS[pY[o]][:, 0:n], ALU.mult, ALU.add, [bu32, bsp2, bPS[pY[o]]], [byss])
        gelu_glu(n)
        conv_ln(lambda m, k: cbs16[:, m, :, k:k + LS], n, lambda ap: ap.rearrange("p (b t) -> p b t", t=LS))
        if DBG and l == 0:
            dsb = xs[:, 0:6, :].rearrange("p (a k) (h t) -> p a (k h) t", k=2, t=TSM); bdsb = bx[0]
            memset("dve", dsb, 0.0, bx)
            cp("dve", dsb[0:64, 0, :, :], att16[:, :, 0:n], [batt, bdsb], [bdsb])
            cp("dve", dsb[:, 1, 0:2, :], os16[:, :, 0:n], [bos, bdsb], [bdsb])
            cp("dve", dsb[:, 2, 0:2, :], oc16[:, :, 0:n], [boc, bdsb], [bdsb])
            st(dbg.rearrange("a p h t -> p a h t"), dsb, bdsb, [bdsb])
        outproj_ffn(xsm, bxs, n, l, True)
        if l == nl - 1:
            rmsnorm(xsm, bxs, n, None, 0, 0, l, True)
            st(ysT.rearrange("(k p) t -> p k t", p=128), xsm[:], bxsm, [bxsm])

    STG = int(os.environ.get("KSTAGE", "9"))
    for l in range(nl):
        if STG >= 1:
            layer_params(l)
        slot_begin(l)
        for c in range(NCH):
            if STG >= 3 or (STG == 2 and c == 0):
                prompt_chunk(l, c)
        if l < nl - 1:
            slot_end(l)
        if STG >= 4:
            sample_layer(l)

    P.add("sp", lambda e: e.nop(), extra_deps=stores)
    P.emit()
    return nc


def _perm_win(w):
    q = w[:, 0:512].reshape(D, 8, 64)
    k = w[:, 512:640].reshape(D, 2, 64)
    v = w[:, 640:768]
    u = w[:, 768:1024]
    a = w[:, 1024:1280]
    g = w[:, 1280:1536]

    def swap(t):
        return np.concatenate([t[..., 32:], t[..., :32]], axis=-1)
    order = [0, 4, 1, 5, 2, 6, 3, 7]
    qt = q[:, order, :].reshape(D, 512)
    qs = swap(q)[:, order, :].reshape(D, 512)
    kt = k.reshape(D, 128)
    ks = swap(k).reshape(D, 128)
    return np.ascontiguousarray(np.concatenate([qt, kt, qs, ks, u, a, g, v], axis=1))


def _rope_tab(pos):
    half = 32
    inv = (np.float32(10000.0) ** (-(np.arange(half, dtype=np.float32) / np.float32(half)))).astype(np.float32)
    ang = (pos.astype(np.float32)[None, :] * inv[:, None]).astype(np.float32)
    c = np.cos(ang.astype(np.float64)).astype(np.float32)
    s = np.sin(ang.astype(np.float64)).astype(np.float32)
    cos = np.concatenate([c, c, c, c], axis=0)
    sins = np.concatenate([-s, s, -s, s], axis=0)
    return np.ascontiguousarray(np.stack([cos, sins], axis=0))


_NC_CACHE = {}


def kernel(**inp):
    f = lambda a: np.ascontiguousarray(np.asarray(a, dtype=np.float32))
    I = {k: np.asarray(v) for k, v in inp.items()}
    nlr = _NC_CACHE.get("nl", NS)
    if "nc" not in _NC_CACHE:
        _NC_CACHE["nc"] = build(nlr)
    nc = _NC_CACHE["nc"]
    import os
    ncr = int(os.environ.get("KCORES", "8"))

    SLOT = {0: [0, 1, 2, 3, 0], 1: [0, 0, 1, 2, 3]}
    DUMMY = {0: 4, 1: 0}

    def pk(a):
        return a.reshape(NL, 8, 128).transpose(0, 2, 1)

    def p2(a):
        return a.reshape(NL, 2, 128).transpose(0, 2, 1)

    per_layer = {
        "w_mod": I["w_mod"],
        "b_modT": I["b_mod"].reshape(NL, 48, 128).transpose(0, 2, 1),
        "g1T": pk(I["norm1_g"]), "g2T": pk(I["norm2_g"]),
        "w_in2": np.stack([_perm_win(I["w_in"][l]) for l in range(NL)]),
        "sinkT": np.broadcast_to(I["attn_sinks"][:, None, :], (NL, 64, 8)),
        "lamre": I["ssm_lam_re"].reshape(NL, 8, 128).transpose(0, 2, 1),
        "lamim": I["ssm_lam_im"].reshape(NL, 8, 128).transpose(0, 2, 1),
        "logdt": np.repeat(I["ssm_log_dt"], 64, axis=1).reshape(NL, 8, 128).transpose(0, 2, 1),
        "bre": I["ssm_b_re"].reshape(NL, 8, 128, 16).transpose(0, 2, 1, 3),
        "bim": I["ssm_b_im"].reshape(NL, 8, 128, 16).transpose(0, 2, 1, 3),
        "cre": I["ssm_c_re"].reshape(NL, 8, 2, 16, 64).transpose(0, 2, 4, 1, 3).reshape(NL, 128, 8, 16),
        "cim": I["ssm_c_im"].reshape(NL, 8, 2, 16, 64).transpose(0, 2, 4, 1, 3).reshape(NL, 128, 8, 16),
        "dskipT": p2(I["ssm_d"]), "wglu": I["ssm_w_glu"], "bgluT": p2(I["ssm_b_glu"]),
        "convwT": I["conv_w"].reshape(NL, 31, 2, 128).transpose(0, 3, 2, 1),
        "convbT": p2(I["conv_b"]), "lngT": p2(I["conv_ln_g"]), "lnbT": p2(I["conv_ln_b"]),
        "w_out": I["w_out"], "w_gate": I["w_gate"], "w_up": I["w_up"], "w_down": I["w_down"],
    }
    role_w = {}
    for role in (0, 1):
        d = {}
        for k, a in per_layer.items():
            arr = f(np.asarray(a)[SLOT[role]])
            if k in ("w_out", "w_down"):
                arr[DUMMY[role]] = 0.0
            d[k] = arr
        role_w[role] = d
    const = {
        "gfT": f(I["final_norm_g"].reshape(8, 128).T),
        "ropeS": _rope_tab(PAST + np.tile(np.arange(LS), NB)),
        "identd": np.eye(128, dtype=np.float32),
        "jrow": f(np.broadcast_to(np.arange(TS + 1, dtype=np.float32)[None, :], (128, TS + 1))),
    }
    kk = np.arange(128)[:, None]
    qq = np.tile(np.arange(128), 4)[None, :]
    m_own = np.where(qq >= kk, 0.0, -30000.0).astype(np.float32)
    m_prev = np.where(kk > qq, 0.0, -30000.0).astype(np.float32)
    const["maskP"] = f(np.stack([m_own, m_prev]))
    tq = np.tile(np.arange(LS), 128)[None, :]
    m_c = (kk > tq).astype(np.float32)
    m_n = (kk <= tq).astype(np.float32)
    const["maskS"] = f(np.stack([m_c, m_n]))
    rope_role = {0: _rope_tab(np.arange(NTOK)), 1: _rope_tab(NTOK + np.arange(NTOK))}
    hp_role = {0: f(np.tile(np.array([[0.0, -1.0e4]], np.float32), (128, 1))),
               1: f(np.tile(np.array([[1.0, 0.0]], np.float32), (128, 1)))}

    in_maps = []
    for c in range(ncr):
        role = c // 4
        b = c % 4
        sbs = slice(NB * c, NB * (c + 1))
        sl = SLOT[role]
        m = dict(const)
        m.update(role_w[role])
        m["ropeP"] = rope_role[role]
        m["hprev"] = hp_role[role]
        m["xT"] = f(I["x_prompt"][b, role * NTOK:(role + 1) * NTOK].T)
        m["xsT"] = f(I["x_sample"][sbs].reshape(TSM, D).T)
        m["cT"] = f(np.concatenate([I["c_prompt"][b][None, :], I["c_sample"][sbs]], axis=0).T)
        ck = I["cache_k"][:, sbs].reshape(NL, NB, 128, 128)[sl]
        cv = I["cache_v"][:, sbs].reshape(NL, NB, 128, 128)[sl]
        m["kcT"] = f(ck.transpose(0, 1, 3, 2))
        m["kcn"] = f(ck)
        m["vc"] = f(cv)
        m["ssmre_in"] = f(I["state_ssm_re"][:, sbs].reshape(NL, NB, 8, 128).transpose(0, 3, 2, 1)[sl])
        m["ssmim_in"] = f(I["state_ssm_im"][:, sbs].reshape(NL, NB, 8, 128).transpose(0, 3, 2, 1)[sl])
        m["sconv_in"] = f(I["state_conv"][:, sbs].reshape(NL, NB, 30, 2, 128).transpose(0, 4, 3, 1, 2)[sl])
        in_maps.append(m)

    res = run_bass_kernel_spmd(nc, in_maps, core_ids=list(range(ncr)))
    R = list(res.results)
    while len(R) < 8:
        R.append(R[0])
    if "dbg" in R[0]:
        _NC_CACHE["dbg"] = R[0]["dbg"]
    SA = slice(0, 4)
    SB = slice(1, 5)

    y_prompt = np.stack([np.concatenate([R[b]["yT"].T, R[4 + b]["yT"].T], axis=0) for b in range(4)])
    y_sample = np.concatenate([R[c]["ysT"].T.reshape(NB, LS, D) for c in range(8)], axis=0)
    nk_p = np.stack([R[4 + b]["nkT"][SB].transpose(0, 2, 1).reshape(NL, 128, 2, 64) for b in range(4)], axis=1)
    nv_p = np.stack([R[4 + b]["nv"][SB].reshape(NL, 128, 2, 64) for b in range(4)], axis=1)

    def unst(a):
        return a.transpose(0, 2, 1).reshape(NL, 16, 64)
    re_p = np.stack([unst(R[4 + b]["nre"][SB]) for b in range(4)], axis=1)
    im_p = np.stack([unst(R[4 + b]["nim"][SB]) for b in range(4)], axis=1)
    cv_p = np.stack([R[4 + b]["ncv"][SB].transpose(0, 3, 2, 1).reshape(NL, 30, 256) for b in range(4)], axis=1)
    nk_s, nv_s, re_s, im_s, cv_s = [], [], [], [], []
    for c in range(8):
        r = R[c]
        S_ = SA if c < 4 else SB
        knew = r["skT"][S_].transpose(0, 2, 1).reshape(NL, NB, LS, 128)
        nk_s.append(np.concatenate([r["nks_c"][S_], knew], axis=2).reshape(NL, NB, 128, 2, 64))
        vnew = r["svn"][S_].transpose(0, 2, 1, 3)
        nv_s.append(np.concatenate([r["nvs_c"][S_], vnew], axis=2).reshape(NL, NB, 128, 2, 64))
        re_s.append(r["sre"][S_].transpose(0, 3, 2, 1).reshape(NL, NB, 16, 64))
        im_s.append(r["sim_o"][S_].transpose(0, 3, 2, 1).reshape(NL, NB, 16, 64))
        cv_s.append(r["scv"][S_].transpose(0, 3, 4, 2, 1).reshape(NL, NB, 30, 256))
    cat = lambda xs_: np.ascontiguousarray(np.concatenate(xs_, axis=1).astype(np.float32))
    outs = (y_prompt, y_sample, nk_p, nv_p, re_p, im_p, cv_p, cat(nk_s), cat(nv_s), cat(re_s), cat(im_s), cat(cv_s))
    return tuple(np.ascontiguousarray(o.astype(np.float32)) for o in outs)
```

```python
import math
import numpy as np
import concourse.bass as bass
import concourse.mybir as mybir
from concourse.bass_utils import run_bass_kernel_spmd

F32 = mybir.dt.float32
BF16 = mybir.dt.bfloat16
ALU = mybir.AluOpType
AF = mybir.ActivationFunctionType

SEG = 30000
NL = 4
NS = 5
HF = 332
D = 1024
KT = 8
NTOK = 2048
T = 256
NCH = NTOK // T
TS = 128
NB = 16
LS = 4
TSM = NB * LS
DFF = 2816
JT = 22
WIN = 2176
PAST = 8192
TWO_PI = 2.0 * math.pi


class Buf:
    __slots__ = ("name", "lw", "rd", "sem", "cnt", "excl")

    def __init__(self, name, excl=False):
        self.name = name
        self.excl = excl
        self.lw = None
        self.rd = {}
        self.sem = None
        self.cnt = 0


class Op:
    __slots__ = ("eng", "fn", "deps", "idx", "dma", "owner", "dcnt", "marked", "ev", "waits", "dinc")

    def __init__(self, eng, fn, idx):
        self.eng = eng
        self.fn = fn
        self.idx = idx
        self.deps = []
        self.dma = False
        self.owner = None
        self.dcnt = 0
        self.marked = False
        self.ev = None
        self.waits = []


class Prog:
    ENGS = ("pe", "act", "dve", "pool", "sp")

    def __init__(self, nc):
        self.nc = nc
        self.ops = []

    def add(self, eng, fn, reads=(), writes=(), dma_owner=None, extra_deps=(), dinc=16):
        i = len(self.ops)
        op = Op(eng, fn, i)
        if dma_owner is not None:
            op.dma = True
            op.owner = dma_owner
            op.dinc = dinc
            dma_owner.cnt += dinc
            op.dcnt = dma_owner.cnt
        deps = {}
        for b in reads:
            if b.lw is not None:
                deps[b.lw] = "raw"
            if b.excl:
                for r in b.rd.values():
                    if r not in deps:
                        deps[r] = "war"
        for b in writes:
            if b.lw is not None and b.lw not in deps:
                deps[b.lw] = "waw"
            for r in b.rd.values():
                if r not in deps:
                    deps[r] = "war"
        for d in extra_deps:
            deps[d.idx] = "raw"
        deps.pop(i, None)
        for b in reads:
            key = ("d", i) if op.dma else eng
            b.rd[key] = i
        for b in writes:
            b.lw = i
            b.rd = {}
        op.deps = list(deps.items())
        self.ops.append(op)
        return op

    def finalize(self):
        ops = self.ops
        waited = {e: {} for e in self.ENGS}
        for op in ops:
            need = {}
            for d, kind in op.deps:
                p = ops[d]
                if p.dma:
                    key = ("dma", id(p.owner))
                    if need.get(key, (0, None))[0] < p.dcnt:
                        need[key] = (p.dcnt, p)
                else:
                    if p.eng == op.eng and not op.dma:
                        if op.eng == "pe" or kind != "raw":
                            continue
                    key = ("eng", p.eng)
                    if need.get(key, (-1, None))[0] < p.idx:
                        need[key] = (p.idx, p)
            w = waited[op.eng]
            for key, (val, p) in need.items():
                if w.get(key, -1) >= val:
                    continue
                w[key] = val
                op.waits.append(p)
                if not p.dma:
                    p.marked = True
        cnt = {e: 0 for e in self.ENGS}
        for op in ops:
            if not op.dma and op.marked:
                cnt[op.eng] += 1
                op.ev = cnt[op.eng]
        self.evcount = cnt

    def emit(self):
        nc = self.nc
        self.finalize()
        esems = {}
        for e in self.ENGS:
            n = (self.evcount[e] + SEG - 1) // SEG
            esems[e] = [nc.alloc_semaphore(f"ev_{e}_{k}") for k in range(max(n, 1))]
        for op in self.ops:
            if op.dma and op.owner.sem is None:
                op.owner.sem = nc.alloc_semaphore("d_" + op.owner.name)

        def semval(p):
            if p.dma:
                return p.owner.sem, p.dcnt
            k = (p.ev - 1) // SEG
            return esems[p.eng][k], (p.ev - 1) % SEG + 1

        per = {e: [op for op in self.ops if op.eng == e] for e in self.ENGS}

        def run(eng, lst):
            for op in lst:
                for p in op.waits:
                    s, v = semval(p)
                    eng.wait_ge(s, v)
                ins = op.fn(eng)
                if op.dma:
                    ins.then_inc(op.owner.sem, op.dinc)
                elif op.marked:
                    s, _ = semval(op)
                    ins.then_inc(s, 1)

        with nc.Block() as block:
            @block.tensor
            def _(e):
                run(e, per["pe"])

            @block.scalar
            def _(e):
                run(e, per["act"])

            @block.vector
            def _(e):
                run(e, per["dve"])

            @block.gpsimd
            def _(e):
                run(e, per["pool"])

            @block.sync
            def _(e):
                run(e, per["sp"])


def build(nl=NS):
    import os
    KSUB = int(os.environ.get("KSUB", "9"))
    KS2 = int(os.environ.get("KS2", "9"))
    nc = bass.Bass("TRN2", target_bir_lowering=False)
    P = Prog(nc)
    stores = []

    def din(name, shape):
        return nc.dram_tensor(name, list(shape), F32, kind="ExternalInput").ap()

    def dout(name, shape):
        return nc.dram_tensor(name, list(shape), F32, kind="ExternalOutput").ap()

    def sb(name, shape, dt=F32):
        return nc.alloc_sbuf_tensor(name, list(shape), dt)

    xT = din("xT", [D, NTOK])
    xsT = din("xsT", [D, TSM])
    cT = din("cT", [D, 17])
    w_mod = din("w_mod", [NS, D, 6 * D])
    b_modT = din("b_modT", [NS, 128, 48])
    g1T = din("g1T", [NS, 128, KT])
    g2T = din("g2T", [NS, 128, KT])
    gfT = din("gfT", [128, KT])
    w_in2 = din("w_in2", [NS, D, WIN])
    ropeP = din("ropeP", [2, 128, NTOK])
    ropeS = din("ropeS", [2, 128, TSM])
    maskP = din("maskP", [2, 128, 512])
    maskS = din("maskS", [2, 128, 512])
    sinkT = din("sinkT", [NS, 64, 8])
    lamre = din("lamre", [NS, 128, 8])
    lamim = din("lamim", [NS, 128, 8])
    logdt = din("logdt", [NS, 128, 8])
    bre = din("bre", [NS, 128, 8, 16])
    bim = din("bim", [NS, 128, 8, 16])
    cre = din("cre", [NS, 128, 8, 16])
    cim = din("cim", [NS, 128, 8, 16])
    dskipT = din("dskipT", [NS, 128, 2])
    wglu = din("wglu", [NS, 256, 256])
    bgluT = din("bgluT", [NS, 128, 2])
    convwT = din("convwT", [NS, 128, 2, 31])
    convbT = din("convbT", [NS, 128, 2])
    lngT = din("lngT", [NS, 128, 2])
    lnbT = din("lnbT", [NS, 128, 2])
    w_out = din("w_out", [NS, D, D])
    w_gate = din("w_gate", [NS, D, DFF])
    w_up = din("w_up", [NS, D, DFF])
    w_down = din("w_down", [NS, DFF, D])
    kcT = din("kcT", [NS, NB, 128, 128])
    vc = din("vc", [NS, NB, 128, 128])
    kcn = din("kcn", [NS, NB, 128, 128])
    ssmre_in = din("ssmre_in", [NS, 128, 8, NB])
    ssmim_in = din("ssmim_in", [NS, 128, 8, NB])
    sconv_in = din("sconv_in", [NS, 128, 2, NB, 30])
    identd = din("identd", [128, 128])
    hprev = din("hprev", [128, 2])
    hin = nc.dram_tensor("hin", [128, HF], F32)
    hall = nc.dram_tensor("hall", [256, HF], F32)
    bhin = Buf("hin"); bhall = Buf("hall")
    jrow = din("jrow", [128, TS + 1])

    yT = dout("yT", [D, NTOK])
    ysT = dout("ysT", [D, TSM])
    nkT = dout("nkT", [NS, 128, 128])
    nv = dout("nv", [NS, 128, 128])
    nre = dout("nre", [NS, 128, 8])
    nim = dout("nim", [NS, 128, 8])
    ncv = dout("ncv", [NS, 128, 2, 30])
    nks_c = dout("nks_c", [NS, NB, 124, 128])
    nvs_c = dout("nvs_c", [NS, NB, 124, 128])
    skT = dout("skT", [NS, 128, TSM])
    svn = dout("svn", [NS, 4, NB, 128])
    sre = dout("sre", [NS, 128, 8, NB])
    sim_o = dout("sim_o", [NS, 128, 8, NB])
    scv = dout("scv", [NS, 128, 2, NB, 30])
    xscr = nc.dram_tensor("xscr", [D, NTOK], F32, kind="Internal").ap()
    wgu_c = nc.dram_tensor("wgu_c", [JT, 128, 2 * KT * 128], BF16, kind="Internal").ap()
    wdn_c = nc.dram_tensor("wdn_c", [16, 128, 11 * 128], BF16, kind="Internal").ap()
    woa_c = nc.dram_tensor("woa_c", [8, 64, 8 * 128], BF16, kind="Internal").ap()
    wor_c = nc.dram_tensor("wor_c", [8, 128, 4 * 128], BF16, kind="Internal").ap()
    bwgu_c = [Buf(f"wguc{j}") for j in range(JT)]
    bwdn_c = [Buf(f"wdnc{j}") for j in range(16)]
    bwo_c = [Buf(f"woc{j}") for j in range(8)]
    DBG = bool(int(os.environ.get("KDBG", "0")))
    if DBG:
        dbg = dout("dbg", [3, 128, 8, TSM])

    PS = [nc.alloc_psum_tensor(f"ps{i}", [128, 512], F32) for i in range(8)]
    bPS = [Buf(f"ps{i}", excl=True) for i in range(8)]
    psrr = [0]

    def nps():
        i = psrr[0]
        psrr[0] = (i + 1) % 6
        return i

    def mm(out, lhsT, rhs, start, stop, r, w):
        return P.add("pe", lambda e: e.matmul(out, lhsT=lhsT, rhs=rhs, start=start, stop=stop), reads=r, writes=w)

    def act(out, in_, func, r, w, bias=None, scale=None):
        kw = {}
        if bias is not None:
            kw["bias"] = bias
        if scale is not None:
            kw["scale"] = scale
        return P.add("act", lambda e: e.activation(out=out, in_=in_, func=func, **kw), reads=r, writes=w)

    def tt(eng, out, in0, in1, op, r, w):
        return P.add(eng, lambda e: e.tensor_tensor(out=out, in0=in0, in1=in1, op=op), reads=r, writes=w)

    def ts(eng, out, in0, s1, s2, op0, op1, r, w):
        if op1 is None:
            return P.add(eng, lambda e: e.tensor_scalar(out=out, in0=in0, scalar1=s1, scalar2=None, op0=op0), reads=r, writes=w)
        return P.add(eng, lambda e: e.tensor_scalar(out=out, in0=in0, scalar1=s1, scalar2=s2, op0=op0, op1=op1), reads=r, writes=w)

    def stt(eng, out, in0, scalar, in1, op0, op1, r, w):
        return P.add(eng, lambda e: e.scalar_tensor_tensor(out=out, in0=in0, scalar=scalar, in1=in1, op0=op0, op1=op1), reads=r, writes=w)

    def cp(eng, out, in_, r, w):
        if eng == "act":
            return P.add("act", lambda e: e.activation(out=out, in_=in_, func=AF.Copy), reads=r, writes=w)
        return P.add(eng, lambda e: e.tensor_copy(out=out, in_=in_), reads=r, writes=w)

    def memset(eng, ap, val, w):
        return P.add(eng, lambda e: e.memset(ap, val), writes=w)

    def ld(out, in_, owner, w, q="sp"):
        return P.add(q, lambda e: e.dma_start(out=out, in_=in_), writes=w, dma_owner=owner)

    def st(out, in_, owner, r):
        o = P.add("sp", lambda e: e.dma_start(out=out, in_=in_), reads=r, dma_owner=owner)
        stores.append(o)
        return o

    ident = sb("ident", [128, 128]); bident = Buf("ident")
    ld(ident[:], identd, bident, [bident])
    ones16 = sb("ones16", [128, 128], BF16); bones = Buf("ones")
    memset("dve", ones16[:], 1.0, [bones])
    jr = sb("jr", [128, TS + 1]); bjr = Buf("jr")
    ld(jr[:], jrow, bjr, [bjr])
    mk = sb("mk", [128, 4, 512], BF16); bmk = Buf("mk")
    mkn = mk[:, 0:2, :]; bmkn = bmk
    ld(mk[:, 0:2, :], maskP.rearrange("a p n -> p a n"), bmk, [bmk], q="pool")
    ld(mk[:, 2:4, :], maskS.rearrange("a p n -> p a n"), bmk, [bmk], q="pool")
    ident16 = sb("ident16", [128, 128], BF16); bident16 = Buf("ident16")
    cp("dve", ident16[:], ident[:], [bident], [bident16])
    rps = sb("rps", [128, 2, TSM]); brps = Buf("rps")
    ld(rps[:], ropeS.rearrange("a p n -> p a n"), brps, [brps])
    gf = sb("gf", [128, KT]); bgf = Buf("gf")
    ld(gf[:], gfT, bgf, [bgf])
    pat = sb("pat", [128, NB, LS]); bpat = Buf("pat")
    memset("dve", pat[:], 1.0, [bpat])
    memset("dve", pat[:, :, 0:1], 0.0, [bpat])

    modT1 = sb("modT1", [128, 48, 17]); bmod = Buf("modT")
    modD = nc.dram_tensor("modD", [NS, 128, 48 * 17], F32).ap()
    bmodD = [Buf(f"modD{i}") for i in range(NS)]
    csb = sb("csb", [128, KT, 17]); bcs = Buf("csb")
    sgc = sb("sgc", [128, KT, 17]); bsgc = Buf("sgc")
    ld(csb[:], cT.rearrange("(k p) n -> p k n", p=128), bcs, [bcs])
    act(sgc[:], csb[:], AF.Sigmoid, [bcs], [bsgc])
    tt("dve", csb[:], csb[:], sgc[:], ALU.mult, [bcs, bsgc], [bcs])
    bmt = sb("bmt", [128, NS, 48]); bbmt = Buf("bmt")
    ld(bmt[:], b_modT.rearrange("l p m -> p l m"), bbmt, [bbmt])
    WMB = 256
    _g0 = nc.sbuf_tensor("wmr0", [128, KT, WMB], F32)
    _g1 = nc.sbuf_tensor("wmr1", [128, KT, WMB], F32)
    wmr = [_g0.__enter__(), _g1.__enter__()]
    bwmr = [Buf(f"wmr{i}") for i in range(2)]
    lastmod = None
    it = 0
    for l in range(nl):
        for blk in range(6 * D // WMB):
            s = it % 2
            it += 1
            ld(wmr[s][:], w_mod[l, :, blk * WMB:(blk + 1) * WMB].rearrange("(k p) n -> p k n", p=128), bwmr[s], [bwmr[s]])
            for mi in range(WMB // 128):
                m = blk * (WMB // 128) + mi
                pi = nps()
                for k in range(KT):
                    mm(PS[pi][:, 0:17], wmr[s][:, k, mi * 128:(mi + 1) * 128], csb[:, k, :], k == 0, k == KT - 1,
                       [bwmr[s], bcs], [bPS[pi]])
                lastmod = ts("dve", modT1[:, m, :], PS[pi][:, 0:17], bmt[:, l, m:m + 1], None, ALU.add, None, [bPS[pi], bbmt], [bmod])
        lastmod = P.add("sp", lambda e, l=l: e.dma_start(out=modD[l], in_=modT1[:].rearrange("p m c -> p (m c)")),
                        reads=[bmod], writes=[bmodD[l]], dma_owner=bmod)
    _g1.__exit__(None, None, None)
    _g0.__exit__(None, None, None)
    for _e in ("pe", "act", "pool", "sp"):
        P.add(_e, lambda e: e.nop(), extra_deps=[lastmod])

    xs = sb("xs", [128, KT, T]); bx = [Buf(f"x{k}") for k in range(KT)]
    xsm = sb("xsm", [128, KT, TSM]); bxsm = Buf("xsm")
    ld(xsm[:], xsT.rearrange("(k p) n -> p k n", p=128), bxsm, [bxsm])
    h16 = sb("h16", [128, KT, T], BF16); bh = Buf("h16")
    scr16 = sb("scr16", [128, JT, T], BF16); bscr = Buf("scr16")
    rstd = sb("rstd", [128, T]); brstd = Buf("rstd")
    tmpAB = sb("tmpAB", [128, 2, T])
    tmpA = tmpAB[:, 0, :]; btA = Buf("tmpA")
    tmpB = tmpAB[:, 1, :]; btB = Buf("tmpB")
    tmpC = sb("tmpC", [128, T]); btC = Buf("tmpC")
    tmpD = sb("tmpD", [128, T]); btD = Buf("tmpD")
    rp = sb("rp", [128, 2, T]); brp = Buf("rp")
    q16 = sb("q16", [128, 4, T], BF16); bq = Buf("q16")
    kb16 = sb("kb16", [128, 128 + T], BF16); bkb = Buf("kb16")
    k32 = sb("k32", [128, T]); bk32 = Buf("k32")
    v16 = sb("v16", [128, 1 + T // 128, 128], BF16); bv16 = Buf("v16")
    v32 = sb("v32", [128, 128]); bv32 = Buf("v32")
    pown = sb("pown", [128, 512], BF16); bpown = Buf("pown")
    pprev = sb("pprev", [128, 512], BF16); bpprev = Buf("pprev")
    den = tmpAB[0:64].rearrange("p a t -> p (a t)")
    att16 = sb("att16", [64, 8, T], BF16); batt = Buf("att16")
    u32 = sb("u32", [128, 2, T]); bu32 = Buf("u32")
    u16 = sb("u16", [128, 2, T], BF16); bu16 = Buf("u16")
    cb32 = sb("cb32", [128, 2, 30 + T]); bcb32 = Buf("cb32")
    cb16 = sb("cb16", [128, 2, 30 + T], BF16); bcb16 = Buf("cb16")
    cbs32 = sb("cbs32", [128, 2, NB, 34]); bcbs32 = Buf("cbs32")
    cbs16 = sb("cbs16", [128, 2, NB, 34], BF16); bcbs16 = Buf("cbs16")
    ycf = sb("ycf", [128, 2, T]); bycf = Buf("ycf")
    cva = sb("cva", [128, T]); bcva = Buf("cva")
    cvb = sb("cvb", [128, T]); bcvb = Buf("cvb")
    cvc = sb("cvc", [128, 2, T]); bcvc = Buf("cvc")
    yc16 = scr16[:, 8:12, :]; byc16 = bscr
    oc16 = sb("oc16", [128, 2, T], BF16); boc = Buf("oc16")
    os16 = sb("os16", [128, 2, T], BF16); bos = Buf("os16")
    yss = sb("yss", [128, 2, T]); byss = Buf("yss")
    z32 = sb("z32", [128, 2, T]); bz32 = Buf("z32")
    assert 2 * T == 4 * 8 * 16
    z16 = sb("z16", [128, 2, T], BF16); bz16 = Buf("z16")
    sx = [sb(f"sx{i}", [128, TS]) for i in range(8)]
    bsx = [Buf(f"sx{i}") for i in range(8)]
    sq = [[sb(f"sq{i}{j}", [128, TS]) for j in range(2)] for i in range(2)]
    bsq = [[Buf(f"sq{i}{j}") for j in range(2)] for i in range(2)]
    sh16 = [[sb(f"sh16{i}{j}", [128, TS], BF16) for j in range(2)] for i in range(2)]
    bsh16 = [[Buf(f"sh16{i}{j}") for j in range(2)] for i in range(2)]
    ssi = [0]
    hre16 = sh16[0][0]; bhre = bsh16[0][0]
    him16 = sh16[0][1]; bhim = bsh16[0][1]
    qlre = sb("qlre", [128, 8]); qlim = sb("qlim", [128, 8]); bql = Buf("ql")
    hlre = sb("hlre", [128, 8]); hlim = sb("hlim", [128, 8]); bhl = Buf("hl")
    inre = sb("inre", [128, 8]); inim = sb("inim", [128, 8]); binit = Buf("init")
    sm8 = [sb(f"sm8_{i}", [128, 8]) for i in range(4)]; bsm8 = Buf("sm8")
    h0re = sb("h0re", [128, 8, NB]); h0im = sb("h0im", [128, 8, NB]); bh0 = Buf("h0")
    ahre = sb("ahre", [128, 8, NB]); ahim = sb("ahim", [128, 8, NB]); bah = Buf("ah")
    hsre = sb("hsre", [128, 8, NB]); hsim = sb("hsim", [128, 8, NB]); bhs = Buf("hs")
    st16 = [sb(f"st16_{i}", [128, NB]) for i in range(4)]; bst16 = Buf("st16")
    r_kc = sb("r_kc", [128, 2048], BF16); bkc = Buf("kc16")
    r_vc = sb("r_vc", [128, 2048], BF16); bvc = Buf("vc16")
    kc16 = r_kc[:].rearrange("p (b k) -> p b k", k=128)
    vc16 = r_vc[:].rearrange("p (b k) -> p b k", k=128)
    vn16 = sb("vn16", [4, NB, 128], BF16); bvn16 = Buf("vn16")
    vn32 = sb("vn32", [4, NB, 128]); bvn32 = Buf("vn32")
    pn16 = sb("pn16", [4, 512], BF16); bpn = Buf("pn16")

    win16 = sb("win16", [128, KT, WIN], BF16); bwin = Buf("win16")
    wgl16 = sb("wgl16", [128, 2, 256], BF16); bwgl = Buf("wgl16")
    sp8 = sb("sp8", [128, 12, 8]); bsp8 = Buf("sp8")
    sp2 = sb("sp2", [128, 8, 2]); bsp2 = Buf("sp2")
    cw = sb("cw", [128, 2, 31]); bcw = Buf("cw")
    sk = sb("sk", [64, 8]); bsk = Buf("sk")
    bc32 = yss[:].rearrange("p a (b c) -> p (a b) c", c=16).rearrange("p (a b) c -> p a b c", a=4); bbc = byss
    bbt = z32[:].rearrange("p a (b c) -> p (a b) c", c=16).rearrange("p (a b) c -> p a b c", a=4); bbbt = bz32
    exq = sb("exq", [128, 128]); bexq = Buf("exq")
    LB = sb("LB", [128, 4, 8, 128], BF16); bLB = Buf("LB")
    cosT = sb("cosT", [128, 8, TS + 1]); sinT = sb("sinT", [128, 8, TS + 1]); btab = Buf("tab")
    r4 = sb("r4", [128, 8, NB, LS]); btab4 = Buf("tab4")
    diag16 = sb("diag16", [128, 2, 31, 128], BF16); bdiag = Buf("diag16")
    gsc = sb("gsc", [128, 2, KT, 17]); bgsc = Buf("gsc")
    NG = 7
    _raw = [sb("wr0", [128, 2048], BF16), sb("wr1", [128, 2048], BF16), r_kc, r_vc, sb("wr4", [128, 2048], BF16),
            sb("wr5", [128, 2048], BF16), sb("wr6", [128, 2048], BF16)]
    bwgu = [Buf("wr0"), Buf("wr1"), bkc, bvc, Buf("wr4"), Buf("wr5"), Buf("wr6")]
    wfl = [r[:] for r in _raw]
    wgu = [r[:].rearrange("p (a k c) -> p a k c", a=2, k=KT) for r in _raw]
    wdn = [r[:, 0:1408].rearrange("p (j c) -> p j c", c=128) for r in _raw]
    bwdn = bwgu
    ffi = [0, 0]

    def S8(i):
        return sp8[:, i, :]

    angt = tmpC[:, 0:TS + 1]; angk = tmpD[:, 0:TS + 1]; bang = btC
    CM = 12582912.0

    def sin_of(out, x, shift, tmp, r, w, wt):
        xs_ = x
        if shift != 0.0:
            ts("dve", out, x, shift, None, ALU.add, None, r, w)
            xs_ = out
        ts("dve", tmp, xs_, 1.0 / TWO_PI, CM, ALU.mult, ALU.add, r + w, wt)
        ts("dve", tmp, tmp, -CM, None, ALU.add, None, r + wt, wt)
        stt("dve", tmp, tmp, -TWO_PI, xs_, ALU.mult, ALU.add, r + w + wt, wt)
        ts("dve", tmp, tmp, math.pi, -math.pi, ALU.min, ALU.max, r + wt, wt)
        act(out, tmp, AF.Sin, r + wt, w)

    def layer_params(l):
        for k in range(KT):
            ld(win16[:, k, :], w_in2[l, k * 128:(k + 1) * 128, :], bwin, [bwin], q="pool")
        ld(wgl16[:], wglu[l].rearrange("(k p) n -> p k n", p=128), bwgl, [bwgl], q="pool")
        ld(sp8[:, 0, :], lamre[l], bsp8, [bsp8])
        ld(sp8[:, 1, :], lamim[l], bsp8, [bsp8])
        ld(sp8[:, 2, :], logdt[l], bsp8, [bsp8])
        ld(sp2[:, 0, :], dskipT[l], bsp2, [bsp2])
        ld(sp2[:, 1, :], bgluT[l], bsp2, [bsp2])
        ld(sp2[:, 2, :], convbT[l], bsp2, [bsp2])
        ld(sp2[:, 3, :], lngT[l], bsp2, [bsp2])
        ld(sp2[:, 4, :], lnbT[l], bsp2, [bsp2])
        ld(cw[:], convwT[l], bcw, [bcw])
        ld(sk[:], sinkT[l], bsk, [bsk])
        act(sk[:], sk[:], AF.Exp, [bsk], [bsk])
        ld(bc32[:, 0], bre[l], bbc, [bbc])
        ld(bc32[:, 1], bim[l], bbc, [bbc])
        ld(bc32[:, 2], cre[l], bbc, [bbc])
        ld(bc32[:, 3], cim[l], bbc, [bbc])
        P.add("sp", lambda e, l=l: e.dma_start(out=modT1[:].rearrange("p m c -> p (m c)"), in_=modD[l]),
              reads=[bmodD[l]], writes=[bmod], dma_owner=bmod)
        for a, (gT, off) in enumerate(((g1T, 8), (g2T, 32))):
            ld(sp8[:, 3, :], gT[l], bsp8, [bsp8])
            ts("dve", gsc[:, a], modT1[:, off:off + 8, :], 1.0, None, ALU.add, None, [bmod], [bgsc])
            tt("dve", gsc[:, a], gsc[:, a], sp8[:, 3, :].unsqueeze(2).to_broadcast([128, KT, 17]), ALU.mult, [bgsc, bsp8], [bgsc])
        R = [bsp8]
        W = [bsp8]
        act(S8(2), S8(2), AF.Exp, R, W)
        tt("dve", S8(3), S8(0), S8(2), ALU.mult, R, W)
        tt("dve", S8(4), S8(1), S8(2), ALU.mult, R, W)
        act(S8(5), S8(3), AF.Exp, R, W)
        sin_of(S8(7), S8(4), 0.0, sm8[0][:], R + [bsm8], W, [bsm8])
        sin_of(S8(6), S8(4), 0.5 * math.pi, sm8[0][:], R + [bsm8], W, [bsm8])
        tt("dve", S8(8), S8(5), S8(6), ALU.mult, R, W)
        tt("dve", S8(9), S8(5), S8(7), ALU.mult, R, W)
        a0, a1, a2, a3 = (sm8[i][:] for i in range(4))
        R2 = [bsp8, bsm8]
        tt("dve", a0, S8(0), S8(0), ALU.mult, R2, [bsm8])
        tt("dve", a1, S8(1), S8(1), ALU.mult, R2, [bsm8])
        tt("dve", a0, a0, a1, ALU.add, R2, [bsm8])
        P.add("dve", lambda e: e.reciprocal(out=a0, in_=a0), reads=R2, writes=[bsm8])
        ts("dve", a1, S8(8), -1.0, None, ALU.add, None, R2, [bsm8])
        tt("dve", a2, a1, S8(0), ALU.mult, R2, [bsm8])
        tt("dve", a3, S8(9), S8(1), ALU.mult, R2, [bsm8])
        tt("dve", a2, a2, a3, ALU.add, R2, [bsm8])
        tt("dve", S8(10), a2, a0, ALU.mult, R2, W)
        tt("dve", a2, S8(9), S8(0), ALU.mult, R2, [bsm8])
        tt("dve", a3, a1, S8(1), ALU.mult, R2, [bsm8])
        tt("dve", a2, a2, a3, ALU.subtract, R2, [bsm8])
        tt("dve", S8(11), a2, a0, ALU.mult, R2, W)
        cre_b = sp8[:, 10, :].unsqueeze(2).to_broadcast([128, 8, 16])
        cim_b = sp8[:, 11, :].unsqueeze(2).to_broadcast([128, 8, 16])
        Rb = [bbc, bsp8, bbbt]
        tt("dve", bbt[:, 0], bc32[:, 0], cre_b, ALU.mult, Rb, [bbbt])
        tt("dve", bbt[:, 2], bc32[:, 1], cim_b, ALU.mult, Rb, [bbbt])
        tt("dve", bbt[:, 0], bbt[:, 0], bbt[:, 2], ALU.subtract, Rb, [bbbt])
        tt("dve", bbt[:, 1], bc32[:, 1], cre_b, ALU.mult, Rb, [bbbt])
        tt("dve", bbt[:, 2], bc32[:, 0], cim_b, ALU.mult, Rb, [bbbt])
        tt("dve", bbt[:, 1], bbt[:, 1], bbt[:, 2], ALU.add, Rb, [bbbt])
        cp("dve", bbt[:, 2], bc32[:, 2], Rb, [bbbt])
        ts("dve", bbt[:, 3], bc32[:, 3], -1.0, None, ALU.mult, None, Rb, [bbbt])
        xsf = xs[:].rearrange("p k t -> p (k t)")
        Eb = [xsf[:, 0:1024].rearrange("p (c n) -> p c n", n=128), xsf[:, 1024:2048].rearrange("p (c n) -> p c n", n=128)]
        bEb = [bx[0:4], bx[4:8]]
        for mi in range(4):
            E = Eb[mi % 2]
            bE = bEb[mi % 2]
            memset("dve", E, 0.0, bE)
            for ct in range(8):
                for gg in range(2):
                    gp = (2 * ct + gg) % 8
                    cp("dve", E[64 * gg:64 * gg + 64, ct, 16 * gp:16 * gp + 16], bbt[64 * gg:64 * gg + 64, mi, ct, :], [bbbt], bE)
            if mi < 2:
                for ct in range(8):
                    pi = nps()
                    P.add("pe", lambda e, pi=pi, E=E, ct=ct: e.transpose(out=PS[pi][:, 0:128], in_=E[:, ct, :], identity=ident[:]),
                          reads=bE + [bident], writes=[bPS[pi]])
                    cp("act", LB[:, mi, ct, :], PS[pi][:, 0:128], [bPS[pi]], [bLB])
            else:
                cp("act", LB[:, mi], E, bE, [bLB])
        angt3 = xsf[:, 0:8 * (TS + 1)].rearrange("p (c j) -> p c j", j=TS + 1)
        angk3 = scr16[:, 0:9, :].rearrange("p a b -> p (a b)").bitcast(F32)[:, 0:8 * (TS + 1)].rearrange("p (c j) -> p c j", j=TS + 1)
        tt("dve", angt3, jr[:].unsqueeze(1).to_broadcast([128, 8, TS + 1]), sp8[:, 4, :].unsqueeze(2).to_broadcast([128, 8, TS + 1]),
           ALU.mult, [bjr, bsp8], bx)
        sin_of(sinT[:], angt3, 0.0, angk3, bx, [btab], [bscr])
        sin_of(cosT[:], angt3, 0.5 * math.pi, angk3, bx, [btab], [bscr])
        for ct in range(8):
            ts("dve", r4[:, ct], pat[:], sp8[:, 5, ct:ct + 1], None, ALU.mult, None, [bpat, bsp8], [btab4])
        for m in range(2):
            for k in range(31):
                act(diag16[:, m, k, :], ident[:], AF.Identity, [bident, bcw], [bdiag], scale=cw[:, m, k:k + 1])

    def rmsnorm(xa, bxs, n, gcol, shm, a, l, sample):
        for k in range(KT):
            act(scr16[:, k, 0:n], xa[:, k, :], AF.Square, [bxs[k]], [bscr])
        pi = nps()
        for k in range(KT):
            mm(PS[pi][:, 0:n], ones16[:], scr16[:, k, 0:n], k == 0, k == KT - 1, [bones, bscr], [bPS[pi]])
        ts("dve", rstd[:, 0:n], PS[pi][:, 0:n], 1.0 / D, 1e-6, ALU.mult, ALU.add, [bPS[pi]], [brstd])
        act(rstd[:, 0:n], rstd[:, 0:n], AF.Sqrt, [brstd], [brstd])
        P.add("dve", lambda e: e.reciprocal(out=rstd[:, 0:n], in_=rstd[:, 0:n]), reads=[brstd], writes=[brstd])
        for k in range(KT):
            tA = tmpA if k % 2 == 0 else tmpB
            bA = btA if k % 2 == 0 else btB
            tt("dve", tA[:, 0:n], xa[:, k, :], rstd[:, 0:n], ALU.mult, [bxs[k], brstd], [bA])
            if gcol is None:
                ts("pool", xa[:, k, :], tA[:, 0:n], gf[:, k:k + 1], None, ALU.mult, None, [bA, bgf], [bxs[k]])
            elif not sample:
                if False:
                    pass
                else:
                    act(h16[:, k, 0:n], tA[:, 0:n], AF.Identity, [bA, bgsc, bmod], [bh],
                        bias=modT1[:, shm + k, 0:1], scale=gsc[:, a, k, 0:1])
            else:
                v3 = tA[:, 0:n].rearrange("p (b t) -> p b t", t=LS)
                if True:
                    tt("pool", v3, v3, gsc[:, a, k, 1:17].unsqueeze(2).to_broadcast([128, NB, LS]), ALU.mult, [bA, bgsc], [bA])
                    tt("pool", h16[:, k, 0:n].rearrange("p (b t) -> p b t", t=LS), v3,
                       modT1[:, shm + k, 1:17].unsqueeze(2).to_broadcast([128, NB, LS]), ALU.add, [bA, bmod], [bh])

    def resid(xa, bxs, n, mo, pi, l, gm, sample):
        if not sample:
            stt("dve", xa[:, mo, :], PS[pi][:, 0:n], modT1[:, gm + mo, 0:1], xa[:, mo, :], ALU.mult, ALU.add,
                [bPS[pi], bmod, bxs[mo]], [bxs[mo]])
        else:
            tt("dve", tmpC[:, 0:n].rearrange("p (b t) -> p b t", t=LS), PS[pi][:, 0:n].rearrange("p (b t) -> p b t", t=LS),
               modT1[:, gm + mo, 1:17].unsqueeze(2).to_broadcast([128, NB, LS]), ALU.mult, [bPS[pi], bmod], [btC])
            tt("dve", xa[:, mo, :], xa[:, mo, :], tmpC[:, 0:n], ALU.add, [btC, bxs[mo]], [bxs[mo]])

    def inproj_tile(m, n):
        pi = nps()
        for k in range(KT):
            mm(PS[pi][:, 0:n], win16[:, k, m * 128:(m + 1) * 128], h16[:, k, 0:n], k == 0, k == KT - 1, [bwin, bh], [bPS[pi]])
        return pi

    def rope_pair(m_a, m_b, cos_ap, sin_ap, brope, out_ap, bout, n, extra32=None):
        pa = inproj_tile(m_a, n)
        pb = inproj_tile(m_b, n)
        tt("dve", tmpA[:, 0:n], PS[pa][:, 0:n], cos_ap, ALU.mult, [bPS[pa], brope], [btA])
        tt("dve", tmpB[:, 0:n], PS[pb][:, 0:n], sin_ap, ALU.mult, [bPS[pb], brope], [btB])
        tt("pool", out_ap, tmpA[:, 0:n], tmpB[:, 0:n], ALU.add, [btA, btB], [bout])
        if extra32 is not None:
            tt("pool", extra32[0], tmpA[:, 0:n], tmpB[:, 0:n], ALU.add, [btA, btB], [extra32[1]])

    def gelu_glu(n):
        tmps = [(tmpC, btC), (tmpD, btD)]
        for step in range(3):
            for o in range(2):
                tq, btq = tmps[o]
                if step == 0:
                    tt("pool", tq[:, 0:n], yss[:, o, 0:n], yss[:, o, 0:n], ALU.mult, [byss], [btq])
                elif step == 1:
                    ts("pool", tq[:, 0:n], tq[:, 0:n], 0.044715, 1.0, ALU.mult, ALU.add, [btq], [btq])
                else:
                    tt("pool", tq[:, 0:n], tq[:, 0:n], yss[:, o, 0:n], ALU.mult, [btq, byss], [btq])
        for o in range(2):
            tq, btq = tmps[o]
            act(tq[:, 0:n], tq[:, 0:n], AF.Sigmoid, [btq], [btq], scale=2.0 * math.sqrt(2.0 / math.pi))
        for o in range(2):
            tq, btq = tmps[o]
            tt("dve", z32[:, o, 0:n], yss[:, o, 0:n], tq[:, 0:n], ALU.mult, [byss, btq], [bz32])
            cp("pool", z16[:, o, 0:n], z32[:, o, 0:n], [bz32], [bz16])
        pis = []
        for o in range(2):
            pi = nps()
            pis.append(pi)
            for k in range(2):
                mm(PS[pi][:, 0:n], wgl16[:, k, o * 128:(o + 1) * 128], z16[:, k, 0:n], k == 0, k == 1, [bwgl, bz16], [bPS[pi]])
        for o in range(2):
            tq, btq = tmps[o]
            act(tq[:, 0:n], PS[pis[o]][:, 0:n], AF.Sigmoid, [bPS[pis[o]], bsp2], [btq], bias=sp2[:, 1, o:o + 1])
        for o in range(2):
            tq, btq = tmps[o]
            tt("dve", os16[:, o, 0:n], z32[:, o, 0:n], tq[:, 0:n], ALU.mult, [bz32, btq], [bos])

    def conv_ln_stages(rhs_fn, n, view):
        pcs = {}

        def st_mm(m):
            def f():
                pi = nps()
                pcs[m] = pi
                for k in range(31):
                    mm(view(PS[pi][:, 0:n]), diag16[:, m, k, :], rhs_fn(m, k), k == 0, k == 30, [bdiag, bcb16, bcbs16], [bPS[pi]])
            return f

        def st_evac(m):
            def f():
                pi = pcs[m]
                act(ycf[:, m, 0:n], PS[pi][:, 0:n], AF.Identity, [bPS[pi], bsp2], [bycf], bias=sp2[:, 2, m:m + 1])
                act(yc16[:, 2 + m, 0:n], PS[pi][:, 0:n], AF.Square, [bPS[pi], bsp2], [byc16], bias=sp2[:, 2, m:m + 1])
            return f

        def st_cast(m):
            def f():
                cp("dve", yc16[:, m, 0:n], ycf[:, m, 0:n], [bycf], [byc16])
            return f

        def st_statmm():
            p1 = nps()
            for m in range(2):
                mm(PS[p1][:, 0:n], ones16[:], yc16[:, m, 0:n], m == 0, m == 1, [bones, byc16], [bPS[p1]])
            p2 = nps()
            for m in range(2):
                mm(PS[p2][:, 0:n], ones16[:], yc16[:, 2 + m, 0:n], m == 0, m == 1, [bones, byc16], [bPS[p2]])
            pcs["p1"] = p1
            pcs["p2"] = p2

        def st_stat1():
            p1, p2 = pcs["p1"], pcs["p2"]
            ts("dve", cva[:, 0:n], PS[p1][:, 0:n], 1.0 / 256, None, ALU.mult, None, [bPS[p1]], [bcva])
            tt("dve", cvb[:, 0:n], cva[:, 0:n], cva[:, 0:n], ALU.mult, [bcva], [bcvb])
            stt("dve", cvb[:, 0:n], PS[p2][:, 0:n], 1.0 / 256, cvb[:, 0:n], ALU.mult, ALU.subtract, [bPS[p2], bcvb], [bcvb])
            ts("dve", cvb[:, 0:n], cvb[:, 0:n], 1e-6, None, ALU.add, None, [bcvb], [bcvb])
            act(cvb[:, 0:n], cvb[:, 0:n], AF.Sqrt, [bcvb], [bcvb])

        def st_stat2():
            P.add("dve", lambda e: e.reciprocal(out=cvb[:, 0:n], in_=cvb[:, 0:n]), reads=[bcvb], writes=[bcvb])

        def st_apply1(m):
            def f():
                tt("dve", ycf[:, m, 0:n], ycf[:, m, 0:n], cva[:, 0:n], ALU.subtract, [bycf, bcva], [bycf])
                tt("dve", ycf[:, m, 0:n], ycf[:, m, 0:n], cvb[:, 0:n], ALU.mult, [bycf, bcvb], [bycf])
                act(ycf[:, m, 0:n], ycf[:, m, 0:n], AF.Identity, [bycf, bsp2], [bycf], bias=sp2[:, 4, m:m + 1], scale=sp2[:, 3, m:m + 1])
                act(cvc[:, m, 0:n], ycf[:, m, 0:n], AF.Sigmoid, [bycf], [bcvc])
            return f

        def st_apply2(m):
            def f():
                tt("dve", oc16[:, m, 0:n], ycf[:, m, 0:n], cvc[:, m, 0:n], ALU.mult, [bcvc, bycf], [boc])
            return f
        return [st_mm(0), st_mm(1), st_evac(0), st_evac(1), st_cast(0), st_cast(1), st_statmm, st_stat1, st_stat2,
                st_apply1(0), st_apply1(1), st_apply2(0), st_apply2(1)]

    def conv_ln(rhs_fn, n, view):
        for f in conv_ln_stages(rhs_fn, n, view):
            f()

    def outproj_ffn(xa, bxs, n, l, sample, first=False):
        for mo in range(8):
            s = ffi[0] % NG
            ffi[0] += 1
            wa_v = wfl[s][0:64, 0:1024].rearrange("p (h c) -> p h c", c=128)
            wr_v = wfl[s][:, 1024:1536].rearrange("p (h c) -> p h c", c=128)
            fa = wfl[s][0:64, 0:1024]
            fr = wfl[s][:, 1024:1536]
            bw = bwgu[s]
            if first:
                ld(wa_v, w_out[l, 0:512, mo * 128:(mo + 1) * 128].rearrange("(h d) n -> d h n", d=64), bw, [bw], q="pool")
                ld(wr_v, w_out[l, 512:1024, mo * 128:(mo + 1) * 128].rearrange("(j p) n -> p j n", p=128), bw, [bw], q="pool")
                P.add("sp", lambda e, fa=fa, mo=mo: e.dma_start(out=woa_c[mo], in_=fa), reads=[bw], writes=[bwo_c[mo]], dma_owner=bw)
                P.add("sp", lambda e, fr=fr, mo=mo: e.dma_start(out=wor_c[mo], in_=fr), reads=[bw], writes=[bwo_c[mo]], dma_owner=bw)
            else:
                P.add("sp", lambda e, fa=fa, mo=mo: e.dma_start(out=fa, in_=woa_c[mo]), reads=[bwo_c[mo]], writes=[bw], dma_owner=bw)
                P.add("sp", lambda e, fr=fr, mo=mo: e.dma_start(out=fr, in_=wor_c[mo]), reads=[bwo_c[mo]], writes=[bw], dma_owner=bw)
            pi = nps()
            for hq in range(8):
                mm(PS[pi][:, 0:n], wa_v[:, hq, :], att16[:, hq, 0:n], hq == 0, False, [bw, batt], [bPS[pi]])
            for j in range(2):
                mm(PS[pi][:, 0:n], wr_v[:, 2 + j, :], oc16[:, j, 0:n], False, False, [bw, boc], [bPS[pi]])
            for j in range(2):
                mm(PS[pi][:, 0:n], wr_v[:, j, :], os16[:, j, 0:n], False, j == 1, [bw, bos], [bPS[pi]])
            resid(xa, bxs, n, mo, pi, l, 16, sample)
        rmsnorm(xa, bxs, n, 1, 24, 1, l, sample)
        for j in range(JT):
            s = ffi[0] % NG
            ffi[0] += 1
            fg = wfl[s]
            if first:
                ld(wgu[s][:, 0], w_gate[l, :, j * 128:(j + 1) * 128].rearrange("(k p) n -> p k n", p=128), bwgu[s], [bwgu[s]], q="pool")
                ld(wgu[s][:, 1], w_up[l, :, j * 128:(j + 1) * 128].rearrange("(k p) n -> p k n", p=128), bwgu[s], [bwgu[s]], q="pool")
                P.add("sp", lambda e, fg=fg, j=j: e.dma_start(out=wgu_c[j], in_=fg), reads=[bwgu[s]], writes=[bwgu_c[j]], dma_owner=bwgu[s])
            else:
                P.add("sp", lambda e, fg=fg, j=j: e.dma_start(out=fg, in_=wgu_c[j]), reads=[bwgu_c[j]], writes=[bwgu[s]], dma_owner=bwgu[s])
            pg = nps()
            for k in range(KT):
                mm(PS[pg][:, 0:n], wgu[s][:, 0, k, :], h16[:, k, 0:n], k == 0, k == KT - 1, [bwgu[s], bh], [bPS[pg]])
            pu = nps()
            for k in range(KT):
                mm(PS[pu][:, 0:n], wgu[s][:, 1, k, :], h16[:, k, 0:n], k == 0, k == KT - 1, [bwgu[s], bh], [bPS[pu]])
            tA = tmpA if j % 2 == 0 else tmpB
            bA = btA if j % 2 == 0 else btB
            act(tA[:, 0:n], PS[pg][:, 0:n], AF.Silu, [bPS[pg]], [bA])
            tt("dve", scr16[:, j, 0:n], tA[:, 0:n], PS[pu][:, 0:n], ALU.mult, [bA, bPS[pu]], [bscr])
        for mo in range(8):
            pi = nps()
            for jh in range(2):
                s = ffi[0] % NG
                ffi[0] += 1
                ci = mo * 2 + jh
                fd = wfl[s][:, 0:1408]
                if first:
                    ld(wdn[s], w_down[l, jh * 1408:(jh + 1) * 1408, mo * 128:(mo + 1) * 128].rearrange("(j p) n -> p j n", p=128), bwdn[s], [bwdn[s]], q="pool")
                    P.add("sp", lambda e, fd=fd, ci=ci: e.dma_start(out=wdn_c[ci], in_=fd), reads=[bwdn[s]], writes=[bwdn_c[ci]], dma_owner=bwdn[s])
                else:
                    P.add("sp", lambda e, fd=fd, ci=ci: e.dma_start(out=fd, in_=wdn_c[ci]), reads=[bwdn_c[ci]], writes=[bwdn[s]], dma_owner=bwdn[s])
                for jj in range(11):
                    j = jh * 11 + jj
                    mm(PS[pi][:, 0:n], wdn[s][:, jj, :], scr16[:, j, 0:n], j == 0, j == JT - 1, [bwdn[s], bscr], [bPS[pi]])
            resid(xa, bxs, n, mo, pi, l, 40, sample)

    hp = sb("hp", [128, 2]); bhp = Buf("hp")
    ld(hp[:], hprev, bhp, [bhp])
    hst = ycf[:].rearrange("p a t -> p (a t)")[:, 0:HF]
    GROUPS = [[0, 4], [1, 5], [2, 6], [3, 7]]

    def slot_begin(l):
        if l == 0:
            memset("dve", kb16[:, 0:128], 0.0, [bkb])
            memset("dve", v16[:, 0, :], 0.0, [bv16])
            memset("dve", cb32[:, :, 0:30], 0.0, [bcb32])
            memset("dve", inre[:], 0.0, [binit])
            memset("dve", inim[:], 0.0, [binit])
            return
        P.add("sp", lambda e: e.dma_start(out=hst, in_=hall.ap()[0:128, :]), reads=[bhall], writes=[bycf], dma_owner=bycf)
        ts("dve", hst, hst, hp[:, 0:1], None, ALU.mult, None, [bycf, bhp], [bycf])
        cp("dve", kb16[:, 0:128], hst[:, 0:128], [bycf], [bkb])
        cp("dve", v16[:, 0, :], hst[:, 128:256], [bycf], [bv16])
        cp("dve", cb32[:, :, 0:30], hst[:, 256:316].rearrange("p (a r) -> p a r", a=2), [bycf], [bcb32])
        cp("dve", inre[:], hst[:, 316:324], [bycf], [binit])
        cp("dve", inim[:], hst[:, 324:332], [bycf], [binit])

    def slot_end(l):
        cp("dve", hst[:, 0:128], kb16[:, 0:128], [bkb], [bycf])
        cp("dve", hst[:, 128:256], v16[:, 0, :], [bv16], [bycf])
        cp("dve", hst[:, 256:316].rearrange("p (a r) -> p a r", a=2), cb32[:, :, 0:30], [bcb32], [bycf])
        cp("dve", hst[:, 316:324], inre[:], [binit], [bycf])
        cp("dve", hst[:, 324:332], inim[:], [binit], [bycf])
        P.add("sp", lambda e: e.dma_start(out=hin.ap(), in_=hst), reads=[bycf], writes=[bhin], dma_owner=bycf)
        P.add("pool", lambda e: e.collective_compute("AllGather", ALU.bypass, replica_groups=GROUPS,
                                                     ins=[hin.ap().opt()], outs=[hall.ap().opt()]),
              reads=[bhin], writes=[bhall], dma_owner=bhall, dinc=1)

    def prompt_chunk(l, c):
        n = T
        t0 = c * T
        src = xT if l == 0 else xscr
        xdr = src[:, t0:t0 + T].rearrange("(k p) t -> p k t", p=128)
        ld(xs[:], xdr, bx[0], bx)
        ld(rp[:], ropeP[:, :, t0:t0 + T].rearrange("a p t -> p a t"), brp, [brp])
        if KS2 < 1:
            return
        rmsnorm(xs, bx, n, 1, 0, 0, l, False)
        if KS2 < 2:
            return
        for j in range(4):
            rope_pair(j, 5 + j, rp[:, 0, :], rp[:, 1, :], brp, q16[:, j, :], bq, n)
        rope_pair(4, 9, rp[:, 0, :], rp[:, 1, :], brp, kb16[:, 128:128 + T], bkb, n, extra32=(k32[:, 0:n], bk32))
        if KS2 < 3:
            return
        for o in range(2):
            pi = inproj_tile(10 + o, n)
            cp("act", u32[:, o, :], PS[pi][:, 0:n], [bPS[pi]], [bu32])
            cp("dve", u16[:, o, :], PS[pi][:, 0:n], [bPS[pi]], [bu16])
        for o in range(2):
            pa = inproj_tile(12 + o, n)
            pg = inproj_tile(14 + o, n)
            act(tmpC[:, 0:n], PS[pg][:, 0:n], AF.Sigmoid, [bPS[pg]], [btC])
            tt("dve", cb32[:, o, 30:30 + T], PS[pa][:, 0:n], tmpC[:, 0:n], ALU.mult, [bPS[pa], btC], [bcb32])
            cp("pool", cb16[:, o, :], cb32[:, o, :], [bcb32], [bcb16])
        if KS2 < 4:
            return
        for tb in range(T // 128):
            pi = nps()
            for k in range(KT):
                mm(PS[pi][:, 0:128], h16[:, k, tb * 128:(tb + 1) * 128], win16[:, k, 2048:2176], k == 0, k == KT - 1, [bh, bwin], [bPS[pi]])
            cp("act", v16[:, 1 + tb, :], PS[pi][:, 0:128], [bPS[pi]], [bv16])
            if c == NCH - 1 and tb == T // 128 - 1:
                cp("dve", v32[:], PS[pi][:, 0:128], [bPS[pi]], [bv32])
                st(nv[l], v32[:], bv32, [bv32])
        if c == NCH - 1:
            st(nkT[l], k32[:, T - 128:T], bk32, [bk32])
        if KSUB < 1:
            return
        blocks = [(qb_, hh_) for qb_ in range(T // 128) for hh_ in range(2)]

        def emit_scores(qb_, hh_):
            hs_ = slice(64 * hh_, 64 * hh_ + 64)
            qrhs_ = q16[hs_, :, qb_ * 128:(qb_ + 1) * 128]
            first_ = False
            po_ = nps()
            mm(PS[po_][:].rearrange("p (g q) -> p g q", g=4), kb16[hs_, 128 + qb_ * 128:128 + (qb_ + 1) * 128], qrhs_, True, False, [bkb, bq], [bPS[po_]])
            mm(PS[po_][:], ident16[:], mkn[:, 0, :], False, True, [bident16, bmkn], [bPS[po_]])
            pp_ = None
            if not first_:
                pp_ = nps()
                mm(PS[pp_][:].rearrange("p (g q) -> p g q", g=4), kb16[hs_, qb_ * 128:(qb_ + 1) * 128], qrhs_, True, False, [bkb, bq], [bPS[pp_]])
                mm(PS[pp_][:], ident16[:], mkn[:, 1, :], False, True, [bident16, bmkn], [bPS[pp_]])
            return po_, pp_
        pend = emit_scores(*blocks[0])
        for bi, (qb, hh) in enumerate(blocks):
            hs = slice(64 * hh, 64 * hh + 64)
            first = False
            po, pp = pend
            act(pown[:], PS[po][:], AF.Exp, [bPS[po]], [bpown], scale=0.125)
            if c == 0 and qb == 0:
                act(pprev[:], PS[pp][:], AF.Exp, [bPS[pp], bhp], [bpprev], scale=0.125, bias=hp[:, 1:2])
            else:
                act(pprev[:], PS[pp][:], AF.Exp, [bPS[pp]], [bpprev], scale=0.125)
            if bi + 1 < len(blocks):
                pend = emit_scores(*blocks[bi + 1])
            pO = 6
            pD = 7
            if not first:
                mm(PS[pO][0:64, :], v16[:, qb, hs], pprev[:], True, False, [bv16, bpprev], [bPS[pO]])
                mm(PS[pD][0:64, :], ones16[:, 0:64], pprev[:], True, False, [bones, bpprev], [bPS[pD]])
            mm(PS[pO][0:64, :], v16[:, qb + 1, hs], pown[:], first, True, [bv16, bpown], [bPS[pO]])
            mm(PS[pD][0:64, :], ones16[:, 0:64], pown[:], first, True, [bones, bpown], [bPS[pD]])
            tt("dve", den[:].rearrange("p (g q) -> p g q", g=4), PS[pD][0:64, :].rearrange("p (g q) -> p g q", g=4),
               sk[:, 4 * hh:4 * hh + 4].unsqueeze(2).to_broadcast([64, 4, 128]), ALU.add, [bPS[pD], bsk], [btA, btB])
            P.add("dve", lambda e: e.reciprocal(out=den[:], in_=den[:]), reads=[btA, btB], writes=[btA, btB])
            tt("dve", att16[:, 4 * hh:4 * hh + 4, qb * 128:(qb + 1) * 128], PS[pO][0:64, :].rearrange("p (g q) -> p g q", g=4),
               den[:].rearrange("p (g q) -> p g q", g=4), ALU.mult, [bPS[pO], btA, btB], [batt])
        cp("pool", kb16[:, 0:128], kb16[:, T:T + 128], [bkb], [bkb])
        cp("pool", v16[:, 0, :], v16[:, T // 128, :], [bv16], [bv16])
        if KSUB < 2:
            return
        pY = [6, 7]
        def emit_bu(sc_, ct_):
            pr_ = nps()
            pim_ = nps()
            mm(PS[pr_][:, 0:TS], LB[:, 0, ct_, :], u16[:, ct_ // 4, sc_ * TS:(sc_ + 1) * TS], True, True, [bLB, bu16], [bPS[pr_]])
            mm(PS[pim_][:, 0:TS], LB[:, 1, ct_, :], u16[:, ct_ // 4, sc_ * TS:(sc_ + 1) * TS], True, True, [bLB, bu16], [bPS[pim_]])
            return pr_, pim_
        iters = [(sc_, ct_) for sc_ in range(T // TS) for ct_ in range(8)]
        cstages = conv_ln_stages(lambda m, k: cb16[:, m, k:k + T], n, lambda ap: ap)
        csched = {0: [0], 1: [1], 2: [2], 3: [3], 4: [4], 5: [5], 6: [6], 8: [7], 9: [8], 10: [9], 11: [10], 13: [11], 14: [12]}
        pend = emit_bu(*iters[0])
        for sc in range(T // TS):
            c0 = sc * TS
            firstsub = (c == 0 and sc == 0)
            for ct in range(8):
                uh = ct // 4
                pr, pim = pend
                nxt = sc * 8 + ct + 1
                if nxt < len(iters):
                    pend = emit_bu(*iters[nxt])
                cs_ = cosT[:, ct, 0:TS]
                sn_ = sinT[:, ct, 0:TS]
                par = ssi[0] % 2
                ssi[0] += 1
                qA, bqA = sq[par][0], bsq[par][0]
                qB, bqB = sq[par][1], bsq[par][1]
                hA, bhA = sh16[par][0], bsh16[par][0]
                hB, bhB = sh16[par][1], bsh16[par][1]
                tt("dve", sx[0][:], PS[pr][:, 0:TS], cs_, ALU.mult, [bPS[pr], btab], [bsx[0]])
                tt("dve", sx[1][:], PS[pim][:, 0:TS], sn_, ALU.mult, [bPS[pim], btab], [bsx[1]])
                tt("dve", sx[0][:], sx[0][:], sx[1][:], ALU.add, [bsx[0], bsx[1]], [bsx[0]])
                tt("dve", sx[2][:], PS[pim][:, 0:TS], cs_, ALU.mult, [bPS[pim], btab], [bsx[2]])
                tt("dve", sx[3][:], PS[pr][:, 0:TS], sn_, ALU.mult, [bPS[pr], btab], [bsx[3]])
                tt("dve", sx[2][:], sx[2][:], sx[3][:], ALU.subtract, [bsx[2], bsx[3]], [bsx[2]])
                rbc = sp8[:, 5, ct:ct + 1].to_broadcast([128, TS])
                ire = inre[:, ct:ct + 1]
                iim = inim[:, ct:ct + 1]
                P.add("dve", lambda e, ire=ire, rbc=rbc, qA=qA: e.tensor_tensor_scan(out=qA[:], data0=rbc, data1=sx[0][:], initial=ire, op0=ALU.mult, op1=ALU.add),
                      reads=[bsp8, bsx[0], binit], writes=[bqA])
                P.add("dve", lambda e, iim=iim, rbc=rbc, qB=qB: e.tensor_tensor_scan(out=qB[:], data0=rbc, data1=sx[2][:], initial=iim, op0=ALU.mult, op1=ALU.add),
                      reads=[bsp8, bsx[2], binit], writes=[bqB])
                cp("pool", qlre[:, ct:ct + 1], qA[:, TS - 1:TS], [bqA], [bql])
                cp("pool", qlim[:, ct:ct + 1], qB[:, TS - 1:TS], [bqB], [bql])
                tt("pool", sx[6][:], qA[:], cs_, ALU.mult, [bqA, btab], [bsx[6]])
                tt("pool", sx[7][:], qB[:], sn_, ALU.mult, [bqB, btab], [bsx[7]])
                tt("pool", hA[:], sx[6][:], sx[7][:], ALU.subtract, [bsx[6], bsx[7]], [bhA])
                tt("pool", sx[4][:], qA[:], sn_, ALU.mult, [bqA, btab], [bsx[4]])
                tt("pool", sx[5][:], qB[:], cs_, ALU.mult, [bqB, btab], [bsx[5]])
                tt("pool", hB[:], sx[4][:], sx[5][:], ALU.add, [bsx[4], bsx[5]], [bhB])
                ot = ct // 4
                mm(PS[pY[ot]][:, c0:c0 + TS], LB[:, 2, ct, :], hA[:], ct % 4 == 0, False, [bLB, bhA], [bPS[pY[ot]]])
                mm(PS[pY[ot]][:, c0:c0 + TS], LB[:, 3, ct, :], hB[:], False, ct % 4 == 3, [bLB, bhB], [bPS[pY[ot]]])
                for si_ in csched.get(sc * 8 + ct, []):
                    cstages[si_]()
            cl = cosT[:, :, TS - 1]
            sl = sinT[:, :, TS - 1]
            a0, a1 = sm8[0][:], sm8[1][:]
            tt("dve", a0, qlre[:], cl, ALU.mult, [bql, btab], [bsm8])
            tt("dve", a1, qlim[:], sl, ALU.mult, [bql, btab], [bsm8])
            tt("dve", hlre[:], a0, a1, ALU.subtract, [bsm8], [bhl])
            tt("dve", a0, qlre[:], sl, ALU.mult, [bql, btab], [bsm8])
            tt("dve", a1, qlim[:], cl, ALU.mult, [bql, btab], [bsm8])
            tt("dve", hlim[:], a0, a1, ALU.add, [bsm8], [bhl])
            tt("dve", a0, hlre[:], S8(6), ALU.mult, [bhl, bsp8], [bsm8])
            tt("dve", a1, hlim[:], S8(7), ALU.mult, [bhl, bsp8], [bsm8])
            tt("dve", inre[:], a0, a1, ALU.subtract, [bsm8], [binit])
            tt("dve", a0, hlre[:], S8(7), ALU.mult, [bhl, bsp8], [bsm8])
            tt("dve", a1, hlim[:], S8(6), ALU.mult, [bhl, bsp8], [bsm8])
            tt("dve", inim[:], a0, a1, ALU.add, [bsm8], [binit])
            if c == NCH - 1 and sc == T // TS - 1:
                st(nre[l], hlre[:], bhl, [bhl])
                st(nim[l], hlim[:], bhl, [bhl])
        for o in range(2):
            stt("dve", yss[:, o, 0:n], u32[:, o, 0:n], sp2[:, 0, o:o + 1], PS[pY[o]][:, 0:n], ALU.mult, ALU.add, [bu32, bsp2, bPS[pY[o]]], [byss])
        gelu_glu(n)
        if KSUB < 3:
            return
        if c == NCH - 1:
            st(ncv[l], cb32[:, :, T:T + 30], bcb32, [bcb32])
        for o in range(2):
            cp("pool", cb32[:, o, 0:30], cb32[:, o, T:T + 30], [bcb32], [bcb32])
        if KSUB < 4:
            return
        outproj_ffn(xs, bx, n, l, False, first=(c == 0))
        if l < nl - 1:
            st(xscr[:, t0:t0 + T].rearrange("(k p) t -> p k t", p=128), xs[:], bx[0], bx)
        else:
            rmsnorm(xs, bx, n, None, 0, 0, l, False)
            st(yT[:, t0:t0 + T].rearrange("(k p) t -> p k t", p=128), xs[:], bx[0], bx)

    def sample_layer(l):
        n = TSM
        bxs = [bxsm] * KT
        ld(kc16, kcT[l].rearrange("b f k -> f b k"), bkc, [bkc], q="pool")
        ld(vc16, vc[l].rearrange("b k f -> k b f"), bvc, [bvc], q="pool")
        ld(h0re[:], ssmre_in[l], bh0, [bh0])
        ld(h0im[:], ssmim_in[l], bh0, [bh0])
        ld(cbs32[:, :, :, 0:30], sconv_in[l], bcbs32, [bcbs32])
        dd = Buf(f"dd{l}")
        o1 = P.add("sp", lambda e: e.dma_start(out=nks_c[l], in_=kcn[l, :, 4:128, :]), dma_owner=dd)
        stores.append(o1)
        vcn_src = vc[l, :, 4:128, :]
        o2 = P.add("sp", lambda e: e.dma_start(out=nvs_c[l], in_=vcn_src), dma_owner=dd)
        stores.append(o2)
        rmsnorm(xsm, bxs, n, 1, 0, 0, l, True)
        for j in range(4):
            rope_pair(j, 5 + j, rps[:, 0, :], rps[:, 1, :], brps, q16[:, j, 0:n], bq, n)
        rope_pair(4, 9, rps[:, 0, :], rps[:, 1, :], brps, kb16[:, 128:128 + n], bkb, n, extra32=(k32[:, 0:n], bk32))
        st(skT[l], k32[:, 0:n], bk32, [bk32])
        for o in range(2):
            pi = inproj_tile(10 + o, n)
            cp("act", u32[:, o, 0:n], PS[pi][:, 0:n], [bPS[pi]], [bu32])
            cp("dve", u16[:, o, 0:n], PS[pi][:, 0:n], [bPS[pi]], [bu16])
        for o in range(2):
            pa = inproj_tile(12 + o, n)
            pg = inproj_tile(14 + o, n)
            act(tmpC[:, 0:n], PS[pg][:, 0:n], AF.Sigmoid, [bPS[pg]], [btC])
            tt("dve", cbs32[:, o, :, 30:34], PS[pa][:, 0:n].rearrange("p (b t) -> p b t", t=LS),
               tmpC[:, 0:n].rearrange("p (b t) -> p b t", t=LS), ALU.mult, [bPS[pa], btC], [bcbs32])
            cp("pool", cbs16[:, o], cbs32[:, o], [bcbs32], [bcbs16])
        st(scv[l], cbs32[:, :, :, 4:34], bcbs32, [bcbs32])
        pv = nps()
        for b in range(NB):
            for k in range(KT):
                mm(PS[pv][0:4, :].rearrange("p (b f) -> p b f", b=4)[:, b % 4, :] if False else PS[pv][0:4, (b % 4) * 128:(b % 4 + 1) * 128],
                   h16[:, k, 4 * b:4 * b + 4], win16[:, k, 2048:2176], k == 0, k == KT - 1, [bh, bwin], [bPS[pv]])
            if b % 4 == 3:
                g0 = b - 3
                cp("act", vn16[:, g0:g0 + 4, :], PS[pv][0:4, :].rearrange("p (b f) -> p b f", b=4), [bPS[pv]], [bvn16])
                cp("dve", vn32[:, g0:g0 + 4, :], PS[pv][0:4, :].rearrange("p (b f) -> p b f", b=4), [bPS[pv]], [bvn32])
                if b < NB - 1:
                    pv = nps()
        st(svn[l], vn32[:], bvn32, [bvn32])
        pC = nps()
        pN = nps()
        for b in range(NB):
            for hh in range(2):
                hs = slice(64 * hh, 64 * hh + 64)
                col = (b * 2 + hh) * 16
                qrhs = q16[hs, :, 4 * b:4 * b + 4]
                mm(PS[pC][:, col:col + 16].rearrange("p (g t) -> p g t", g=4), kc16[hs, b, :], qrhs, True, True, [bkc, bq], [bPS[pC]])
                mm(PS[pN][0:4, col:col + 16].rearrange("p (g t) -> p g t", g=4), kb16[hs, 128 + 4 * b:128 + 4 * b + 4], qrhs, True, True, [bkb, bq], [bPS[pN]])
        act(pown[:], PS[pC][:], AF.Exp, [bPS[pC]], [bpown], scale=0.125)
        tt("pool", pown[:], pown[:], mk[:, 2, :], ALU.mult, [bpown, bmk], [bpown])
        act(pn16[:], PS[pN][0:4, :], AF.Exp, [bPS[pN]], [bpn], scale=0.125)
        tt("pool", pn16[:], pn16[:], mk[0:4, 3, :], ALU.mult, [bpn, bmk], [bpn])
        pO = nps()
        pD = nps()
        for b in range(NB):
            for hh in range(2):
                hs = slice(64 * hh, 64 * hh + 64)
                col = (b * 2 + hh) * 16
                mm(PS[pO][0:64, col:col + 16], vc16[:, b, hs], pown[:, col:col + 16], True, False, [bvc, bpown], [bPS[pO]])
                mm(PS[pO][0:64, col:col + 16], vn16[0:4, b, hs], pn16[0:4, col:col + 16], False, True, [bvn16, bpn], [bPS[pO]])
                mm(PS[pD][0:64, col:col + 16], ones16[:, 0:64], pown[:, col:col + 16], True, False, [bones, bpown], [bPS[pD]])
                mm(PS[pD][0:64, col:col + 16], ones16[0:4, 0:64], pn16[0:4, col:col + 16], False, True, [bones, bpn], [bPS[pD]])
        tt("dve", den[:].rearrange("p (b h t) -> p b h t", b=NB, t=LS), PS[pD][0:64, :].rearrange("p (b h t) -> p b h t", b=NB, t=LS),
           sk[:, :].unsqueeze(1).unsqueeze(3).to_broadcast([64, NB, 8, LS]), ALU.add, [bPS[pD], bsk], [btA, btB])
        P.add("dve", lambda e: e.reciprocal(out=den[:], in_=den[:]), reads=[btA, btB], writes=[btA, btB])
        tt("dve", att16[:, :, 0:n].rearrange("p h (b t) -> p b h t", t=LS), PS[pO][0:64, :].rearrange("p (b h t) -> p b h t", b=NB, t=LS),
           den[:].rearrange("p (b h t) -> p b h t", b=NB, t=LS), ALU.mult, [bPS[pO], btA, btB], [batt])
        for (dst, x1, y1, x2, y2, op) in ((ahre, 8, h0re, 9, h0im, ALU.subtract), (ahim, 8, h0im, 9, h0re, ALU.add)):
            tt("dve", dst[:], y1[:], sp8[:, x1, :].unsqueeze(2).to_broadcast([128, 8, NB]), ALU.mult, [bh0, bsp8], [bah])
            tt("dve", hsre[:], y2[:], sp8[:, x2, :].unsqueeze(2).to_broadcast([128, 8, NB]), ALU.mult, [bh0, bsp8], [bhs])
            tt("dve", dst[:], dst[:], hsre[:], op, [bah, bhs], [bah])
        pY = [6, 7]
        for ct in range(8):
            uh = ct // 4
            pr = nps()
            pim = nps()
            mm(PS[pr][:, 0:n], LB[:, 0, ct, :], u16[:, uh, 0:n], True, True, [bLB, bu16], [bPS[pr]])
            mm(PS[pim][:, 0:n], LB[:, 1, ct, :], u16[:, uh, 0:n], True, True, [bLB, bu16], [bPS[pim]])
            cs_ = cosT[:, ct, 0:LS].unsqueeze(1).to_broadcast([128, NB, LS])
            sn_ = sinT[:, ct, 0:LS].unsqueeze(1).to_broadcast([128, NB, LS])
            V3 = lambda ap: ap.rearrange("p (b t) -> p b t", t=LS)
            X = [s_[:, 0:n] for s_ in sx]
            tt("dve", V3(X[0]), V3(PS[pr][:, 0:n]), cs_, ALU.mult, [bPS[pr], btab], [bsx[0]])
            tt("dve", V3(X[1]), V3(PS[pim][:, 0:n]), sn_, ALU.mult, [bPS[pim], btab], [bsx[1]])
            tt("pool", X[0], X[0], X[1], ALU.add, [bsx[0], bsx[1]], [bsx[0]])
            tt("dve", V3(X[2]), V3(PS[pim][:, 0:n]), cs_, ALU.mult, [bPS[pim], btab], [bsx[2]])
            tt("dve", V3(X[3]), V3(PS[pr][:, 0:n]), sn_, ALU.mult, [bPS[pr], btab], [bsx[3]])
            tt("pool", X[2], X[2], X[3], ALU.subtract, [bsx[2], bsx[3]], [bsx[2]])
            tt("dve", sx[0][:, 0:n:LS], sx[0][:, 0:n:LS], ahre[:, ct, :], ALU.add, [bsx[0], bah], [bsx[0]])
            tt("dve", sx[2][:, 0:n:LS], sx[2][:, 0:n:LS], ahim[:, ct, :], ALU.add, [bsx[2], bah], [bsx[2]])
            r4v = r4[:, ct].rearrange("p b t -> p (b t)")
            P.add("dve", lambda e, r4v=r4v, X=X: e.tensor_tensor_scan(out=X[4], data0=r4v, data1=X[0], initial=0.0, op0=ALU.mult, op1=ALU.add),
                  reads=[btab4, bsx[0]], writes=[bsx[4]])
            P.add("dve", lambda e, r4v=r4v, X=X: e.tensor_tensor_scan(out=X[5], data0=r4v, data1=X[2], initial=0.0, op0=ALU.mult, op1=ALU.add),
                  reads=[btab4, bsx[2]], writes=[bsx[5]])
            tt("pool", V3(X[6]), V3(X[4]), cs_, ALU.mult, [bsx[4], btab], [bsx[6]])
            tt("pool", V3(X[7]), V3(X[5]), sn_, ALU.mult, [bsx[5], btab], [bsx[7]])
            tt("pool", X[6], X[6], X[7], ALU.subtract, [bsx[6], bsx[7]], [bsx[6]])
            cp("act", hre16[:, 0:n], X[6], [bsx[6]], [bhre])
            cp("act", hsre[:, ct, :], sx[6][:, LS - 1:n:LS], [bsx[6]], [bhs])
            tt("dve", V3(X[1]), V3(X[4]), sn_, ALU.mult, [bsx[4], btab], [bsx[1]])
            tt("dve", V3(X[3]), V3(X[5]), cs_, ALU.mult, [bsx[5], btab], [bsx[3]])
            tt("dve", X[1], X[1], X[3], ALU.add, [bsx[1], bsx[3]], [bsx[1]])
            cp("act", him16[:, 0:n], X[1], [bsx[1]], [bhim])
            cp("act", hsim[:, ct, :], sx[1][:, LS - 1:n:LS], [bsx[1]], [bhs])
            ot = ct // 4
            mm(PS[pY[ot]][:, 0:n], LB[:, 2, ct, :], hre16[:, 0:n], ct % 4 == 0, False, [bLB, bhre], [bPS[pY[ot]]])
            mm(PS[pY[ot]][:, 0:n], LB[:, 3, ct, :], him16[:, 0:n], False, ct % 4 == 3, [bLB, bhim], [bPS[pY[ot]]])
        st(sre[l], hsre[:], bhs, [bhs])
        st(sim_o[l], hsim[:], bhs, [bhs])
        for o in range(2):
            stt("dve", yss[:, o, 0:n], u32[:, o, 0:n], sp2[:, 0, o:o + 1], PS[pY[o]][:, 0:n], ALU.mult, ALU.add, [bu32, bsp2, bPS[pY[o]]], [byss])
        gelu_glu(n)
        conv_ln(lambda m, k: cbs16[:, m, :, k:k + LS], n, lambda ap: ap.rearrange("p (b t) -> p b t", t=LS))
        if DBG and l == 0:
            dsb = xs[:, 0:6, :].rearrange("p (a k) (h t) -> p a (k h) t", k=2, t=TSM); bdsb = bx[0]
            memset("dve", dsb, 0.0, bx)
            cp("dve", dsb[0:64, 0, :, :], att16[:, :, 0:n], [batt, bdsb], [bdsb])
            cp("dve", dsb[:, 1, 0:2, :], os16[:, :, 0:n], [bos, bdsb], [bdsb])
            cp("dve", dsb[:, 2, 0:2, :], oc16[:, :, 0:n], [boc, bdsb], [bdsb])
            st(dbg.rearrange("a p h t -> p a h t"), dsb, bdsb, [bdsb])
        outproj_ffn(xsm, bxs, n, l, True)
        if l == nl - 1:
            rmsnorm(xsm, bxs, n, None, 0, 0, l, True)
            st(ysT.rearrange("(k p) t -> p k t", p=128), xsm[:], bxsm, [bxsm])

    STG = int(os.environ.get("KSTAGE", "9"))
    for l in range(nl):
        if STG >= 1:
            layer_params(l)
        slot_begin(l)
        for c in range(NCH):
            if STG >= 3 or (STG == 2 and c == 0):
                prompt_chunk(l, c)
        if l < nl - 1:
            slot_end(l)
        if STG >= 4:
            sample_layer(l)

    P.add("sp", lambda e: e.nop(), extra_deps=stores)
    P.emit()
    return nc


def _perm_win(w):
    q = w[:, 0:512].reshape(D, 8, 64)
    k = w[:, 512:640].reshape(D, 2, 64)
    v = w[:, 640:768]
    u = w[:, 768:1024]
    a = w[:, 1024:1280]
    g = w[:, 1280:1536]

    def swap(t):
        return np.concatenate([t[..., 32:], t[..., :32]], axis=-1)
    order = [0, 4, 1, 5, 2, 6, 3, 7]
    qt = q[:, order, :].reshape(D, 512)
    qs = swap(q)[:, order, :].reshape(D, 512)
    kt = k.reshape(D, 128)
    ks = swap(k).reshape(D, 128)
    return np.ascontiguousarray(np.concatenate([qt, kt, qs, ks, u, a, g, v], axis=1))


def _rope_tab(pos):
    half = 32
    inv = (np.float32(10000.0) ** (-(np.arange(half, dtype=np.float32) / np.float32(half)))).astype(np.float32)
    ang = (pos.astype(np.float32)[None, :] * inv[:, None]).astype(np.float32)
    c = np.cos(ang.astype(np.float64)).astype(np.float32)
    s = np.sin(ang.astype(np.float64)).astype(np.float32)
    cos = np.concatenate([c, c, c, c], axis=0)
    sins = np.concatenate([-s, s, -s, s], axis=0)
    return np.ascontiguousarray(np.stack([cos, sins], axis=0))


_NC_CACHE = {}


def kernel(**inp):
    f = lambda a: np.ascontiguousarray(np.asarray(a, dtype=np.float32))
    I = {k: np.asarray(v) for k, v in inp.items()}
    nlr = _NC_CACHE.get("nl", NS)
    if "nc" not in _NC_CACHE:
        _NC_CACHE["nc"] = build(nlr)
    nc = _NC_CACHE["nc"]
    import os
    ncr = int(os.environ.get("KCORES", "8"))

    SLOT = {0: [0, 1, 2, 3, 0], 1: [0, 0, 1, 2, 3]}
    DUMMY = {0: 4, 1: 0}

    def pk(a):
        return a.reshape(NL, 8, 128).transpose(0, 2, 1)

    def p2(a):
        return a.reshape(NL, 2, 128).transpose(0, 2, 1)

    per_layer = {
        "w_mod": I["w_mod"],
        "b_modT": I["b_mod"].reshape(NL, 48, 128).transpose(0, 2, 1),
        "g1T": pk(I["norm1_g"]), "g2T": pk(I["norm2_g"]),
        "w_in2": np.stack([_perm_win(I["w_in"][l]) for l in range(NL)]),
        "sinkT": np.broadcast_to(I["attn_sinks"][:, None, :], (NL, 64, 8)),
        "lamre": I["ssm_lam_re"].reshape(NL, 8, 128).transpose(0, 2, 1),
        "lamim": I["ssm_lam_im"].reshape(NL, 8, 128).transpose(0, 2, 1),
        "logdt": np.repeat(I["ssm_log_dt"], 64, axis=1).reshape(NL, 8, 128).transpose(0, 2, 1),
        "bre": I["ssm_b_re"].reshape(NL, 8, 128, 16).transpose(0, 2, 1, 3),
        "bim": I["ssm_b_im"].reshape(NL, 8, 128, 16).transpose(0, 2, 1, 3),
        "cre": I["ssm_c_re"].reshape(NL, 8, 2, 16, 64).transpose(0, 2, 4, 1, 3).reshape(NL, 128, 8, 16),
        "cim": I["ssm_c_im"].reshape(NL, 8, 2, 16, 64).transpose(0, 2, 4, 1, 3).reshape(NL, 128, 8, 16),
        "dskipT": p2(I["ssm_d"]), "wglu": I["ssm_w_glu"], "bgluT": p2(I["ssm_b_glu"]),
        "convwT": I["conv_w"].reshape(NL, 31, 2, 128).transpose(0, 3, 2, 1),
        "convbT": p2(I["conv_b"]), "lngT": p2(I["conv_ln_g"]), "lnbT": p2(I["conv_ln_b"]),
        "w_out": I["w_out"], "w_gate": I["w_gate"], "w_up": I["w_up"], "w_down": I["w_down"],
    }
    role_w = {}
    for role in (0, 1):
        d = {}
        for k, a in per_layer.items():
            arr = f(np.asarray(a)[SLOT[role]])
            if k in ("w_out", "w_down"):
                arr[DUMMY[role]] = 0.0
            d[k] = arr
        role_w[role] = d
    const = {
        "gfT": f(I["final_norm_g"].reshape(8, 128).T),
        "ropeS": _rope_tab(PAST + np.tile(np.arange(LS), NB)),
        "identd": np.eye(128, dtype=np.float32),
        "jrow": f(np.broadcast_to(np.arange(TS + 1, dtype=np.float32)[None, :], (128, TS + 1))),
    }
    kk = np.arange(128)[:, None]
    qq = np.tile(np.arange(128), 4)[None, :]
    m_own = np.where(qq >= kk, 0.0, -30000.0).astype(np.float32)
    m_prev = np.where(kk > qq, 0.0, -30000.0).astype(np.float32)
    const["maskP"] = f(np.stack([m_own, m_prev]))
    tq = np.tile(np.arange(LS), 128)[None, :]
    m_c = (kk > tq).astype(np.float32)
    m_n = (kk <= tq).astype(np.float32)
    const["maskS"] = f(np.stack([m_c, m_n]))
    rope_role = {0: _rope_tab(np.arange(NTOK)), 1: _rope_tab(NTOK + np.arange(NTOK))}
    hp_role = {0: f(np.tile(np.array([[0.0, -1.0e4]], np.float32), (128, 1))),
               1: f(np.tile(np.array([[1.0, 0.0]], np.float32), (128, 1)))}

    in_maps = []
    for c in range(ncr):
        role = c // 4
        b = c % 4
        sbs = slice(NB * c, NB * (c + 1))
        sl = SLOT[role]
        m = dict(const)
        m.update(role_w[role])
        m["ropeP"] = rope_role[role]
        m["hprev"] = hp_role[role]
        m["xT"] = f(I["x_prompt"][b, role * NTOK:(role + 1) * NTOK].T)
        m["xsT"] = f(I["x_sample"][sbs].reshape(TSM, D).T)
        m["cT"] = f(np.concatenate([I["c_prompt"][b][None, :], I["c_sample"][sbs]], axis=0).T)
        ck = I["cache_k"][:, sbs].reshape(NL, NB, 128, 128)[sl]
        cv = I["cache_v"][:, sbs].reshape(NL, NB, 128, 128)[sl]
        m["kcT"] = f(ck.transpose(0, 1, 3, 2))
        m["kcn"] = f(ck)
        m["vc"] = f(cv)
        m["ssmre_in"] = f(I["state_ssm_re"][:, sbs].reshape(NL, NB, 8, 128).transpose(0, 3, 2, 1)[sl])
        m["ssmim_in"] = f(I["state_ssm_im"][:, sbs].reshape(NL, NB, 8, 128).transpose(0, 3, 2, 1)[sl])
        m["sconv_in"] = f(I["state_conv"][:, sbs].reshape(NL, NB, 30, 2, 128).transpose(0, 4, 3, 1, 2)[sl])
        in_maps.append(m)

    res = run_bass_kernel_spmd(nc, in_maps, core_ids=list(range(ncr)))
    R = list(res.results)
    while len(R) < 8:
        R.append(R[0])
    if "dbg" in R[0]:
        _NC_CACHE["dbg"] = R[0]["dbg"]
    SA = slice(0, 4)
    SB = slice(1, 5)

    y_prompt = np.stack([np.concatenate([R[b]["yT"].T, R[4 + b]["yT"].T], axis=0) for b in range(4)])
    y_sample = np.concatenate([R[c]["ysT"].T.reshape(NB, LS, D) for c in range(8)], axis=0)
    nk_p = np.stack([R[4 + b]["nkT"][SB].transpose(0, 2, 1).reshape(NL, 128, 2, 64) for b in range(4)], axis=1)
    nv_p = np.stack([R[4 + b]["nv"][SB].reshape(NL, 128, 2, 64) for b in range(4)], axis=1)

    def unst(a):
        return a.transpose(0, 2, 1).reshape(NL, 16, 64)
    re_p = np.stack([unst(R[4 + b]["nre"][SB]) for b in range(4)], axis=1)
    im_p = np.stack([unst(R[4 + b]["nim"][SB]) for b in range(4)], axis=1)
    cv_p = np.stack([R[4 + b]["ncv"][SB].transpose(0, 3, 2, 1).reshape(NL, 30, 256) for b in range(4)], axis=1)
    nk_s, nv_s, re_s, im_s, cv_s = [], [], [], [], []
    for c in range(8):
        r = R[c]
        S_ = SA if c < 4 else SB
        knew = r["skT"][S_].transpose(0, 2, 1).reshape(NL, NB, LS, 128)
        nk_s.append(np.concatenate([r["nks_c"][S_], knew], axis=2).reshape(NL, NB, 128, 2, 64))
        vnew = r["svn"][S_].transpose(0, 2, 1, 3)
        nv_s.append(np.concatenate([r["nvs_c"][S_], vnew], axis=2).reshape(NL, NB, 128, 2, 64))
        re_s.append(r["sre"][S_].transpose(0, 3, 2, 1).reshape(NL, NB, 16, 64))
        im_s.append(r["sim_o"][S_].transpose(0, 3, 2, 1).reshape(NL, NB, 16, 64))
        cv_s.append(r["scv"][S_].transpose(0, 3, 4, 2, 1).reshape(NL, NB, 30, 256))
    cat = lambda xs_: np.ascontiguousarray(np.concatenate(xs_, axis=1).astype(np.float32))
    outs = (y_prompt, y_sample, nk_p, nv_p, re_p, im_p, cv_p, cat(nk_s), cat(nv_s), cat(re_s), cat(im_s), cat(cv_s))
    return tuple(np.ascontiguousarray(o.astype(np.float32)) for o in outs)
```

```python
import math
import numpy as np
import concourse.bass as bass
import concourse.mybir as mybir
from concourse.bass_utils import run_bass_kernel_spmd

F32 = mybir.dt.float32
BF16 = mybir.dt.bfloat16
ALU = mybir.AluOpType
AF = mybir.ActivationFunctionType

SEG = 30000
NL = 4
NS = 5
HF = 332
D = 1024
KT = 8
NTOK = 2048
T = 256
NCH = NTOK // T
TS = 128
NB = 16
LS = 4
TSM = NB * LS
DFF = 2816
JT = 22
WIN = 2176
PAST = 8192
TWO_PI = 2.0 * math.pi


class Buf:
    __slots__ = ("name", "lw", "rd", "sem", "cnt", "excl")

    def __init__(self, name, excl=False):
        self.name = name
        self.excl = excl
        self.lw = None
        self.rd = {}
        self.sem = None
        self.cnt = 0


class Op:
    __slots__ = ("eng", "fn", "deps", "idx", "dma", "owner", "dcnt", "marked", "ev", "waits", "dinc")

    def __init__(self, eng, fn, idx):
        self.eng = eng
        self.fn = fn
        self.idx = idx
        self.deps = []
        self.dma = False
        self.owner = None
        self.dcnt = 0
        self.marked = False
        self.ev = None
        self.waits = []


class Prog:
    ENGS = ("pe", "act", "dve", "pool", "sp")

    def __init__(self, nc):
        self.nc = nc
        self.ops = []

    def add(self, eng, fn, reads=(), writes=(), dma_owner=None, extra_deps=(), dinc=16):
        i = len(self.ops)
        op = Op(eng, fn, i)
        if dma_owner is not None:
            op.dma = True
            op.owner = dma_owner
            op.dinc = dinc
            dma_owner.cnt += dinc
            op.dcnt = dma_owner.cnt
        deps = {}
        for b in reads:
            if b.lw is not None:
                deps[b.lw] = "raw"
            if b.excl:
                for r in b.rd.values():
                    if r not in deps:
                        deps[r] = "war"
        for b in writes:
            if b.lw is not None and b.lw not in deps:
                deps[b.lw] = "waw"
            for r in b.rd.values():
                if r not in deps:
                    deps[r] = "war"
        for d in extra_deps:
            deps[d.idx] = "raw"
        deps.pop(i, None)
        for b in reads:
            key = ("d", i) if op.dma else eng
            b.rd[key] = i
        for b in writes:
            b.lw = i
            b.rd = {}
        op.deps = list(deps.items())
        self.ops.append(op)
        return op

    def finalize(self):
        ops = self.ops
        waited = {e: {} for e in self.ENGS}
        for op in ops:
            need = {}
            for d, kind in op.deps:
                p = ops[d]
                if p.dma:
                    key = ("dma", id(p.owner))
                    if need.get(key, (0, None))[0] < p.dcnt:
                        need[key] = (p.dcnt, p)
                else:
                    if p.eng == op.eng and not op.dma:
                        if op.eng == "pe" or kind != "raw":
                            continue
                    key = ("eng", p.eng)
                    if need.get(key, (-1, None))[0] < p.idx:
                        need[key] = (p.idx, p)
            w = waited[op.eng]
            for key, (val, p) in need.items():
                if w.get(key, -1) >= val:
                    continue
                w[key] = val
                op.waits.append(p)
                if not p.dma:
                    p.marked = True
        cnt = {e: 0 for e in self.ENGS}
        for op in ops:
            if not op.dma and op.marked:
                cnt[op.eng] += 1
                op.ev = cnt[op.eng]
        self.evcount = cnt

    def emit(self):
        nc = self.nc
        self.finalize()
        esems = {}
        for e in self.ENGS:
            n = (self.evcount[e] + SEG - 1) // SEG
            esems[e] = [nc.alloc_semaphore(f"ev_{e}_{k}") for k in range(max(n, 1))]
        for op in self.ops:
            if op.dma and op.owner.sem is None:
                op.owner.sem = nc.alloc_semaphore("d_" + op.owner.name)

        def semval(p):
            if p.dma:
                return p.owner.sem, p.dcnt
            k = (p.ev - 1) // SEG
            return esems[p.eng][k], (p.ev - 1) % SEG + 1

        per = {e: [op for op in self.ops if op.eng == e] for e in self.ENGS}

        def run(eng, lst):
            for op in lst:
                for p in op.waits:
                    s, v = semval(p)
                    eng.wait_ge(s, v)
                ins = op.fn(eng)
                if op.dma:
                    ins.then_inc(op.owner.sem, op.dinc)
                elif op.marked:
                    s, _ = semval(op)
                    ins.then_inc(s, 1)

        with nc.Block() as block:
            @block.tensor
            def _(e):
                run(e, per["pe"])

            @block.scalar
            def _(e):
                run(e, per["act"])

            @block.vector
            def _(e):
                run(e, per["dve"])

            @block.gpsimd
            def _(e):
                run(e, per["pool"])

            @block.sync
            def _(e):
                run(e, per["sp"])


def build(nl=NS):
    import os
    KSUB = int(os.environ.get("KSUB", "9"))
    KS2 = int(os.environ.get("KS2", "9"))
    nc = bass.Bass("TRN2", target_bir_lowering=False)
    P = Prog(nc)
    stores = []

    def din(name, shape):
        return nc.dram_tensor(name, list(shape), F32, kind="ExternalInput").ap()

    def dout(name, shape):
        return nc.dram_tensor(name, list(shape), F32, kind="ExternalOutput").ap()

    def sb(name, shape, dt=F32):
        return nc.alloc_sbuf_tensor(name, list(shape), dt)

    xT = din("xT", [D, NTOK])
    xsT = din("xsT", [D, TSM])
    cT = din("cT", [D, 17])
    w_mod = din("w_mod", [NS, D, 6 * D])
    b_modT = din("b_modT", [NS, 128, 48])
    g1T = din("g1T", [NS, 128, KT])
    g2T = din("g2T", [NS, 128, KT])
    gfT = din("gfT", [128, KT])
    w_in2 = din("w_in2", [NS, D, WIN])
    ropeP = din("ropeP", [2, 128, NTOK])
    ropeS = din("ropeS", [2, 128, TSM])
    maskP = din("maskP", [2, 128, 512])
    maskS = din("maskS", [2, 128, 512])
    sinkT = din("sinkT", [NS, 64, 8])
    lamre = din("lamre", [NS, 128, 8])
    lamim = din("lamim", [NS, 128, 8])
    logdt = din("logdt", [NS, 128, 8])
    bre = din("bre", [NS, 128, 8, 16])
    bim = din("bim", [NS, 128, 8, 16])
    cre = din("cre", [NS, 128, 8, 16])
    cim = din("cim", [NS, 128, 8, 16])
    dskipT = din("dskipT", [NS, 128, 2])
    wglu = din("wglu", [NS, 256, 256])
    bgluT = din("bgluT", [NS, 128, 2])
    convwT = din("convwT", [NS, 128, 2, 31])
    convbT = din("convbT", [NS, 128, 2])
    lngT = din("lngT", [NS, 128, 2])
    lnbT = din("lnbT", [NS, 128, 2])
    w_out = din("w_out", [NS, D, D])
    w_gate = din("w_gate", [NS, D, DFF])
    w_up = din("w_up", [NS, D, DFF])
    w_down = din("w_down", [NS, DFF, D])
    kcT = din("kcT", [NS, NB, 128, 128])
    vc = din("vc", [NS, NB, 128, 128])
    kcn = din("kcn", [NS, NB, 128, 128])
    ssmre_in = din("ssmre_in", [NS, 128, 8, NB])
    ssmim_in = din("ssmim_in", [NS, 128, 8, NB])
    sconv_in = din("sconv_in", [NS, 128, 2, NB, 30])
    identd = din("identd", [128, 128])
    hprev = din("hprev", [128, 2])
    hin = nc.dram_tensor("hin", [128, HF], F32)
    hall = nc.dram_tensor("hall", [256, HF], F32)
    bhin = Buf("hin"); bhall = Buf("hall")
    jrow = din("jrow", [128, TS + 1])

    yT = dout("yT", [D, NTOK])
    ysT = dout("ysT", [D, TSM])
    nkT = dout("nkT", [NS, 128, 128])
    nv = dout("nv", [NS, 128, 128])
    nre = dout("nre", [NS, 128, 8])
    nim = dout("nim", [NS, 128, 8])
    ncv = dout("ncv", [NS, 128, 2, 30])
    nks_c = dout("nks_c", [NS, NB, 124, 128])
    nvs_c = dout("nvs_c", [NS, NB, 124, 128])
    skT = dout("skT", [NS, 128, TSM])
    svn = dout("svn", [NS, 4, NB, 128])
    sre = dout("sre", [NS, 128, 8, NB])
    sim_o = dout("sim_o", [NS, 128, 8, NB])
    scv = dout("scv", [NS, 128, 2, NB, 30])
    xscr = nc.dram_tensor("xscr", [D, NTOK], F32, kind="Internal").ap()
    wgu_c = nc.dram_tensor("wgu_c", [JT, 128, 2 * KT * 128], BF16, kind="Internal").ap()
    wdn_c = nc.dram_tensor("wdn_c", [16, 128, 11 * 128], BF16, kind="Internal").ap()
    woa_c = nc.dram_tensor("woa_c", [8, 64, 8 * 128], BF16, kind="Internal").ap()
    wor_c = nc.dram_tensor("wor_c", [8, 128, 4 * 128], BF16, kind="Internal").ap()
    bwgu_c = [Buf(f"wguc{j}") for j in range(JT)]
    bwdn_c = [Buf(f"wdnc{j}") for j in range(16)]
    bwo_c = [Buf(f"woc{j}") for j in range(8)]
    DBG = bool(int(os.environ.get("KDBG", "0")))
    if DBG:
        dbg = dout("dbg", [3, 128, 8, TSM])

    PS = [nc.alloc_psum_tensor(f"ps{i}", [128, 512], F32) for i in range(8)]
    bPS = [Buf(f"ps{i}", excl=True) for i in range(8)]
    psrr = [0]

    def nps():
        i = psrr[0]
        psrr[0] = (i + 1) % 6
        return i

    def mm(out, lhsT, rhs, start, stop, r, w):
        return P.add("pe", lambda e: e.matmul(out, lhsT=lhsT, rhs=rhs, start=start, stop=stop), reads=r, writes=w)

    def act(out, in_, func, r, w, bias=None, scale=None):
        kw = {}
        if bias is not None:
            kw["bias"] = bias
        if scale is not None:
            kw["scale"] = scale
        return P.add("act", lambda e: e.activation(out=out, in_=in_, func=func, **kw), reads=r, writes=w)

    def tt(eng, out, in0, in1, op, r, w):
        return P.add(eng, lambda e: e.tensor_tensor(out=out, in0=in0, in1=in1, op=op), reads=r, writes=w)

    def ts(eng, out, in0, s1, s2, op0, op1, r, w):
        if op1 is None:
            return P.add(eng, lambda e: e.tensor_scalar(out=out, in0=in0, scalar1=s1, scalar2=None, op0=op0), reads=r, writes=w)
        return P.add(eng, lambda e: e.tensor_scalar(out=out, in0=in0, scalar1=s1, scalar2=s2, op0=op0, op1=op1), reads=r, writes=w)

    def stt(eng, out, in0, scalar, in1, op0, op1, r, w):
        return P.add(eng, lambda e: e.scalar_tensor_tensor(out=out, in0=in0, scalar=scalar, in1=in1, op0=op0, op1=op1), reads=r, writes=w)

    def cp(eng, out, in_, r, w):
        if eng == "act":
            return P.add("act", lambda e: e.activation(out=out, in_=in_, func=AF.Copy), reads=r, writes=w)
        return P.add(eng, lambda e: e.tensor_copy(out=out, in_=in_), reads=r, writes=w)

    def memset(eng, ap, val, w):
        return P.add(eng, lambda e: e.memset(ap, val), writes=w)

    def ld(out, in_, owner, w, q="sp"):
        return P.add(q, lambda e: e.dma_start(out=out, in_=in_), writes=w, dma_owner=owner)

    def st(out, in_, owner, r):
        o = P.add("sp", lambda e: e.dma_start(out=out, in_=in_), reads=r, dma_owner=owner)
        stores.append(o)
        return o

    ident = sb("ident", [128, 128]); bident = Buf("ident")
    ld(ident[:], identd, bident, [bident])
    ones16 = sb("ones16", [128, 128], BF16); bones = Buf("ones")
    memset("dve", ones16[:], 1.0, [bones])
    jr = sb("jr", [128, TS + 1]); bjr = Buf("jr")
    ld(jr[:], jrow, bjr, [bjr])
    mk = sb("mk", [128, 4, 512], BF16); bmk = Buf("mk")
    mkn = mk[:, 0:2, :]; bmkn = bmk
    ld(mk[:, 0:2, :], maskP.rearrange("a p n -> p a n"), bmk, [bmk], q="pool")
    ld(mk[:, 2:4, :], maskS.rearrange("a p n -> p a n"), bmk, [bmk], q="pool")
    ident16 = sb("ident16", [128, 128], BF16); bident16 = Buf("ident16")
    cp("dve", ident16[:], ident[:], [bident], [bident16])
    rps = sb("rps", [128, 2, TSM]); brps = Buf("rps")
    ld(rps[:], ropeS.rearrange("a p n -> p a n"), brps, [brps])
    gf = sb("gf", [128, KT]); bgf = Buf("gf")
    ld(gf[:], gfT, bgf, [bgf])
    pat = sb("pat", [128, NB, LS]); bpat = Buf("pat")
    memset("dve", pat[:], 1.0, [bpat])
    memset("dve", pat[:, :, 0:1], 0.0, [bpat])

    modT1 = sb("modT1", [128, 48, 17]); bmod = Buf("modT")
    modD = nc.dram_tensor("modD", [NS, 128, 48 * 17], F32).ap()
    bmodD = [Buf(f"modD{i}") for i in range(NS)]
    csb = sb("csb", [128, KT, 17]); bcs = Buf("csb")
    sgc = sb("sgc", [128, KT, 17]); bsgc = Buf("sgc")
    ld(csb[:], cT.rearrange("(k p) n -> p k n", p=128), bcs, [bcs])
    act(sgc[:], csb[:], AF.Sigmoid, [bcs], [bsgc])
    tt("dve", csb[:], csb[:], sgc[:], ALU.mult, [bcs, bsgc], [bcs])
    bmt = sb("bmt", [128, NS, 48]); bbmt = Buf("bmt")
    ld(bmt[:], b_modT.rearrange("l p m -> p l m"), bbmt, [bbmt])
    WMB = 256
    _g0 = nc.sbuf_tensor("wmr0", [128, KT, WMB], F32)
    _g1 = nc.sbuf_tensor("wmr1", [128, KT, WMB], F32)
    wmr = [_g0.__enter__(), _g1.__enter__()]
    bwmr = [Buf(f"wmr{i}") for i in range(2)]
    lastmod = None
    it = 0
    for l in range(nl):
        for blk in range(6 * D // WMB):
            s = it % 2
            it += 1
            ld(wmr[s][:], w_mod[l, :, blk * WMB:(blk + 1) * WMB].rearrange("(k p) n -> p k n", p=128), bwmr[s], [bwmr[s]])
            for mi in range(WMB // 128):
                m = blk * (WMB // 128) + mi
                pi = nps()
                for k in range(KT):
                    mm(PS[pi][:, 0:17], wmr[s][:, k, mi * 128:(mi + 1) * 128], csb[:, k, :], k == 0, k == KT - 1,
                       [bwmr[s], bcs], [bPS[pi]])
                lastmod = ts("dve", modT1[:, m, :], PS[pi][:, 0:17], bmt[:, l, m:m + 1], None, ALU.add, None, [bPS[pi], bbmt], [bmod])
        lastmod = P.add("sp", lambda e, l=l: e.dma_start(out=modD[l], in_=modT1[:].rearrange("p m c -> p (m c)")),
                        reads=[bmod], writes=[bmodD[l]], dma_owner=bmod)
    _g1.__exit__(None, None, None)
    _g0.__exit__(None, None, None)
    for _e in ("pe", "act", "pool", "sp"):
        P.add(_e, lambda e: e.nop(), extra_deps=[lastmod])

    xs = sb("xs", [128, KT, T]); bx = [Buf(f"x{k}") for k in range(KT)]
    xsm = sb("xsm", [128, KT, TSM]); bxsm = Buf("xsm")
    ld(xsm[:], xsT.rearrange("(k p) n -> p k n", p=128), bxsm, [bxsm])
    h16 = sb("h16", [128, KT, T], BF16); bh = Buf("h16")
    scr16 = sb("scr16", [128, JT, T], BF16); bscr = Buf("scr16")
    rstd = sb("rstd", [128, T]); brstd = Buf("rstd")
    tmpAB = sb("tmpAB", [128, 2, T])
    tmpA = tmpAB[:, 0, :]; btA = Buf("tmpA")
    tmpB = tmpAB[:, 1, :]; btB = Buf("tmpB")
    tmpC = sb("tmpC", [128, T]); btC = Buf("tmpC")
    tmpD = sb("tmpD", [128, T]); btD = Buf("tmpD")
    rp = sb("rp", [128, 2, T]); brp = Buf("rp")
    q16 = sb("q16", [128, 4, T], BF16); bq = Buf("q16")
    kb16 = sb("kb16", [128, 128 + T], BF16); bkb = Buf("kb16")
    k32 = sb("k32", [128, T]); bk32 = Buf("k32")
    v16 = sb("v16", [128, 1 + T // 128, 128], BF16); bv16 = Buf("v16")
    v32 = sb("v32", [128, 128]); bv32 = Buf("v32")
    pown = sb("pown", [128, 512], BF16); bpown = Buf("pown")
    pprev = sb("pprev", [128, 512], BF16); bpprev = Buf("pprev")
    den = tmpAB[0:64].rearrange("p a t -> p (a t)")
    att16 = sb("att16", [64, 8, T], BF16); batt = Buf("att16")
    u32 = sb("u32", [128, 2, T]); bu32 = Buf("u32")
    u16 = sb("u16", [128, 2, T], BF16); bu16 = Buf("u16")
    cb32 = sb("cb32", [128, 2, 30 + T]); bcb32 = Buf("cb32")
    cb16 = sb("cb16", [128, 2, 30 + T], BF16); bcb16 = Buf("cb16")
    cbs32 = sb("cbs32", [128, 2, NB, 34]); bcbs32 = Buf("cbs32")
    cbs16 = sb("cbs16", [128, 2, NB, 34], BF16); bcbs16 = Buf("cbs16")
    ycf = sb("ycf", [128, 2, T]); bycf = Buf("ycf")
    cva = sb("cva", [128, T]); bcva = Buf("cva")
    cvb = sb("cvb", [128, T]); bcvb = Buf("cvb")
    cvc = sb("cvc", [128, 2, T]); bcvc = Buf("cvc")
    yc16 = scr16[:, 8:12, :]; byc16 = bscr
    oc16 = sb("oc16", [128, 2, T], BF16); boc = Buf("oc16")
    os16 = sb("os16", [128, 2, T], BF16); bos = Buf("os16")
    yss = sb("yss", [128, 2, T]); byss = Buf("yss")
    z32 = sb("z32", [128, 2, T]); bz32 = Buf("z32")
    assert 2 * T == 4 * 8 * 16
    z16 = sb("z16", [128, 2, T], BF16); bz16 = Buf("z16")
    sx = [sb(f"sx{i}", [128, TS]) for i in range(8)]
    bsx = [Buf(f"sx{i}") for i in range(8)]
    sq = [[sb(f"sq{i}{j}", [128, TS]) for j in range(2)] for i in range(2)]
    bsq = [[Buf(f"sq{i}{j}") for j in range(2)] for i in range(2)]
    sh16 = [[sb(f"sh16{i}{j}", [128, TS], BF16) for j in range(2)] for i in range(2)]
    bsh16 = [[Buf(f"sh16{i}{j}") for j in range(2)] for i in range(2)]
    ssi = [0]
    hre16 = sh16[0][0]; bhre = bsh16[0][0]
    him16 = sh16[0][1]; bhim = bsh16[0][1]
    qlre = sb("qlre", [128, 8]); qlim = sb("qlim", [128, 8]); bql = Buf("ql")
    hlre = sb("hlre", [128, 8]); hlim = sb("hlim", [128, 8]); bhl = Buf("hl")
    inre = sb("inre", [128, 8]); inim = sb("inim", [128, 8]); binit = Buf("init")
    sm8 = [sb(f"sm8_{i}", [128, 8]) for i in range(4)]; bsm8 = Buf("sm8")
    h0re = sb("h0re", [128, 8, NB]); h0im = sb("h0im", [128, 8, NB]); bh0 = Buf("h0")
    ahre = sb("ahre", [128, 8, NB]); ahim = sb("ahim", [128, 8, NB]); bah = Buf("ah")
    hsre = sb("hsre", [128, 8, NB]); hsim = sb("hsim", [128, 8, NB]); bhs = Buf("hs")
    st16 = [sb(f"st16_{i}", [128, NB]) for i in range(4)]; bst16 = Buf("st16")
    r_kc = sb("r_kc", [128, 2048], BF16); bkc = Buf("kc16")
    r_vc = sb("r_vc", [128, 2048], BF16); bvc = Buf("vc16")
    kc16 = r_kc[:].rearrange("p (b k) -> p b k", k=128)
    vc16 = r_vc[:].rearrange("p (b k) -> p b k", k=128)
    vn16 = sb("vn16", [4, NB, 128], BF16); bvn16 = Buf("vn16")
    vn32 = sb("vn32", [4, NB, 128]); bvn32 = Buf("vn32")
    pn16 = sb("pn16", [4, 512], BF16); bpn = Buf("pn16")

    win16 = sb("win16", [128, KT, WIN], BF16); bwin = Buf("win16")
    wgl16 = sb("wgl16", [128, 2, 256], BF16); bwgl = Buf("wgl16")
    sp8 = sb("sp8", [128, 12, 8]); bsp8 = Buf("sp8")
    sp2 = sb("sp2", [128, 8, 2]); bsp2 = Buf("sp2")
    cw = sb("cw", [128, 2, 31]); bcw = Buf("cw")
    sk = sb("sk", [64, 8]); bsk = Buf("sk")
    bc32 = yss[:].rearrange("p a (b c) -> p (a b) c", c=16).rearrange("p (a b) c -> p a b c", a=4); bbc = byss
    bbt = z32[:].rearrange("p a (b c) -> p (a b) c", c=16).rearrange("p (a b) c -> p a b c", a=4); bbbt = bz32
    exq = sb("exq", [128, 128]); bexq = Buf("exq")
    LB = sb("LB", [128, 4, 8, 128], BF16); bLB = Buf("LB")
    cosT = sb("cosT", [128, 8, TS + 1]); sinT = sb("sinT", [128, 8, TS + 1]); btab = Buf("tab")
    r4 = sb("r4", [128, 8, NB, LS]); btab4 = Buf("tab4")
    diag16 = sb("diag16", [128, 2, 31, 128], BF16); bdiag = Buf("diag16")
    gsc = sb("gsc", [128, 2, KT, 17]); bgsc = Buf("gsc")
    NG = 7
    _raw = [sb("wr0", [128, 2048], BF16), sb("wr1", [128, 2048], BF16), r_kc, r_vc, sb("wr4", [128, 2048], BF16),
            sb("wr5", [128, 2048], BF16), sb("wr6", [128, 2048], BF16)]
    bwgu = [Buf("wr0"), Buf("wr1"), bkc, bvc, Buf("wr4"), Buf("wr5"), Buf("wr6")]
    wfl = [r[:] for r in _raw]
    wgu = [r[:].rearrange("p (a k c) -> p a k c", a=2, k=KT) for r in _raw]
    wdn = [r[:, 0:1408].rearrange("p (j c) -> p j c", c=128) for r in _raw]
    bwdn = bwgu
    ffi = [0, 0]

    def S8(i):
        return sp8[:, i, :]

    angt = tmpC[:, 0:TS + 1]; angk = tmpD[:, 0:TS + 1]; bang = btC
    CM = 12582912.0

    def sin_of(out, x, shift, tmp, r, w, wt):
        xs_ = x
        if shift != 0.0:
            ts("dve", out, x, shift, None, ALU.add, None, r, w)
            xs_ = out
        ts("dve", tmp, xs_, 1.0 / TWO_PI, CM, ALU.mult, ALU.add, r + w, wt)
        ts("dve", tmp, tmp, -CM, None, ALU.add, None, r + wt, wt)
        stt("dve", tmp, tmp, -TWO_PI, xs_, ALU.mult, ALU.add, r + w + wt, wt)
        ts("dve", tmp, tmp, math.pi, -math.pi, ALU.min, ALU.max, r + wt, wt)
        act(out, tmp, AF.Sin, r + wt, w)

    def layer_params(l):
        for k in range(KT):
            ld(win16[:, k, :], w_in2[l, k * 128:(k + 1) * 128, :], bwin, [bwin], q="pool")
        ld(wgl16[:], wglu[l].rearrange("(k p) n -> p k n", p=128), bwgl, [bwgl], q="pool")
        ld(sp8[:, 0, :], lamre[l], bsp8, [bsp8])
        ld(sp8[:, 1, :], lamim[l], bsp8, [bsp8])
        ld(sp8[:, 2, :], logdt[l], bsp8, [bsp8])
        ld(sp2[:, 0, :], dskipT[l], bsp2, [bsp2])
        ld(sp2[:, 1, :], bgluT[l], bsp2, [bsp2])
        ld(sp2[:, 2, :], convbT[l], bsp2, [bsp2])
        ld(sp2[:, 3, :], lngT[l], bsp2, [bsp2])
        ld(sp2[:, 4, :], lnbT[l], bsp2, [bsp2])
        ld(cw[:], convwT[l], bcw, [bcw])
        ld(sk[:], sinkT[l], bsk, [bsk])
        act(sk[:], sk[:], AF.Exp, [bsk], [bsk])
        ld(bc32[:, 0], bre[l], bbc, [bbc])
        ld(bc32[:, 1], bim[l], bbc, [bbc])
        ld(bc32[:, 2], cre[l], bbc, [bbc])
        ld(bc32[:, 3], cim[l], bbc, [bbc])
        P.add("sp", lambda e, l=l: e.dma_start(out=modT1[:].rearrange("p m c -> p (m c)"), in_=modD[l]),
              reads=[bmodD[l]], writes=[bmod], dma_owner=bmod)
        for a, (gT, off) in enumerate(((g1T, 8), (g2T, 32))):
            ld(sp8[:, 3, :], gT[l], bsp8, [bsp8])
            ts("dve", gsc[:, a], modT1[:, off:off + 8, :], 1.0, None, ALU.add, None, [bmod], [bgsc])
            tt("dve", gsc[:, a], gsc[:, a], sp8[:, 3, :].unsqueeze(2).to_broadcast([128, KT, 17]), ALU.mult, [bgsc, bsp8], [bgsc])
        R = [bsp8]
        W = [bsp8]
        act(S8(2), S8(2), AF.Exp, R, W)
        tt("dve", S8(3), S8(0), S8(2), ALU.mult, R, W)
        tt("dve", S8(4), S8(1), S8(2), ALU.mult, R, W)
        act(S8(5), S8(3), AF.Exp, R, W)
        sin_of(S8(7), S8(4), 0.0, sm8[0][:], R + [bsm8], W, [bsm8])
        sin_of(S8(6), S8(4), 0.5 * math.pi, sm8[0][:], R + [bsm8], W, [bsm8])
        tt("dve", S8(8), S8(5), S8(6), ALU.mult, R, W)
        tt("dve", S8(9), S8(5), S8(7), ALU.mult, R, W)
        a0, a1, a2, a3 = (sm8[i][:] for i in range(4))
        R2 = [bsp8, bsm8]
        tt("dve", a0, S8(0), S8(0), ALU.mult, R2, [bsm8])
        tt("dve", a1, S8(1), S8(1), ALU.mult, R2, [bsm8])
        tt("dve", a0, a0, a1, ALU.add, R2, [bsm8])
        P.add("dve", lambda e: e.reciprocal(out=a0, in_=a0), reads=R2, writes=[bsm8])
        ts("dve", a1, S8(8), -1.0, None, ALU.add, None, R2, [bsm8])
        tt("dve", a2, a1, S8(0), ALU.mult, R2, [bsm8])
        tt("dve", a3, S8(9), S8(1), ALU.mult, R2, [bsm8])
        tt("dve", a2, a2, a3, ALU.add, R2, [bsm8])
        tt("dve", S8(10), a2, a0, ALU.mult, R2, W)
        tt("dve", a2, S8(9), S8(0), ALU.mult, R2, [bsm8])
        tt("dve", a3, a1, S8(1), ALU.mult, R2, [bsm8])
        tt("dve", a2, a2, a3, ALU.subtract, R2, [bsm8])
        tt("dve", S8(11), a2, a0, ALU.mult, R2, W)
        cre_b = sp8[:, 10, :].unsqueeze(2).to_broadcast([128, 8, 16])
        cim_b = sp8[:, 11, :].unsqueeze(2).to_broadcast([128, 8, 16])
        Rb = [bbc, bsp8, bbbt]
        tt("dve", bbt[:, 0], bc32[:, 0], cre_b, ALU.mult, Rb, [bbbt])
        tt("dve", bbt[:, 2], bc32[:, 1], cim_b, ALU.mult, Rb, [bbbt])
        tt("dve", bbt[:, 0], bbt[:, 0], bbt[:, 2], ALU.subtract, Rb, [bbbt])
        tt("dve", bbt[:, 1], bc32[:, 1], cre_b, ALU.mult, Rb, [bbbt])
        tt("dve", bbt[:, 2], bc32[:, 0], cim_b, ALU.mult, Rb, [bbbt])
        tt("dve", bbt[:, 1], bbt[:, 1], bbt[:, 2], ALU.add, Rb, [bbbt])
        cp("dve", bbt[:, 2], bc32[:, 2], Rb, [bbbt])
        ts("dve", bbt[:, 3], bc32[:, 3], -1.0, None, ALU.mult, None, Rb, [bbbt])
        xsf = xs[:].rearrange("p k t -> p (k t)")
        Eb = [xsf[:, 0:1024].rearrange("p (c n) -> p c n", n=128), xsf[:, 1024:2048].rearrange("p (c n) -> p c n", n=128)]
        bEb = [bx[0:4], bx[4:8]]
        for mi in range(4):
            E = Eb[mi % 2]
            bE = bEb[mi % 2]
            memset("dve", E, 0.0, bE)
            for ct in range(8):
                for gg in range(2):
                    gp = (2 * ct + gg) % 8
                    cp("dve", E[64 * gg:64 * gg + 64, ct, 16 * gp:16 * gp + 16], bbt[64 * gg:64 * gg + 64, mi, ct, :], [bbbt], bE)
            if mi < 2:
                for ct in range(8):
                    pi = nps()
                    P.add("pe", lambda e, pi=pi, E=E, ct=ct: e.transpose(out=PS[pi][:, 0:128], in_=E[:, ct, :], identity=ident[:]),
                          reads=bE + [bident], writes=[bPS[pi]])
                    cp("act", LB[:, mi, ct, :], PS[pi][:, 0:128], [bPS[pi]], [bLB])
            else:
                cp("act", LB[:, mi], E, bE, [bLB])
        angt3 = xsf[:, 0:8 * (TS + 1)].rearrange("p (c j) -> p c j", j=TS + 1)
        angk3 = scr16[:, 0:9, :].rearrange("p a b -> p (a b)").bitcast(F32)[:, 0:8 * (TS + 1)].rearrange("p (c j) -> p c j", j=TS + 1)
        tt("dve", angt3, jr[:].unsqueeze(1).to_broadcast([128, 8, TS + 1]), sp8[:, 4, :].unsqueeze(2).to_broadcast([128, 8, TS + 1]),
           ALU.mult, [bjr, bsp8], bx)
        sin_of(sinT[:], angt3, 0.0, angk3, bx, [btab], [bscr])
        sin_of(cosT[:], angt3, 0.5 * math.pi, angk3, bx, [btab], [bscr])
        for ct in range(8):
            ts("dve", r4[:, ct], pat[:], sp8[:, 5, ct:ct + 1], None, ALU.mult, None, [bpat, bsp8], [btab4])
        for m in range(2):
            for k in range(31):
                act(diag16[:, m, k, :], ident[:], AF.Identity, [bident, bcw], [bdiag], scale=cw[:, m, k:k + 1])

    def rmsnorm(xa, bxs, n, gcol, shm, a, l, sample):
        for k in range(KT):
            act(scr16[:, k, 0:n], xa[:, k, :], AF.Square, [bxs[k]], [bscr])
        pi = nps()
        for k in range(KT):
            mm(PS[pi][:, 0:n], ones16[:], scr16[:, k, 0:n], k == 0, k == KT - 1, [bones, bscr], [bPS[pi]])
        ts("dve", rstd[:, 0:n], PS[pi][:, 0:n], 1.0 / D, 1e-6, ALU.mult, ALU.add, [bPS[pi]], [brstd])
        act(rstd[:, 0:n], rstd[:, 0:n], AF.Sqrt, [brstd], [brstd])
        P.add("dve", lambda e: e.reciprocal(out=rstd[:, 0:n], in_=rstd[:, 0:n]), reads=[brstd], writes=[brstd])
        for k in range(KT):
            tA = tmpA if k % 2 == 0 else tmpB
            bA = btA if k % 2 == 0 else btB
            tt("dve", tA[:, 0:n], xa[:, k, :], rstd[:, 0:n], ALU.mult, [bxs[k], brstd], [bA])
            if gcol is None:
                ts("pool", xa[:, k, :], tA[:, 0:n], gf[:, k:k + 1], None, ALU.mult, None, [bA, bgf], [bxs[k]])
            elif not sample:
                if False:
                    pass
                else:
                    act(h16[:, k, 0:n], tA[:, 0:n], AF.Identity, [bA, bgsc, bmod], [bh],
                        bias=modT1[:, shm + k, 0:1], scale=gsc[:, a, k, 0:1])
            else:
                v3 = tA[:, 0:n].rearrange("p (b t) -> p b t", t=LS)
                if True:
                    tt("pool", v3, v3, gsc[:, a, k, 1:17].unsqueeze(2).to_broadcast([128, NB, LS]), ALU.mult, [bA, bgsc], [bA])
                    tt("pool", h16[:, k, 0:n].rearrange("p (b t) -> p b t", t=LS), v3,
                       modT1[:, shm + k, 1:17].unsqueeze(2).to_broadcast([128, NB, LS]), ALU.add, [bA, bmod], [bh])

    def resid(xa, bxs, n, mo, pi, l, gm, sample):
        if not sample:
            stt("dve", xa[:, mo, :], PS[pi][:, 0:n], modT1[:, gm + mo, 0:1], xa[:, mo, :], ALU.mult, ALU.add,
                [bPS[pi], bmod, bxs[mo]], [bxs[mo]])
        else:
            tt("dve", tmpC[:, 0:n].rearrange("p (b t) -> p b t", t=LS), PS[pi][:, 0:n].rearrange("p (b t) -> p b t", t=LS),
               modT1[:, gm + mo, 1:17].unsqueeze(2).to_broadcast([128, NB, LS]), ALU.mult, [bPS[pi], bmod], [btC])
            tt("dve", xa[:, mo, :], xa[:, mo, :], tmpC[:, 0:n], ALU.add, [btC, bxs[mo]], [bxs[mo]])

    def inproj_tile(m, n):
        pi = nps()
        for k in range(KT):
            mm(PS[pi][:, 0:n], win16[:, k, m * 128:(m + 1) * 128], h16[:, k, 0:n], k == 0, k == KT - 1, [bwin, bh], [bPS[pi]])
        return pi

    ri = [0]

    def rope_pair(m_a, m_b, cos_ap, sin_ap, brope, out_ap, bout, n, extra32=None):
        pa = inproj_tile(m_a, n)
        pb = inproj_tile(m_b, n)
        ri[0] += 1
        if ri[0] % 2 == 0:
            tX, bX, tY, bY = tmpA, btA, tmpB, btB
        else:
            tX, bX, tY, bY = tmpC, btC, tmpD, btD
        tt("dve", tX[:, 0:n], PS[pa][:, 0:n], cos_ap, ALU.mult, [bPS[pa], brope], [bX])
        tt("dve", tY[:, 0:n], PS[pb][:, 0:n], sin_ap, ALU.mult, [bPS[pb], brope], [bY])
        tt("pool", out_ap, tX[:, 0:n], tY[:, 0:n], ALU.add, [bX, bY], [bout])
        if extra32 is not None:
            tt("pool", extra32[0], tX[:, 0:n], tY[:, 0:n], ALU.add, [bX, bY], [extra32[1]])

    def gelu_glu(n):
        tmps = [(tmpC, btC), (tmpD, btD)]
        for step in range(3):
            for o in range(2):
                tq, btq = tmps[o]
                if step == 0:
                    act(tq[:, 0:n], yss[:, o, 0:n], AF.Square, [byss], [btq])
                elif step == 1:
                    ts("dve", tq[:, 0:n], tq[:, 0:n], 0.044715, 1.0, ALU.mult, ALU.add, [btq], [btq])
                else:
                    tt("dve", tq[:, 0:n], tq[:, 0:n], yss[:, o, 0:n], ALU.mult, [btq, byss], [btq])
        for o in range(2):
            tq, btq = tmps[o]
            act(tq[:, 0:n], tq[:, 0:n], AF.Sigmoid, [btq], [btq], scale=2.0 * math.sqrt(2.0 / math.pi))
        for o in range(2):
            tq, btq = tmps[o]
            tt("dve", z32[:, o, 0:n], yss[:, o, 0:n], tq[:, 0:n], ALU.mult, [byss, btq], [bz32])
            cp("act", z16[:, o, 0:n], z32[:, o, 0:n], [bz32], [bz16])
        pis = []
        for o in range(2):
            pi = nps()
            pis.append(pi)
            for k in range(2):
                mm(PS[pi][:, 0:n], wgl16[:, k, o * 128:(o + 1) * 128], z16[:, k, 0:n], k == 0, k == 1, [bwgl, bz16], [bPS[pi]])
        for o in range(2):
            tq, btq = tmps[o]
            act(tq[:, 0:n], PS[pis[o]][:, 0:n], AF.Sigmoid, [bPS[pis[o]], bsp2], [btq], bias=sp2[:, 1, o:o + 1])
        for o in range(2):
            tq, btq = tmps[o]
            tt("dve", os16[:, o, 0:n], z32[:, o, 0:n], tq[:, 0:n], ALU.mult, [bz32, btq], [bos])

    def conv_ln_stages(rhs_fn, n, view):
        pcs = {}

        def st_mm(m):
            def f():
                pi = nps()
                pcs[m] = pi
                for k in range(31):
                    mm(view(PS[pi][:, 0:n]), diag16[:, m, k, :], rhs_fn(m, k), k == 0, k == 30, [bdiag, bcb16, bcbs16], [bPS[pi]])
            return f

        def st_evac(m):
            def f():
                pi = pcs[m]
                act(ycf[:, m, 0:n], PS[pi][:, 0:n], AF.Identity, [bPS[pi], bsp2], [bycf], bias=sp2[:, 2, m:m + 1])
                act(yc16[:, 2 + m, 0:n], PS[pi][:, 0:n], AF.Square, [bPS[pi], bsp2], [byc16], bias=sp2[:, 2, m:m + 1])
            return f

        def st_cast(m):
            def f():
                cp("dve", yc16[:, m, 0:n], ycf[:, m, 0:n], [bycf], [byc16])
            return f

        def st_statmm():
            p1 = nps()
            for m in range(2):
                mm(PS[p1][:, 0:n], ones16[:], yc16[:, m, 0:n], m == 0, m == 1, [bones, byc16], [bPS[p1]])
            p2 = nps()
            for m in range(2):
                mm(PS[p2][:, 0:n], ones16[:], yc16[:, 2 + m, 0:n], m == 0, m == 1, [bones, byc16], [bPS[p2]])
            pcs["p1"] = p1
            pcs["p2"] = p2

        def st_stat1():
            p1, p2 = pcs["p1"], pcs["p2"]
            ts("dve", cva[:, 0:n], PS[p1][:, 0:n], 1.0 / 256, None, ALU.mult, None, [bPS[p1]], [bcva])
            tt("dve", cvb[:, 0:n], cva[:, 0:n], cva[:, 0:n], ALU.mult, [bcva], [bcvb])
            stt("dve", cvb[:, 0:n], PS[p2][:, 0:n], 1.0 / 256, cvb[:, 0:n], ALU.mult, ALU.subtract, [bPS[p2], bcvb], [bcvb])
            ts("dve", cvb[:, 0:n], cvb[:, 0:n], 1e-6, None, ALU.add, None, [bcvb], [bcvb])
            act(cvb[:, 0:n], cvb[:, 0:n], AF.Sqrt, [bcvb], [bcvb])

        def st_stat2():
            P.add("dve", lambda e: e.reciprocal(out=cvb[:, 0:n], in_=cvb[:, 0:n]), reads=[bcvb], writes=[bcvb])

        def st_apply1(m):
            def f():
                tt("dve", ycf[:, m, 0:n], ycf[:, m, 0:n], cva[:, 0:n], ALU.subtract, [bycf, bcva], [bycf])
                tt("dve", ycf[:, m, 0:n], ycf[:, m, 0:n], cvb[:, 0:n], ALU.mult, [bycf, bcvb], [bycf])
                act(ycf[:, m, 0:n], ycf[:, m, 0:n], AF.Identity, [bycf, bsp2], [bycf], bias=sp2[:, 4, m:m + 1], scale=sp2[:, 3, m:m + 1])
                act(cvc[:, m, 0:n], ycf[:, m, 0:n], AF.Sigmoid, [bycf], [bcvc])
            return f

        def st_apply2(m):
            def f():
                tt("dve", oc16[:, m, 0:n], ycf[:, m, 0:n], cvc[:, m, 0:n], ALU.mult, [bcvc, bycf], [boc])
            return f
        return [st_mm(0), st_mm(1), st_evac(0), st_evac(1), st_cast(0), st_cast(1), st_statmm, st_stat1, st_stat2,
                st_apply1(0), st_apply1(1), st_apply2(0), st_apply2(1)]

    def conv_ln(rhs_fn, n, view):
        for f in conv_ln_stages(rhs_fn, n, view):
            f()

    def outproj_ffn(xa, bxs, n, l, sample, first=False):
        for mo in range(8):
            s = ffi[0] % NG
            ffi[0] += 1
            wa_v = wfl[s][0:64, 0:1024].rearrange("p (h c) -> p h c", c=128)
            wr_v = wfl[s][:, 1024:1536].rearrange("p (h c) -> p h c", c=128)
            fa = wfl[s][0:64, 0:1024]
            fr = wfl[s][:, 1024:1536]
            bw = bwgu[s]
            if first:
                ld(wa_v, w_out[l, 0:512, mo * 128:(mo + 1) * 128].rearrange("(h d) n -> d h n", d=64), bw, [bw], q="pool")
                ld(wr_v, w_out[l, 512:1024, mo * 128:(mo + 1) * 128].rearrange("(j p) n -> p j n", p=128), bw, [bw], q="pool")
                P.add("sp", lambda e, fa=fa, mo=mo: e.dma_start(out=woa_c[mo], in_=fa), reads=[bw], writes=[bwo_c[mo]], dma_owner=bw)
                P.add("sp", lambda e, fr=fr, mo=mo: e.dma_start(out=wor_c[mo], in_=fr), reads=[bw], writes=[bwo_c[mo]], dma_owner=bw)
            else:
                P.add("sp", lambda e, fa=fa, mo=mo: e.dma_start(out=fa, in_=woa_c[mo]), reads=[bwo_c[mo]], writes=[bw], dma_owner=bw)
                P.add("sp", lambda e, fr=fr, mo=mo: e.dma_start(out=fr, in_=wor_c[mo]), reads=[bwo_c[mo]], writes=[bw], dma_owner=bw)
            pi = nps()
            for hq in range(8):
                mm(PS[pi][:, 0:n], wa_v[:, hq, :], att16[:, hq, 0:n], hq == 0, False, [bw, batt], [bPS[pi]])
            for j in range(2):
                mm(PS[pi][:, 0:n], wr_v[:, 2 + j, :], oc16[:, j, 0:n], False, False, [bw, boc], [bPS[pi]])
            for j in range(2):
                mm(PS[pi][:, 0:n], wr_v[:, j, :], os16[:, j, 0:n], False, j == 1, [bw, bos], [bPS[pi]])
            resid(xa, bxs, n, mo, pi, l, 16, sample)
        rmsnorm(xa, bxs, n, 1, 24, 1, l, sample)
        for j in range(JT):
            s = ffi[0] % NG
            ffi[0] += 1
            fg = wfl[s]
            if first:
                ld(wgu[s][:, 0], w_gate[l, :, j * 128:(j + 1) * 128].rearrange("(k p) n -> p k n", p=128), bwgu[s], [bwgu[s]], q="pool")
                ld(wgu[s][:, 1], w_up[l, :, j * 128:(j + 1) * 128].rearrange("(k p) n -> p k n", p=128), bwgu[s], [bwgu[s]], q="pool")
                P.add("sp", lambda e, fg=fg, j=j: e.dma_start(out=wgu_c[j], in_=fg), reads=[bwgu[s]], writes=[bwgu_c[j]], dma_owner=bwgu[s])
            else:
                P.add("sp", lambda e, fg=fg, j=j: e.dma_start(out=fg, in_=wgu_c[j]), reads=[bwgu_c[j]], writes=[bwgu[s]], dma_owner=bwgu[s])
            pg = nps()
            for k in range(KT):
                mm(PS[pg][:, 0:n], wgu[s][:, 0, k, :], h16[:, k, 0:n], k == 0, k == KT - 1, [bwgu[s], bh], [bPS[pg]])
            pu = nps()
            for k in range(KT):
                mm(PS[pu][:, 0:n], wgu[s][:, 1, k, :], h16[:, k, 0:n], k == 0, k == KT - 1, [bwgu[s], bh], [bPS[pu]])
            tA = tmpA if j % 2 == 0 else tmpB
            bA = btA if j % 2 == 0 else btB
            act(tA[:, 0:n], PS[pg][:, 0:n], AF.Silu, [bPS[pg]], [bA])
            tt("dve", scr16[:, j, 0:n], tA[:, 0:n], PS[pu][:, 0:n], ALU.mult, [bA, bPS[pu]], [bscr])
        for mo in range(8):
            pi = nps()
            for jh in range(2):
                s = ffi[0] % NG
                ffi[0] += 1
                ci = mo * 2 + jh
                fd = wfl[s][:, 0:1408]
                if first:
                    ld(wdn[s], w_down[l, jh * 1408:(jh + 1) * 1408, mo * 128:(mo + 1) * 128].rearrange("(j p) n -> p j n", p=128), bwdn[s], [bwdn[s]], q="pool")
                    P.add("sp", lambda e, fd=fd, ci=ci: e.dma_start(out=wdn_c[ci], in_=fd), reads=[bwdn[s]], writes=[bwdn_c[ci]], dma_owner=bwdn[s])
                else:
                    P.add("sp", lambda e, fd=fd, ci=ci: e.dma_start(out=fd, in_=wdn_c[ci]), reads=[bwdn_c[ci]], writes=[bwdn[s]], dma_owner=bwdn[s])
                for jj in range(11):
                    j = jh * 11 + jj
                    mm(PS[pi][:, 0:n], wdn[s][:, jj, :], scr16[:, j, 0:n], j == 0, j == JT - 1, [bwdn[s], bscr], [bPS[pi]])
            resid(xa, bxs, n, mo, pi, l, 40, sample)

    hp = sb("hp", [128, 2]); bhp = Buf("hp")
    ld(hp[:], hprev, bhp, [bhp])
    hst = ycf[:].rearrange("p a t -> p (a t)")[:, 0:HF]
    GROUPS = [[0, 4], [1, 5], [2, 6], [3, 7]]

    def slot_begin(l):
        if l == 0:
            memset("dve", kb16[:, 0:128], 0.0, [bkb])
            memset("dve", v16[:, 0, :], 0.0, [bv16])
            memset("dve", cb32[:, :, 0:30], 0.0, [bcb32])
            memset("dve", inre[:], 0.0, [binit])
            memset("dve", inim[:], 0.0, [binit])
            return
        P.add("sp", lambda e: e.dma_start(out=hst, in_=hall.ap()[0:128, :]), reads=[bhall], writes=[bycf], dma_owner=bycf)
        ts("dve", hst, hst, hp[:, 0:1], None, ALU.mult, None, [bycf, bhp], [bycf])
        cp("dve", kb16[:, 0:128], hst[:, 0:128], [bycf], [bkb])
        cp("dve", v16[:, 0, :], hst[:, 128:256], [bycf], [bv16])
        cp("dve", cb32[:, :, 0:30], hst[:, 256:316].rearrange("p (a r) -> p a r", a=2), [bycf], [bcb32])
        cp("dve", inre[:], hst[:, 316:324], [bycf], [binit])
        cp("dve", inim[:], hst[:, 324:332], [bycf], [binit])

    def slot_end(l):
        cp("dve", hst[:, 0:128], kb16[:, 0:128], [bkb], [bycf])
        cp("dve", hst[:, 128:256], v16[:, 0, :], [bv16], [bycf])
        cp("dve", hst[:, 256:316].rearrange("p (a r) -> p a r", a=2), cb32[:, :, 0:30], [bcb32], [bycf])
        cp("dve", hst[:, 316:324], inre[:], [binit], [bycf])
        cp("dve", hst[:, 324:332], inim[:], [binit], [bycf])
        P.add("sp", lambda e: e.dma_start(out=hin.ap(), in_=hst), reads=[bycf], writes=[bhin], dma_owner=bycf)
        P.add("pool", lambda e: e.collective_compute("AllGather", ALU.bypass, replica_groups=GROUPS,
                                                     ins=[hin.ap().opt()], outs=[hall.ap().opt()]),
              reads=[bhin], writes=[bhall], dma_owner=bhall, dinc=1)

    def prompt_chunk(l, c):
        n = T
        t0 = c * T
        src = xT if l == 0 else xscr
        xdr = src[:, t0:t0 + T].rearrange("(k p) t -> p k t", p=128)
        ld(xs[:], xdr, bx[0], bx)
        ld(rp[:], ropeP[:, :, t0:t0 + T].rearrange("a p t -> p a t"), brp, [brp])
        if KS2 < 1:
            return
        rmsnorm(xs, bx, n, 1, 0, 0, l, False)
        if KS2 < 2:
            return
        for j in range(4):
            rope_pair(j, 5 + j, rp[:, 0, :], rp[:, 1, :], brp, q16[:, j, :], bq, n)
        rope_pair(4, 9, rp[:, 0, :], rp[:, 1, :], brp, kb16[:, 128:128 + T], bkb, n, extra32=(k32[:, 0:n], bk32))
        if KS2 < 3:
            return
        for o in range(2):
            pi = inproj_tile(10 + o, n)
            cp("act", u32[:, o, :], PS[pi][:, 0:n], [bPS[pi]], [bu32])
            cp("dve", u16[:, o, :], PS[pi][:, 0:n], [bPS[pi]], [bu16])
        for o in range(2):
            pa = inproj_tile(12 + o, n)
            pg = inproj_tile(14 + o, n)
            tS, bS = (tmpA, btA) if o == 0 else (tmpB, btB)
            act(tS[:, 0:n], PS[pg][:, 0:n], AF.Sigmoid, [bPS[pg]], [bS])
            tt("dve", cb32[:, o, 30:30 + T], PS[pa][:, 0:n], tS[:, 0:n], ALU.mult, [bPS[pa], bS], [bcb32])
            cp("pool", cb16[:, o, :], cb32[:, o, :], [bcb32], [bcb16])
        if KS2 < 4:
            return
        for tb in range(T // 128):
            pi = nps()
            for k in range(KT):
                mm(PS[pi][:, 0:128], h16[:, k, tb * 128:(tb + 1) * 128], win16[:, k, 2048:2176], k == 0, k == KT - 1, [bh, bwin], [bPS[pi]])
            cp("act", v16[:, 1 + tb, :], PS[pi][:, 0:128], [bPS[pi]], [bv16])
            if c == NCH - 1 and tb == T // 128 - 1:
                cp("dve", v32[:], PS[pi][:, 0:128], [bPS[pi]], [bv32])
                st(nv[l], v32[:], bv32, [bv32])
        if c == NCH - 1:
            st(nkT[l], k32[:, T - 128:T], bk32, [bk32])
        if KSUB < 1:
            return
        blocks = [(qb_, hh_) for qb_ in range(T // 128) for hh_ in range(2)]

        def emit_scores(qb_, hh_):
            hs_ = slice(64 * hh_, 64 * hh_ + 64)
            qrhs_ = q16[hs_, :, qb_ * 128:(qb_ + 1) * 128]
            first_ = False
            po_ = nps()
            mm(PS[po_][:].rearrange("p (g q) -> p g q", g=4), kb16[hs_, 128 + qb_ * 128:128 + (qb_ + 1) * 128], qrhs_, True, False, [bkb, bq], [bPS[po_]])
            mm(PS[po_][:], ident16[:], mkn[:, 0, :], False, True, [bident16, bmkn], [bPS[po_]])
            pp_ = None
            if not first_:
                pp_ = nps()
                mm(PS[pp_][:].rearrange("p (g q) -> p g q", g=4), kb16[hs_, qb_ * 128:(qb_ + 1) * 128], qrhs_, True, False, [bkb, bq], [bPS[pp_]])
                mm(PS[pp_][:], ident16[:], mkn[:, 1, :], False, True, [bident16, bmkn], [bPS[pp_]])
            return po_, pp_
        pend = emit_scores(*blocks[0])
        for bi, (qb, hh) in enumerate(blocks):
            hs = slice(64 * hh, 64 * hh + 64)
            first = False
            po, pp = pend
            act(pown[:], PS[po][:], AF.Exp, [bPS[po]], [bpown], scale=0.125)
            if c == 0 and qb == 0:
                act(pprev[:], PS[pp][:], AF.Exp, [bPS[pp], bhp], [bpprev], scale=0.125, bias=hp[:, 1:2])
            else:
                act(pprev[:], PS[pp][:], AF.Exp, [bPS[pp]], [bpprev], scale=0.125)
            if bi + 1 < len(blocks):
                pend = emit_scores(*blocks[bi + 1])
            pO = 6
            pD = 7
            if not first:
                mm(PS[pO][0:64, :], v16[:, qb, hs], pprev[:], True, False, [bv16, bpprev], [bPS[pO]])
                mm(PS[pD][0:64, :], ones16[:, 0:64], pprev[:], True, False, [bones, bpprev], [bPS[pD]])
            mm(PS[pO][0:64, :], v16[:, qb + 1, hs], pown[:], first, True, [bv16, bpown], [bPS[pO]])
            mm(PS[pD][0:64, :], ones16[:, 0:64], pown[:], first, True, [bones, bpown], [bPS[pD]])
            tt("dve", den[:].rearrange("p (g q) -> p g q", g=4), PS[pD][0:64, :].rearrange("p (g q) -> p g q", g=4),
               sk[:, 4 * hh:4 * hh + 4].unsqueeze(2).to_broadcast([64, 4, 128]), ALU.add, [bPS[pD], bsk], [btA, btB])
            P.add("dve", lambda e: e.reciprocal(out=den[:], in_=den[:]), reads=[btA, btB], writes=[btA, btB])
            tt("dve", att16[:, 4 * hh:4 * hh + 4, qb * 128:(qb + 1) * 128], PS[pO][0:64, :].rearrange("p (g q) -> p g q", g=4),
               den[:].rearrange("p (g q) -> p g q", g=4), ALU.mult, [bPS[pO], btA, btB], [batt])
        cp("pool", kb16[:, 0:128], kb16[:, T:T + 128], [bkb], [bkb])
        cp("pool", v16[:, 0, :], v16[:, T // 128, :], [bv16], [bv16])
        if KSUB < 2:
            return
        pY = [6, 7]
        def emit_bu(sc_, ct_):
            pr_ = nps()
            pim_ = nps()
            mm(PS[pr_][:, 0:TS], LB[:, 0, ct_, :], u16[:, ct_ // 4, sc_ * TS:(sc_ + 1) * TS], True, True, [bLB, bu16], [bPS[pr_]])
            mm(PS[pim_][:, 0:TS], LB[:, 1, ct_, :], u16[:, ct_ // 4, sc_ * TS:(sc_ + 1) * TS], True, True, [bLB, bu16], [bPS[pim_]])
            return pr_, pim_
        iters = [(sc_, ct_) for sc_ in range(T // TS) for ct_ in range(8)]
        cstages = conv_ln_stages(lambda m, k: cb16[:, m, k:k + T], n, lambda ap: ap)
        csched = {0: [0], 1: [1], 2: [2], 3: [3], 4: [4], 5: [5], 6: [6], 8: [7], 9: [8], 10: [9], 11: [10], 13: [11], 14: [12]}
        pend = emit_bu(*iters[0])
        for sc in range(T // TS):
            c0 = sc * TS
            firstsub = (c == 0 and sc == 0)
            for ct in range(8):
                uh = ct // 4
                pr, pim = pend
                nxt = sc * 8 + ct + 1
                if nxt < len(iters):
                    pend = emit_bu(*iters[nxt])
                cs_ = cosT[:, ct, 0:TS]
                sn_ = sinT[:, ct, 0:TS]
                par = ssi[0] % 2
                ssi[0] += 1
                qA, bqA = sq[par][0], bsq[par][0]
                qB, bqB = sq[par][1], bsq[par][1]
                hA, bhA = sh16[par][0], bsh16[par][0]
                hB, bhB = sh16[par][1], bsh16[par][1]
                tt("dve", sx[0][:], PS[pr][:, 0:TS], cs_, ALU.mult, [bPS[pr], btab], [bsx[0]])
                tt("dve", sx[1][:], PS[pim][:, 0:TS], sn_, ALU.mult, [bPS[pim], btab], [bsx[1]])
                tt("dve", sx[2][:], PS[pim][:, 0:TS], cs_, ALU.mult, [bPS[pim], btab], [bsx[2]])
                tt("dve", sx[3][:], PS[pr][:, 0:TS], sn_, ALU.mult, [bPS[pr], btab], [bsx[3]])
                tt("dve", sx[0][:], sx[0][:], sx[1][:], ALU.add, [bsx[0], bsx[1]], [bsx[0]])
                tt("dve", sx[2][:], sx[2][:], sx[3][:], ALU.subtract, [bsx[2], bsx[3]], [bsx[2]])
                rbc = sp8[:, 5, ct:ct + 1].to_broadcast([128, TS])
                ire = inre[:, ct:ct + 1]
                iim = inim[:, ct:ct + 1]
                P.add("dve", lambda e, ire=ire, rbc=rbc, qA=qA: e.tensor_tensor_scan(out=qA[:], data0=rbc, data1=sx[0][:], initial=ire, op0=ALU.mult, op1=ALU.add),
                      reads=[bsp8, bsx[0], binit], writes=[bqA])
                P.add("dve", lambda e, iim=iim, rbc=rbc, qB=qB: e.tensor_tensor_scan(out=qB[:], data0=rbc, data1=sx[2][:], initial=iim, op0=ALU.mult, op1=ALU.add),
                      reads=[bsp8, bsx[2], binit], writes=[bqB])
                cp("pool", qlre[:, ct:ct + 1], qA[:, TS - 1:TS], [bqA], [bql])
                cp("pool", qlim[:, ct:ct + 1], qB[:, TS - 1:TS], [bqB], [bql])
                tt("pool", sx[6][:], qA[:], cs_, ALU.mult, [bqA, btab], [bsx[6]])
                tt("pool", sx[7][:], qB[:], sn_, ALU.mult, [bqB, btab], [bsx[7]])
                tt("pool", sx[4][:], qA[:], sn_, ALU.mult, [bqA, btab], [bsx[4]])
                tt("pool", sx[5][:], qB[:], cs_, ALU.mult, [bqB, btab], [bsx[5]])
                tt("pool", hA[:], sx[6][:], sx[7][:], ALU.subtract, [bsx[6], bsx[7]], [bhA])
                tt("pool", hB[:], sx[4][:], sx[5][:], ALU.add, [bsx[4], bsx[5]], [bhB])
                ot = ct // 4
                mm(PS[pY[ot]][:, c0:c0 + TS], LB[:, 2, ct, :], hA[:], ct % 4 == 0, False, [bLB, bhA], [bPS[pY[ot]]])
                mm(PS[pY[ot]][:, c0:c0 + TS], LB[:, 3, ct, :], hB[:], False, ct % 4 == 3, [bLB, bhB], [bPS[pY[ot]]])
                for si_ in csched.get(sc * 8 + ct, []):
                    cstages[si_]()
            a0, a1, a2, a3 = (sm8[i_][:] for i_ in range(4))
            cT_ = cosT[:, :, TS]
            sT_ = sinT[:, :, TS]
            tt("dve", a0, qlre[:], cT_, ALU.mult, [bql, btab], [bsm8])
            tt("dve", a1, qlim[:], sT_, ALU.mult, [bql, btab], [bsm8])
            tt("dve", a2, qlre[:], sT_, ALU.mult, [bql, btab], [bsm8])
            tt("dve", a3, qlim[:], cT_, ALU.mult, [bql, btab], [bsm8])
            tt("dve", inre[:], a0, a1, ALU.subtract, [bsm8], [binit])
            tt("dve", inim[:], a2, a3, ALU.add, [bsm8], [binit])
            if c == NCH - 1 and sc == T // TS - 1:
                cl = cosT[:, :, TS - 1]
                sl = sinT[:, :, TS - 1]
                tt("dve", a0, qlre[:], cl, ALU.mult, [bql, btab, binit], [bsm8])
                tt("dve", a1, qlim[:], sl, ALU.mult, [bql, btab], [bsm8])
                tt("dve", a2, qlre[:], sl, ALU.mult, [bql, btab], [bsm8])
                tt("dve", a3, qlim[:], cl, ALU.mult, [bql, btab], [bsm8])
                tt("dve", hlre[:], a0, a1, ALU.subtract, [bsm8], [bhl])
                tt("dve", hlim[:], a2, a3, ALU.add, [bsm8], [bhl])
                st(nre[l], hlre[:], bhl, [bhl])
                st(nim[l], hlim[:], bhl, [bhl])
        for o in range(2):
            stt("dve", yss[:, o, 0:n], u32[:, o, 0:n], sp2[:, 0, o:o + 1], PS[pY[o]][:, 0:n], ALU.mult, ALU.add, [bu32, bsp2, bPS[pY[o]]], [byss])
        gelu_glu(n)
        if KSUB < 3:
            return
        if c == NCH - 1:
            st(ncv[l], cb32[:, :, T:T + 30], bcb32, [bcb32])
        for o in range(2):
            cp("pool", cb32[:, o, 0:30], cb32[:, o, T:T + 30], [bcb32], [bcb32])
        if KSUB < 4:
            return
        outproj_ffn(xs, bx, n, l, False, first=(c == 0))
        if l < nl - 1:
            st(xscr[:, t0:t0 + T].rearrange("(k p) t -> p k t", p=128), xs[:], bx[0], bx)
        else:
            rmsnorm(xs, bx, n, None, 0, 0, l, False)
            st(yT[:, t0:t0 + T].rearrange("(k p) t -> p k t", p=128), xs[:], bx[0], bx)

    def sample_layer(l):
        n = TSM
        bxs = [bxsm] * KT
        ld(kc16, kcT[l].rearrange("b f k -> f b k"), bkc, [bkc], q="pool")
        ld(vc16, vc[l].rearrange("b k f -> k b f"), bvc, [bvc], q="pool")
        ld(h0re[:], ssmre_in[l], bh0, [bh0])
        ld(h0im[:], ssmim_in[l], bh0, [bh0])
        ld(cbs32[:, :, :, 0:30], sconv_in[l], bcbs32, [bcbs32])
        dd = Buf(f"dd{l}")
        o1 = P.add("sp", lambda e: e.dma_start(out=nks_c[l], in_=kcn[l, :, 4:128, :]), dma_owner=dd)
        stores.append(o1)
        vcn_src = vc[l, :, 4:128, :]
        o2 = P.add("sp", lambda e: e.dma_start(out=nvs_c[l], in_=vcn_src), dma_owner=dd)
        stores.append(o2)
        rmsnorm(xsm, bxs, n, 1, 0, 0, l, True)
        for j in range(4):
            rope_pair(j, 5 + j, rps[:, 0, :], rps[:, 1, :], brps, q16[:, j, 0:n], bq, n)
        rope_pair(4, 9, rps[:, 0, :], rps[:, 1, :], brps, kb16[:, 128:128 + n], bkb, n, extra32=(k32[:, 0:n], bk32))
        st(skT[l], k32[:, 0:n], bk32, [bk32])
        for o in range(2):
            pi = inproj_tile(10 + o, n)
            cp("act", u32[:, o, 0:n], PS[pi][:, 0:n], [bPS[pi]], [bu32])
            cp("dve", u16[:, o, 0:n], PS[pi][:, 0:n], [bPS[pi]], [bu16])
        for o in range(2):
            pa = inproj_tile(12 + o, n)
            pg = inproj_tile(14 + o, n)
            act(tmpC[:, 0:n], PS[pg][:, 0:n], AF.Sigmoid, [bPS[pg]], [btC])
            tt("dve", cbs32[:, o, :, 30:34], PS[pa][:, 0:n].rearrange("p (b t) -> p b t", t=LS),
               tmpC[:, 0:n].rearrange("p (b t) -> p b t", t=LS), ALU.mult, [bPS[pa], btC], [bcbs32])
            cp("pool", cbs16[:, o], cbs32[:, o], [bcbs32], [bcbs16])
        st(scv[l], cbs32[:, :, :, 4:34], bcbs32, [bcbs32])
        pv = nps()
        for b in range(NB):
            for k in range(KT):
                mm(PS[pv][0:4, :].rearrange("p (b f) -> p b f", b=4)[:, b % 4, :] if False else PS[pv][0:4, (b % 4) * 128:(b % 4 + 1) * 128],
                   h16[:, k, 4 * b:4 * b + 4], win16[:, k, 2048:2176], k == 0, k == KT - 1, [bh, bwin], [bPS[pv]])
            if b % 4 == 3:
                g0 = b - 3
                cp("act", vn16[:, g0:g0 + 4, :], PS[pv][0:4, :].rearrange("p (b f) -> p b f", b=4), [bPS[pv]], [bvn16])
                cp("dve", vn32[:, g0:g0 + 4, :], PS[pv][0:4, :].rearrange("p (b f) -> p b f", b=4), [bPS[pv]], [bvn32])
                if b < NB - 1:
                    pv = nps()
        st(svn[l], vn32[:], bvn32, [bvn32])
        pC = nps()
        pN = nps()
        for b in range(NB):
            for hh in range(2):
                hs = slice(64 * hh, 64 * hh + 64)
                col = (b * 2 + hh) * 16
                qrhs = q16[hs, :, 4 * b:4 * b + 4]
                mm(PS[pC][:, col:col + 16].rearrange("p (g t) -> p g t", g=4), kc16[hs, b, :], qrhs, True, True, [bkc, bq], [bPS[pC]])
                mm(PS[pN][0:4, col:col + 16].rearrange("p (g t) -> p g t", g=4), kb16[hs, 128 + 4 * b:128 + 4 * b + 4], qrhs, True, True, [bkb, bq], [bPS[pN]])
        act(pown[:], PS[pC][:], AF.Exp, [bPS[pC]], [bpown], scale=0.125)
        tt("pool", pown[:], pown[:], mk[:, 2, :], ALU.mult, [bpown, bmk], [bpown])
        act(pn16[:], PS[pN][0:4, :], AF.Exp, [bPS[pN]], [bpn], scale=0.125)
        tt("pool", pn16[:], pn16[:], mk[0:4, 3, :], ALU.mult, [bpn, bmk], [bpn])
        pO = nps()
        pD = nps()
        for b in range(NB):
            for hh in range(2):
                hs = slice(64 * hh, 64 * hh + 64)
                col = (b * 2 + hh) * 16
                mm(PS[pO][0:64, col:col + 16], vc16[:, b, hs], pown[:, col:col + 16], True, False, [bvc, bpown], [bPS[pO]])
                mm(PS[pO][0:64, col:col + 16], vn16[0:4, b, hs], pn16[0:4, col:col + 16], False, True, [bvn16, bpn], [bPS[pO]])
                mm(PS[pD][0:64, col:col + 16], ones16[:, 0:64], pown[:, col:col + 16], True, False, [bones, bpown], [bPS[pD]])
                mm(PS[pD][0:64, col:col + 16], ones16[0:4, 0:64], pn16[0:4, col:col + 16], False, True, [bones, bpn], [bPS[pD]])
        tt("dve", den[:].rearrange("p (b h t) -> p b h t", b=NB, t=LS), PS[pD][0:64, :].rearrange("p (b h t) -> p b h t", b=NB, t=LS),
           sk[:, :].unsqueeze(1).unsqueeze(3).to_broadcast([64, NB, 8, LS]), ALU.add, [bPS[pD], bsk], [btA, btB])
        P.add("dve", lambda e: e.reciprocal(out=den[:], in_=den[:]), reads=[btA, btB], writes=[btA, btB])
        tt("dve", att16[:, :, 0:n].rearrange("p h (b t) -> p b h t", t=LS), PS[pO][0:64, :].rearrange("p (b h t) -> p b h t", b=NB, t=LS),
           den[:].rearrange("p (b h t) -> p b h t", b=NB, t=LS), ALU.mult, [bPS[pO], btA, btB], [batt])
        for (dst, x1, y1, x2, y2, op) in ((ahre, 8, h0re, 9, h0im, ALU.subtract), (ahim, 8, h0im, 9, h0re, ALU.add)):
            tt("dve", dst[:], y1[:], sp8[:, x1, :].unsqueeze(2).to_broadcast([128, 8, NB]), ALU.mult, [bh0, bsp8], [bah])
            tt("dve", hsre[:], y2[:], sp8[:, x2, :].unsqueeze(2).to_broadcast([128, 8, NB]), ALU.mult, [bh0, bsp8], [bhs])
            tt("dve", dst[:], dst[:], hsre[:], op, [bah, bhs], [bah])
        pY = [6, 7]
        for ct in range(8):
            uh = ct // 4
            pr = nps()
            pim = nps()
            mm(PS[pr][:, 0:n], LB[:, 0, ct, :], u16[:, uh, 0:n], True, True, [bLB, bu16], [bPS[pr]])
            mm(PS[pim][:, 0:n], LB[:, 1, ct, :], u16[:, uh, 0:n], True, True, [bLB, bu16], [bPS[pim]])
            cs_ = cosT[:, ct, 0:LS].unsqueeze(1).to_broadcast([128, NB, LS])
            sn_ = sinT[:, ct, 0:LS].unsqueeze(1).to_broadcast([128, NB, LS])
            V3 = lambda ap: ap.rearrange("p (b t) -> p b t", t=LS)
            X = [s_[:, 0:n] for s_ in sx]
            tt("dve", V3(X[0]), V3(PS[pr][:, 0:n]), cs_, ALU.mult, [bPS[pr], btab], [bsx[0]])
            tt("dve", V3(X[1]), V3(PS[pim][:, 0:n]), sn_, ALU.mult, [bPS[pim], btab], [bsx[1]])
            tt("pool", X[0], X[0], X[1], ALU.add, [bsx[0], bsx[1]], [bsx[0]])
            tt("dve", V3(X[2]), V3(PS[pim][:, 0:n]), cs_, ALU.mult, [bPS[pim], btab], [bsx[2]])
            tt("dve", V3(X[3]), V3(PS[pr][:, 0:n]), sn_, ALU.mult, [bPS[pr], btab], [bsx[3]])
            tt("pool", X[2], X[2], X[3], ALU.subtract, [bsx[2], bsx[3]], [bsx[2]])
            tt("dve", sx[0][:, 0:n:LS], sx[0][:, 0:n:LS], ahre[:, ct, :], ALU.add, [bsx[0], bah], [bsx[0]])
            tt("dve", sx[2][:, 0:n:LS], sx[2][:, 0:n:LS], ahim[:, ct, :], ALU.add, [bsx[2], bah], [bsx[2]])
            r4v = r4[:, ct].rearrange("p b t -> p (b t)")
            P.add("dve", lambda e, r4v=r4v, X=X: e.tensor_tensor_scan(out=X[4], data0=r4v, data1=X[0], initial=0.0, op0=ALU.mult, op1=ALU.add),
                  reads=[btab4, bsx[0]], writes=[bsx[4]])
            P.add("dve", lambda e, r4v=r4v, X=X: e.tensor_tensor_scan(out=X[5], data0=r4v, data1=X[2], initial=0.0, op0=ALU.mult, op1=ALU.add),
                  reads=[btab4, bsx[2]], writes=[bsx[5]])
            tt("pool", V3(X[6]), V3(X[4]), cs_, ALU.mult, [bsx[4], btab], [bsx[6]])
            tt("pool", V3(X[7]), V3(X[5]), sn_, ALU.mult, [bsx[5], btab], [bsx[7]])
            tt("pool", X[6], X[6], X[7], ALU.subtract, [bsx[6], bsx[7]], [bsx[6]])
            cp("act", hre16[:, 0:n], X[6], [bsx[6]], [bhre])
            cp("act", hsre[:, ct, :], sx[6][:, LS - 1:n:LS], [bsx[6]], [bhs])
            tt("dve", V3(X[1]), V3(X[4]), sn_, ALU.mult, [bsx[4], btab], [bsx[1]])
            tt("dve", V3(X[3]), V3(X[5]), cs_, ALU.mult, [bsx[5], btab], [bsx[3]])
            tt("dve", X[1], X[1], X[3], ALU.add, [bsx[1], bsx[3]], [bsx[1]])
            cp("act", him16[:, 0:n], X[1], [bsx[1]], [bhim])
            cp("act", hsim[:, ct, :], sx[1][:, LS - 1:n:LS], [bsx[1]], [bhs])
            ot = ct // 4
            mm(PS[pY[ot]][:, 0:n], LB[:, 2, ct, :], hre16[:, 0:n], ct % 4 == 0, False, [bLB, bhre], [bPS[pY[ot]]])
            mm(PS[pY[ot]][:, 0:n], LB[:, 3, ct, :], him16[:, 0:n], False, ct % 4 == 3, [bLB, bhim], [bPS[pY[ot]]])
        st(sre[l], hsre[:], bhs, [bhs])
        st(sim_o[l], hsim[:], bhs, [bhs])
        for o in range(2):
            stt("dve", yss[:, o, 0:n], u32[:, o, 0:n], sp2[:, 0, o:o + 1], PS[pY[o]][:, 0:n], ALU.mult, ALU.add, [bu32, bsp2, bPS[pY[o]]], [byss])
        gelu_glu(n)
        conv_ln(lambda m, k: cbs16[:, m, :, k:k + LS], n, lambda ap: ap.rearrange("p (b t) -> p b t", t=LS))
        if DBG and l == 0:
            dsb = xs[:, 0:6, :].rearrange("p (a k) (h t) -> p a (k h) t", k=2, t=TSM); bdsb = bx[0]
            memset("dve", dsb, 0.0, bx)
            cp("dve", dsb[0:64, 0, :, :], att16[:, :, 0:n], [batt, bdsb], [bdsb])
            cp("dve", dsb[:, 1, 0:2, :], os16[:, :, 0:n], [bos, bdsb], [bdsb])
            cp("dve", dsb[:, 2, 0:2, :], oc16[:, :, 0:n], [boc, bdsb], [bdsb])
            st(dbg.rearrange("a p h t -> p a h t"), dsb, bdsb, [bdsb])
        outproj_ffn(xsm, bxs, n, l, True)
        if l == nl - 1:
            rmsnorm(xsm, bxs, n, None, 0, 0, l, True)
            st(ysT.rearrange("(k p) t -> p k t", p=128), xsm[:], bxsm, [bxsm])

    STG = int(os.environ.get("KSTAGE", "9"))
    for l in range(nl):
        if STG >= 1:
            layer_params(l)
        slot_begin(l)
        for c in range(NCH):
            if STG >= 3 or (STG == 2 and c == 0):
                prompt_chunk(l, c)
        if l < nl - 1:
            slot_end(l)
        if STG >= 4:
            sample_layer(l)

    P.add("sp", lambda e: e.nop(), extra_deps=stores)
    P.emit()
    return nc


def _perm_win(w):
    q = w[:, 0:512].reshape(D, 8, 64)
    k = w[:, 512:640].reshape(D, 2, 64)
    v = w[:, 640:768]
    u = w[:, 768:1024]
    a = w[:, 1024:1280]
    g = w[:, 1280:1536]

    def swap(t):
        return np.concatenate([t[..., 32:], t[..., :32]], axis=-1)
    order = [0, 4, 1, 5, 2, 6, 3, 7]
    qt = q[:, order, :].reshape(D, 512)
    qs = swap(q)[:, order, :].reshape(D, 512)
    kt = k.reshape(D, 128)
    ks = swap(k).reshape(D, 128)
    return np.ascontiguousarray(np.concatenate([qt, kt, qs, ks, u, a, g, v], axis=1))


def _rope_tab(pos):
    half = 32
    inv = (np.float32(10000.0) ** (-(np.arange(half, dtype=np.float32) / np.float32(half)))).astype(np.float32)
    ang = (pos.astype(np.float32)[None, :] * inv[:, None]).astype(np.float32)
    c = np.cos(ang.astype(np.float64)).astype(np.float32)
    s = np.sin(ang.astype(np.float64)).astype(np.float32)
    cos = np.concatenate([c, c, c, c], axis=0)
    sins = np.concatenate([-s, s, -s, s], axis=0)
    return np.ascontiguousarray(np.stack([cos, sins], axis=0))


_NC_CACHE = {}


def kernel(**inp):
    f = lambda a: np.ascontiguousarray(np.asarray(a, dtype=np.float32))
    I = {k: np.asarray(v) for k, v in inp.items()}
    nlr = _NC_CACHE.get("nl", NS)
    if "nc" not in _NC_CACHE:
        _NC_CACHE["nc"] = build(nlr)
    nc = _NC_CACHE["nc"]
    import os
    ncr = int(os.environ.get("KCORES", "8"))

    SLOT = {0: [0, 1, 2, 3, 0], 1: [0, 0, 1, 2, 3]}
    DUMMY = {0: 4, 1: 0}

    def pk(a):
        return a.reshape(NL, 8, 128).transpose(0, 2, 1)

    def p2(a):
        return a.reshape(NL, 2, 128).transpose(0, 2, 1)

    per_layer = {
        "w_mod": I["w_mod"],
        "b_modT": I["b_mod"].reshape(NL, 48, 128).transpose(0, 2, 1),
        "g1T": pk(I["norm1_g"]), "g2T": pk(I["norm2_g"]),
        "w_in2": np.stack([_perm_win(I["w_in"][l]) for l in range(NL)]),
        "sinkT": np.broadcast_to(I["attn_sinks"][:, None, :], (NL, 64, 8)),
        "lamre": I["ssm_lam_re"].reshape(NL, 8, 128).transpose(0, 2, 1),
        "lamim": I["ssm_lam_im"].reshape(NL, 8, 128).transpose(0, 2, 1),
        "logdt": np.repeat(I["ssm_log_dt"], 64, axis=1).reshape(NL, 8, 128).transpose(0, 2, 1),
        "bre": I["ssm_b_re"].reshape(NL, 8, 128, 16).transpose(0, 2, 1, 3),
        "bim": I["ssm_b_im"].reshape(NL, 8, 128, 16).transpose(0, 2, 1, 3),
        "cre": I["ssm_c_re"].reshape(NL, 8, 2, 16, 64).transpose(0, 2, 4, 1, 3).reshape(NL, 128, 8, 16),
        "cim": I["ssm_c_im"].reshape(NL, 8, 2, 16, 64).transpose(0, 2, 4, 1, 3).reshape(NL, 128, 8, 16),
        "dskipT": p2(I["ssm_d"]), "wglu": I["ssm_w_glu"], "bgluT": p2(I["ssm_b_glu"]),
        "convwT": I["conv_w"].reshape(NL, 31, 2, 128).transpose(0, 3, 2, 1),
        "convbT": p2(I["conv_b"]), "lngT": p2(I["conv_ln_g"]), "lnbT": p2(I["conv_ln_b"]),
        "w_out": I["w_out"], "w_gate": I["w_gate"], "w_up": I["w_up"], "w_down": I["w_down"],
    }
    role_w = {}
    for role in (0, 1):
        d = {}
        for k, a in per_layer.items():
            arr = f(np.asarray(a)[SLOT[role]])
            if k in ("w_out", "w_down"):
                arr[DUMMY[role]] = 0.0
            d[k] = arr
        role_w[role] = d
    const = {
        "gfT": f(I["final_norm_g"].reshape(8, 128).T),
        "ropeS": _rope_tab(PAST + np.tile(np.arange(LS), NB)),
        "identd": np.eye(128, dtype=np.float32),
        "jrow": f(np.broadcast_to(np.arange(TS + 1, dtype=np.float32)[None, :], (128, TS + 1))),
    }
    kk = np.arange(128)[:, None]
    qq = np.tile(np.arange(128), 4)[None, :]
    m_own = np.where(qq >= kk, 0.0, -30000.0).astype(np.float32)
    m_prev = np.where(kk > qq, 0.0, -30000.0).astype(np.float32)
    const["maskP"] = f(np.stack([m_own, m_prev]))
    tq = np.tile(np.arange(LS), 128)[None, :]
    m_c = (kk > tq).astype(np.float32)
    m_n = (kk <= tq).astype(np.float32)
    const["maskS"] = f(np.stack([m_c, m_n]))
    rope_role = {0: _rope_tab(np.arange(NTOK)), 1: _rope_tab(NTOK + np.arange(NTOK))}
    hp_role = {0: f(np.tile(np.array([[0.0, -1.0e4]], np.float32), (128, 1))),
               1: f(np.tile(np.array([[1.0, 0.0]], np.float32), (128, 1)))}

    in_maps = []
    for c in range(ncr):
        role = c // 4
        b = c % 4
        sbs = slice(NB * c, NB * (c + 1))
        sl = SLOT[role]
        m = dict(const)
        m.update(role_w[role])
        m["ropeP"] = rope_role[role]
        m["hprev"] = hp_role[role]
        m["xT"] = f(I["x_prompt"][b, role * NTOK:(role + 1) * NTOK].T)
        m["xsT"] = f(I["x_sample"][sbs].reshape(TSM, D).T)
        m["cT"] = f(np.concatenate([I["c_prompt"][b][None, :], I["c_sample"][sbs]], axis=0).T)
        ck = I["cache_k"][:, sbs].reshape(NL, NB, 128, 128)[sl]
        cv = I["cache_v"][:, sbs].reshape(NL, NB, 128, 128)[sl]
        m["kcT"] = f(ck.transpose(0, 1, 3, 2))
        m["kcn"] = f(ck)
        m["vc"] = f(cv)
        m["ssmre_in"] = f(I["state_ssm_re"][:, sbs].reshape(NL, NB, 8, 128).transpose(0, 3, 2, 1)[sl])
        m["ssmim_in"] = f(I["state_ssm_im"][:, sbs].reshape(NL, NB, 8, 128).transpose(0, 3, 2, 1)[sl])
        m["sconv_in"] = f(I["state_conv"][:, sbs].reshape(NL, NB, 30, 2, 128).transpose(0, 4, 3, 1, 2)[sl])
        in_maps.append(m)

    res = run_bass_kernel_spmd(nc, in_maps, core_ids=list(range(ncr)))
    R = list(res.results)
    while len(R) < 8:
        R.append(R[0])
    if "dbg" in R[0]:
        _NC_CACHE["dbg"] = R[0]["dbg"]
    SA = slice(0, 4)
    SB = slice(1, 5)

    y_prompt = np.stack([np.concatenate([R[b]["yT"].T, R[4 + b]["yT"].T], axis=0) for b in range(4)])
    y_sample = np.concatenate([R[c]["ysT"].T.reshape(NB, LS, D) for c in range(8)], axis=0)
    nk_p = np.stack([R[4 + b]["nkT"][SB].transpose(0, 2, 1).reshape(NL, 128, 2, 64) for b in range(4)], axis=1)
    nv_p = np.stack([R[4 + b]["nv"][SB].reshape(NL, 128, 2, 64) for b in range(4)], axis=1)

    def unst(a):
        return a.transpose(0, 2, 1).reshape(NL, 16, 64)
    re_p = np.stack([unst(R[4 + b]["nre"][SB]) for b in range(4)], axis=1)
    im_p = np.stack([unst(R[4 + b]["nim"][SB]) for b in range(4)], axis=1)
    cv_p = np.stack([R[4 + b]["ncv"][SB].transpose(0, 3, 2, 1).reshape(NL, 30, 256) for b in range(4)], axis=1)
    nk_s, nv_s, re_s, im_s, cv_s = [], [], [], [], []
    for c in range(8):
        r = R[c]
        S_ = SA if c < 4 else SB
        knew = r["skT"][S_].transpose(0, 2, 1).reshape(NL, NB, LS, 128)
        nk_s.append(np.concatenate([r["nks_c"][S_], knew], axis=2).reshape(NL, NB, 128, 2, 64))
        vnew = r["svn"][S_].transpose(0, 2, 1, 3)
        nv_s.append(np.concatenate([r["nvs_c"][S_], vnew], axis=2).reshape(NL, NB, 128, 2, 64))
        re_s.append(r["sre"][S_].transpose(0, 3, 2, 1).reshape(NL, NB, 16, 64))
        im_s.append(r["sim_o"][S_].transpose(0, 3, 2, 1).reshape(NL, NB, 16, 64))
        cv_s.append(r["scv"][S_].transpose(0, 3, 4, 2, 1).reshape(NL, NB, 30, 256))
    cat = lambda xs_: np.ascontiguousarray(np.concatenate(xs_, axis=1).astype(np.float32))
    outs = (y_prompt, y_sample, nk_p, nv_p, re_p, im_p, cv_p, cat(nk_s), cat(nv_s), cat(re_s), cat(im_s), cat(cv_s))
    return tuple(np.ascontiguousarray(o.astype(np.float32)) for o in outs)
```

```python
import math
import numpy as np
import concourse.bass as bass
import concourse.mybir as mybir
from concourse.bass_utils import run_bass_kernel_spmd

F32 = mybir.dt.float32
BF16 = mybir.dt.bfloat16
ALU = mybir.AluOpType
AF = mybir.ActivationFunctionType

SEG = 30000
NL = 4
NS = 5
HF = 332
D = 1024
KT = 8
NTOK = 2048
T = 256
NCH = NTOK // T
TS = 128
NB = 16
LS = 4
TSM = NB * LS
DFF = 2816
JT = 22
WIN = 2176
PAST = 8192
TWO_PI = 2.0 * math.pi


class Buf:
    __slots__ = ("name", "lw", "rd", "sem", "cnt", "excl")

    def __init__(self, name, excl=False):
        self.name = name
        self.excl = excl
        self.lw = None
        self.rd = {}
        self.sem = None
        self.cnt = 0


class Op:
    __slots__ = ("eng", "fn", "deps", "idx", "dma", "owner", "dcnt", "marked", "ev", "waits", "dinc")

    def __init__(self, eng, fn, idx):
        self.eng = eng
        self.fn = fn
        self.idx = idx
        self.deps = []
        self.dma = False
        self.owner = None
        self.dcnt = 0
        self.marked = False
        self.ev = None
        self.waits = []


class Prog:
    ENGS = ("pe", "act", "dve", "pool", "sp")

    def __init__(self, nc):
        self.nc = nc
        self.ops = []

    def add(self, eng, fn, reads=(), writes=(), dma_owner=None, extra_deps=(), dinc=16):
        i = len(self.ops)
        op = Op(eng, fn, i)
        if dma_owner is not None:
            op.dma = True
            op.owner = dma_owner
            op.dinc = dinc
            dma_owner.cnt += dinc
            op.dcnt = dma_owner.cnt
        deps = {}
        for b in reads:
            if b.lw is not None:
                deps[b.lw] = "raw"
            if b.excl:
                for r in b.rd.values():
                    if r not in deps:
                        deps[r] = "war"
        for b in writes:
            if b.lw is not None and b.lw not in deps:
                deps[b.lw] = "waw"
            for r in b.rd.values():
                if r not in deps:
                    deps[r] = "war"
        for d in extra_deps:
            deps[d.idx] = "raw"
        deps.pop(i, None)
        for b in reads:
            key = ("d", i) if op.dma else eng
            b.rd[key] = i
        for b in writes:
            b.lw = i
            b.rd = {}
        op.deps = list(deps.items())
        self.ops.append(op)
        return op

    def finalize(self):
        ops = self.ops
        waited = {e: {} for e in self.ENGS}
        for op in ops:
            need = {}
            for d, kind in op.deps:
                p = ops[d]
                if p.dma:
                    key = ("dma", id(p.owner))
                    if need.get(key, (0, None))[0] < p.dcnt:
                        need[key] = (p.dcnt, p)
                else:
                    if p.eng == op.eng and not op.dma:
                        if op.eng == "pe" or kind != "raw":
                            continue
                    key = ("eng", p.eng)
                    if need.get(key, (-1, None))[0] < p.idx:
                        need[key] = (p.idx, p)
            w = waited[op.eng]
            for key, (val, p) in need.items():
                if w.get(key, -1) >= val:
                    continue
                w[key] = val
                op.waits.append(p)
                if not p.dma:
                    p.marked = True
        cnt = {e: 0 for e in self.ENGS}
        for op in ops:
            if not op.dma and op.marked:
                cnt[op.eng] += 1
                op.ev = cnt[op.eng]
        self.evcount = cnt

    def emit(self):
        nc = self.nc
        self.finalize()
        esems = {}
        for e in self.ENGS:
            n = (self.evcount[e] + SEG - 1) // SEG
            esems[e] = [nc.alloc_semaphore(f"ev_{e}_{k}") for k in range(max(n, 1))]
        for op in self.ops:
            if op.dma and op.owner.sem is None:
                op.owner.sem = nc.alloc_semaphore("d_" + op.owner.name)

        def semval(p):
            if p.dma:
                return p.owner.sem, p.dcnt
            k = (p.ev - 1) // SEG
            return esems[p.eng][k], (p.ev - 1) % SEG + 1

        per = {e: [op for op in self.ops if op.eng == e] for e in self.ENGS}

        def run(eng, lst):
            for op in lst:
                for p in op.waits:
                    s, v = semval(p)
                    eng.wait_ge(s, v)
                ins = op.fn(eng)
                if op.dma:
                    ins.then_inc(op.owner.sem, op.dinc)
                elif op.marked:
                    s, _ = semval(op)
                    ins.then_inc(s, 1)

        with nc.Block() as block:
            @block.tensor
            def _(e):
                run(e, per["pe"])

            @block.scalar
            def _(e):
                run(e, per["act"])

            @block.vector
            def _(e):
                run(e, per["dve"])

            @block.gpsimd
            def _(e):
                run(e, per["pool"])

            @block.sync
            def _(e):
                run(e, per["sp"])


def build(nl=NS):
    import os
    KSUB = int(os.environ.get("KSUB", "9"))
    KS2 = int(os.environ.get("KS2", "9"))
    nc = bass.Bass("TRN2", target_bir_lowering=False)
    P = Prog(nc)
    stores = []

    def din(name, shape):
        return nc.dram_tensor(name, list(shape), F32, kind="ExternalInput").ap()

    def dout(name, shape):
        return nc.dram_tensor(name, list(shape), F32, kind="ExternalOutput").ap()

    def sb(name, shape, dt=F32):
        return nc.alloc_sbuf_tensor(name, list(shape), dt)

    xT = din("xT", [D, NTOK])
    xsT = din("xsT", [D, TSM])
    cT = din("cT", [D, 17])
    w_mod = din("w_mod", [NS, D, 6 * D])
    b_modT = din("b_modT", [NS, 128, 48])
    g1T = din("g1T", [NS, 128, KT])
    g2T = din("g2T", [NS, 128, KT])
    gfT = din("gfT", [128, KT])
    w_in2 = din("w_in2", [NS, D, WIN])
    ropeP = din("ropeP", [2, 128, NTOK])
    ropeS = din("ropeS", [2, 128, TSM])
    maskP = din("maskP", [2, 128, 512])
    maskS = din("maskS", [2, 128, 512])
    sinkT = din("sinkT", [NS, 64, 8])
    lamre = din("lamre", [NS, 128, 8])
    lamim = din("lamim", [NS, 128, 8])
    logdt = din("logdt", [NS, 128, 8])
    bre = din("bre", [NS, 128, 8, 16])
    bim = din("bim", [NS, 128, 8, 16])
    cre = din("cre", [NS, 128, 8, 16])
    cim = din("cim", [NS, 128, 8, 16])
    dskipT = din("dskipT", [NS, 128, 2])
    wglu = din("wglu", [NS, 256, 256])
    bgluT = din("bgluT", [NS, 128, 2])
    convwT = din("convwT", [NS, 128, 2, 31])
    convbT = din("convbT", [NS, 128, 2])
    lngT = din("lngT", [NS, 128, 2])
    lnbT = din("lnbT", [NS, 128, 2])
    w_out = din("w_out", [NS, D, D])
    w_gate = din("w_gate", [NS, D, DFF])
    w_up = din("w_up", [NS, D, DFF])
    w_down = din("w_down", [NS, DFF, D])
    kcT = din("kcT", [NS, NB, 128, 128])
    vc = din("vc", [NS, NB, 128, 128])
    kcn = din("kcn", [NS, NB, 128, 128])
    ssmre_in = din("ssmre_in", [NS, 128, 8, NB])
    ssmim_in = din("ssmim_in", [NS, 128, 8, NB])
    sconv_in = din("sconv_in", [NS, 128, 2, NB, 30])
    identd = din("identd", [128, 128])
    hprev = din("hprev", [128, 2])
    hin = nc.dram_tensor("hin", [128, HF], F32)
    hall = nc.dram_tensor("hall", [256, HF], F32)
    bhin = Buf("hin"); bhall = Buf("hall")
    jrow = din("jrow", [128, TS + 1])

    yT = dout("yT", [D, NTOK])
    ysT = dout("ysT", [D, TSM])
    nkT = dout("nkT", [NS, 128, 128])
    nv = dout("nv", [NS, 128, 128])
    nre = dout("nre", [NS, 128, 8])
    nim = dout("nim", [NS, 128, 8])
    ncv = dout("ncv", [NS, 128, 2, 30])
    nks_c = dout("nks_c", [NS, NB, 124, 128])
    nvs_c = dout("nvs_c", [NS, NB, 124, 128])
    skT = dout("skT", [NS, 128, TSM])
    svn = dout("svn", [NS, 4, NB, 128])
    sre = dout("sre", [NS, 128, 8, NB])
    sim_o = dout("sim_o", [NS, 128, 8, NB])
    scv = dout("scv", [NS, 128, 2, NB, 30])
    xscr = nc.dram_tensor("xscr", [D, NTOK], F32, kind="Internal").ap()
    wgu_c = nc.dram_tensor("wgu_c", [JT, 128, 2 * KT * 128], BF16, kind="Internal").ap()
    wdn_c = nc.dram_tensor("wdn_c", [16, 128, 11 * 128], BF16, kind="Internal").ap()
    woa_c = nc.dram_tensor("woa_c", [8, 64, 8 * 128], BF16, kind="Internal").ap()
    wor_c = nc.dram_tensor("wor_c", [8, 128, 4 * 128], BF16, kind="Internal").ap()
    bwgu_c = [Buf(f"wguc{j}") for j in range(JT)]
    bwdn_c = [Buf(f"wdnc{j}") for j in range(16)]
    bwo_c = [Buf(f"woc{j}") for j in range(8)]
    DBG = bool(int(os.environ.get("KDBG", "0")))
    if DBG:
        dbg = dout("dbg", [3, 128, 8, TSM])

    PS = [nc.alloc_psum_tensor(f"ps{i}", [128, 512], F32) for i in range(8)]
    bPS = [Buf(f"ps{i}", excl=True) for i in range(8)]
    psrr = [0]

    def nps():
        i = psrr[0]
        psrr[0] = (i + 1) % 6
        return i

    def mm(out, lhsT, rhs, start, stop, r, w):
        return P.add("pe", lambda e: e.matmul(out, lhsT=lhsT, rhs=rhs, start=start, stop=stop), reads=r, writes=w)

    def act(out, in_, func, r, w, bias=None, scale=None):
        kw = {}
        if bias is not None:
            kw["bias"] = bias
        if scale is not None:
            kw["scale"] = scale
        return P.add("act", lambda e: e.activation(out=out, in_=in_, func=func, **kw), reads=r, writes=w)

    def tt(eng, out, in0, in1, op, r, w):
        return P.add(eng, lambda e: e.tensor_tensor(out=out, in0=in0, in1=in1, op=op), reads=r, writes=w)

    def ts(eng, out, in0, s1, s2, op0, op1, r, w):
        if op1 is None:
            return P.add(eng, lambda e: e.tensor_scalar(out=out, in0=in0, scalar1=s1, scalar2=None, op0=op0), reads=r, writes=w)
        return P.add(eng, lambda e: e.tensor_scalar(out=out, in0=in0, scalar1=s1, scalar2=s2, op0=op0, op1=op1), reads=r, writes=w)

    def stt(eng, out, in0, scalar, in1, op0, op1, r, w):
        return P.add(eng, lambda e: e.scalar_tensor_tensor(out=out, in0=in0, scalar=scalar, in1=in1, op0=op0, op1=op1), reads=r, writes=w)

    def cp(eng, out, in_, r, w):
        if eng == "act":
            return P.add("act", lambda e: e.activation(out=out, in_=in_, func=AF.Copy), reads=r, writes=w)
        return P.add(eng, lambda e: e.tensor_copy(out=out, in_=in_), reads=r, writes=w)

    def memset(eng, ap, val, w):
        return P.add(eng, lambda e: e.memset(ap, val), writes=w)

    def ld(out, in_, owner, w, q="sp"):
        return P.add(q, lambda e: e.dma_start(out=out, in_=in_), writes=w, dma_owner=owner)

    def st(out, in_, owner, r):
        o = P.add("sp", lambda e: e.dma_start(out=out, in_=in_), reads=r, dma_owner=owner)
        stores.append(o)
        return o

    ident = sb("ident", [128, 128]); bident = Buf("ident")
    ld(ident[:], identd, bident, [bident])
    ones16 = sb("ones16", [128, 128], BF16); bones = Buf("ones")
    memset("dve", ones16[:], 1.0, [bones])
    jr = sb("jr", [128, TS + 1]); bjr = Buf("jr")
    ld(jr[:], jrow, bjr, [bjr])
    mk = sb("mk", [128, 4, 512], BF16); bmk = Buf("mk")
    mkn = mk[:, 0:2, :]; bmkn = bmk
    ld(mk[:, 0:2, :], maskP.rearrange("a p n -> p a n"), bmk, [bmk], q="pool")
    ld(mk[:, 2:4, :], maskS.rearrange("a p n -> p a n"), bmk, [bmk], q="pool")
    ident16 = sb("ident16", [128, 128], BF16); bident16 = Buf("ident16")
    cp("dve", ident16[:], ident[:], [bident], [bident16])
    rps = sb("rps", [128, 2, TSM]); brps = Buf("rps")
    ld(rps[:], ropeS.rearrange("a p n -> p a n"), brps, [brps])
    gf = sb("gf", [128, KT]); bgf = Buf("gf")
    ld(gf[:], gfT, bgf, [bgf])
    pat = sb("pat", [128, NB, LS]); bpat = Buf("pat")
    memset("dve", pat[:], 1.0, [bpat])
    memset("dve", pat[:, :, 0:1], 0.0, [bpat])

    modT1 = sb("modT1", [128, 48, 17]); bmod = Buf("modT")
    modD = nc.dram_tensor("modD", [NS, 128, 48 * 17], F32).ap()
    bmodD = [Buf(f"modD{i}") for i in range(NS)]
    csb = sb("csb", [128, KT, 17]); bcs = Buf("csb")
    sgc = sb("sgc", [128, KT, 17]); bsgc = Buf("sgc")
    ld(csb[:], cT.rearrange("(k p) n -> p k n", p=128), bcs, [bcs])
    act(sgc[:], csb[:], AF.Sigmoid, [bcs], [bsgc])
    tt("dve", csb[:], csb[:], sgc[:], ALU.mult, [bcs, bsgc], [bcs])
    bmt = sb("bmt", [128, NS, 48]); bbmt = Buf("bmt")
    ld(bmt[:], b_modT.rearrange("l p m -> p l m"), bbmt, [bbmt])
    WMB = 256
    _g0 = nc.sbuf_tensor("wmr0", [128, KT, WMB], F32)
    _g1 = nc.sbuf_tensor("wmr1", [128, KT, WMB], F32)
    wmr = [_g0.__enter__(), _g1.__enter__()]
    bwmr = [Buf(f"wmr{i}") for i in range(2)]
    lastmod = None
    it = 0
    for l in range(nl):
        for blk in range(6 * D // WMB):
            s = it % 2
            it += 1
            ld(wmr[s][:], w_mod[l, :, blk * WMB:(blk + 1) * WMB].rearrange("(k p) n -> p k n", p=128), bwmr[s], [bwmr[s]])
            for mi in range(WMB // 128):
                m = blk * (WMB // 128) + mi
                pi = nps()
                for k in range(KT):
                    mm(PS[pi][:, 0:17], wmr[s][:, k, mi * 128:(mi + 1) * 128], csb[:, k, :], k == 0, k == KT - 1,
                       [bwmr[s], bcs], [bPS[pi]])
                lastmod = ts("dve", modT1[:, m, :], PS[pi][:, 0:17], bmt[:, l, m:m + 1], None, ALU.add, None, [bPS[pi], bbmt], [bmod])
        lastmod = P.add("sp", lambda e, l=l: e.dma_start(out=modD[l], in_=modT1[:].rearrange("p m c -> p (m c)")),
                        reads=[bmod], writes=[bmodD[l]], dma_owner=bmod)
    _g1.__exit__(None, None, None)
    _g0.__exit__(None, None, None)
    for _e in ("pe", "act", "pool", "sp"):
        P.add(_e, lambda e: e.nop(), extra_deps=[lastmod])

    xs = sb("xs", [128, KT, T]); bx = [Buf(f"x{k}") for k in range(KT)]
    xsm = sb("xsm", [128, KT, TSM]); bxsm = Buf("xsm")
    ld(xsm[:], xsT.rearrange("(k p) n -> p k n", p=128), bxsm, [bxsm])
    h16 = sb("h16", [128, KT, T], BF16); bh = Buf("h16")
    scr16 = sb("scr16", [128, JT, T], BF16); bscr = Buf("scr16")
    rstd = sb("rstd", [128, T]); brstd = Buf("rstd")
    tmpAB = sb("tmpAB", [128, 2, T])
    tmpA = tmpAB[:, 0, :]; btA = Buf("tmpA")
    tmpB = tmpAB[:, 1, :]; btB = Buf("tmpB")
    tmpC = sb("tmpC", [128, T]); btC = Buf("tmpC")
    tmpD = sb("tmpD", [128, T]); btD = Buf("tmpD")
    rp = sb("rp", [128, 2, T]); brp = Buf("rp")
    q16 = sb("q16", [128, 4, T], BF16); bq = Buf("q16")
    kb16 = sb("kb16", [128, 128 + T], BF16); bkb = Buf("kb16")
    k32 = sb("k32", [128, T]); bk32 = Buf("k32")
    v16 = sb("v16", [128, 1 + T // 128, 128], BF16); bv16 = Buf("v16")
    v32 = sb("v32", [128, 128]); bv32 = Buf("v32")
    pown = sb("pown", [128, 512], BF16); bpown = Buf("pown")
    pprev = sb("pprev", [128, 512], BF16); bpprev = Buf("pprev")
    den = tmpAB[0:64].rearrange("p a t -> p (a t)")
    att16 = sb("att16", [64, 8, T], BF16); batt = Buf("att16")
    u32 = sb("u32", [128, 2, T]); bu32 = Buf("u32")
    u16 = sb("u16", [128, 2, T], BF16); bu16 = Buf("u16")
    cb32 = sb("cb32", [128, 2, 30 + T]); bcb32 = Buf("cb32")
    cb16 = sb("cb16", [128, 2, 30 + T], BF16); bcb16 = Buf("cb16")
    cbs32 = sb("cbs32", [128, 2, NB, 34]); bcbs32 = Buf("cbs32")
    cbs16 = sb("cbs16", [128, 2, NB, 34], BF16); bcbs16 = Buf("cbs16")
    ycf = sb("ycf", [128, 2, T]); bycf = Buf("ycf")
    cva = sb("cva", [128, T]); bcva = Buf("cva")
    cvb = sb("cvb", [128, T]); bcvb = Buf("cvb")
    cvc = sb("cvc", [128, 2, T]); bcvc = Buf("cvc")
    yc16 = scr16[:, 8:12, :]; byc16 = bscr
    oc16 = sb("oc16", [128, 2, T], BF16); boc = Buf("oc16")
    os16 = sb("os16", [128, 2, T], BF16); bos = Buf("os16")
    yss = sb("yss", [128, 2, T]); byss = Buf("yss")
    z32 = sb("z32", [128, 2, T]); bz32 = Buf("z32")
    assert 2 * T == 4 * 8 * 16
    z16 = sb("z16", [128, 2, T], BF16); bz16 = Buf("z16")
    sx = [sb(f"sx{i}", [128, TS]) for i in range(8)]
    bsx = [Buf(f"sx{i}") for i in range(8)]
    sq = [[sb(f"sq{i}{j}", [128, TS]) for j in range(2)] for i in range(2)]
    bsq = [[Buf(f"sq{i}{j}") for j in range(2)] for i in range(2)]
    sh16 = [[sb(f"sh16{i}{j}", [128, TS], BF16) for j in range(2)] for i in range(2)]
    bsh16 = [[Buf(f"sh16{i}{j}") for j in range(2)] for i in range(2)]
    ssi = [0]
    hre16 = sh16[0][0]; bhre = bsh16[0][0]
    him16 = sh16[0][1]; bhim = bsh16[0][1]
    qlre = sb("qlre", [128, 8]); qlim = sb("qlim", [128, 8]); bql = Buf("ql")
    hlre = sb("hlre", [128, 8]); hlim = sb("hlim", [128, 8]); bhl = Buf("hl")
    inre = sb("inre", [128, 8]); inim = sb("inim", [128, 8]); binit = Buf("init")
    sm8 = [sb(f"sm8_{i}", [128, 8]) for i in range(4)]; bsm8 = Buf("sm8")
    h0re = sb("h0re", [128, 8, NB]); h0im = sb("h0im", [128, 8, NB]); bh0 = Buf("h0")
    ahre = sb("ahre", [128, 8, NB]); ahim = sb("ahim", [128, 8, NB]); bah = Buf("ah")
    hsre = sb("hsre", [128, 8, NB]); hsim = sb("hsim", [128, 8, NB]); bhs = Buf("hs")
    st16 = [sb(f"st16_{i}", [128, NB]) for i in range(4)]; bst16 = Buf("st16")
    r_kc = sb("r_kc", [128, 2048], BF16); bkc = Buf("kc16")
    r_vc = sb("r_vc", [128, 2048], BF16); bvc = Buf("vc16")
    kc16 = r_kc[:].rearrange("p (b k) -> p b k", k=128)
    vc16 = r_vc[:].rearrange("p (b k) -> p b k", k=128)
    vn16 = sb("vn16", [4, NB, 128], BF16); bvn16 = Buf("vn16")
    vn32 = sb("vn32", [4, NB, 128]); bvn32 = Buf("vn32")
    pn16 = sb("pn16", [4, 512], BF16); bpn = Buf("pn16")

    win16 = sb("win16", [128, KT, WIN], BF16); bwin = Buf("win16")
    wgl16 = sb("wgl16", [128, 2, 256], BF16); bwgl = Buf("wgl16")
    sp8 = sb("sp8", [128, 12, 8]); bsp8 = Buf("sp8")
    sp2 = sb("sp2", [128, 8, 2]); bsp2 = Buf("sp2")
    cw = sb("cw", [128, 2, 31]); bcw = Buf("cw")
    sk = sb("sk", [64, 8]); bsk = Buf("sk")
    bc32 = yss[:].rearrange("p a (b c) -> p (a b) c", c=16).rearrange("p (a b) c -> p a b c", a=4); bbc = byss
    bbt = z32[:].rearrange("p a (b c) -> p (a b) c", c=16).rearrange("p (a b) c -> p a b c", a=4); bbbt = bz32
    exq = sb("exq", [128, 128]); bexq = Buf("exq")
    LB = sb("LB", [128, 4, 8, 128], BF16); bLB = Buf("LB")
    cosT = sb("cosT", [128, 8, TS + 1]); sinT = sb("sinT", [128, 8, TS + 1]); btab = Buf("tab")
    r4 = sb("r4", [128, 8, NB, LS]); btab4 = Buf("tab4")
    diag16 = sb("diag16", [128, 2, 31, 128], BF16); bdiag = Buf("diag16")
    gsc = sb("gsc", [128, 2, KT, 17]); bgsc = Buf("gsc")
    NG = 8
    _raw = [sb("wr0", [128, 2048], BF16), sb("wr1", [128, 2048], BF16), r_kc, r_vc, sb("wr4", [128, 2048], BF16),
            sb("wr5", [128, 2048], BF16), sb("wr6", [128, 2048], BF16), sb("wr7", [128, 2048], BF16)]
    bwgu = [Buf("wr0"), Buf("wr1"), bkc, bvc, Buf("wr4"), Buf("wr5"), Buf("wr6"), Buf("wr7")]
    wfl = [r[:] for r in _raw]
    wgu = [r[:].rearrange("p (a k c) -> p a k c", a=2, k=KT) for r in _raw]
    wdn = [r[:, 0:1408].rearrange("p (j c) -> p j c", c=128) for r in _raw]
    bwdn = bwgu
    ffi = [0, 0]

    def S8(i):
        return sp8[:, i, :]

    angt = tmpC[:, 0:TS + 1]; angk = tmpD[:, 0:TS + 1]; bang = btC
    CM = 12582912.0

    def sin_of(out, x, shift, tmp, r, w, wt):
        xs_ = x
        if shift != 0.0:
            ts("dve", out, x, shift, None, ALU.add, None, r, w)
            xs_ = out
        ts("dve", tmp, xs_, 1.0 / TWO_PI, CM, ALU.mult, ALU.add, r + w, wt)
        ts("dve", tmp, tmp, -CM, None, ALU.add, None, r + wt, wt)
        stt("dve", tmp, tmp, -TWO_PI, xs_, ALU.mult, ALU.add, r + w + wt, wt)
        ts("dve", tmp, tmp, math.pi, -math.pi, ALU.min, ALU.max, r + wt, wt)
        act(out, tmp, AF.Sin, r + wt, w)

    def layer_params(l):
        for k in range(KT):
            ld(win16[:, k, :], w_in2[l, k * 128:(k + 1) * 128, :], bwin, [bwin], q="pool")
        ld(wgl16[:], wglu[l].rearrange("(k p) n -> p k n", p=128), bwgl, [bwgl], q="pool")
        ld(sp8[:, 0, :], lamre[l], bsp8, [bsp8])
        ld(sp8[:, 1, :], lamim[l], bsp8, [bsp8])
        ld(sp8[:, 2, :], logdt[l], bsp8, [bsp8])
        ld(sp2[:, 0, :], dskipT[l], bsp2, [bsp2])
        ld(sp2[:, 1, :], bgluT[l], bsp2, [bsp2])
        ld(sp2[:, 2, :], convbT[l], bsp2, [bsp2])
        ld(sp2[:, 3, :], lngT[l], bsp2, [bsp2])
        ld(sp2[:, 4, :], lnbT[l], bsp2, [bsp2])
        ld(cw[:], convwT[l], bcw, [bcw])
        ld(sk[:], sinkT[l], bsk, [bsk])
        act(sk[:], sk[:], AF.Exp, [bsk], [bsk])
        ld(bc32[:, 0], bre[l], bbc, [bbc])
        ld(bc32[:, 1], bim[l], bbc, [bbc])
        ld(bc32[:, 2], cre[l], bbc, [bbc])
        ld(bc32[:, 3], cim[l], bbc, [bbc])
        P.add("sp", lambda e, l=l: e.dma_start(out=modT1[:].rearrange("p m c -> p (m c)"), in_=modD[l]),
              reads=[bmodD[l]], writes=[bmod], dma_owner=bmod)
        for a, (gT, off) in enumerate(((g1T, 8), (g2T, 32))):
            ld(sp8[:, 3, :], gT[l], bsp8, [bsp8])
            ts("dve", gsc[:, a], modT1[:, off:off + 8, :], 1.0, None, ALU.add, None, [bmod], [bgsc])
            tt("dve", gsc[:, a], gsc[:, a], sp8[:, 3, :].unsqueeze(2).to_broadcast([128, KT, 17]), ALU.mult, [bgsc, bsp8], [bgsc])
        R = [bsp8]
        W = [bsp8]
        act(S8(2), S8(2), AF.Exp, R, W)
        tt("dve", S8(3), S8(0), S8(2), ALU.mult, R, W)
        tt("dve", S8(4), S8(1), S8(2), ALU.mult, R, W)
        act(S8(5), S8(3), AF.Exp, R, W)
        sin_of(S8(7), S8(4), 0.0, sm8[0][:], R + [bsm8], W, [bsm8])
        sin_of(S8(6), S8(4), 0.5 * math.pi, sm8[0][:], R + [bsm8], W, [bsm8])
        tt("dve", S8(8), S8(5), S8(6), ALU.mult, R, W)
        tt("dve", S8(9), S8(5), S8(7), ALU.mult, R, W)
        a0, a1, a2, a3 = (sm8[i][:] for i in range(4))
        R2 = [bsp8, bsm8]
        tt("dve", a0, S8(0), S8(0), ALU.mult, R2, [bsm8])
        tt("dve", a1, S8(1), S8(1), ALU.mult, R2, [bsm8])
        tt("dve", a0, a0, a1, ALU.add, R2, [bsm8])
        P.add("dve", lambda e: e.reciprocal(out=a0, in_=a0), reads=R2, writes=[bsm8])
        ts("dve", a1, S8(8), -1.0, None, ALU.add, None, R2, [bsm8])
        tt("dve", a2, a1, S8(0), ALU.mult, R2, [bsm8])
        tt("dve", a3, S8(9), S8(1), ALU.mult, R2, [bsm8])
        tt("dve", a2, a2, a3, ALU.add, R2, [bsm8])
        tt("dve", S8(10), a2, a0, ALU.mult, R2, W)
        tt("dve", a2, S8(9), S8(0), ALU.mult, R2, [bsm8])
        tt("dve", a3, a1, S8(1), ALU.mult, R2, [bsm8])
        tt("dve", a2, a2, a3, ALU.subtract, R2, [bsm8])
        tt("dve", S8(11), a2, a0, ALU.mult, R2, W)
        cre_b = sp8[:, 10, :].unsqueeze(2).to_broadcast([128, 8, 16])
        cim_b = sp8[:, 11, :].unsqueeze(2).to_broadcast([128, 8, 16])
        Rb = [bbc, bsp8, bbbt]
        tt("dve", bbt[:, 0], bc32[:, 0], cre_b, ALU.mult, Rb, [bbbt])
        tt("dve", bbt[:, 2], bc32[:, 1], cim_b, ALU.mult, Rb, [bbbt])
        tt("dve", bbt[:, 0], bbt[:, 0], bbt[:, 2], ALU.subtract, Rb, [bbbt])
        tt("dve", bbt[:, 1], bc32[:, 1], cre_b, ALU.mult, Rb, [bbbt])
        tt("dve", bbt[:, 2], bc32[:, 0], cim_b, ALU.mult, Rb, [bbbt])
        tt("dve", bbt[:, 1], bbt[:, 1], bbt[:, 2], ALU.add, Rb, [bbbt])
        cp("dve", bbt[:, 2], bc32[:, 2], Rb, [bbbt])
        ts("dve", bbt[:, 3], bc32[:, 3], -1.0, None, ALU.mult, None, Rb, [bbbt])
        xsf = xs[:].rearrange("p k t -> p (k t)")
        Eb = [xsf[:, 0:1024].rearrange("p (c n) -> p c n", n=128), xsf[:, 1024:2048].rearrange("p (c n) -> p c n", n=128)]
        bEb = [bx[0:4], bx[4:8]]
        for mi in range(4):
            E = Eb[mi % 2]
            bE = bEb[mi % 2]
            memset("dve", E, 0.0, bE)
            for ct in range(8):
                for gg in range(2):
                    gp = (2 * ct + gg) % 8
                    cp("dve", E[64 * gg:64 * gg + 64, ct, 16 * gp:16 * gp + 16], bbt[64 * gg:64 * gg + 64, mi, ct, :], [bbbt], bE)
            if mi < 2:
                for ct in range(8):
                    pi = nps()
                    P.add("pe", lambda e, pi=pi, E=E, ct=ct: e.transpose(out=PS[pi][:, 0:128], in_=E[:, ct, :], identity=ident[:]),
                          reads=bE + [bident], writes=[bPS[pi]])
                    cp("act", LB[:, mi, ct, :], PS[pi][:, 0:128], [bPS[pi]], [bLB])
            else:
                cp("act", LB[:, mi], E, bE, [bLB])
        angt3 = xsf[:, 0:8 * (TS + 1)].rearrange("p (c j) -> p c j", j=TS + 1)
        angk3 = scr16[:, 0:9, :].rearrange("p a b -> p (a b)").bitcast(F32)[:, 0:8 * (TS + 1)].rearrange("p (c j) -> p c j", j=TS + 1)
        tt("dve", angt3, jr[:].unsqueeze(1).to_broadcast([128, 8, TS + 1]), sp8[:, 4, :].unsqueeze(2).to_broadcast([128, 8, TS + 1]),
           ALU.mult, [bjr, bsp8], bx)
        sin_of(sinT[:], angt3, 0.0, angk3, bx, [btab], [bscr])
        sin_of(cosT[:], angt3, 0.5 * math.pi, angk3, bx, [btab], [bscr])
        for ct in range(8):
            ts("dve", r4[:, ct], pat[:], sp8[:, 5, ct:ct + 1], None, ALU.mult, None, [bpat, bsp8], [btab4])
        for m in range(2):
            for k in range(31):
                act(diag16[:, m, k, :], ident[:], AF.Identity, [bident, bcw], [bdiag], scale=cw[:, m, k:k + 1])

    def rmsnorm(xa, bxs, n, gcol, shm, a, l, sample):
        for k in range(KT):
            act(scr16[:, k, 0:n], xa[:, k, :], AF.Square, [bxs[k]], [bscr])
        pi = nps()
        for k in range(KT):
            mm(PS[pi][:, 0:n], ones16[:], scr16[:, k, 0:n], k == 0, k == KT - 1, [bones, bscr], [bPS[pi]])
        ts("dve", rstd[:, 0:n], PS[pi][:, 0:n], 1.0 / D, 1e-6, ALU.mult, ALU.add, [bPS[pi]], [brstd])
        act(rstd[:, 0:n], rstd[:, 0:n], AF.Sqrt, [brstd], [brstd])
        P.add("dve", lambda e: e.reciprocal(out=rstd[:, 0:n], in_=rstd[:, 0:n]), reads=[brstd], writes=[brstd])
        for k in range(KT):
            tA = tmpA if k % 2 == 0 else tmpB
            bA = btA if k % 2 == 0 else btB
            tt("dve", tA[:, 0:n], xa[:, k, :], rstd[:, 0:n], ALU.mult, [bxs[k], brstd], [bA])
            if gcol is None:
                ts("pool", xa[:, k, :], tA[:, 0:n], gf[:, k:k + 1], None, ALU.mult, None, [bA, bgf], [bxs[k]])
            elif not sample:
                if False:
                    pass
                else:
                    act(h16[:, k, 0:n], tA[:, 0:n], AF.Identity, [bA, bgsc, bmod], [bh],
                        bias=modT1[:, shm + k, 0:1], scale=gsc[:, a, k, 0:1])
            else:
                v3 = tA[:, 0:n].rearrange("p (b t) -> p b t", t=LS)
                if True:
                    tt("pool", v3, v3, gsc[:, a, k, 1:17].unsqueeze(2).to_broadcast([128, NB, LS]), ALU.mult, [bA, bgsc], [bA])
                    tt("pool", h16[:, k, 0:n].rearrange("p (b t) -> p b t", t=LS), v3,
                       modT1[:, shm + k, 1:17].unsqueeze(2).to_broadcast([128, NB, LS]), ALU.add, [bA, bmod], [bh])

    def resid(xa, bxs, n, mo, pi, l, gm, sample):
        if not sample:
            stt("dve", xa[:, mo, :], PS[pi][:, 0:n], modT1[:, gm + mo, 0:1], xa[:, mo, :], ALU.mult, ALU.add,
                [bPS[pi], bmod, bxs[mo]], [bxs[mo]])
        else:
            tt("dve", tmpC[:, 0:n].rearrange("p (b t) -> p b t", t=LS), PS[pi][:, 0:n].rearrange("p (b t) -> p b t", t=LS),
               modT1[:, gm + mo, 1:17].unsqueeze(2).to_broadcast([128, NB, LS]), ALU.mult, [bPS[pi], bmod], [btC])
            tt("dve", xa[:, mo, :], xa[:, mo, :], tmpC[:, 0:n], ALU.add, [btC, bxs[mo]], [bxs[mo]])

    def inproj_tile(m, n):
        pi = nps()
        for k in range(KT):
            mm(PS[pi][:, 0:n], win16[:, k, m * 128:(m + 1) * 128], h16[:, k, 0:n], k == 0, k == KT - 1, [bwin, bh], [bPS[pi]])
        return pi

    ri = [0]

    def rope_pair(m_a, m_b, cos_ap, sin_ap, brope, out_ap, bout, n, extra32=None):
        pa = inproj_tile(m_a, n)
        pb = inproj_tile(m_b, n)
        ri[0] += 1
        if ri[0] % 2 == 0:
            tX, bX, tY, bY = tmpA, btA, tmpB, btB
        else:
            tX, bX, tY, bY = tmpC, btC, tmpD, btD
        tt("dve", tX[:, 0:n], PS[pa][:, 0:n], cos_ap, ALU.mult, [bPS[pa], brope], [bX])
        tt("dve", tY[:, 0:n], PS[pb][:, 0:n], sin_ap, ALU.mult, [bPS[pb], brope], [bY])
        tt("pool", out_ap, tX[:, 0:n], tY[:, 0:n], ALU.add, [bX, bY], [bout])
        if extra32 is not None:
            tt("pool", extra32[0], tX[:, 0:n], tY[:, 0:n], ALU.add, [bX, bY], [extra32[1]])

    def gelu_glu(n):
        tmps = [(tmpC, btC), (tmpD, btD)]
        for step in range(3):
            for o in range(2):
                tq, btq = tmps[o]
                if step == 0:
                    act(tq[:, 0:n], yss[:, o, 0:n], AF.Square, [byss], [btq])
                elif step == 1:
                    ts("dve", tq[:, 0:n], tq[:, 0:n], 0.044715, 1.0, ALU.mult, ALU.add, [btq], [btq])
                else:
                    tt("dve", tq[:, 0:n], tq[:, 0:n], yss[:, o, 0:n], ALU.mult, [btq, byss], [btq])
        for o in range(2):
            tq, btq = tmps[o]
            act(tq[:, 0:n], tq[:, 0:n], AF.Sigmoid, [btq], [btq], scale=2.0 * math.sqrt(2.0 / math.pi))
        for o in range(2):
            tq, btq = tmps[o]
            tt("dve", z32[:, o, 0:n], yss[:, o, 0:n], tq[:, 0:n], ALU.mult, [byss, btq], [bz32])
            cp("act", z16[:, o, 0:n], z32[:, o, 0:n], [bz32], [bz16])
        pis = []
        for o in range(2):
            pi = nps()
            pis.append(pi)
            for k in range(2):
                mm(PS[pi][:, 0:n], wgl16[:, k, o * 128:(o + 1) * 128], z16[:, k, 0:n], k == 0, k == 1, [bwgl, bz16], [bPS[pi]])
        for o in range(2):
            tq, btq = tmps[o]
            act(tq[:, 0:n], PS[pis[o]][:, 0:n], AF.Sigmoid, [bPS[pis[o]], bsp2], [btq], bias=sp2[:, 1, o:o + 1])
        for o in range(2):
            tq, btq = tmps[o]
            tt("dve", os16[:, o, 0:n], z32[:, o, 0:n], tq[:, 0:n], ALU.mult, [bz32, btq], [bos])

    def conv_ln_stages(rhs_fn, n, view):
        pcs = {}

        def st_mm(m):
            def f():
                pi = nps()
                pcs[m] = pi
                for k in range(31):
                    mm(view(PS[pi][:, 0:n]), diag16[:, m, k, :], rhs_fn(m, k), k == 0, k == 30, [bdiag, bcb16, bcbs16], [bPS[pi]])
            return f

        def st_evac(m):
            def f():
                pi = pcs[m]
                act(ycf[:, m, 0:n], PS[pi][:, 0:n], AF.Identity, [bPS[pi], bsp2], [bycf], bias=sp2[:, 2, m:m + 1])
                act(yc16[:, 2 + m, 0:n], PS[pi][:, 0:n], AF.Square, [bPS[pi], bsp2], [byc16], bias=sp2[:, 2, m:m + 1])
            return f

        def st_cast(m):
            def f():
                cp("dve", yc16[:, m, 0:n], ycf[:, m, 0:n], [bycf], [byc16])
            return f

        def st_statmm():
            p1 = nps()
            for m in range(2):
                mm(PS[p1][:, 0:n], ones16[:], yc16[:, m, 0:n], m == 0, m == 1, [bones, byc16], [bPS[p1]])
            p2 = nps()
            for m in range(2):
                mm(PS[p2][:, 0:n], ones16[:], yc16[:, 2 + m, 0:n], m == 0, m == 1, [bones, byc16], [bPS[p2]])
            pcs["p1"] = p1
            pcs["p2"] = p2

        def st_stat1():
            p1, p2 = pcs["p1"], pcs["p2"]
            ts("dve", cva[:, 0:n], PS[p1][:, 0:n], 1.0 / 256, None, ALU.mult, None, [bPS[p1]], [bcva])
            tt("dve", cvb[:, 0:n], cva[:, 0:n], cva[:, 0:n], ALU.mult, [bcva], [bcvb])
            stt("dve", cvb[:, 0:n], PS[p2][:, 0:n], 1.0 / 256, cvb[:, 0:n], ALU.mult, ALU.subtract, [bPS[p2], bcvb], [bcvb])
            ts("dve", cvb[:, 0:n], cvb[:, 0:n], 1e-6, None, ALU.add, None, [bcvb], [bcvb])
            act(cvb[:, 0:n], cvb[:, 0:n], AF.Sqrt, [bcvb], [bcvb])

        def st_stat2():
            P.add("dve", lambda e: e.reciprocal(out=cvb[:, 0:n], in_=cvb[:, 0:n]), reads=[bcvb], writes=[bcvb])

        def st_apply1(m):
            def f():
                tt("dve", ycf[:, m, 0:n], ycf[:, m, 0:n], cva[:, 0:n], ALU.subtract, [bycf, bcva], [bycf])
                tt("dve", ycf[:, m, 0:n], ycf[:, m, 0:n], cvb[:, 0:n], ALU.mult, [bycf, bcvb], [bycf])
                act(ycf[:, m, 0:n], ycf[:, m, 0:n], AF.Identity, [bycf, bsp2], [bycf], bias=sp2[:, 4, m:m + 1], scale=sp2[:, 3, m:m + 1])
                act(cvc[:, m, 0:n], ycf[:, m, 0:n], AF.Sigmoid, [bycf], [bcvc])
            return f

        def st_apply2(m):
            def f():
                tt("dve", oc16[:, m, 0:n], ycf[:, m, 0:n], cvc[:, m, 0:n], ALU.mult, [bcvc, bycf], [boc])
            return f
        return [st_mm(0), st_mm(1), st_evac(0), st_evac(1), st_cast(0), st_cast(1), st_statmm, st_stat1, st_stat2,
                st_apply1(0), st_apply1(1), st_apply2(0), st_apply2(1)]

    def conv_ln(rhs_fn, n, view):
        for f in conv_ln_stages(rhs_fn, n, view):
            f()

    def outproj_ffn(xa, bxs, n, l, sample, first=False):
        for mo in range(8):
            s = ffi[0] % NG
            ffi[0] += 1
            wa_v = wfl[s][0:64, 0:1024].rearrange("p (h c) -> p h c", c=128)
            wr_v = wfl[s][:, 1024:1536].rearrange("p (h c) -> p h c", c=128)
            fa = wfl[s][0:64, 0:1024]
            fr = wfl[s][:, 1024:1536]
            bw = bwgu[s]
            if first:
                ld(wa_v, w_out[l, 0:512, mo * 128:(mo + 1) * 128].rearrange("(h d) n -> d h n", d=64), bw, [bw], q="pool")
                ld(wr_v, w_out[l, 512:1024, mo * 128:(mo + 1) * 128].rearrange("(j p) n -> p j n", p=128), bw, [bw], q="pool")
                P.add("sp", lambda e, fa=fa, mo=mo: e.dma_start(out=woa_c[mo], in_=fa), reads=[bw], writes=[bwo_c[mo]], dma_owner=bw)
                P.add("sp", lambda e, fr=fr, mo=mo: e.dma_start(out=wor_c[mo], in_=fr), reads=[bw], writes=[bwo_c[mo]], dma_owner=bw)
            else:
                P.add("sp", lambda e, fa=fa, mo=mo: e.dma_start(out=fa, in_=woa_c[mo]), reads=[bwo_c[mo]], writes=[bw], dma_owner=bw)
                P.add("sp", lambda e, fr=fr, mo=mo: e.dma_start(out=fr, in_=wor_c[mo]), reads=[bwo_c[mo]], writes=[bw], dma_owner=bw)
            pi = nps()
            for hq in range(8):
                mm(PS[pi][:, 0:n], wa_v[:, hq, :], att16[:, hq, 0:n], hq == 0, False, [bw, batt], [bPS[pi]])
            for j in range(2):
                mm(PS[pi][:, 0:n], wr_v[:, 2 + j, :], oc16[:, j, 0:n], False, False, [bw, boc], [bPS[pi]])
            for j in range(2):
                mm(PS[pi][:, 0:n], wr_v[:, j, :], os16[:, j, 0:n], False, j == 1, [bw, bos], [bPS[pi]])
            resid(xa, bxs, n, mo, pi, l, 16, sample)
        rmsnorm(xa, bxs, n, 1, 24, 1, l, sample)
        for j in range(JT):
            s = ffi[0] % NG
            ffi[0] += 1
            fg = wfl[s]
            if first:
                ld(wgu[s][:, 0], w_gate[l, :, j * 128:(j + 1) * 128].rearrange("(k p) n -> p k n", p=128), bwgu[s], [bwgu[s]], q="pool")
                ld(wgu[s][:, 1], w_up[l, :, j * 128:(j + 1) * 128].rearrange("(k p) n -> p k n", p=128), bwgu[s], [bwgu[s]], q="pool")
                P.add("sp", lambda e, fg=fg, j=j: e.dma_start(out=wgu_c[j], in_=fg), reads=[bwgu[s]], writes=[bwgu_c[j]], dma_owner=bwgu[s])
            else:
                P.add("sp", lambda e, fg=fg, j=j: e.dma_start(out=fg, in_=wgu_c[j]), reads=[bwgu_c[j]], writes=[bwgu[s]], dma_owner=bwgu[s])
            pg = nps()
            for k in range(KT):
                mm(PS[pg][:, 0:n], wgu[s][:, 0, k, :], h16[:, k, 0:n], k == 0, k == KT - 1, [bwgu[s], bh], [bPS[pg]])
            pu = nps()
            for k in range(KT):
                mm(PS[pu][:, 0:n], wgu[s][:, 1, k, :], h16[:, k, 0:n], k == 0, k == KT - 1, [bwgu[s], bh], [bPS[pu]])
            tA = tmpA if j % 2 == 0 else tmpB
            bA = btA if j % 2 == 0 else btB
            act(tA[:, 0:n], PS[pg][:, 0:n], AF.Silu, [bPS[pg]], [bA])
            tt("dve", scr16[:, j, 0:n], tA[:, 0:n], PS[pu][:, 0:n], ALU.mult, [bA, bPS[pu]], [bscr])
        for mo in range(8):
            pi = nps()
            for jh in range(2):
                s = ffi[0] % NG
                ffi[0] += 1
                ci = mo * 2 + jh
                fd = wfl[s][:, 0:1408]
                if first:
                    ld(wdn[s], w_down[l, jh * 1408:(jh + 1) * 1408, mo * 128:(mo + 1) * 128].rearrange("(j p) n -> p j n", p=128), bwdn[s], [bwdn[s]], q="pool")
                    P.add("sp", lambda e, fd=fd, ci=ci: e.dma_start(out=wdn_c[ci], in_=fd), reads=[bwdn[s]], writes=[bwdn_c[ci]], dma_owner=bwdn[s])
                else:
                    P.add("sp", lambda e, fd=fd, ci=ci: e.dma_start(out=fd, in_=wdn_c[ci]), reads=[bwdn_c[ci]], writes=[bwdn[s]], dma_owner=bwdn[s])
                for jj in range(11):
                    j = jh * 11 + jj
                    mm(PS[pi][:, 0:n], wdn[s][:, jj, :], scr16[:, j, 0:n], j == 0, j == JT - 1, [bwdn[s], bscr], [bPS[pi]])
            resid(xa, bxs, n, mo, pi, l, 40, sample)

    hp = sb("hp", [128, 2]); bhp = Buf("hp")
    ld(hp[:], hprev, bhp, [bhp])
    hst = ycf[:].rearrange("p a t -> p (a t)")[:, 0:HF]
    GROUPS = [[0, 4], [1, 5], [2, 6], [3, 7]]

    def slot_begin(l):
        if l == 0:
            memset("dve", kb16[:, 0:128], 0.0, [bkb])
            memset("dve", v16[:, 0, :], 0.0, [bv16])
            memset("dve", cb32[:, :, 0:30], 0.0, [bcb32])
            memset("dve", inre[:], 0.0, [binit])
            memset("dve", inim[:], 0.0, [binit])
            return
        P.add("sp", lambda e: e.dma_start(out=hst, in_=hall.ap()[0:128, :]), reads=[bhall], writes=[bycf], dma_owner=bycf)
        ts("dve", hst, hst, hp[:, 0:1], None, ALU.mult, None, [bycf, bhp], [bycf])
        cp("dve", kb16[:, 0:128], hst[:, 0:128], [bycf], [bkb])
        cp("dve", v16[:, 0, :], hst[:, 128:256], [bycf], [bv16])
        cp("dve", cb32[:, :, 0:30], hst[:, 256:316].rearrange("p (a r) -> p a r", a=2), [bycf], [bcb32])
        cp("dve", inre[:], hst[:, 316:324], [bycf], [binit])
        cp("dve", inim[:], hst[:, 324:332], [bycf], [binit])

    def slot_end(l):
        cp("dve", hst[:, 0:128], kb16[:, 0:128], [bkb], [bycf])
        cp("dve", hst[:, 128:256], v16[:, 0, :], [bv16], [bycf])
        cp("dve", hst[:, 256:316].rearrange("p (a r) -> p a r", a=2), cb32[:, :, 0:30], [bcb32], [bycf])
        cp("dve", hst[:, 316:324], inre[:], [binit], [bycf])
        cp("dve", hst[:, 324:332], inim[:], [binit], [bycf])
        P.add("sp", lambda e: e.dma_start(out=hin.ap(), in_=hst), reads=[bycf], writes=[bhin], dma_owner=bycf)
        P.add("pool", lambda e: e.collective_compute("AllGather", ALU.bypass, replica_groups=GROUPS,
                                                     ins=[hin.ap().opt()], outs=[hall.ap().opt()]),
              reads=[bhin], writes=[bhall], dma_owner=bhall, dinc=1)

    def prompt_chunk(l, c):
        n = T
        t0 = c * T
        src = xT if l == 0 else xscr
        xdr = src[:, t0:t0 + T].rearrange("(k p) t -> p k t", p=128)
        ld(xs[:], xdr, bx[0], bx)
        ld(rp[:], ropeP[:, :, t0:t0 + T].rearrange("a p t -> p a t"), brp, [brp])
        if KS2 < 1:
            return
        rmsnorm(xs, bx, n, 1, 0, 0, l, False)
        if KS2 < 2:
            return
        for j in range(4):
            rope_pair(j, 5 + j, rp[:, 0, :], rp[:, 1, :], brp, q16[:, j, :], bq, n)
        rope_pair(4, 9, rp[:, 0, :], rp[:, 1, :], brp, kb16[:, 128:128 + T], bkb, n, extra32=(k32[:, 0:n], bk32))
        if KS2 < 3:
            return
        for o in range(2):
            pi = inproj_tile(10 + o, n)
            cp("act", u32[:, o, :], PS[pi][:, 0:n], [bPS[pi]], [bu32])
            cp("dve", u16[:, o, :], PS[pi][:, 0:n], [bPS[pi]], [bu16])
        for o in range(2):
            pa = inproj_tile(12 + o, n)
            pg = inproj_tile(14 + o, n)
            tS, bS = (tmpA, btA) if o == 0 else (tmpB, btB)
            act(tS[:, 0:n], PS[pg][:, 0:n], AF.Sigmoid, [bPS[pg]], [bS])
            tt("dve", cb32[:, o, 30:30 + T], PS[pa][:, 0:n], tS[:, 0:n], ALU.mult, [bPS[pa], bS], [bcb32])
            cp("pool", cb16[:, o, :], cb32[:, o, :], [bcb32], [bcb16])
        if KS2 < 4:
            return
        for tb in range(T // 128):
            pi = nps()
            for k in range(KT):
                mm(PS[pi][:, 0:128], h16[:, k, tb * 128:(tb + 1) * 128], win16[:, k, 2048:2176], k == 0, k == KT - 1, [bh, bwin], [bPS[pi]])
            cp("act", v16[:, 1 + tb, :], PS[pi][:, 0:128], [bPS[pi]], [bv16])
            if c == NCH - 1 and tb == T // 128 - 1:
                cp("dve", v32[:], PS[pi][:, 0:128], [bPS[pi]], [bv32])
                st(nv[l], v32[:], bv32, [bv32])
        if c == NCH - 1:
            st(nkT[l], k32[:, T - 128:T], bk32, [bk32])
        if KSUB < 1:
            return
        blocks = [(qb_, hh_) for qb_ in range(T // 128) for hh_ in range(2)]

        def emit_scores(qb_, hh_):
            hs_ = slice(64 * hh_, 64 * hh_ + 64)
            qrhs_ = q16[hs_, :, qb_ * 128:(qb_ + 1) * 128]
            first_ = False
            po_ = nps()
            mm(PS[po_][:].rearrange("p (g q) -> p g q", g=4), kb16[hs_, 128 + qb_ * 128:128 + (qb_ + 1) * 128], qrhs_, True, False, [bkb, bq], [bPS[po_]])
            mm(PS[po_][:], ident16[:], mkn[:, 0, :], False, True, [bident16, bmkn], [bPS[po_]])
            pp_ = None
            if not first_:
                pp_ = nps()
                mm(PS[pp_][:].rearrange("p (g q) -> p g q", g=4), kb16[hs_, qb_ * 128:(qb_ + 1) * 128], qrhs_, True, False, [bkb, bq], [bPS[pp_]])
                mm(PS[pp_][:], ident16[:], mkn[:, 1, :], False, True, [bident16, bmkn], [bPS[pp_]])
            return po_, pp_
        pend = emit_scores(*blocks[0])
        for bi, (qb, hh) in enumerate(blocks):
            hs = slice(64 * hh, 64 * hh + 64)
            first = False
            po, pp = pend
            act(pown[:], PS[po][:], AF.Exp, [bPS[po]], [bpown], scale=0.125)
            if c == 0 and qb == 0:
                act(pprev[:], PS[pp][:], AF.Exp, [bPS[pp], bhp], [bpprev], scale=0.125, bias=hp[:, 1:2])
            else:
                act(pprev[:], PS[pp][:], AF.Exp, [bPS[pp]], [bpprev], scale=0.125)
            if bi + 1 < len(blocks):
                pend = emit_scores(*blocks[bi + 1])
            pO = 6
            pD = 7
            if not first:
                mm(PS[pO][0:64, :], v16[:, qb, hs], pprev[:], True, False, [bv16, bpprev], [bPS[pO]])
                mm(PS[pD][0:64, :], ones16[:, 0:64], pprev[:], True, False, [bones, bpprev], [bPS[pD]])
            mm(PS[pO][0:64, :], v16[:, qb + 1, hs], pown[:], first, True, [bv16, bpown], [bPS[pO]])
            mm(PS[pD][0:64, :], ones16[:, 0:64], pown[:], first, True, [bones, bpown], [bPS[pD]])
            tt("dve", den[:].rearrange("p (g q) -> p g q", g=4), PS[pD][0:64, :].rearrange("p (g q) -> p g q", g=4),
               sk[:, 4 * hh:4 * hh + 4].unsqueeze(2).to_broadcast([64, 4, 128]), ALU.add, [bPS[pD], bsk], [btA, btB])
            P.add("dve", lambda e: e.reciprocal(out=den[:], in_=den[:]), reads=[btA, btB], writes=[btA, btB])
            tt("dve", att16[:, 4 * hh:4 * hh + 4, qb * 128:(qb + 1) * 128], PS[pO][0:64, :].rearrange("p (g q) -> p g q", g=4),
               den[:].rearrange("p (g q) -> p g q", g=4), ALU.mult, [bPS[pO], btA, btB], [batt])
        cp("pool", kb16[:, 0:128], kb16[:, T:T + 128], [bkb], [bkb])
        cp("pool", v16[:, 0, :], v16[:, T // 128, :], [bv16], [bv16])
        if KSUB < 2:
            return
        pY = [6, 7]
        def emit_bu(sc_, ct_):
            pr_ = nps()
            pim_ = nps()
            mm(PS[pr_][:, 0:TS], LB[:, 0, ct_, :], u16[:, ct_ // 4, sc_ * TS:(sc_ + 1) * TS], True, True, [bLB, bu16], [bPS[pr_]])
            mm(PS[pim_][:, 0:TS], LB[:, 1, ct_, :], u16[:, ct_ // 4, sc_ * TS:(sc_ + 1) * TS], True, True, [bLB, bu16], [bPS[pim_]])
            return pr_, pim_
        iters = [(sc_, ct_) for sc_ in range(T // TS) for ct_ in range(8)]
        cstages = conv_ln_stages(lambda m, k: cb16[:, m, k:k + T], n, lambda ap: ap)
        csched = {0: [0], 1: [1], 2: [2], 3: [3], 4: [4], 5: [5], 6: [6], 8: [7], 9: [8], 10: [9], 11: [10], 13: [11], 14: [12]}
        pend = emit_bu(*iters[0])
        for sc in range(T // TS):
            c0 = sc * TS
            firstsub = (c == 0 and sc == 0)
            for ct in range(8):
                uh = ct // 4
                pr, pim = pend
                nxt = sc * 8 + ct + 1
                if nxt < len(iters):
                    pend = emit_bu(*iters[nxt])
                cs_ = cosT[:, ct, 0:TS]
                sn_ = sinT[:, ct, 0:TS]
                par = ssi[0] % 2
                ssi[0] += 1
                qA, bqA = sq[par][0], bsq[par][0]
                qB, bqB = sq[par][1], bsq[par][1]
                hA, bhA = sh16[par][0], bsh16[par][0]
                hB, bhB = sh16[par][1], bsh16[par][1]
                tt("dve", sx[0][:], PS[pr][:, 0:TS], cs_, ALU.mult, [bPS[pr], btab], [bsx[0]])
                tt("dve", sx[1][:], PS[pim][:, 0:TS], sn_, ALU.mult, [bPS[pim], btab], [bsx[1]])
                tt("dve", sx[2][:], PS[pim][:, 0:TS], cs_, ALU.mult, [bPS[pim], btab], [bsx[2]])
                tt("dve", sx[3][:], PS[pr][:, 0:TS], sn_, ALU.mult, [bPS[pr], btab], [bsx[3]])
                tt("dve", sx[0][:], sx[0][:], sx[1][:], ALU.add, [bsx[0], bsx[1]], [bsx[0]])
                tt("dve", sx[2][:], sx[2][:], sx[3][:], ALU.subtract, [bsx[2], bsx[3]], [bsx[2]])
                rbc = sp8[:, 5, ct:ct + 1].to_broadcast([128, TS])
                ire = inre[:, ct:ct + 1]
                iim = inim[:, ct:ct + 1]
                P.add("dve", lambda e, ire=ire, rbc=rbc, qA=qA: e.tensor_tensor_scan(out=qA[:], data0=rbc, data1=sx[0][:], initial=ire, op0=ALU.mult, op1=ALU.add),
                      reads=[bsp8, bsx[0], binit], writes=[bqA])
                P.add("dve", lambda e, iim=iim, rbc=rbc, qB=qB: e.tensor_tensor_scan(out=qB[:], data0=rbc, data1=sx[2][:], initial=iim, op0=ALU.mult, op1=ALU.add),
                      reads=[bsp8, bsx[2], binit], writes=[bqB])
                cp("pool", qlre[:, ct:ct + 1], qA[:, TS - 1:TS], [bqA], [bql])
                cp("pool", qlim[:, ct:ct + 1], qB[:, TS - 1:TS], [bqB], [bql])
                tt("pool", sx[6][:], qA[:], cs_, ALU.mult, [bqA, btab], [bsx[6]])
                tt("pool", sx[7][:], qB[:], sn_, ALU.mult, [bqB, btab], [bsx[7]])
                tt("pool", sx[4][:], qA[:], sn_, ALU.mult, [bqA, btab], [bsx[4]])
                tt("pool", sx[5][:], qB[:], cs_, ALU.mult, [bqB, btab], [bsx[5]])
                tt("pool", hA[:], sx[6][:], sx[7][:], ALU.subtract, [bsx[6], bsx[7]], [bhA])
                tt("pool", hB[:], sx[4][:], sx[5][:], ALU.add, [bsx[4], bsx[5]], [bhB])
                ot = ct // 4
                mm(PS[pY[ot]][:, c0:c0 + TS], LB[:, 2, ct, :], hA[:], ct % 4 == 0, False, [bLB, bhA], [bPS[pY[ot]]])
                mm(PS[pY[ot]][:, c0:c0 + TS], LB[:, 3, ct, :], hB[:], False, ct % 4 == 3, [bLB, bhB], [bPS[pY[ot]]])
                for si_ in csched.get(sc * 8 + ct, []):
                    cstages[si_]()
            a0, a1, a2, a3 = (sm8[i_][:] for i_ in range(4))
            cT_ = cosT[:, :, TS]
            sT_ = sinT[:, :, TS]
            tt("dve", a0, qlre[:], cT_, ALU.mult, [bql, btab], [bsm8])
            tt("dve", a1, qlim[:], sT_, ALU.mult, [bql, btab], [bsm8])
            tt("dve", a2, qlre[:], sT_, ALU.mult, [bql, btab], [bsm8])
            tt("dve", a3, qlim[:], cT_, ALU.mult, [bql, btab], [bsm8])
            tt("dve", inre[:], a0, a1, ALU.subtract, [bsm8], [binit])
            tt("dve", inim[:], a2, a3, ALU.add, [bsm8], [binit])
            if c == NCH - 1 and sc == T // TS - 1:
                cl = cosT[:, :, TS - 1]
                sl = sinT[:, :, TS - 1]
                tt("dve", a0, qlre[:], cl, ALU.mult, [bql, btab, binit], [bsm8])
                tt("dve", a1, qlim[:], sl, ALU.mult, [bql, btab], [bsm8])
                tt("dve", a2, qlre[:], sl, ALU.mult, [bql, btab], [bsm8])
                tt("dve", a3, qlim[:], cl, ALU.mult, [bql, btab], [bsm8])
                tt("dve", hlre[:], a0, a1, ALU.subtract, [bsm8], [bhl])
                tt("dve", hlim[:], a2, a3, ALU.add, [bsm8], [bhl])
                st(nre[l], hlre[:], bhl, [bhl])
                st(nim[l], hlim[:], bhl, [bhl])
        for o in range(2):
            stt("dve", yss[:, o, 0:n], u32[:, o, 0:n], sp2[:, 0, o:o + 1], PS[pY[o]][:, 0:n], ALU.mult, ALU.add, [bu32, bsp2, bPS[pY[o]]], [byss])
        gelu_glu(n)
        if KSUB < 3:
            return
        if c == NCH - 1:
            st(ncv[l], cb32[:, :, T:T + 30], bcb32, [bcb32])
        for o in range(2):
            cp("pool", cb32[:, o, 0:30], cb32[:, o, T:T + 30], [bcb32], [bcb32])
        if KSUB < 4:
            return
        outproj_ffn(xs, bx, n, l, False, first=(c == 0))
        if l < nl - 1:
            st(xscr[:, t0:t0 + T].rearrange("(k p) t -> p k t", p=128), xs[:], bx[0], bx)
        else:
            rmsnorm(xs, bx, n, None, 0, 0, l, False)
            st(yT[:, t0:t0 + T].rearrange("(k p) t -> p k t", p=128), xs[:], bx[0], bx)

    def sample_layer(l):
        n = TSM
        bxs = [bxsm] * KT
        ld(kc16, kcT[l].rearrange("b f k -> f b k"), bkc, [bkc], q="pool")
        ld(vc16, vc[l].rearrange("b k f -> k b f"), bvc, [bvc], q="pool")
        ld(h0re[:], ssmre_in[l], bh0, [bh0])
        ld(h0im[:], ssmim_in[l], bh0, [bh0])
        ld(cbs32[:, :, :, 0:30], sconv_in[l], bcbs32, [bcbs32])
        dd = Buf(f"dd{l}")
        o1 = P.add("sp", lambda e: e.dma_start(out=nks_c[l], in_=kcn[l, :, 4:128, :]), dma_owner=dd)
        stores.append(o1)
        vcn_src = vc[l, :, 4:128, :]
        o2 = P.add("sp", lambda e: e.dma_start(out=nvs_c[l], in_=vcn_src), dma_owner=dd)
        stores.append(o2)
        rmsnorm(xsm, bxs, n, 1, 0, 0, l, True)
        for j in range(4):
            rope_pair(j, 5 + j, rps[:, 0, :], rps[:, 1, :], brps, q16[:, j, 0:n], bq, n)
        rope_pair(4, 9, rps[:, 0, :], rps[:, 1, :], brps, kb16[:, 128:128 + n], bkb, n, extra32=(k32[:, 0:n], bk32))
        st(skT[l], k32[:, 0:n], bk32, [bk32])
        for o in range(2):
            pi = inproj_tile(10 + o, n)
            cp("act", u32[:, o, 0:n], PS[pi][:, 0:n], [bPS[pi]], [bu32])
            cp("dve", u16[:, o, 0:n], PS[pi][:, 0:n], [bPS[pi]], [bu16])
        for o in range(2):
            pa = inproj_tile(12 + o, n)
            pg = inproj_tile(14 + o, n)
            act(tmpC[:, 0:n], PS[pg][:, 0:n], AF.Sigmoid, [bPS[pg]], [btC])
            tt("dve", cbs32[:, o, :, 30:34], PS[pa][:, 0:n].rearrange("p (b t) -> p b t", t=LS),
               tmpC[:, 0:n].rearrange("p (b t) -> p b t", t=LS), ALU.mult, [bPS[pa], btC], [bcbs32])
            cp("pool", cbs16[:, o], cbs32[:, o], [bcbs32], [bcbs16])
        st(scv[l], cbs32[:, :, :, 4:34], bcbs32, [bcbs32])
        pv = nps()
        for b in range(NB):
            for k in range(KT):
                mm(PS[pv][0:4, :].rearrange("p (b f) -> p b f", b=4)[:, b % 4, :] if False else PS[pv][0:4, (b % 4) * 128:(b % 4 + 1) * 128],
                   h16[:, k, 4 * b:4 * b + 4], win16[:, k, 2048:2176], k == 0, k == KT - 1, [bh, bwin], [bPS[pv]])
            if b % 4 == 3:
                g0 = b - 3
                cp("act", vn16[:, g0:g0 + 4, :], PS[pv][0:4, :].rearrange("p (b f) -> p b f", b=4), [bPS[pv]], [bvn16])
                cp("dve", vn32[:, g0:g0 + 4, :], PS[pv][0:4, :].rearrange("p (b f) -> p b f", b=4), [bPS[pv]], [bvn32])
                if b < NB - 1:
                    pv = nps()
        st(svn[l], vn32[:], bvn32, [bvn32])
        pC = nps()
        pN = nps()
        for b in range(NB):
            for hh in range(2):
                hs = slice(64 * hh, 64 * hh + 64)
                col = (b * 2 + hh) * 16
                qrhs = q16[hs, :, 4 * b:4 * b + 4]
                mm(PS[pC][:, col:col + 16].rearrange("p (g t) -> p g t", g=4), kc16[hs, b, :], qrhs, True, True, [bkc, bq], [bPS[pC]])
                mm(PS[pN][0:4, col:col + 16].rearrange("p (g t) -> p g t", g=4), kb16[hs, 128 + 4 * b:128 + 4 * b + 4], qrhs, True, True, [bkb, bq], [bPS[pN]])
        act(pown[:], PS[pC][:], AF.Exp, [bPS[pC]], [bpown], scale=0.125)
        tt("pool", pown[:], pown[:], mk[:, 2, :], ALU.mult, [bpown, bmk], [bpown])
        act(pn16[:], PS[pN][0:4, :], AF.Exp, [bPS[pN]], [bpn], scale=0.125)
        tt("pool", pn16[:], pn16[:], mk[0:4, 3, :], ALU.mult, [bpn, bmk], [bpn])
        pO = nps()
        pD = nps()
        for b in range(NB):
            for hh in range(2):
                hs = slice(64 * hh, 64 * hh + 64)
                col = (b * 2 + hh) * 16
                mm(PS[pO][0:64, col:col + 16], vc16[:, b, hs], pown[:, col:col + 16], True, False, [bvc, bpown], [bPS[pO]])
                mm(PS[pO][0:64, col:col + 16], vn16[0:4, b, hs], pn16[0:4, col:col + 16], False, True, [bvn16, bpn], [bPS[pO]])
                mm(PS[pD][0:64, col:col + 16], ones16[:, 0:64], pown[:, col:col + 16], True, False, [bones, bpown], [bPS[pD]])
                mm(PS[pD][0:64, col:col + 16], ones16[0:4, 0:64], pn16[0:4, col:col + 16], False, True, [bones, bpn], [bPS[pD]])
        tt("dve", den[:].rearrange("p (b h t) -> p b h t", b=NB, t=LS), PS[pD][0:64, :].rearrange("p (b h t) -> p b h t", b=NB, t=LS),
           sk[:, :].unsqueeze(1).unsqueeze(3).to_broadcast([64, NB, 8, LS]), ALU.add, [bPS[pD], bsk], [btA, btB])
        P.add("dve", lambda e: e.reciprocal(out=den[:], in_=den[:]), reads=[btA, btB], writes=[btA, btB])
        tt("dve", att16[:, :, 0:n].rearrange("p h (b t) -> p b h t", t=LS), PS[pO][0:64, :].rearrange("p (b h t) -> p b h t", b=NB, t=LS),
           den[:].rearrange("p (b h t) -> p b h t", b=NB, t=LS), ALU.mult, [bPS[pO], btA, btB], [batt])
        for (dst, x1, y1, x2, y2, op) in ((ahre, 8, h0re, 9, h0im, ALU.subtract), (ahim, 8, h0im, 9, h0re, ALU.add)):
            tt("dve", dst[:], y1[:], sp8[:, x1, :].unsqueeze(2).to_broadcast([128, 8, NB]), ALU.mult, [bh0, bsp8], [bah])
            tt("dve", hsre[:], y2[:], sp8[:, x2, :].unsqueeze(2).to_broadcast([128, 8, NB]), ALU.mult, [bh0, bsp8], [bhs])
            tt("dve", dst[:], dst[:], hsre[:], op, [bah, bhs], [bah])
        pY = [6, 7]
        for ct in range(8):
            uh = ct // 4
            pr = nps()
            pim = nps()
            mm(PS[pr][:, 0:n], LB[:, 0, ct, :], u16[:, uh, 0:n], True, True, [bLB, bu16], [bPS[pr]])
            mm(PS[pim][:, 0:n], LB[:, 1, ct, :], u16[:, uh, 0:n], True, True, [bLB, bu16], [bPS[pim]])
            cs_ = cosT[:, ct, 0:LS].unsqueeze(1).to_broadcast([128, NB, LS])
            sn_ = sinT[:, ct, 0:LS].unsqueeze(1).to_broadcast([128, NB, LS])
            V3 = lambda ap: ap.rearrange("p (b t) -> p b t", t=LS)
            X = [s_[:, 0:n] for s_ in sx]
            tt("dve", V3(X[0]), V3(PS[pr][:, 0:n]), cs_, ALU.mult, [bPS[pr], btab], [bsx[0]])
            tt("dve", V3(X[1]), V3(PS[pim][:, 0:n]), sn_, ALU.mult, [bPS[pim], btab], [bsx[1]])
            tt("pool", X[0], X[0], X[1], ALU.add, [bsx[0], bsx[1]], [bsx[0]])
            tt("dve", V3(X[2]), V3(PS[pim][:, 0:n]), cs_, ALU.mult, [bPS[pim], btab], [bsx[2]])
            tt("dve", V3(X[3]), V3(PS[pr][:, 0:n]), sn_, ALU.mult, [bPS[pr], btab], [bsx[3]])
            tt("pool", X[2], X[2], X[3], ALU.subtract, [bsx[2], bsx[3]], [bsx[2]])
            tt("dve", sx[0][:, 0:n:LS], sx[0][:, 0:n:LS], ahre[:, ct, :], ALU.add, [bsx[0], bah], [bsx[0]])
            tt("dve", sx[2][:, 0:n:LS], sx[2][:, 0:n:LS], ahim[:, ct, :], ALU.add, [bsx[2], bah], [bsx[2]])
            r4v = r4[:, ct].rearrange("p b t -> p (b t)")
            P.add("dve", lambda e, r4v=r4v, X=X: e.tensor_tensor_scan(out=X[4], data0=r4v, data1=X[0], initial=0.0, op0=ALU.mult, op1=ALU.add),
                  reads=[btab4, bsx[0]], writes=[bsx[4]])
            P.add("dve", lambda e, r4v=r4v, X=X: e.tensor_tensor_scan(out=X[5], data0=r4v, data1=X[2], initial=0.0, op0=ALU.mult, op1=ALU.add),
                  reads=[btab4, bsx[2]], writes=[bsx[5]])
            tt("pool", V3(X[6]), V3(X[4]), cs_, ALU.mult, [bsx[4], btab], [bsx[6]])
            tt("pool", V3(X[7]), V3(X[5]), sn_, ALU.mult, [bsx[5], btab], [bsx[7]])
            tt("pool", X[6], X[6], X[7], ALU.subtract, [bsx[6], bsx[7]], [bsx[6]])
            cp("act", hre16[:, 0:n], X[6], [bsx[6]], [bhre])
            cp("act", hsre[:, ct, :], sx[6][:, LS - 1:n:LS], [bsx[6]], [bhs])
            tt("dve", V3(X[1]), V3(X[4]), sn_, ALU.mult, [bsx[4], btab], [bsx[1]])
            tt("dve", V3(X[3]), V3(X[5]), cs_, ALU.mult, [bsx[5], btab], [bsx[3]])
            tt("dve", X[1], X[1], X[3], ALU.add, [bsx[1], bsx[3]], [bsx[1]])
            cp("act", him16[:, 0:n], X[1], [bsx[1]], [bhim])
            cp("act", hsim[:, ct, :], sx[1][:, LS - 1:n:LS], [bsx[1]], [bhs])
            ot = ct // 4
            mm(PS[pY[ot]][:, 0:n], LB[:, 2, ct, :], hre16[:, 0:n], ct % 4 == 0, False, [bLB, bhre], [bPS[pY[ot]]])
            mm(PS[pY[ot]][:, 0:n], LB[:, 3, ct, :], him16[:, 0:n], False, ct % 4 == 3, [bLB, bhim], [bPS[pY[ot]]])
        st(sre[l], hsre[:], bhs, [bhs])
        st(sim_o[l], hsim[:], bhs, [bhs])
        for o in range(2):
            stt("dve", yss[:, o, 0:n], u32[:, o, 0:n], sp2[:, 0, o:o + 1], PS[pY[o]][:, 0:n], ALU.mult, ALU.add, [bu32, bsp2, bPS[pY[o]]], [byss])
        gelu_glu(n)
        conv_ln(lambda m, k: cbs16[:, m, :, k:k + LS], n, lambda ap: ap.rearrange("p (b t) -> p b t", t=LS))
        if DBG and l == 0:
            dsb = xs[:, 0:6, :].rearrange("p (a k) (h t) -> p a (k h) t", k=2, t=TSM); bdsb = bx[0]
            memset("dve", dsb, 0.0, bx)
            cp("dve", dsb[0:64, 0, :, :], att16[:, :, 0:n], [batt, bdsb], [bdsb])
            cp("dve", dsb[:, 1, 0:2, :], os16[:, :, 0:n], [bos, bdsb], [bdsb])
            cp("dve", dsb[:, 2, 0:2, :], oc16[:, :, 0:n], [boc, bdsb], [bdsb])
            st(dbg.rearrange("a p h t -> p a h t"), dsb, bdsb, [bdsb])
        outproj_ffn(xsm, bxs, n, l, True)
        if l == nl - 1:
            rmsnorm(xsm, bxs, n, None, 0, 0, l, True)
            st(ysT.rearrange("(k p) t -> p k t", p=128), xsm[:], bxsm, [bxsm])

    STG = int(os.environ.get("KSTAGE", "9"))
    for l in range(nl):
        if STG >= 1:
            layer_params(l)
        slot_begin(l)
        for c in range(NCH):
            if STG >= 3 or (STG == 2 and c == 0):
                prompt_chunk(l, c)
        if l < nl - 1:
            slot_end(l)
        if STG >= 4:
            sample_layer(l)

    P.add("sp", lambda e: e.nop(), extra_deps=stores)
    P.emit()
    return nc


def _perm_win(w):
    q = w[:, 0:512].reshape(D, 8, 64)
    k = w[:, 512:640].reshape(D, 2, 64)
    v = w[:, 640:768]
    u = w[:, 768:1024]
    a = w[:, 1024:1280]
    g = w[:, 1280:1536]

    def swap(t):
        return np.concatenate([t[..., 32:], t[..., :32]], axis=-1)
    order = [0, 4, 1, 5, 2, 6, 3, 7]
    qt = q[:, order, :].reshape(D, 512)
    qs = swap(q)[:, order, :].reshape(D, 512)
    kt = k.reshape(D, 128)
    ks = swap(k).reshape(D, 128)
    return np.ascontiguousarray(np.concatenate([qt, kt, qs, ks, u, a, g, v], axis=1))


def _rope_tab(pos):
    half = 32
    inv = (np.float32(10000.0) ** (-(np.arange(half, dtype=np.float32) / np.float32(half)))).astype(np.float32)
    ang = (pos.astype(np.float32)[None, :] * inv[:, None]).astype(np.float32)
    c = np.cos(ang.astype(np.float64)).astype(np.float32)
    s = np.sin(ang.astype(np.float64)).astype(np.float32)
    cos = np.concatenate([c, c, c, c], axis=0)
    sins = np.concatenate([-s, s, -s, s], axis=0)
    return np.ascontiguousarray(np.stack([cos, sins], axis=0))


_NC_CACHE = {}


def kernel(**inp):
    f = lambda a: np.ascontiguousarray(np.asarray(a, dtype=np.float32))
    I = {k: np.asarray(v) for k, v in inp.items()}
    nlr = _NC_CACHE.get("nl", NS)
    if "nc" not in _NC_CACHE:
        _NC_CACHE["nc"] = build(nlr)
    nc = _NC_CACHE["nc"]
    import os
    ncr = int(os.environ.get("KCORES", "8"))

    SLOT = {0: [0, 1, 2, 3, 0], 1: [0, 0, 1, 2, 3]}
    DUMMY = {0: 4, 1: 0}

    def pk(a):
        return a.reshape(NL, 8, 128).transpose(0, 2, 1)

    def p2(a):
        return a.reshape(NL, 2, 128).transpose(0, 2, 1)

    per_layer = {
        "w_mod": I["w_mod"],
        "b_modT": I["b_mod"].reshape(NL, 48, 128).transpose(0, 2, 1),
        "g1T": pk(I["norm1_g"]), "g2T": pk(I["norm2_g"]),
        "w_in2": np.stack([_perm_win(I["w_in"][l]) for l in range(NL)]),
        "sinkT": np.broadcast_to(I["attn_sinks"][:, None, :], (NL, 64, 8)),
        "lamre": I["ssm_lam_re"].reshape(NL, 8, 128).transpose(0, 2, 1),
        "lamim": I["ssm_lam_im"].reshape(NL, 8, 128).transpose(0, 2, 1),
        "logdt": np.repeat(I["ssm_log_dt"], 64, axis=1).reshape(NL, 8, 128).transpose(0, 2, 1),
        "bre": I["ssm_b_re"].reshape(NL, 8, 128, 16).transpose(0, 2, 1, 3),
        "bim": I["ssm_b_im"].reshape(NL, 8, 128, 16).transpose(0, 2, 1, 3),
        "cre": I["ssm_c_re"].reshape(NL, 8, 2, 16, 64).transpose(0, 2, 4, 1, 3).reshape(NL, 128, 8, 16),
        "cim": I["ssm_c_im"].reshape(NL, 8, 2, 16, 64).transpose(0, 2, 4, 1, 3).reshape(NL, 128, 8, 16),
        "dskipT": p2(I["ssm_d"]), "wglu": I["ssm_w_glu"], "bgluT": p2(I["ssm_b_glu"]),
        "convwT": I["conv_w"].reshape(NL, 31, 2, 128).transpose(0, 3, 2, 1),
        "convbT": p2(I["conv_b"]), "lngT": p2(I["conv_ln_g"]), "lnbT": p2(I["conv_ln_b"]),
        "w_out": I["w_out"], "w_gate": I["w_gate"], "w_up": I["w_up"], "w_down": I["w_down"],
    }
    role_w = {}
    for role in (0, 1):
        d = {}
        for k, a in per_layer.items():
            arr = f(np.asarray(a)[SLOT[role]])
            if k in ("w_out", "w_down"):
                arr[DUMMY[role]] = 0.0
            d[k] = arr
        role_w[role] = d
    const = {
        "gfT": f(I["final_norm_g"].reshape(8, 128).T),
        "ropeS": _rope_tab(PAST + np.tile(np.arange(LS), NB)),
        "identd": np.eye(128, dtype=np.float32),
        "jrow": f(np.broadcast_to(np.arange(TS + 1, dtype=np.float32)[None, :], (128, TS + 1))),
    }
    kk = np.arange(128)[:, None]
    qq = np.tile(np.arange(128), 4)[None, :]
    m_own = np.where(qq >= kk, 0.0, -30000.0).astype(np.float32)
    m_prev = np.where(kk > qq, 0.0, -30000.0).astype(np.float32)
    const["maskP"] = f(np.stack([m_own, m_prev]))
    tq = np.tile(np.arange(LS), 128)[None, :]
    m_c = (kk > tq).astype(np.float32)
    m_n = (kk <= tq).astype(np.float32)
    const["maskS"] = f(np.stack([m_c, m_n]))
    rope_role = {0: _rope_tab(np.arange(NTOK)), 1: _rope_tab(NTOK + np.arange(NTOK))}
    hp_role = {0: f(np.tile(np.array([[0.0, -1.0e4]], np.float32), (128, 1))),
               1: f(np.tile(np.array([[1.0, 0.0]], np.float32), (128, 1)))}

    in_maps = []
    for c in range(ncr):
        role = c // 4
        b = c % 4
        sbs = slice(NB * c, NB * (c + 1))
        sl = SLOT[role]
        m = dict(const)
        m.update(role_w[role])
        m["ropeP"] = rope_role[role]
        m["hprev"] = hp_role[role]
        m["xT"] = f(I["x_prompt"][b, role * NTOK:(role + 1) * NTOK].T)
        m["xsT"] = f(I["x_sample"][sbs].reshape(TSM, D).T)
        m["cT"] = f(np.concatenate([I["c_prompt"][b][None, :], I["c_sample"][sbs]], axis=0).T)
        ck = I["cache_k"][:, sbs].reshape(NL, NB, 128, 128)[sl]
        cv = I["cache_v"][:, sbs].reshape(NL, NB, 128, 128)[sl]
        m["kcT"] = f(ck.transpose(0, 1, 3, 2))
        m["kcn"] = f(ck)
        m["vc"] = f(cv)
        m["ssmre_in"] = f(I["state_ssm_re"][:, sbs].reshape(NL, NB, 8, 128).transpose(0, 3, 2, 1)[sl])
        m["ssmim_in"] = f(I["state_ssm_im"][:, sbs].reshape(NL, NB, 8, 128).transpose(0, 3, 2, 1)[sl])
        m["sconv_in"] = f(I["state_conv"][:, sbs].reshape(NL, NB, 30, 2, 128).transpose(0, 4, 3, 1, 2)[sl])
        in_maps.append(m)

    res = run_bass_kernel_spmd(nc, in_maps, core_ids=list(range(ncr)))
    R = list(res.results)
    while len(R) < 8:
        R.append(R[0])
    if "dbg" in R[0]:
        _NC_CACHE["dbg"] = R[0]["dbg"]
    SA = slice(0, 4)
    SB = slice(1, 5)

    y_prompt = np.stack([np.concatenate([R[b]["yT"].T, R[4 + b]["yT"].T], axis=0) for b in range(4)])
    y_sample = np.concatenate([R[c]["ysT"].T.reshape(NB, LS, D) for c in range(8)], axis=0)
    nk_p = np.stack([R[4 + b]["nkT"][SB].transpose(0, 2, 1).reshape(NL, 128, 2, 64) for b in range(4)], axis=1)
    nv_p = np.stack([R[4 + b]["nv"][SB].reshape(NL, 128, 2, 64) for b in range(4)], axis=1)

    def unst(a):
        return a.transpose(0, 2, 1).reshape(NL, 16, 64)
    re_p = np.stack([unst(R[4 + b]["nre"][SB]) for b in range(4)], axis=1)
    im_p = np.stack([unst(R[4 + b]["nim"][SB]) for b in range(4)], axis=1)
    cv_p = np.stack([R[4 + b]["ncv"][SB].transpose(0, 3, 2, 1).reshape(NL, 30, 256) for b in range(4)], axis=1)
    nk_s, nv_s, re_s, im_s, cv_s = [], [], [], [], []
    for c in range(8):
        r = R[c]
        S_ = SA if c < 4 else SB
        knew = r["skT"][S_].transpose(0, 2, 1).reshape(NL, NB, LS, 128)
        nk_s.append(np.concatenate([r["nks_c"][S_], knew], axis=2).reshape(NL, NB, 128, 2, 64))
        vnew = r["svn"][S_].transpose(0, 2, 1, 3)
        nv_s.append(np.concatenate([r["nvs_c"][S_], vnew], axis=2).reshape(NL, NB, 128, 2, 64))
        re_s.append(r["sre"][S_].transpose(0, 3, 2, 1).reshape(NL, NB, 16, 64))
        im_s.append(r["sim_o"][S_].transpose(0, 3, 2, 1).reshape(NL, NB, 16, 64))
        cv_s.append(r["scv"][S_].transpose(0, 3, 4, 2, 1).reshape(NL, NB, 30, 256))
    cat = lambda xs_: np.ascontiguousarray(np.concatenate(xs_, axis=1).astype(np.float32))
    outs = (y_prompt, y_sample, nk_p, nv_p, re_p, im_p, cv_p, cat(nk_s), cat(nv_s), cat(re_s), cat(im_s), cat(cv_s))
    return tuple(np.ascontiguousarray(o.astype(np.float32)) for o in outs)
```

```python
import math
import numpy as np
import concourse.bass as bass
import concourse.mybir as mybir
from concourse.bass_utils import run_bass_kernel_spmd

F32 = mybir.dt.float32
BF16 = mybir.dt.bfloat16
ALU = mybir.AluOpType
AF = mybir.ActivationFunctionType

SEG = 30000
NL = 4
NS = 5
HF = 332
D = 1024
KT = 8
NTOK = 2048
T = 256
NCH = NTOK // T
TS = 128
NB = 16
LS = 4
TSM = NB * LS
DFF = 2816
JT = 22
WIN = 2176
PAST = 8192
TWO_PI = 2.0 * math.pi


class Buf:
    __slots__ = ("name", "lw", "rd", "sem", "cnt", "excl")

    def __init__(self, name, excl=False):
        self.name = name
        self.excl = excl
        self.lw = None
        self.rd = {}
        self.sem = None
        self.cnt = 0


class Op:
    __slots__ = ("eng", "fn", "deps", "idx", "dma", "owner", "dcnt", "marked", "ev", "waits", "dinc")

    def __init__(self, eng, fn, idx):
        self.eng = eng
        self.fn = fn
        self.idx = idx
        self.deps = []
        self.dma = False
        self.owner = None
        self.dcnt = 0
        self.marked = False
        self.ev = None
        self.waits = []


class Prog:
    ENGS = ("pe", "act", "dve", "pool", "sp")

    def __init__(self, nc):
        self.nc = nc
        self.ops = []

    def add(self, eng, fn, reads=(), writes=(), dma_owner=None, extra_deps=(), dinc=16):
        i = len(self.ops)
        op = Op(eng, fn, i)
        if dma_owner is not None:
            op.dma = True
            op.owner = dma_owner
            op.dinc = dinc
            dma_owner.cnt += dinc
            op.dcnt = dma_owner.cnt
        deps = {}
        for b in reads:
            if b.lw is not None:
                deps[b.lw] = "raw"
            if b.excl:
                for r in b.rd.values():
                    if r not in deps:
                        deps[r] = "war"
        for b in writes:
            if b.lw is not None and b.lw not in deps:
                deps[b.lw] = "waw"
            for r in b.rd.values():
                if r not in deps:
                    deps[r] = "war"
        for d in extra_deps:
            deps[d.idx] = "raw"
        deps.pop(i, None)
        for b in reads:
            key = ("d", i) if op.dma else eng
            b.rd[key] = i
        for b in writes:
            b.lw = i
            b.rd = {}
        op.deps = list(deps.items())
        self.ops.append(op)
        return op

    def finalize(self):
        ops = self.ops
        waited = {e: {} for e in self.ENGS}
        for op in ops:
            need = {}
            for d, kind in op.deps:
                p = ops[d]
                if p.dma:
                    key = ("dma", id(p.owner))
                    if need.get(key, (0, None))[0] < p.dcnt:
                        need[key] = (p.dcnt, p)
                else:
                    if p.eng == op.eng and not op.dma:
                        if op.eng == "pe" or kind != "raw":
                            continue
                    key = ("eng", p.eng)
                    if need.get(key, (-1, None))[0] < p.idx:
                        need[key] = (p.idx, p)
            w = waited[op.eng]
            for key, (val, p) in need.items():
                if w.get(key, -1) >= val:
                    continue
                w[key] = val
                op.waits.append(p)
                if not p.dma:
                    p.marked = True
        cnt = {e: 0 for e in self.ENGS}
        for op in ops:
            if not op.dma and op.marked:
                cnt[op.eng] += 1
                op.ev = cnt[op.eng]
        self.evcount = cnt

    def emit(self):
        nc = self.nc
        self.finalize()
        esems = {}
        for e in self.ENGS:
            n = (self.evcount[e] + SEG - 1) // SEG
            esems[e] = [nc.alloc_semaphore(f"ev_{e}_{k}") for k in range(max(n, 1))]
        for op in self.ops:
            if op.dma and op.owner.sem is None:
                op.owner.sem = nc.alloc_semaphore("d_" + op.owner.name)

        def semval(p):
            if p.dma:
                return p.owner.sem, p.dcnt
            k = (p.ev - 1) // SEG
            return esems[p.eng][k], (p.ev - 1) % SEG + 1

        per = {e: [op for op in self.ops if op.eng == e] for e in self.ENGS}

        def run(eng, lst):
            for op in lst:
                for p in op.waits:
                    s, v = semval(p)
                    eng.wait_ge(s, v)
                ins = op.fn(eng)
                if op.dma:
                    ins.then_inc(op.owner.sem, op.dinc)
                elif op.marked:
                    s, _ = semval(op)
                    ins.then_inc(s, 1)

        with nc.Block() as block:
            @block.tensor
            def _(e):
                run(e, per["pe"])

            @block.scalar
            def _(e):
                run(e, per["act"])

            @block.vector
            def _(e):
                run(e, per["dve"])

            @block.gpsimd
            def _(e):
                run(e, per["pool"])

            @block.sync
            def _(e):
                run(e, per["sp"])


def build(nl=NS):
    import os
    KSUB = int(os.environ.get("KSUB", "9"))
    KS2 = int(os.environ.get("KS2", "9"))
    nc = bass.Bass("TRN2", target_bir_lowering=False)
    P = Prog(nc)
    stores = []

    def din(name, shape):
        return nc.dram_tensor(name, list(shape), F32, kind="ExternalInput").ap()

    def dout(name, shape):
        return nc.dram_tensor(name, list(shape), F32, kind="ExternalOutput").ap()

    def sb(name, shape, dt=F32):
        return nc.alloc_sbuf_tensor(name, list(shape), dt)

    xT = din("xT", [D, NTOK])
    xsT = din("xsT", [D, TSM])
    cT = din("cT", [D, 17])
    w_mod = din("w_mod", [NS, D, 6 * D])
    b_modT = din("b_modT", [NS, 128, 48])
    g1T = din("g1T", [NS, 128, KT])
    g2T = din("g2T", [NS, 128, KT])
    gfT = din("gfT", [128, KT])
    w_in2 = din("w_in2", [NS, D, WIN])
    ropeP = din("ropeP", [2, 128, NTOK])
    ropeS = din("ropeS", [2, 128, TSM])
    maskP = din("maskP", [2, 128, 512])
    maskS = din("maskS", [2, 128, 512])
    sinkT = din("sinkT", [NS, 64, 8])
    lamre = din("lamre", [NS, 128, 8])
    lamim = din("lamim", [NS, 128, 8])
    logdt = din("logdt", [NS, 128, 8])
    bre = din("bre", [NS, 128, 8, 16])
    bim = din("bim", [NS, 128, 8, 16])
    cre = din("cre", [NS, 128, 8, 16])
    cim = din("cim", [NS, 128, 8, 16])
    dskipT = din("dskipT", [NS, 128, 2])
    wglu = din("wglu", [NS, 256, 256])
    bgluT = din("bgluT", [NS, 128, 2])
    convwT = din("convwT", [NS, 128, 2, 31])
    convbT = din("convbT", [NS, 128, 2])
    lngT = din("lngT", [NS, 128, 2])
    lnbT = din("lnbT", [NS, 128, 2])
    w_out = din("w_out", [NS, D, D])
    w_gate = din("w_gate", [NS, D, DFF])
    w_up = din("w_up", [NS, D, DFF])
    w_down = din("w_down", [NS, DFF, D])
    kcT = din("kcT", [NS, NB, 128, 128])
    vc = din("vc", [NS, NB, 128, 128])
    kcn = din("kcn", [NS, NB, 128, 128])
    ssmre_in = din("ssmre_in", [NS, 128, 8, NB])
    ssmim_in = din("ssmim_in", [NS, 128, 8, NB])
    sconv_in = din("sconv_in", [NS, 128, 2, NB, 30])
    identd = din("identd", [128, 128])
    hprev = din("hprev", [128, 2])
    hin = nc.dram_tensor("hin", [128, HF], F32)
    hall = nc.dram_tensor("hall", [256, HF], F32)
    bhin = Buf("hin"); bhall = Buf("hall")
    jrow = din("jrow", [128, TS + 1])

    yT = dout("yT", [D, NTOK])
    ysT = dout("ysT", [D, TSM])
    nkT = dout("nkT", [NS, 128, 128])
    nv = dout("nv", [NS, 128, 128])
    nre = dout("nre", [NS, 128, 8])
    nim = dout("nim", [NS, 128, 8])
    ncv = dout("ncv", [NS, 128, 2, 30])
    nks_c = dout("nks_c", [NS, NB, 124, 128])
    nvs_c = dout("nvs_c", [NS, NB, 124, 128])
    skT = dout("skT", [NS, 128, TSM])
    svn = dout("svn", [NS, 4, NB, 128])
    sre = dout("sre", [NS, 128, 8, NB])
    sim_o = dout("sim_o", [NS, 128, 8, NB])
    scv = dout("scv", [NS, 128, 2, NB, 30])
    xscr = nc.dram_tensor("xscr", [D, NTOK], F32, kind="Internal").ap()
    wgu_c = nc.dram_tensor("wgu_c", [JT, 128, 2 * KT * 128], BF16, kind="Internal").ap()
    wdn_c = nc.dram_tensor("wdn_c", [16, 128, 11 * 128], BF16, kind="Internal").ap()
    woa_c = nc.dram_tensor("woa_c", [8, 64, 8 * 128], BF16, kind="Internal").ap()
    wor_c = nc.dram_tensor("wor_c", [8, 128, 4 * 128], BF16, kind="Internal").ap()
    bwgu_c = [Buf(f"wguc{j}") for j in range(JT)]
    bwdn_c = [Buf(f"wdnc{j}") for j in range(16)]
    bwo_c = [Buf(f"woc{j}") for j in range(8)]
    DBG = bool(int(os.environ.get("KDBG", "0")))
    if DBG:
        dbg = dout("dbg", [3, 128, 8, TSM])

    PS = [nc.alloc_psum_tensor(f"ps{i}", [128, 512], F32) for i in range(8)]
    bPS = [Buf(f"ps{i}", excl=True) for i in range(8)]
    psrr = [0]

    def nps():
        i = psrr[0]
        psrr[0] = (i + 1) % 6
        return i

    def mm(out, lhsT, rhs, start, stop, r, w):
        return P.add("pe", lambda e: e.matmul(out, lhsT=lhsT, rhs=rhs, start=start, stop=stop), reads=r, writes=w)

    def act(out, in_, func, r, w, bias=None, scale=None):
        kw = {}
        if bias is not None:
            kw["bias"] = bias
        if scale is not None:
            kw["scale"] = scale
        return P.add("act", lambda e: e.activation(out=out, in_=in_, func=func, **kw), reads=r, writes=w)

    def tt(eng, out, in0, in1, op, r, w):
        return P.add(eng, lambda e: e.tensor_tensor(out=out, in0=in0, in1=in1, op=op), reads=r, writes=w)

    def ts(eng, out, in0, s1, s2, op0, op1, r, w):
        if op1 is None:
            return P.add(eng, lambda e: e.tensor_scalar(out=out, in0=in0, scalar1=s1, scalar2=None, op0=op0), reads=r, writes=w)
        return P.add(eng, lambda e: e.tensor_scalar(out=out, in0=in0, scalar1=s1, scalar2=s2, op0=op0, op1=op1), reads=r, writes=w)

    def stt(eng, out, in0, scalar, in1, op0, op1, r, w):
        return P.add(eng, lambda e: e.scalar_tensor_tensor(out=out, in0=in0, scalar=scalar, in1=in1, op0=op0, op1=op1), reads=r, writes=w)

    def cp(eng, out, in_, r, w):
        if eng == "act":
            return P.add("act", lambda e: e.activation(out=out, in_=in_, func=AF.Copy), reads=r, writes=w)
        return P.add(eng, lambda e: e.tensor_copy(out=out, in_=in_), reads=r, writes=w)

    def memset(eng, ap, val, w):
        return P.add(eng, lambda e: e.memset(ap, val), writes=w)

    def ld(out, in_, owner, w, q="sp"):
        return P.add(q, lambda e: e.dma_start(out=out, in_=in_), writes=w, dma_owner=owner)

    def st(out, in_, owner, r):
        o = P.add("sp", lambda e: e.dma_start(out=out, in_=in_), reads=r, dma_owner=owner)
        stores.append(o)
        return o

    ident = sb("ident", [128, 128]); bident = Buf("ident")
    ld(ident[:], identd, bident, [bident])
    ones16 = sb("ones16", [128, 128], BF16); bones = Buf("ones")
    memset("dve", ones16[:], 1.0, [bones])
    jr = sb("jr", [128, TS + 1]); bjr = Buf("jr")
    ld(jr[:], jrow, bjr, [bjr])
    mk = sb("mk", [128, 4, 512], BF16); bmk = Buf("mk")
    mkn = mk[:, 0:2, :]; bmkn = bmk
    ld(mk[:, 0:2, :], maskP.rearrange("a p n -> p a n"), bmk, [bmk], q="pool")
    ld(mk[:, 2:4, :], maskS.rearrange("a p n -> p a n"), bmk, [bmk], q="pool")
    ident16 = sb("ident16", [128, 128], BF16); bident16 = Buf("ident16")
    cp("dve", ident16[:], ident[:], [bident], [bident16])
    rps = sb("rps", [128, 2, TSM]); brps = Buf("rps")
    ld(rps[:], ropeS.rearrange("a p n -> p a n"), brps, [brps])
    gf = sb("gf", [128, KT]); bgf = Buf("gf")
    ld(gf[:], gfT, bgf, [bgf])
    pat = sb("pat", [128, NB, LS]); bpat = Buf("pat")
    memset("dve", pat[:], 1.0, [bpat])
    memset("dve", pat[:, :, 0:1], 0.0, [bpat])

    modT1 = sb("modT1", [128, 48, 17]); bmod = Buf("modT")
    modD = nc.dram_tensor("modD", [NS, 128, 48 * 17], F32).ap()
    bmodD = [Buf(f"modD{i}") for i in range(NS)]
    csb = sb("csb", [128, KT, 17]); bcs = Buf("csb")
    sgc = sb("sgc", [128, KT, 17]); bsgc = Buf("sgc")
    ld(csb[:], cT.rearrange("(k p) n -> p k n", p=128), bcs, [bcs])
    act(sgc[:], csb[:], AF.Sigmoid, [bcs], [bsgc])
    tt("dve", csb[:], csb[:], sgc[:], ALU.mult, [bcs, bsgc], [bcs])
    bmt = sb("bmt", [128, NS, 48]); bbmt = Buf("bmt")
    ld(bmt[:], b_modT.rearrange("l p m -> p l m"), bbmt, [bbmt])
    WMB = 512
    _g0 = nc.sbuf_tensor("wmr0", [128, KT, WMB], F32)
    _g1 = nc.sbuf_tensor("wmr1", [128, KT, WMB], F32)
    _g2 = nc.sbuf_tensor("modrow", [17, 6 * D], F32)
    wmr = [_g0.__enter__(), _g1.__enter__()]
    modrow = _g2.__enter__()
    bwmr = [Buf(f"wmr{i}") for i in range(2)]
    bmrow = Buf("modrow")
    lastmod = None
    it = 0
    for l in range(nl):
        for blk in range(6 * D // WMB):
            s = it % 2
            it += 1
            ld(wmr[s][:], w_mod[l, :, blk * WMB:(blk + 1) * WMB].rearrange("(k p) n -> p k n", p=128), bwmr[s], [bwmr[s]])
            pi = nps()
            for k in range(KT):
                mm(PS[pi][0:17, 0:WMB], csb[:, k, :], wmr[s][:, k, :], k == 0, k == KT - 1, [bwmr[s], bcs], [bPS[pi]])
            cp("act" if blk % 2 == 0 else "dve", modrow[:, blk * WMB:(blk + 1) * WMB], PS[pi][0:17, 0:WMB], [bPS[pi]], [bmrow])
        for m in range(48):
            pi = nps()
            P.add("pe", lambda e, pi=pi, m=m: e.transpose(out=PS[pi][:, 0:17], in_=modrow[:, m * 128:(m + 1) * 128], identity=ident[0:17, 0:17]),
                  reads=[bmrow, bident], writes=[bPS[pi]])
            lastmod = ts("dve", modT1[:, m, :], PS[pi][:, 0:17], bmt[:, l, m:m + 1], None, ALU.add, None, [bPS[pi], bbmt], [bmod])
        lastmod = P.add("sp", lambda e, l=l: e.dma_start(out=modD[l], in_=modT1[:].rearrange("p m c -> p (m c)")),
                        reads=[bmod], writes=[bmodD[l]], dma_owner=bmod)
    _g2.__exit__(None, None, None)
    _g1.__exit__(None, None, None)
    _g0.__exit__(None, None, None)
    for _e in ("pe", "act", "pool", "sp"):
        P.add(_e, lambda e: e.nop(), extra_deps=[lastmod])

    xs = sb("xs", [128, KT, T]); bx = [Buf(f"x{k}") for k in range(KT)]
    xsm = sb("xsm", [128, KT, TSM]); bxsm = Buf("xsm")
    ld(xsm[:], xsT.rearrange("(k p) n -> p k n", p=128), bxsm, [bxsm])
    h16 = sb("h16", [128, KT, T], BF16); bh = Buf("h16")
    scr16 = sb("scr16", [128, JT, T], BF16); bscr = Buf("scr16")
    rstd = sb("rstd", [128, T]); brstd = Buf("rstd")
    tmpAB = sb("tmpAB", [128, 2, T])
    tmpA = tmpAB[:, 0, :]; btA = Buf("tmpA")
    tmpB = tmpAB[:, 1, :]; btB = Buf("tmpB")
    tmpC = sb("tmpC", [128, T]); btC = Buf("tmpC")
    tmpD = sb("tmpD", [128, T]); btD = Buf("tmpD")
    rp = sb("rp", [128, 2, T]); brp = Buf("rp")
    q16 = sb("q16", [128, 4, T], BF16); bq = Buf("q16")
    kb16 = sb("kb16", [128, 128 + T], BF16); bkb = Buf("kb16")
    k32 = sb("k32", [128, T]); bk32 = Buf("k32")
    v16 = sb("v16", [128, 1 + T // 128, 128], BF16); bv16 = Buf("v16")
    v32 = sb("v32", [128, 128]); bv32 = Buf("v32")
    pown = sb("pown", [128, 512], BF16); bpown = Buf("pown")
    pprev = sb("pprev", [128, 512], BF16); bpprev = Buf("pprev")
    den = tmpAB[0:64].rearrange("p a t -> p (a t)")
    att16 = sb("att16", [64, 8, T], BF16); batt = Buf("att16")
    u32 = sb("u32", [128, 2, T]); bu32 = Buf("u32")
    u16 = sb("u16", [128, 2, T], BF16); bu16 = Buf("u16")
    cb32 = sb("cb32", [128, 2, 30 + T]); bcb32 = Buf("cb32")
    cb16 = sb("cb16", [128, 2, 30 + T], BF16); bcb16 = Buf("cb16")
    cbs32 = sb("cbs32", [128, 2, NB, 34]); bcbs32 = Buf("cbs32")
    cbs16 = sb("cbs16", [128, 2, NB, 34], BF16); bcbs16 = Buf("cbs16")
    ycf = sb("ycf", [128, 2, T]); bycf = Buf("ycf")
    cva = sb("cva", [128, T]); bcva = Buf("cva")
    cvb = sb("cvb", [128, T]); bcvb = Buf("cvb")
    cvc = sb("cvc", [128, 2, T]); bcvc = Buf("cvc")
    yc16 = scr16[:, 8:12, :]; byc16 = bscr
    oc16 = sb("oc16", [128, 2, T], BF16); boc = Buf("oc16")
    os16 = sb("os16", [128, 2, T], BF16); bos = Buf("os16")
    yss = sb("yss", [128, 2, T]); byss = Buf("yss")
    z32 = sb("z32", [128, 2, T]); bz32 = Buf("z32")
    assert 2 * T == 4 * 8 * 16
    z16 = sb("z16", [128, 2, T], BF16); bz16 = Buf("z16")
    sx = [sb(f"sx{i}", [128, TS]) for i in range(8)]
    bsx = [Buf(f"sx{i}") for i in range(8)]
    sq = [[sb(f"sq{i}{j}", [128, TS]) for j in range(2)] for i in range(2)]
    bsq = [[Buf(f"sq{i}{j}") for j in range(2)] for i in range(2)]
    sh16 = [[sb(f"sh16{i}{j}", [128, TS], BF16) for j in range(2)] for i in range(2)]
    bsh16 = [[Buf(f"sh16{i}{j}") for j in range(2)] for i in range(2)]
    ssi = [0]
    hre16 = sh16[0][0]; bhre = bsh16[0][0]
    him16 = sh16[0][1]; bhim = bsh16[0][1]
    qlre = sb("qlre", [128, 8]); qlim = sb("qlim", [128, 8]); bql = Buf("ql")
    hlre = sb("hlre", [128, 8]); hlim = sb("hlim", [128, 8]); bhl = Buf("hl")
    inre = sb("inre", [128, 8]); inim = sb("inim", [128, 8]); binit = Buf("init")
    sm8 = [sb(f"sm8_{i}", [128, 8]) for i in range(4)]; bsm8 = Buf("sm8")
    h0re = sb("h0re", [128, 8, NB]); h0im = sb("h0im", [128, 8, NB]); bh0 = Buf("h0")
    ahre = sb("ahre", [128, 8, NB]); ahim = sb("ahim", [128, 8, NB]); bah = Buf("ah")
    hsre = sb("hsre", [128, 8, NB]); hsim = sb("hsim", [128, 8, NB]); bhs = Buf("hs")
    st16 = [sb(f"st16_{i}", [128, NB]) for i in range(4)]; bst16 = Buf("st16")
    r_kc = sb("r_kc", [128, 2048], BF16); bkc = Buf("kc16")
    r_vc = sb("r_vc", [128, 2048], BF16); bvc = Buf("vc16")
    kc16 = r_kc[:].rearrange("p (b k) -> p b k", k=128)
    vc16 = r_vc[:].rearrange("p (b k) -> p b k", k=128)
    vn16 = sb("vn16", [4, NB, 128], BF16); bvn16 = Buf("vn16")
    vn32 = sb("vn32", [4, NB, 128]); bvn32 = Buf("vn32")
    pn16 = sb("pn16", [4, 512], BF16); bpn = Buf("pn16")

    win16 = sb("win16", [128, KT, WIN], BF16); bwin = Buf("win16")
    wgl16 = sb("wgl16", [128, 2, 256], BF16); bwgl = Buf("wgl16")
    sp8 = sb("sp8", [128, 12, 8]); bsp8 = Buf("sp8")
    sp2 = sb("sp2", [128, 8, 2]); bsp2 = Buf("sp2")
    cw = sb("cw", [128, 2, 31]); bcw = Buf("cw")
    sk = sb("sk", [64, 8]); bsk = Buf("sk")
    bc32 = yss[:].rearrange("p a (b c) -> p (a b) c", c=16).rearrange("p (a b) c -> p a b c", a=4); bbc = byss
    bbt = z32[:].rearrange("p a (b c) -> p (a b) c", c=16).rearrange("p (a b) c -> p a b c", a=4); bbbt = bz32
    exq = sb("exq", [128, 128]); bexq = Buf("exq")
    LB = sb("LB", [128, 4, 8, 128], BF16); bLB = Buf("LB")
    cosT = sb("cosT", [128, 8, TS + 1]); sinT = sb("sinT", [128, 8, TS + 1]); btab = Buf("tab")
    r4 = sb("r4", [128, 8, NB, LS]); btab4 = Buf("tab4")
    diag16 = sb("diag16", [128, 2, 31, 128], BF16); bdiag = Buf("diag16")
    gsc = sb("gsc", [128, 2, KT, 17]); bgsc = Buf("gsc")
    NG = 8
    _raw = [sb("wr0", [128, 2048], BF16), sb("wr1", [128, 2048], BF16), r_kc, r_vc, sb("wr4", [128, 2048], BF16),
            sb("wr5", [128, 2048], BF16), sb("wr6", [128, 2048], BF16), sb("wr7", [128, 2048], BF16)]
    bwgu = [Buf("wr0"), Buf("wr1"), bkc, bvc, Buf("wr4"), Buf("wr5"), Buf("wr6"), Buf("wr7")]
    wfl = [r[:] for r in _raw]
    wgu = [r[:].rearrange("p (a k c) -> p a k c", a=2, k=KT) for r in _raw]
    wdn = [r[:, 0:1408].rearrange("p (j c) -> p j c", c=128) for r in _raw]
    bwdn = bwgu
    ffi = [0, 0]

    def S8(i):
        return sp8[:, i, :]

    angt = tmpC[:, 0:TS + 1]; angk = tmpD[:, 0:TS + 1]; bang = btC
    CM = 12582912.0

    def sin_of(out, x, shift, tmp, r, w, wt):
        xs_ = x
        if shift != 0.0:
            ts("dve", out, x, shift, None, ALU.add, None, r, w)
            xs_ = out
        ts("dve", tmp, xs_, 1.0 / TWO_PI, CM, ALU.mult, ALU.add, r + w, wt)
        ts("dve", tmp, tmp, -CM, None, ALU.add, None, r + wt, wt)
        stt("dve", tmp, tmp, -TWO_PI, xs_, ALU.mult, ALU.add, r + w + wt, wt)
        ts("dve", tmp, tmp, math.pi, -math.pi, ALU.min, ALU.max, r + wt, wt)
        act(out, tmp, AF.Sin, r + wt, w)

    def layer_params(l):
        for k in range(KT):
            ld(win16[:, k, :], w_in2[l, k * 128:(k + 1) * 128, :], bwin, [bwin], q="pool")
        ld(wgl16[:], wglu[l].rearrange("(k p) n -> p k n", p=128), bwgl, [bwgl], q="pool")
        ld(sp8[:, 0, :], lamre[l], bsp8, [bsp8])
        ld(sp8[:, 1, :], lamim[l], bsp8, [bsp8])
        ld(sp8[:, 2, :], logdt[l], bsp8, [bsp8])
        ld(sp2[:, 0, :], dskipT[l], bsp2, [bsp2])
        ld(sp2[:, 1, :], bgluT[l], bsp2, [bsp2])
        ld(sp2[:, 2, :], convbT[l], bsp2, [bsp2])
        ld(sp2[:, 3, :], lngT[l], bsp2, [bsp2])
        ld(sp2[:, 4, :], lnbT[l], bsp2, [bsp2])
        ld(cw[:], convwT[l], bcw, [bcw])
        ld(sk[:], sinkT[l], bsk, [bsk])
        act(sk[:], sk[:], AF.Exp, [bsk], [bsk])
        ld(bc32[:, 0], bre[l], bbc, [bbc])
        ld(bc32[:, 1], bim[l], bbc, [bbc])
        ld(bc32[:, 2], cre[l], bbc, [bbc])
        ld(bc32[:, 3], cim[l], bbc, [bbc])
        P.add("sp", lambda e, l=l: e.dma_start(out=modT1[:].rearrange("p m c -> p (m c)"), in_=modD[l]),
              reads=[bmodD[l]], writes=[bmod], dma_owner=bmod)
        for a, (gT, off) in enumerate(((g1T, 8), (g2T, 32))):
            ld(sp8[:, 3, :], gT[l], bsp8, [bsp8])
            ts("dve", gsc[:, a], modT1[:, off:off + 8, :], 1.0, None, ALU.add, None, [bmod], [bgsc])
            tt("dve", gsc[:, a], gsc[:, a], sp8[:, 3, :].unsqueeze(2).to_broadcast([128, KT, 17]), ALU.mult, [bgsc, bsp8], [bgsc])
        R = [bsp8]
        W = [bsp8]
        act(S8(2), S8(2), AF.Exp, R, W)
        tt("dve", S8(3), S8(0), S8(2), ALU.mult, R, W)
        tt("dve", S8(4), S8(1), S8(2), ALU.mult, R, W)
        act(S8(5), S8(3), AF.Exp, R, W)
        sin_of(S8(7), S8(4), 0.0, sm8[0][:], R + [bsm8], W, [bsm8])
        sin_of(S8(6), S8(4), 0.5 * math.pi, sm8[0][:], R + [bsm8], W, [bsm8])
        tt("dve", S8(8), S8(5), S8(6), ALU.mult, R, W)
        tt("dve", S8(9), S8(5), S8(7), ALU.mult, R, W)
        a0, a1, a2, a3 = (sm8[i][:] for i in range(4))
        R2 = [bsp8, bsm8]
        tt("dve", a0, S8(0), S8(0), ALU.mult, R2, [bsm8])
        tt("dve", a1, S8(1), S8(1), ALU.mult, R2, [bsm8])
        tt("dve", a0, a0, a1, ALU.add, R2, [bsm8])
        P.add("dve", lambda e: e.reciprocal(out=a0, in_=a0), reads=R2, writes=[bsm8])
        ts("dve", a1, S8(8), -1.0, None, ALU.add, None, R2, [bsm8])
        tt("dve", a2, a1, S8(0), ALU.mult, R2, [bsm8])
        tt("dve", a3, S8(9), S8(1), ALU.mult, R2, [bsm8])
        tt("dve", a2, a2, a3, ALU.add, R2, [bsm8])
        tt("dve", S8(10), a2, a0, ALU.mult, R2, W)
        tt("dve", a2, S8(9), S8(0), ALU.mult, R2, [bsm8])
        tt("dve", a3, a1, S8(1), ALU.mult, R2, [bsm8])
        tt("dve", a2, a2, a3, ALU.subtract, R2, [bsm8])
        tt("dve", S8(11), a2, a0, ALU.mult, R2, W)
        cre_b = sp8[:, 10, :].unsqueeze(2).to_broadcast([128, 8, 16])
        cim_b = sp8[:, 11, :].unsqueeze(2).to_broadcast([128, 8, 16])
        Rb = [bbc, bsp8, bbbt]
        tt("dve", bbt[:, 0], bc32[:, 0], cre_b, ALU.mult, Rb, [bbbt])
        tt("dve", bbt[:, 2], bc32[:, 1], cim_b, ALU.mult, Rb, [bbbt])
        tt("dve", bbt[:, 0], bbt[:, 0], bbt[:, 2], ALU.subtract, Rb, [bbbt])
        tt("dve", bbt[:, 1], bc32[:, 1], cre_b, ALU.mult, Rb, [bbbt])
        tt("dve", bbt[:, 2], bc32[:, 0], cim_b, ALU.mult, Rb, [bbbt])
        tt("dve", bbt[:, 1], bbt[:, 1], bbt[:, 2], ALU.add, Rb, [bbbt])
        cp("dve", bbt[:, 2], bc32[:, 2], Rb, [bbbt])
        ts("dve", bbt[:, 3], bc32[:, 3], -1.0, None, ALU.mult, None, Rb, [bbbt])
        xsf = xs[:].rearrange("p k t -> p (k t)")
        Eb = [xsf[:, 0:1024].rearrange("p (c n) -> p c n", n=128), xsf[:, 1024:2048].rearrange("p (c n) -> p c n", n=128)]
        bEb = [bx[0:4], bx[4:8]]
        for mi in range(4):
            E = Eb[mi % 2]
            bE = bEb[mi % 2]
            memset("dve", E, 0.0, bE)
            for ct in range(8):
                for gg in range(2):
                    gp = (2 * ct + gg) % 8
                    cp("dve", E[64 * gg:64 * gg + 64, ct, 16 * gp:16 * gp + 16], bbt[64 * gg:64 * gg + 64, mi, ct, :], [bbbt], bE)
            if mi < 2:
                for ct in range(8):
                    pi = nps()
                    P.add("pe", lambda e, pi=pi, E=E, ct=ct: e.transpose(out=PS[pi][:, 0:128], in_=E[:, ct, :], identity=ident[:]),
                          reads=bE + [bident], writes=[bPS[pi]])
                    cp("act", LB[:, mi, ct, :], PS[pi][:, 0:128], [bPS[pi]], [bLB])
            else:
                cp("act", LB[:, mi], E, bE, [bLB])
        angt3 = xsf[:, 0:8 * (TS + 1)].rearrange("p (c j) -> p c j", j=TS + 1)
        angk3 = scr16[:, 0:9, :].rearrange("p a b -> p (a b)").bitcast(F32)[:, 0:8 * (TS + 1)].rearrange("p (c j) -> p c j", j=TS + 1)
        tt("dve", angt3, jr[:].unsqueeze(1).to_broadcast([128, 8, TS + 1]), sp8[:, 4, :].unsqueeze(2).to_broadcast([128, 8, TS + 1]),
           ALU.mult, [bjr, bsp8], bx)
        sin_of(sinT[:], angt3, 0.0, angk3, bx, [btab], [bscr])
        sin_of(cosT[:], angt3, 0.5 * math.pi, angk3, bx, [btab], [bscr])
        for ct in range(8):
            ts("dve", r4[:, ct], pat[:], sp8[:, 5, ct:ct + 1], None, ALU.mult, None, [bpat, bsp8], [btab4])
        for m in range(2):
            for k in range(31):
                act(diag16[:, m, k, :], ident[:], AF.Identity, [bident, bcw], [bdiag], scale=cw[:, m, k:k + 1])

    def rmsnorm(xa, bxs, n, gcol, shm, a, l, sample):
        for k in range(KT):
            act(scr16[:, k, 0:n], xa[:, k, :], AF.Square, [bxs[k]], [bscr])
        pi = nps()
        for k in range(KT):
            mm(PS[pi][:, 0:n], ones16[:], scr16[:, k, 0:n], k == 0, k == KT - 1, [bones, bscr], [bPS[pi]])
        ts("dve", rstd[:, 0:n], PS[pi][:, 0:n], 1.0 / D, 1e-6, ALU.mult, ALU.add, [bPS[pi]], [brstd])
        act(rstd[:, 0:n], rstd[:, 0:n], AF.Sqrt, [brstd], [brstd])
        P.add("dve", lambda e: e.reciprocal(out=rstd[:, 0:n], in_=rstd[:, 0:n]), reads=[brstd], writes=[brstd])
        for k in range(KT):
            tA = tmpA if k % 2 == 0 else tmpB
            bA = btA if k % 2 == 0 else btB
            tt("dve", tA[:, 0:n], xa[:, k, :], rstd[:, 0:n], ALU.mult, [bxs[k], brstd], [bA])
            if gcol is None:
                ts("pool", xa[:, k, :], tA[:, 0:n], gf[:, k:k + 1], None, ALU.mult, None, [bA, bgf], [bxs[k]])
            elif not sample:
                if False:
                    pass
                else:
                    act(h16[:, k, 0:n], tA[:, 0:n], AF.Identity, [bA, bgsc, bmod], [bh],
                        bias=modT1[:, shm + k, 0:1], scale=gsc[:, a, k, 0:1])
            else:
                v3 = tA[:, 0:n].rearrange("p (b t) -> p b t", t=LS)
                if True:
                    tt("pool", v3, v3, gsc[:, a, k, 1:17].unsqueeze(2).to_broadcast([128, NB, LS]), ALU.mult, [bA, bgsc], [bA])
                    tt("pool", h16[:, k, 0:n].rearrange("p (b t) -> p b t", t=LS), v3,
                       modT1[:, shm + k, 1:17].unsqueeze(2).to_broadcast([128, NB, LS]), ALU.add, [bA, bmod], [bh])

    def resid(xa, bxs, n, mo, pi, l, gm, sample):
        if not sample:
            stt("dve", xa[:, mo, :], PS[pi][:, 0:n], modT1[:, gm + mo, 0:1], xa[:, mo, :], ALU.mult, ALU.add,
                [bPS[pi], bmod, bxs[mo]], [bxs[mo]])
        else:
            tt("dve", tmpC[:, 0:n].rearrange("p (b t) -> p b t", t=LS), PS[pi][:, 0:n].rearrange("p (b t) -> p b t", t=LS),
               modT1[:, gm + mo, 1:17].unsqueeze(2).to_broadcast([128, NB, LS]), ALU.mult, [bPS[pi], bmod], [btC])
            tt("dve", xa[:, mo, :], xa[:, mo, :], tmpC[:, 0:n], ALU.add, [btC, bxs[mo]], [bxs[mo]])

    def inproj_tile(m, n):
        pi = nps()
        for k in range(KT):
            mm(PS[pi][:, 0:n], win16[:, k, m * 128:(m + 1) * 128], h16[:, k, 0:n], k == 0, k == KT - 1, [bwin, bh], [bPS[pi]])
        return pi

    ri = [0]

    def rope_pair(m_a, m_b, cos_ap, sin_ap, brope, out_ap, bout, n, extra32=None):
        pa = inproj_tile(m_a, n)
        pb = inproj_tile(m_b, n)
        ri[0] += 1
        if ri[0] % 2 == 0:
            tX, bX, tY, bY = tmpA, btA, tmpB, btB
        else:
            tX, bX, tY, bY = tmpC, btC, tmpD, btD
        tt("dve", tX[:, 0:n], PS[pa][:, 0:n], cos_ap, ALU.mult, [bPS[pa], brope], [bX])
        tt("dve", tY[:, 0:n], PS[pb][:, 0:n], sin_ap, ALU.mult, [bPS[pb], brope], [bY])
        tt("pool", out_ap, tX[:, 0:n], tY[:, 0:n], ALU.add, [bX, bY], [bout])
        if extra32 is not None:
            tt("pool", extra32[0], tX[:, 0:n], tY[:, 0:n], ALU.add, [bX, bY], [extra32[1]])

    def gelu_glu(n):
        tmps = [(tmpC, btC), (tmpD, btD)]
        for step in range(3):
            for o in range(2):
                tq, btq = tmps[o]
                if step == 0:
                    act(tq[:, 0:n], yss[:, o, 0:n], AF.Square, [byss], [btq])
                elif step == 1:
                    ts("dve", tq[:, 0:n], tq[:, 0:n], 0.044715, 1.0, ALU.mult, ALU.add, [btq], [btq])
                else:
                    tt("dve", tq[:, 0:n], tq[:, 0:n], yss[:, o, 0:n], ALU.mult, [btq, byss], [btq])
        for o in range(2):
            tq, btq = tmps[o]
            act(tq[:, 0:n], tq[:, 0:n], AF.Sigmoid, [btq], [btq], scale=2.0 * math.sqrt(2.0 / math.pi))
        for o in range(2):
            tq, btq = tmps[o]
            tt("dve", z32[:, o, 0:n], yss[:, o, 0:n], tq[:, 0:n], ALU.mult, [byss, btq], [bz32])
            cp("act", z16[:, o, 0:n], z32[:, o, 0:n], [bz32], [bz16])
        pis = []
        for o in range(2):
            pi = nps()
            pis.append(pi)
            for k in range(2):
                mm(PS[pi][:, 0:n], wgl16[:, k, o * 128:(o + 1) * 128], z16[:, k, 0:n], k == 0, k == 1, [bwgl, bz16], [bPS[pi]])
        for o in range(2):
            tq, btq = tmps[o]
            act(tq[:, 0:n], PS[pis[o]][:, 0:n], AF.Sigmoid, [bPS[pis[o]], bsp2], [btq], bias=sp2[:, 1, o:o + 1])
        for o in range(2):
            tq, btq = tmps[o]
            tt("dve", os16[:, o, 0:n], z32[:, o, 0:n], tq[:, 0:n], ALU.mult, [bz32, btq], [bos])

    def conv_ln_stages(rhs_fn, n, view):
        pcs = {}

        def st_mm(m):
            def f():
                pi = nps()
                pcs[m] = pi
                for k in range(31):
                    mm(view(PS[pi][:, 0:n]), diag16[:, m, k, :], rhs_fn(m, k), k == 0, k == 30, [bdiag, bcb16, bcbs16], [bPS[pi]])
            return f

        def st_evac(m):
            def f():
                pi = pcs[m]
                act(ycf[:, m, 0:n], PS[pi][:, 0:n], AF.Identity, [bPS[pi], bsp2], [bycf], bias=sp2[:, 2, m:m + 1])
                act(yc16[:, 2 + m, 0:n], PS[pi][:, 0:n], AF.Square, [bPS[pi], bsp2], [byc16], bias=sp2[:, 2, m:m + 1])
            return f

        def st_cast(m):
            def f():
                cp("dve", yc16[:, m, 0:n], ycf[:, m, 0:n], [bycf], [byc16])
            return f

        def st_statmm():
            p1 = nps()
            for m in range(2):
                mm(PS[p1][:, 0:n], ones16[:], yc16[:, m, 0:n], m == 0, m == 1, [bones, byc16], [bPS[p1]])
            p2 = nps()
            for m in range(2):
                mm(PS[p2][:, 0:n], ones16[:], yc16[:, 2 + m, 0:n], m == 0, m == 1, [bones, byc16], [bPS[p2]])
            pcs["p1"] = p1
            pcs["p2"] = p2

        def st_stat1():
            p1, p2 = pcs["p1"], pcs["p2"]
            ts("dve", cva[:, 0:n], PS[p1][:, 0:n], 1.0 / 256, None, ALU.mult, None, [bPS[p1]], [bcva])
            tt("dve", cvb[:, 0:n], cva[:, 0:n], cva[:, 0:n], ALU.mult, [bcva], [bcvb])
            stt("dve", cvb[:, 0:n], PS[p2][:, 0:n], 1.0 / 256, cvb[:, 0:n], ALU.mult, ALU.subtract, [bPS[p2], bcvb], [bcvb])
            ts("dve", cvb[:, 0:n], cvb[:, 0:n], 1e-6, None, ALU.add, None, [bcvb], [bcvb])
            act(cvb[:, 0:n], cvb[:, 0:n], AF.Sqrt, [bcvb], [bcvb])

        def st_stat2():
            P.add("dve", lambda e: e.reciprocal(out=cvb[:, 0:n], in_=cvb[:, 0:n]), reads=[bcvb], writes=[bcvb])

        def st_apply1(m):
            def f():
                tt("dve", ycf[:, m, 0:n], ycf[:, m, 0:n], cva[:, 0:n], ALU.subtract, [bycf, bcva], [bycf])
                tt("dve", ycf[:, m, 0:n], ycf[:, m, 0:n], cvb[:, 0:n], ALU.mult, [bycf, bcvb], [bycf])
                act(ycf[:, m, 0:n], ycf[:, m, 0:n], AF.Identity, [bycf, bsp2], [bycf], bias=sp2[:, 4, m:m + 1], scale=sp2[:, 3, m:m + 1])
                act(cvc[:, m, 0:n], ycf[:, m, 0:n], AF.Sigmoid, [bycf], [bcvc])
            return f

        def st_apply2(m):
            def f():
                tt("dve", oc16[:, m, 0:n], ycf[:, m, 0:n], cvc[:, m, 0:n], ALU.mult, [bcvc, bycf], [boc])
            return f
        return [st_mm(0), st_mm(1), st_evac(0), st_evac(1), st_cast(0), st_cast(1), st_statmm, st_stat1, st_stat2,
                st_apply1(0), st_apply1(1), st_apply2(0), st_apply2(1)]

    def conv_ln(rhs_fn, n, view):
        for f in conv_ln_stages(rhs_fn, n, view):
            f()

    def outproj_ffn(xa, bxs, n, l, sample, first=False):
        for mo in range(8):
            s = ffi[0] % NG
            ffi[0] += 1
            wa_v = wfl[s][0:64, 0:1024].rearrange("p (h c) -> p h c", c=128)
            wr_v = wfl[s][:, 1024:1536].rearrange("p (h c) -> p h c", c=128)
            fa = wfl[s][0:64, 0:1024]
            fr = wfl[s][:, 1024:1536]
            bw = bwgu[s]
            if first:
                ld(wa_v, w_out[l, 0:512, mo * 128:(mo + 1) * 128].rearrange("(h d) n -> d h n", d=64), bw, [bw], q="pool")
                ld(wr_v, w_out[l, 512:1024, mo * 128:(mo + 1) * 128].rearrange("(j p) n -> p j n", p=128), bw, [bw], q="pool")
                P.add("sp", lambda e, fa=fa, mo=mo: e.dma_start(out=woa_c[mo], in_=fa), reads=[bw], writes=[bwo_c[mo]], dma_owner=bw)
                P.add("sp", lambda e, fr=fr, mo=mo: e.dma_start(out=wor_c[mo], in_=fr), reads=[bw], writes=[bwo_c[mo]], dma_owner=bw)
            else:
                P.add("sp", lambda e, fa=fa, mo=mo: e.dma_start(out=fa, in_=woa_c[mo]), reads=[bwo_c[mo]], writes=[bw], dma_owner=bw)
                P.add("sp", lambda e, fr=fr, mo=mo: e.dma_start(out=fr, in_=wor_c[mo]), reads=[bwo_c[mo]], writes=[bw], dma_owner=bw)
            pi = nps()
            for hq in range(8):
                mm(PS[pi][:, 0:n], wa_v[:, hq, :], att16[:, hq, 0:n], hq == 0, False, [bw, batt], [bPS[pi]])
            for j in range(2):
                mm(PS[pi][:, 0:n], wr_v[:, 2 + j, :], oc16[:, j, 0:n], False, False, [bw, boc], [bPS[pi]])
            for j in range(2):
                mm(PS[pi][:, 0:n], wr_v[:, j, :], os16[:, j, 0:n], False, j == 1, [bw, bos], [bPS[pi]])
            resid(xa, bxs, n, mo, pi, l, 16, sample)
        rmsnorm(xa, bxs, n, 1, 24, 1, l, sample)
        for j in range(JT):
            s = ffi[0] % NG
            ffi[0] += 1
            fg = wfl[s]
            if first:
                ld(wgu[s][:, 0], w_gate[l, :, j * 128:(j + 1) * 128].rearrange("(k p) n -> p k n", p=128), bwgu[s], [bwgu[s]], q="pool")
                ld(wgu[s][:, 1], w_up[l, :, j * 128:(j + 1) * 128].rearrange("(k p) n -> p k n", p=128), bwgu[s], [bwgu[s]], q="pool")
                P.add("sp", lambda e, fg=fg, j=j: e.dma_start(out=wgu_c[j], in_=fg), reads=[bwgu[s]], writes=[bwgu_c[j]], dma_owner=bwgu[s])
            else:
                P.add("sp", lambda e, fg=fg, j=j: e.dma_start(out=fg, in_=wgu_c[j]), reads=[bwgu_c[j]], writes=[bwgu[s]], dma_owner=bwgu[s])
            pg = nps()
            for k in range(KT):
                mm(PS[pg][:, 0:n], wgu[s][:, 0, k, :], h16[:, k, 0:n], k == 0, k == KT - 1, [bwgu[s], bh], [bPS[pg]])
            pu = nps()
            for k in range(KT):
                mm(PS[pu][:, 0:n], wgu[s][:, 1, k, :], h16[:, k, 0:n], k == 0, k == KT - 1, [bwgu[s], bh], [bPS[pu]])
            tA = tmpA if j % 2 == 0 else tmpB
            bA = btA if j % 2 == 0 else btB
            act(tA[:, 0:n], PS[pg][:, 0:n], AF.Silu, [bPS[pg]], [bA])
            tt("dve", scr16[:, j, 0:n], tA[:, 0:n], PS[pu][:, 0:n], ALU.mult, [bA, bPS[pu]], [bscr])
        for mo in range(8):
            pi = nps()
            for jh in range(2):
                s = ffi[0] % NG
                ffi[0] += 1
                ci = mo * 2 + jh
                fd = wfl[s][:, 0:1408]
                if first:
                    ld(wdn[s], w_down[l, jh * 1408:(jh + 1) * 1408, mo * 128:(mo + 1) * 128].rearrange("(j p) n -> p j n", p=128), bwdn[s], [bwdn[s]], q="pool")
                    P.add("sp", lambda e, fd=fd, ci=ci: e.dma_start(out=wdn_c[ci], in_=fd), reads=[bwdn[s]], writes=[bwdn_c[ci]], dma_owner=bwdn[s])
                else:
                    P.add("sp", lambda e, fd=fd, ci=ci: e.dma_start(out=fd, in_=wdn_c[ci]), reads=[bwdn_c[ci]], writes=[bwdn[s]], dma_owner=bwdn[s])
                for jj in range(11):
                    j = jh * 11 + jj
                    mm(PS[pi][:, 0:n], wdn[s][:, jj, :], scr16[:, j, 0:n], j == 0, j == JT - 1, [bwdn[s], bscr], [bPS[pi]])
            resid(xa, bxs, n, mo, pi, l, 40, sample)

    hp = sb("hp", [128, 2]); bhp = Buf("hp")
    ld(hp[:], hprev, bhp, [bhp])
    hst = ycf[:].rearrange("p a t -> p (a t)")[:, 0:HF]
    GROUPS = [[0, 4], [1, 5], [2, 6], [3, 7]]

    def slot_begin(l):
        if l == 0:
            memset("dve", kb16[:, 0:128], 0.0, [bkb])
            memset("dve", v16[:, 0, :], 0.0, [bv16])
            memset("dve", cb32[:, :, 0:30], 0.0, [bcb32])
            memset("dve", inre[:], 0.0, [binit])
            memset("dve", inim[:], 0.0, [binit])
            return
        P.add("sp", lambda e: e.dma_start(out=hst, in_=hall.ap()[0:128, :]), reads=[bhall], writes=[bycf], dma_owner=bycf)
        ts("dve", hst, hst, hp[:, 0:1], None, ALU.mult, None, [bycf, bhp], [bycf])
        cp("dve", kb16[:, 0:128], hst[:, 0:128], [bycf], [bkb])
        cp("dve", v16[:, 0, :], hst[:, 128:256], [bycf], [bv16])
        cp("dve", cb32[:, :, 0:30], hst[:, 256:316].rearrange("p (a r) -> p a r", a=2), [bycf], [bcb32])
        cp("dve", inre[:], hst[:, 316:324], [bycf], [binit])
        cp("dve", inim[:], hst[:, 324:332], [bycf], [binit])

    def slot_end(l):
        cp("dve", hst[:, 0:128], kb16[:, 0:128], [bkb], [bycf])
        cp("dve", hst[:, 128:256], v16[:, 0, :], [bv16], [bycf])
        cp("dve", hst[:, 256:316].rearrange("p (a r) -> p a r", a=2), cb32[:, :, 0:30], [bcb32], [bycf])
        cp("dve", hst[:, 316:324], inre[:], [binit], [bycf])
        cp("dve", hst[:, 324:332], inim[:], [binit], [bycf])
        P.add("sp", lambda e: e.dma_start(out=hin.ap(), in_=hst), reads=[bycf], writes=[bhin], dma_owner=bycf)
        P.add("pool", lambda e: e.collective_compute("AllGather", ALU.bypass, replica_groups=GROUPS,
                                                     ins=[hin.ap().opt()], outs=[hall.ap().opt()]),
              reads=[bhin], writes=[bhall], dma_owner=bhall, dinc=1)

    def prompt_chunk(l, c):
        n = T
        t0 = c * T
        src = xT if l == 0 else xscr
        xdr = src[:, t0:t0 + T].rearrange("(k p) t -> p k t", p=128)
        ld(xs[:], xdr, bx[0], bx)
        ld(rp[:], ropeP[:, :, t0:t0 + T].rearrange("a p t -> p a t"), brp, [brp])
        if KS2 < 1:
            return
        rmsnorm(xs, bx, n, 1, 0, 0, l, False)
        if KS2 < 2:
            return
        for j in range(4):
            rope_pair(j, 5 + j, rp[:, 0, :], rp[:, 1, :], brp, q16[:, j, :], bq, n)
        rope_pair(4, 9, rp[:, 0, :], rp[:, 1, :], brp, kb16[:, 128:128 + T], bkb, n, extra32=(k32[:, 0:n], bk32))
        if KS2 < 3:
            return
        for o in range(2):
            pi = inproj_tile(10 + o, n)
            cp("act", u32[:, o, :], PS[pi][:, 0:n], [bPS[pi]], [bu32])
            cp("dve", u16[:, o, :], PS[pi][:, 0:n], [bPS[pi]], [bu16])
        for o in range(2):
            pa = inproj_tile(12 + o, n)
            pg = inproj_tile(14 + o, n)
            tS, bS = (tmpA, btA) if o == 0 else (tmpB, btB)
            act(tS[:, 0:n], PS[pg][:, 0:n], AF.Sigmoid, [bPS[pg]], [bS])
            tt("dve", cb32[:, o, 30:30 + T], PS[pa][:, 0:n], tS[:, 0:n], ALU.mult, [bPS[pa], bS], [bcb32])
            cp("pool", cb16[:, o, :], cb32[:, o, :], [bcb32], [bcb16])
        if KS2 < 4:
            return
        for tb in range(T // 128):
            pi = nps()
            for k in range(KT):
                mm(PS[pi][:, 0:128], h16[:, k, tb * 128:(tb + 1) * 128], win16[:, k, 2048:2176], k == 0, k == KT - 1, [bh, bwin], [bPS[pi]])
            cp("act", v16[:, 1 + tb, :], PS[pi][:, 0:128], [bPS[pi]], [bv16])
            if c == NCH - 1 and tb == T // 128 - 1:
                cp("dve", v32[:], PS[pi][:, 0:128], [bPS[pi]], [bv32])
                st(nv[l], v32[:], bv32, [bv32])
        if c == NCH - 1:
            st(nkT[l], k32[:, T - 128:T], bk32, [bk32])
        if KSUB < 1:
            return
        blocks = [(qb_, hh_) for qb_ in range(T // 128) for hh_ in range(2)]

        def emit_scores(qb_, hh_):
            hs_ = slice(64 * hh_, 64 * hh_ + 64)
            qrhs_ = q16[hs_, :, qb_ * 128:(qb_ + 1) * 128]
            first_ = False
            po_ = nps()
            mm(PS[po_][:].rearrange("p (g q) -> p g q", g=4), kb16[hs_, 128 + qb_ * 128:128 + (qb_ + 1) * 128], qrhs_, True, False, [bkb, bq], [bPS[po_]])
            mm(PS[po_][:], ident16[:], mkn[:, 0, :], False, True, [bident16, bmkn], [bPS[po_]])
            pp_ = None
            if not first_:
                pp_ = nps()
                mm(PS[pp_][:].rearrange("p (g q) -> p g q", g=4), kb16[hs_, qb_ * 128:(qb_ + 1) * 128], qrhs_, True, False, [bkb, bq], [bPS[pp_]])
                mm(PS[pp_][:], ident16[:], mkn[:, 1, :], False, True, [bident16, bmkn], [bPS[pp_]])
            return po_, pp_
        pend = emit_scores(*blocks[0])
        for bi, (qb, hh) in enumerate(blocks):
            hs = slice(64 * hh, 64 * hh + 64)
            first = False
            po, pp = pend
            act(pown[:], PS[po][:], AF.Exp, [bPS[po]], [bpown], scale=0.125)
            if c == 0 and qb == 0:
                act(pprev[:], PS[pp][:], AF.Exp, [bPS[pp], bhp], [bpprev], scale=0.125, bias=hp[:, 1:2])
            else:
                act(pprev[:], PS[pp][:], AF.Exp, [bPS[pp]], [bpprev], scale=0.125)
            if bi + 1 < len(blocks):
                pend = emit_scores(*blocks[bi + 1])
            pO = 6
            pD = 7
            if not first:
                mm(PS[pO][0:64, :], v16[:, qb, hs], pprev[:], True, False, [bv16, bpprev], [bPS[pO]])
                mm(PS[pD][0:64, :], ones16[:, 0:64], pprev[:], True, False, [bones, bpprev], [bPS[pD]])
            mm(PS[pO][0:64, :], v16[:, qb + 1, hs], pown[:], first, True, [bv16, bpown], [bPS[pO]])
            mm(PS[pD][0:64, :], ones16[:, 0:64], pown[:], first, True, [bones, bpown], [bPS[pD]])
            tt("dve", den[:].rearrange("p (g q) -> p g q", g=4), PS[pD][0:64, :].rearrange("p (g q) -> p g q", g=4),
               sk[:, 4 * hh:4 * hh + 4].unsqueeze(2).to_broadcast([64, 4, 128]), ALU.add, [bPS[pD], bsk], [btA, btB])
            P.add("dve", lambda e: e.reciprocal(out=den[:], in_=den[:]), reads=[btA, btB], writes=[btA, btB])
            tt("dve", att16[:, 4 * hh:4 * hh + 4, qb * 128:(qb + 1) * 128], PS[pO][0:64, :].rearrange("p (g q) -> p g q", g=4),
               den[:].rearrange("p (g q) -> p g q", g=4), ALU.mult, [bPS[pO], btA, btB], [batt])
        cp("pool", kb16[:, 0:128], kb16[:, T:T + 128], [bkb], [bkb])
        cp("pool", v16[:, 0, :], v16[:, T // 128, :], [bv16], [bv16])
        if KSUB < 2:
            return
        pY = [6, 7]
        def emit_bu(sc_, ct_):
            pr_ = nps()
            pim_ = nps()
            mm(PS[pr_][:, 0:TS], LB[:, 0, ct_, :], u16[:, ct_ // 4, sc_ * TS:(sc_ + 1) * TS], True, True, [bLB, bu16], [bPS[pr_]])
            mm(PS[pim_][:, 0:TS], LB[:, 1, ct_, :], u16[:, ct_ // 4, sc_ * TS:(sc_ + 1) * TS], True, True, [bLB, bu16], [bPS[pim_]])
            return pr_, pim_
        iters = [(sc_, ct_) for sc_ in range(T // TS) for ct_ in range(8)]
        cstages = conv_ln_stages(lambda m, k: cb16[:, m, k:k + T], n, lambda ap: ap)
        csched = {0: [0], 1: [1], 2: [2], 3: [3], 4: [4], 5: [5], 6: [6], 8: [7], 9: [8], 10: [9], 11: [10], 13: [11], 14: [12]}
        pend = emit_bu(*iters[0])
        for sc in range(T // TS):
            c0 = sc * TS
            firstsub = (c == 0 and sc == 0)
            for ct in range(8):
                uh = ct // 4
                pr, pim = pend
                nxt = sc * 8 + ct + 1
                if nxt < len(iters):
                    pend = emit_bu(*iters[nxt])
                cs_ = cosT[:, ct, 0:TS]
                sn_ = sinT[:, ct, 0:TS]
                par = ssi[0] % 2
                ssi[0] += 1
                qA, bqA = sq[par][0], bsq[par][0]
                qB, bqB = sq[par][1], bsq[par][1]
                hA, bhA = sh16[par][0], bsh16[par][0]
                hB, bhB = sh16[par][1], bsh16[par][1]
                tt("dve", sx[0][:], PS[pr][:, 0:TS], cs_, ALU.mult, [bPS[pr], btab], [bsx[0]])
                tt("dve", sx[1][:], PS[pim][:, 0:TS], sn_, ALU.mult, [bPS[pim], btab], [bsx[1]])
                tt("dve", sx[2][:], PS[pim][:, 0:TS], cs_, ALU.mult, [bPS[pim], btab], [bsx[2]])
                tt("dve", sx[3][:], PS[pr][:, 0:TS], sn_, ALU.mult, [bPS[pr], btab], [bsx[3]])
                tt("dve", sx[0][:], sx[0][:], sx[1][:], ALU.add, [bsx[0], bsx[1]], [bsx[0]])
                tt("dve", sx[2][:], sx[2][:], sx[3][:], ALU.subtract, [bsx[2], bsx[3]], [bsx[2]])
                rbc = sp8[:, 5, ct:ct + 1].to_broadcast([128, TS])
                ire = inre[:, ct:ct + 1]
                iim = inim[:, ct:ct + 1]
                P.add("dve", lambda e, ire=ire, rbc=rbc, qA=qA: e.tensor_tensor_scan(out=qA[:], data0=rbc, data1=sx[0][:], initial=ire, op0=ALU.mult, op1=ALU.add),
                      reads=[bsp8, bsx[0], binit], writes=[bqA])
                P.add("dve", lambda e, iim=iim, rbc=rbc, qB=qB: e.tensor_tensor_scan(out=qB[:], data0=rbc, data1=sx[2][:], initial=iim, op0=ALU.mult, op1=ALU.add),
                      reads=[bsp8, bsx[2], binit], writes=[bqB])
                cp("pool", qlre[:, ct:ct + 1], qA[:, TS - 1:TS], [bqA], [bql])
                cp("pool", qlim[:, ct:ct + 1], qB[:, TS - 1:TS], [bqB], [bql])
                tt("pool", sx[6][:], qA[:], cs_, ALU.mult, [bqA, btab], [bsx[6]])
                tt("pool", sx[7][:], qB[:], sn_, ALU.mult, [bqB, btab], [bsx[7]])
                tt("pool", sx[4][:], qA[:], sn_, ALU.mult, [bqA, btab], [bsx[4]])
                tt("pool", sx[5][:], qB[:], cs_, ALU.mult, [bqB, btab], [bsx[5]])
                tt("pool", hA[:], sx[6][:], sx[7][:], ALU.subtract, [bsx[6], bsx[7]], [bhA])
                tt("pool", hB[:], sx[4][:], sx[5][:], ALU.add, [bsx[4], bsx[5]], [bhB])
                ot = ct // 4
                mm(PS[pY[ot]][:, c0:c0 + TS], LB[:, 2, ct, :], hA[:], ct % 4 == 0, False, [bLB, bhA], [bPS[pY[ot]]])
                mm(PS[pY[ot]][:, c0:c0 + TS], LB[:, 3, ct, :], hB[:], False, ct % 4 == 3, [bLB, bhB], [bPS[pY[ot]]])
                for si_ in csched.get(sc * 8 + ct, []):
                    cstages[si_]()
            a0, a1, a2, a3 = (sm8[i_][:] for i_ in range(4))
            cT_ = cosT[:, :, TS]
            sT_ = sinT[:, :, TS]
            tt("dve", a0, qlre[:], cT_, ALU.mult, [bql, btab], [bsm8])
            tt("dve", a1, qlim[:], sT_, ALU.mult, [bql, btab], [bsm8])
            tt("dve", a2, qlre[:], sT_, ALU.mult, [bql, btab], [bsm8])
            tt("dve", a3, qlim[:], cT_, ALU.mult, [bql, btab], [bsm8])
            tt("dve", inre[:], a0, a1, ALU.subtract, [bsm8], [binit])
            tt("dve", inim[:], a2, a3, ALU.add, [bsm8], [binit])
            if c == NCH - 1 and sc == T // TS - 1:
                cl = cosT[:, :, TS - 1]
                sl = sinT[:, :, TS - 1]
                tt("dve", a0, qlre[:], cl, ALU.mult, [bql, btab, binit], [bsm8])
                tt("dve", a1, qlim[:], sl, ALU.mult, [bql, btab], [bsm8])
                tt("dve", a2, qlre[:], sl, ALU.mult, [bql, btab], [bsm8])
                tt("dve", a3, qlim[:], cl, ALU.mult, [bql, btab], [bsm8])
                tt("dve", hlre[:], a0, a1, ALU.subtract, [bsm8], [bhl])
                tt("dve", hlim[:], a2, a3, ALU.add, [bsm8], [bhl])
                st(nre[l], hlre[:], bhl, [bhl])
                st(nim[l], hlim[:], bhl, [bhl])
        for o in range(2):
            stt("dve", yss[:, o, 0:n], u32[:, o, 0:n], sp2[:, 0, o:o + 1], PS[pY[o]][:, 0:n], ALU.mult, ALU.add, [bu32, bsp2, bPS[pY[o]]], [byss])
        gelu_glu(n)
        if KSUB < 3:
            return
        if c == NCH - 1:
            st(ncv[l], cb32[:, :, T:T + 30], bcb32, [bcb32])
        for o in range(2):
            cp("pool", cb32[:, o, 0:30], cb32[:, o, T:T + 30], [bcb32], [bcb32])
        if KSUB < 4:
            return
        outproj_ffn(xs, bx, n, l, False, first=(c == 0))
        if l < nl - 1:
            st(xscr[:, t0:t0 + T].rearrange("(k p) t -> p k t", p=128), xs[:], bx[0], bx)
        else:
            rmsnorm(xs, bx, n, None, 0, 0, l, False)
            st(yT[:, t0:t0 + T].rearrange("(k p) t -> p k t", p=128), xs[:], bx[0], bx)

    def sample_layer(l):
        n = TSM
        bxs = [bxsm] * KT
        ld(kc16, kcT[l].rearrange("b f k -> f b k"), bkc, [bkc], q="pool")
        ld(vc16, vc[l].rearrange("b k f -> k b f"), bvc, [bvc], q="pool")
        ld(h0re[:], ssmre_in[l], bh0, [bh0])
        ld(h0im[:], ssmim_in[l], bh0, [bh0])
        ld(cbs32[:, :, :, 0:30], sconv_in[l], bcbs32, [bcbs32])
        dd = Buf(f"dd{l}")
        o1 = P.add("sp", lambda e: e.dma_start(out=nks_c[l], in_=kcn[l, :, 4:128, :]), dma_owner=dd)
        stores.append(o1)
        vcn_src = vc[l, :, 4:128, :]
        o2 = P.add("sp", lambda e: e.dma_start(out=nvs_c[l], in_=vcn_src), dma_owner=dd)
        stores.append(o2)
        rmsnorm(xsm, bxs, n, 1, 0, 0, l, True)
        for j in range(4):
            rope_pair(j, 5 + j, rps[:, 0, :], rps[:, 1, :], brps, q16[:, j, 0:n], bq, n)
        rope_pair(4, 9, rps[:, 0, :], rps[:, 1, :], brps, kb16[:, 128:128 + n], bkb, n, extra32=(k32[:, 0:n], bk32))
        st(skT[l], k32[:, 0:n], bk32, [bk32])
        for o in range(2):
            pi = inproj_tile(10 + o, n)
            cp("act", u32[:, o, 0:n], PS[pi][:, 0:n], [bPS[pi]], [bu32])
            cp("dve", u16[:, o, 0:n], PS[pi][:, 0:n], [bPS[pi]], [bu16])
        for o in range(2):
            pa = inproj_tile(12 + o, n)
            pg = inproj_tile(14 + o, n)
            act(tmpC[:, 0:n], PS[pg][:, 0:n], AF.Sigmoid, [bPS[pg]], [btC])
            tt("dve", cbs32[:, o, :, 30:34], PS[pa][:, 0:n].rearrange("p (b t) -> p b t", t=LS),
               tmpC[:, 0:n].rearrange("p (b t) -> p b t", t=LS), ALU.mult, [bPS[pa], btC], [bcbs32])
            cp("pool", cbs16[:, o], cbs32[:, o], [bcbs32], [bcbs16])
        st(scv[l], cbs32[:, :, :, 4:34], bcbs32, [bcbs32])
        pv = nps()
        for b in range(NB):
            for k in range(KT):
                mm(PS[pv][0:4, :].rearrange("p (b f) -> p b f", b=4)[:, b % 4, :] if False else PS[pv][0:4, (b % 4) * 128:(b % 4 + 1) * 128],
                   h16[:, k, 4 * b:4 * b + 4], win16[:, k, 2048:2176], k == 0, k == KT - 1, [bh, bwin], [bPS[pv]])
            if b % 4 == 3:
                g0 = b - 3
                cp("act", vn16[:, g0:g0 + 4, :], PS[pv][0:4, :].rearrange("p (b f) -> p b f", b=4), [bPS[pv]], [bvn16])
                cp("dve", vn32[:, g0:g0 + 4, :], PS[pv][0:4, :].rearrange("p (b f) -> p b f", b=4), [bPS[pv]], [bvn32])
                if b < NB - 1:
                    pv = nps()
        st(svn[l], vn32[:], bvn32, [bvn32])
        pC = nps()
        pN = nps()
        for b in range(NB):
            for hh in range(2):
                hs = slice(64 * hh, 64 * hh + 64)
                col = (b * 2 + hh) * 16
                qrhs = q16[hs, :, 4 * b:4 * b + 4]
                mm(PS[pC][:, col:col + 16].rearrange("p (g t) -> p g t", g=4), kc16[hs, b, :], qrhs, True, True, [bkc, bq], [bPS[pC]])
                mm(PS[pN][0:4, col:col + 16].rearrange("p (g t) -> p g t", g=4), kb16[hs, 128 + 4 * b:128 + 4 * b + 4], qrhs, True, True, [bkb, bq], [bPS[pN]])
        act(pown[:], PS[pC][:], AF.Exp, [bPS[pC]], [bpown], scale=0.125)
        tt("pool", pown[:], pown[:], mk[:, 2, :], ALU.mult, [bpown, bmk], [bpown])
        act(pn16[:], PS[pN][0:4, :], AF.Exp, [bPS[pN]], [bpn], scale=0.125)
        tt("pool", pn16[:], pn16[:], mk[0:4, 3, :], ALU.mult, [bpn, bmk], [bpn])
        pO = nps()
        pD = nps()
        for b in range(NB):
            for hh in range(2):
                hs = slice(64 * hh, 64 * hh + 64)
                col = (b * 2 + hh) * 16
                mm(PS[pO][0:64, col:col + 16], vc16[:, b, hs], pown[:, col:col + 16], True, False, [bvc, bpown], [bPS[pO]])
                mm(PS[pO][0:64, col:col + 16], vn16[0:4, b, hs], pn16[0:4, col:col + 16], False, True, [bvn16, bpn], [bPS[pO]])
                mm(PS[pD][0:64, col:col + 16], ones16[:, 0:64], pown[:, col:col + 16], True, False, [bones, bpown], [bPS[pD]])
                mm(PS[pD][0:64, col:col + 16], ones16[0:4, 0:64], pn16[0:4, col:col + 16], False, True, [bones, bpn], [bPS[pD]])
        tt("dve", den[:].rearrange("p (b h t) -> p b h t", b=NB, t=LS), PS[pD][0:64, :].rearrange("p (b h t) -> p b h t", b=NB, t=LS),
           sk[:, :].unsqueeze(1).unsqueeze(3).to_broadcast([64, NB, 8, LS]), ALU.add, [bPS[pD], bsk], [btA, btB])
        P.add("dve", lambda e: e.reciprocal(out=den[:], in_=den[:]), reads=[btA, btB], writes=[btA, btB])
        tt("dve", att16[:, :, 0:n].rearrange("p h (b t) -> p b h t", t=LS), PS[pO][0:64, :].rearrange("p (b h t) -> p b h t", b=NB, t=LS),
           den[:].rearrange("p (b h t) -> p b h t", b=NB, t=LS), ALU.mult, [bPS[pO], btA, btB], [batt])
        for (dst, x1, y1, x2, y2, op) in ((ahre, 8, h0re, 9, h0im, ALU.subtract), (ahim, 8, h0im, 9, h0re, ALU.add)):
            tt("dve", dst[:], y1[:], sp8[:, x1, :].unsqueeze(2).to_broadcast([128, 8, NB]), ALU.mult, [bh0, bsp8], [bah])
            tt("dve", hsre[:], y2[:], sp8[:, x2, :].unsqueeze(2).to_broadcast([128, 8, NB]), ALU.mult, [bh0, bsp8], [bhs])
            tt("dve", dst[:], dst[:], hsre[:], op, [bah, bhs], [bah])
        pY = [6, 7]
        for ct in range(8):
            uh = ct // 4
            pr = nps()
            pim = nps()
            mm(PS[pr][:, 0:n], LB[:, 0, ct, :], u16[:, uh, 0:n], True, True, [bLB, bu16], [bPS[pr]])
            mm(PS[pim][:, 0:n], LB[:, 1, ct, :], u16[:, uh, 0:n], True, True, [bLB, bu16], [bPS[pim]])
            cs_ = cosT[:, ct, 0:LS].unsqueeze(1).to_broadcast([128, NB, LS])
            sn_ = sinT[:, ct, 0:LS].unsqueeze(1).to_broadcast([128, NB, LS])
            V3 = lambda ap: ap.rearrange("p (b t) -> p b t", t=LS)
            X = [s_[:, 0:n] for s_ in sx]
            tt("dve", V3(X[0]), V3(PS[pr][:, 0:n]), cs_, ALU.mult, [bPS[pr], btab], [bsx[0]])
            tt("dve", V3(X[1]), V3(PS[pim][:, 0:n]), sn_, ALU.mult, [bPS[pim], btab], [bsx[1]])
            tt("pool", X[0], X[0], X[1], ALU.add, [bsx[0], bsx[1]], [bsx[0]])
            tt("dve", V3(X[2]), V3(PS[pim][:, 0:n]), cs_, ALU.mult, [bPS[pim], btab], [bsx[2]])
            tt("dve", V3(X[3]), V3(PS[pr][:, 0:n]), sn_, ALU.mult, [bPS[pr], btab], [bsx[3]])
            tt("pool", X[2], X[2], X[3], ALU.subtract, [bsx[2], bsx[3]], [bsx[2]])
            tt("dve", sx[0][:, 0:n:LS], sx[0][:, 0:n:LS], ahre[:, ct, :], ALU.add, [bsx[0], bah], [bsx[0]])
            tt("dve", sx[2][:, 0:n:LS], sx[2][:, 0:n:LS], ahim[:, ct, :], ALU.add, [bsx[2], bah], [bsx[2]])
            r4v = r4[:, ct].rearrange("p b t -> p (b t)")
            P.add("dve", lambda e, r4v=r4v, X=X: e.tensor_tensor_scan(out=X[4], data0=r4v, data1=X[0], initial=0.0, op0=ALU.mult, op1=ALU.add),
                  reads=[btab4, bsx[0]], writes=[bsx[4]])
            P.add("dve", lambda e, r4v=r4v, X=X: e.tensor_tensor_scan(out=X[5], data0=r4v, data1=X[2], initial=0.0, op0=ALU.mult, op1=ALU.add),
                  reads=[btab4, bsx[2]], writes=[bsx[5]])
            tt("pool", V3(X[6]), V3(X[4]), cs_, ALU.mult, [bsx[4], btab], [bsx[6]])
            tt("pool", V3(X[7]), V3(X[5]), sn_, ALU.mult, [bsx[5], btab], [bsx[7]])
            tt("pool", X[6], X[6], X[7], ALU.subtract, [bsx[6], bsx[7]], [bsx[6]])
            cp("act", hre16[:, 0:n], X[6], [bsx[6]], [bhre])
            cp("act", hsre[:, ct, :], sx[6][:, LS - 1:n:LS], [bsx[6]], [bhs])
            tt("dve", V3(X[1]), V3(X[4]), sn_, ALU.mult, [bsx[4], btab], [bsx[1]])
            tt("dve", V3(X[3]), V3(X[5]), cs_, ALU.mult, [bsx[5], btab], [bsx[3]])
            tt("dve", X[1], X[1], X[3], ALU.add, [bsx[1], bsx[3]], [bsx[1]])
            cp("act", him16[:, 0:n], X[1], [bsx[1]], [bhim])
            cp("act", hsim[:, ct, :], sx[1][:, LS - 1:n:LS], [bsx[1]], [bhs])
            ot = ct // 4
            mm(PS[pY[ot]][:, 0:n], LB[:, 2, ct, :], hre16[:, 0:n], ct % 4 == 0, False, [bLB, bhre], [bPS[pY[ot]]])
            mm(PS[pY[ot]][:, 0:n], LB[:, 3, ct, :], him16[:, 0:n], False, ct % 4 == 3, [bLB, bhim], [bPS[pY[ot]]])
        st(sre[l], hsre[:], bhs, [bhs])
        st(sim_o[l], hsim[:], bhs, [bhs])
        for o in range(2):
            stt("dve", yss[:, o, 0:n], u32[:, o, 0:n], sp2[:, 0, o:o + 1], PS[pY[o]][:, 0:n], ALU.mult, ALU.add, [bu32, bsp2, bPS[pY[o]]], [byss])
        gelu_glu(n)
        conv_ln(lambda m, k: cbs16[:, m, :, k:k + LS], n, lambda ap: ap.rearrange("p (b t) -> p b t", t=LS))
        if DBG and l == 0:
            dsb = xs[:, 0:6, :].rearrange("p (a k) (h t) -> p a (k h) t", k=2, t=TSM); bdsb = bx[0]
            memset("dve", dsb, 0.0, bx)
            cp("dve", dsb[0:64, 0, :, :], att16[:, :, 0:n], [batt, bdsb], [bdsb])
            cp("dve", dsb[:, 1, 0:2, :], os16[:, :, 0:n], [bos, bdsb], [bdsb])
            cp("dve", dsb[:, 2, 0:2, :], oc16[:, :, 0:n], [boc, bdsb], [bdsb])
            st(dbg.rearrange("a p h t -> p a h t"), dsb, bdsb, [bdsb])
        outproj_ffn(xsm, bxs, n, l, True)
        if l == nl - 1:
            rmsnorm(xsm, bxs, n, None, 0, 0, l, True)
            st(ysT.rearrange("(k p) t -> p k t", p=128), xsm[:], bxsm, [bxsm])

    STG = int(os.environ.get("KSTAGE", "9"))
    for l in range(nl):
        if STG >= 1:
            layer_params(l)
        slot_begin(l)
        for c in range(NCH):
            if STG >= 3 or (STG == 2 and c == 0):
                prompt_chunk(l, c)
        if l < nl - 1:
            slot_end(l)
        if STG >= 4:
            sample_layer(l)

    P.add("sp", lambda e: e.nop(), extra_deps=stores)
    P.emit()
    return nc


def _perm_win(w):
    q = w[:, 0:512].reshape(D, 8, 64)
    k = w[:, 512:640].reshape(D, 2, 64)
    v = w[:, 640:768]
    u = w[:, 768:1024]
    a = w[:, 1024:1280]
    g = w[:, 1280:1536]

    def swap(t):
        return np.concatenate([t[..., 32:], t[..., :32]], axis=-1)
    order = [0, 4, 1, 5, 2, 6, 3, 7]
    qt = q[:, order, :].reshape(D, 512)
    qs = swap(q)[:, order, :].reshape(D, 512)
    kt = k.reshape(D, 128)
    ks = swap(k).reshape(D, 128)
    return np.ascontiguousarray(np.concatenate([qt, kt, qs, ks, u, a, g, v], axis=1))


def _rope_tab(pos):
    half = 32
    inv = (np.float32(10000.0) ** (-(np.arange(half, dtype=np.float32) / np.float32(half)))).astype(np.float32)
    ang = (pos.astype(np.float32)[None, :] * inv[:, None]).astype(np.float32)
    c = np.cos(ang.astype(np.float64)).astype(np.float32)
    s = np.sin(ang.astype(np.float64)).astype(np.float32)
    cos = np.concatenate([c, c, c, c], axis=0)
    sins = np.concatenate([-s, s, -s, s], axis=0)
    return np.ascontiguousarray(np.stack([cos, sins], axis=0))


_NC_CACHE = {}


def kernel(**inp):
    f = lambda a: np.ascontiguousarray(np.asarray(a, dtype=np.float32))
    I = {k: np.asarray(v) for k, v in inp.items()}
    nlr = _NC_CACHE.get("nl", NS)
    if "nc" not in _NC_CACHE:
        _NC_CACHE["nc"] = build(nlr)
    nc = _NC_CACHE["nc"]
    import os
    ncr = int(os.environ.get("KCORES", "8"))

    SLOT = {0: [0, 1, 2, 3, 0], 1: [0, 0, 1, 2, 3]}
    DUMMY = {0: 4, 1: 0}

    def pk(a):
        return a.reshape(NL, 8, 128).transpose(0, 2, 1)

    def p2(a):
        return a.reshape(NL, 2, 128).transpose(0, 2, 1)

    per_layer = {
        "w_mod": I["w_mod"],
        "b_modT": I["b_mod"].reshape(NL, 48, 128).transpose(0, 2, 1),
        "g1T": pk(I["norm1_g"]), "g2T": pk(I["norm2_g"]),
        "w_in2": np.stack([_perm_win(I["w_in"][l]) for l in range(NL)]),
        "sinkT": np.broadcast_to(I["attn_sinks"][:, None, :], (NL, 64, 8)),
        "lamre": I["ssm_lam_re"].reshape(NL, 8, 128).transpose(0, 2, 1),
        "lamim": I["ssm_lam_im"].reshape(NL, 8, 128).transpose(0, 2, 1),
        "logdt": np.repeat(I["ssm_log_dt"], 64, axis=1).reshape(NL, 8, 128).transpose(0, 2, 1),
        "bre": I["ssm_b_re"].reshape(NL, 8, 128, 16).transpose(0, 2, 1, 3),
        "bim": I["ssm_b_im"].reshape(NL, 8, 128, 16).transpose(0, 2, 1, 3),
        "cre": I["ssm_c_re"].reshape(NL, 8, 2, 16, 64).transpose(0, 2, 4, 1, 3).reshape(NL, 128, 8, 16),
        "cim": I["ssm_c_im"].reshape(NL, 8, 2, 16, 64).transpose(0, 2, 4, 1, 3).reshape(NL, 128, 8, 16),
        "dskipT": p2(I["ssm_d"]), "wglu": I["ssm_w_glu"], "bgluT": p2(I["ssm_b_glu"]),
        "convwT": I["conv_w"].reshape(NL, 31, 2, 128).transpose(0, 3, 2, 1),
        "convbT": p2(I["conv_b"]), "lngT": p2(I["conv_ln_g"]), "lnbT": p2(I["conv_ln_b"]),
        "w_out": I["w_out"], "w_gate": I["w_gate"], "w_up": I["w_up"], "w_down": I["w_down"],
    }
    role_w = {}
    for role in (0, 1):
        d = {}
        for k, a in per_layer.items():
            arr = f(np.asarray(a)[SLOT[role]])
            if k in ("w_out", "w_down"):
                arr[DUMMY[role]] = 0.0
            d[k] = arr
        role_w[role] = d
    const = {
        "gfT": f(I["final_norm_g"].reshape(8, 128).T),
        "ropeS": _rope_tab(PAST + np.tile(np.arange(LS), NB)),
        "identd": np.eye(128, dtype=np.float32),
        "jrow": f(np.broadcast_to(np.arange(TS + 1, dtype=np.float32)[None, :], (128, TS + 1))),
    }
    kk = np.arange(128)[:, None]
    qq = np.tile(np.arange(128), 4)[None, :]
    m_own = np.where(qq >= kk, 0.0, -30000.0).astype(np.float32)
    m_prev = np.where(kk > qq, 0.0, -30000.0).astype(np.float32)
    const["maskP"] = f(np.stack([m_own, m_prev]))
    tq = np.tile(np.arange(LS), 128)[None, :]
    m_c = (kk > tq).astype(np.float32)
    m_n = (kk <= tq).astype(np.float32)
    const["maskS"] = f(np.stack([m_c, m_n]))
    rope_role = {0: _rope_tab(np.arange(NTOK)), 1: _rope_tab(NTOK + np.arange(NTOK))}
    hp_role = {0: f(np.tile(np.array([[0.0, -1.0e4]], np.float32), (128, 1))),
               1: f(np.tile(np.array([[1.0, 0.0]], np.float32), (128, 1)))}

    in_maps = []
    for c in range(ncr):
        role = c // 4
        b = c % 4
        sbs = slice(NB * c, NB * (c + 1))
        sl = SLOT[role]
        m = dict(const)
        m.update(role_w[role])
        m["ropeP"] = rope_role[role]
        m["hprev"] = hp_role[role]
        m["xT"] = f(I["x_prompt"][b, role * NTOK:(role + 1) * NTOK].T)
        m["xsT"] = f(I["x_sample"][sbs].reshape(TSM, D).T)
        m["cT"] = f(np.concatenate([I["c_prompt"][b][None, :], I["c_sample"][sbs]], axis=0).T)
        ck = I["cache_k"][:, sbs].reshape(NL, NB, 128, 128)[sl]
        cv = I["cache_v"][:, sbs].reshape(NL, NB, 128, 128)[sl]
        m["kcT"] = f(ck.transpose(0, 1, 3, 2))
        m["kcn"] = f(ck)
        m["vc"] = f(cv)
        m["ssmre_in"] = f(I["state_ssm_re"][:, sbs].reshape(NL, NB, 8, 128).transpose(0, 3, 2, 1)[sl])
        m["ssmim_in"] = f(I["state_ssm_im"][:, sbs].reshape(NL, NB, 8, 128).transpose(0, 3, 2, 1)[sl])
        m["sconv_in"] = f(I["state_conv"][:, sbs].reshape(NL, NB, 30, 2, 128).transpose(0, 4, 3, 1, 2)[sl])
        in_maps.append(m)

    res = run_bass_kernel_spmd(nc, in_maps, core_ids=list(range(ncr)))
    R = list(res.results)
    while len(R) < 8:
        R.append(R[0])
    if "dbg" in R[0]:
        _NC_CACHE["dbg"] = R[0]["dbg"]
    SA = slice(0, 4)
    SB = slice(1, 5)

    y_prompt = np.stack([np.concatenate([R[b]["yT"].T, R[4 + b]["yT"].T], axis=0) for b in range(4)])
    y_sample = np.concatenate([R[c]["ysT"].T.reshape(NB, LS, D) for c in range(8)], axis=0)
    nk_p = np.stack([R[4 + b]["nkT"][SB].transpose(0, 2, 1).reshape(NL, 128, 2, 64) for b in range(4)], axis=1)
    nv_p = np.stack([R[4 + b]["nv"][SB].reshape(NL, 128, 2, 64) for b in range(4)], axis=1)

    def unst(a):
        return a.transpose(0, 2, 1).reshape(NL, 16, 64)
    re_p = np.stack([unst(R[4 + b]["nre"][SB]) for b in range(4)], axis=1)
    im_p = np.stack([unst(R[4 + b]["nim"][SB]) for b in range(4)], axis=1)
    cv_p = np.stack([R[4 + b]["ncv"][SB].transpose(0, 3, 2, 1).reshape(NL, 30, 256) for b in range(4)], axis=1)
    nk_s, nv_s, re_s, im_s, cv_s = [], [], [], [], []
    for c in range(8):
        r = R[c]
        S_ = SA if c < 4 else SB
        knew = r["skT"][S_].transpose(0, 2, 1).reshape(NL, NB, LS, 128)
        nk_s.append(np.concatenate([r["nks_c"][S_], knew], axis=2).reshape(NL, NB, 128, 2, 64))
        vnew = r["svn"][S_].transpose(0, 2, 1, 3)
        nv_s.append(np.concatenate([r["nvs_c"][S_], vnew], axis=2).reshape(NL, NB, 128, 2, 64))
        re_s.append(r["sre"][S_].transpose(0, 3, 2, 1).reshape(NL, NB, 16, 64))
        im_s.append(r["sim_o"][S_].transpose(0, 3, 2, 1).reshape(NL, NB, 16, 64))
        cv_s.append(r["scv"][S_].transpose(0, 3, 4, 2, 1).reshape(NL, NB, 30, 256))
    cat = lambda xs_: np.ascontiguousarray(np.concatenate(xs_, axis=1).astype(np.float32))
    outs = (y_prompt, y_sample, nk_p, nv_p, re_p, im_p, cv_p, cat(nk_s), cat(nv_s), cat(re_s), cat(im_s), cat(cv_s))
    return tuple(np.ascontiguousarray(o.astype(np.float32)) for o in outs)
```

```python
import math
import numpy as np
import concourse.bass as bass
import concourse.mybir as mybir
from concourse.bass_utils import run_bass_kernel_spmd

F32 = mybir.dt.float32
BF16 = mybir.dt.bfloat16
ALU = mybir.AluOpType
AF = mybir.ActivationFunctionType

SEG = 30000
NL = 4
NS = 5
HF = 332
D = 1024
KT = 8
NTOK = 2048
T = 256
NCH = NTOK // T
TS = 128
NB = 16
LS = 4
TSM = NB * LS
DFF = 2816
JT = 22
WIN = 2176
PAST = 8192
TWO_PI = 2.0 * math.pi


class Buf:
    __slots__ = ("name", "lw", "rd", "sem", "cnt", "excl")

    def __init__(self, name, excl=False):
        self.name = name
        self.excl = excl
        self.lw = None
        self.rd = {}
        self.sem = None
        self.cnt = 0


class Op:
    __slots__ = ("eng", "fn", "deps", "idx", "dma", "owner", "dcnt", "marked", "ev", "waits", "dinc")

    def __init__(self, eng, fn, idx):
        self.eng = eng
        self.fn = fn
        self.idx = idx
        self.deps = []
        self.dma = False
        self.owner = None
        self.dcnt = 0
        self.marked = False
        self.ev = None
        self.waits = []


class Prog:
    ENGS = ("pe", "act", "dve", "pool", "sp")

    def __init__(self, nc):
        self.nc = nc
        self.ops = []

    def add(self, eng, fn, reads=(), writes=(), dma_owner=None, extra_deps=(), dinc=16):
        i = len(self.ops)
        op = Op(eng, fn, i)
        if dma_owner is not None:
            op.dma = True
            op.owner = dma_owner
            op.dinc = dinc
            dma_owner.cnt += dinc
            op.dcnt = dma_owner.cnt
        deps = {}
        for b in reads:
            if b.lw is not None:
                deps[b.lw] = "raw"
            if b.excl:
                for r in b.rd.values():
                    if r not in deps:
                        deps[r] = "war"
        for b in writes:
            if b.lw is not None and b.lw not in deps:
                deps[b.lw] = "waw"
            for r in b.rd.values():
                if r not in deps:
                    deps[r] = "war"
        for d in extra_deps:
            deps[d.idx] = "raw"
        deps.pop(i, None)
        for b in reads:
            key = ("d", i) if op.dma else eng
            b.rd[key] = i
        for b in writes:
            b.lw = i
            b.rd = {}
        op.deps = list(deps.items())
        self.ops.append(op)
        return op

    def finalize(self):
        ops = self.ops
        waited = {e: {} for e in self.ENGS}
        for op in ops:
            need = {}
            for d, kind in op.deps:
                p = ops[d]
                if p.dma:
                    key = ("dma", id(p.owner))
                    if need.get(key, (0, None))[0] < p.dcnt:
                        need[key] = (p.dcnt, p)
                else:
                    if p.eng == op.eng and not op.dma:
                        if op.eng == "pe" or kind != "raw":
                            continue
                    key = ("eng", p.eng)
                    if need.get(key, (-1, None))[0] < p.idx:
                        need[key] = (p.idx, p)
            w = waited[op.eng]
            for key, (val, p) in need.items():
                if w.get(key, -1) >= val:
                    continue
                w[key] = val
                op.waits.append(p)
                if not p.dma:
                    p.marked = True
        cnt = {e: 0 for e in self.ENGS}
        for op in ops:
            if not op.dma and op.marked:
                cnt[op.eng] += 1
                op.ev = cnt[op.eng]
        self.evcount = cnt

    def emit(self):
        nc = self.nc
        self.finalize()
        esems = {}
        for e in self.ENGS:
            n = (self.evcount[e] + SEG - 1) // SEG
            esems[e] = [nc.alloc_semaphore(f"ev_{e}_{k}") for k in range(max(n, 1))]
        for op in self.ops:
            if op.dma and op.owner.sem is None:
                op.owner.sem = nc.alloc_semaphore("d_" + op.owner.name)

        def semval(p):
            if p.dma:
                return p.owner.sem, p.dcnt
            k = (p.ev - 1) // SEG
            return esems[p.eng][k], (p.ev - 1) % SEG + 1

        per = {e: [op for op in self.ops if op.eng == e] for e in self.ENGS}

        def run(eng, lst):
            for op in lst:
                for p in op.waits:
                    s, v = semval(p)
                    eng.wait_ge(s, v)
                ins = op.fn(eng)
                if op.dma:
                    ins.then_inc(op.owner.sem, op.dinc)
                elif op.marked:
                    s, _ = semval(op)
                    ins.then_inc(s, 1)

        with nc.Block() as block:
            @block.tensor
            def _(e):
                run(e, per["pe"])

            @block.scalar
            def _(e):
                run(e, per["act"])

            @block.vector
            def _(e):
                run(e, per["dve"])

            @block.gpsimd
            def _(e):
                run(e, per["pool"])

            @block.sync
            def _(e):
                run(e, per["sp"])


def build(nl=NS):
    import os
    KSUB = int(os.environ.get("KSUB", "9"))
    KS2 = int(os.environ.get("KS2", "9"))
    nc = bass.Bass("TRN2", target_bir_lowering=False)
    P = Prog(nc)
    stores = []

    def din(name, shape):
        return nc.dram_tensor(name, list(shape), F32, kind="ExternalInput").ap()

    def dout(name, shape):
        return nc.dram_tensor(name, list(shape), F32, kind="ExternalOutput").ap()

    def sb(name, shape, dt=F32):
        return nc.alloc_sbuf_tensor(name, list(shape), dt)

    xT = din("xT", [D, NTOK])
    xsT = din("xsT", [D, TSM])
    cT = din("cT", [D, 17])
    w_mod = din("w_mod", [NS, D, 6 * D])
    b_modT = din("b_modT", [NS, 128, 48])
    g1T = din("g1T", [NS, 128, KT])
    g2T = din("g2T", [NS, 128, KT])
    gfT = din("gfT", [128, KT])
    w_in2 = din("w_in2", [NS, D, WIN])
    ropeP = din("ropeP", [2, 128, NTOK])
    ropeS = din("ropeS", [2, 128, TSM])
    maskP = din("maskP", [2, 128, 512])
    maskS = din("maskS", [2, 128, 512])
    sinkT = din("sinkT", [NS, 64, 8])
    lamre = din("lamre", [NS, 128, 8])
    lamim = din("lamim", [NS, 128, 8])
    logdt = din("logdt", [NS, 128, 8])
    bre = din("bre", [NS, 128, 8, 16])
    bim = din("bim", [NS, 128, 8, 16])
    cre = din("cre", [NS, 128, 8, 16])
    cim = din("cim", [NS, 128, 8, 16])
    dskipT = din("dskipT", [NS, 128, 2])
    wglu = din("wglu", [NS, 256, 256])
    bgluT = din("bgluT", [NS, 128, 2])
    convwT = din("convwT", [NS, 128, 2, 31])
    convbT = din("convbT", [NS, 128, 2])
    lngT = din("lngT", [NS, 128, 2])
    lnbT = din("lnbT", [NS, 128, 2])
    w_out = din("w_out", [NS, D, D])
    w_gate = din("w_gate", [NS, D, DFF])
    w_up = din("w_up", [NS, D, DFF])
    w_down = din("w_down", [NS, DFF, D])
    kcT = din("kcT", [NS, NB, 128, 128])
    vc = din("vc", [NS, NB, 128, 128])
    kcn = din("kcn", [NS, NB, 128, 128])
    ssmre_in = din("ssmre_in", [NS, 128, 8, NB])
    ssmim_in = din("ssmim_in", [NS, 128, 8, NB])
    sconv_in = din("sconv_in", [NS, 128, 2, NB, 30])
    identd = din("identd", [128, 128])
    hprev = din("hprev", [128, 2])
    hin = nc.dram_tensor("hin", [128, HF], F32)
    hall = nc.dram_tensor("hall", [256, HF], F32)
    bhin = Buf("hin"); bhall = Buf("hall")
    jrow = din("jrow", [128, TS + 1])

    yT = dout("yT", [D, NTOK])
    ysT = dout("ysT", [D, TSM])
    nkT = dout("nkT", [NS, 128, 128])
    nv = dout("nv", [NS, 128, 128])
    nre = dout("nre", [NS, 128, 8])
    nim = dout("nim", [NS, 128, 8])
    ncv = dout("ncv", [NS, 128, 2, 30])
    nks_c = dout("nks_c", [NS, NB, 124, 128])
    nvs_c = dout("nvs_c", [NS, NB, 124, 128])
    skT = dout("skT", [NS, 128, TSM])
    svn = dout("svn", [NS, 4, NB, 128])
    sre = dout("sre", [NS, 128, 8, NB])
    sim_o = dout("sim_o", [NS, 128, 8, NB])
    scv = dout("scv", [NS, 128, 2, NB, 30])
    xscr = nc.dram_tensor("xscr", [D, NTOK], F32, kind="Internal").ap()
    wgu_c = nc.dram_tensor("wgu_c", [JT, 128, 2 * KT * 128], BF16, kind="Internal").ap()
    wdn_c = nc.dram_tensor("wdn_c", [16, 128, 11 * 128], BF16, kind="Internal").ap()
    woa_c = nc.dram_tensor("woa_c", [8, 64, 8 * 128], BF16, kind="Internal").ap()
    wor_c = nc.dram_tensor("wor_c", [8, 128, 4 * 128], BF16, kind="Internal").ap()
    bwgu_c = [Buf(f"wguc{j}") for j in range(JT)]
    bwdn_c = [Buf(f"wdnc{j}") for j in range(16)]
    bwo_c = [Buf(f"woc{j}") for j in range(8)]
    DBG = bool(int(os.environ.get("KDBG", "0")))
    if DBG:
        dbg = dout("dbg", [3, 128, 8, TSM])

    PS = [nc.alloc_psum_tensor(f"ps{i}", [128, 512], F32) for i in range(8)]
    bPS = [Buf(f"ps{i}", excl=True) for i in range(8)]
    psrr = [0]

    def nps():
        i = psrr[0]
        psrr[0] = (i + 1) % 6
        return i

    def mm(out, lhsT, rhs, start, stop, r, w):
        return P.add("pe", lambda e: e.matmul(out, lhsT=lhsT, rhs=rhs, start=start, stop=stop), reads=r, writes=w)

    def act(out, in_, func, r, w, bias=None, scale=None):
        kw = {}
        if bias is not None:
            kw["bias"] = bias
        if scale is not None:
            kw["scale"] = scale
        return P.add("act", lambda e: e.activation(out=out, in_=in_, func=func, **kw), reads=r, writes=w)

    def tt(eng, out, in0, in1, op, r, w):
        return P.add(eng, lambda e: e.tensor_tensor(out=out, in0=in0, in1=in1, op=op), reads=r, writes=w)

    def ts(eng, out, in0, s1, s2, op0, op1, r, w):
        if op1 is None:
            return P.add(eng, lambda e: e.tensor_scalar(out=out, in0=in0, scalar1=s1, scalar2=None, op0=op0), reads=r, writes=w)
        return P.add(eng, lambda e: e.tensor_scalar(out=out, in0=in0, scalar1=s1, scalar2=s2, op0=op0, op1=op1), reads=r, writes=w)

    def stt(eng, out, in0, scalar, in1, op0, op1, r, w):
        return P.add(eng, lambda e: e.scalar_tensor_tensor(out=out, in0=in0, scalar=scalar, in1=in1, op0=op0, op1=op1), reads=r, writes=w)

    def cp(eng, out, in_, r, w):
        if eng == "act":
            return P.add("act", lambda e: e.activation(out=out, in_=in_, func=AF.Copy), reads=r, writes=w)
        return P.add(eng, lambda e: e.tensor_copy(out=out, in_=in_), reads=r, writes=w)

    def memset(eng, ap, val, w):
        return P.add(eng, lambda e: e.memset(ap, val), writes=w)

    def ld(out, in_, owner, w, q="sp"):
        return P.add(q, lambda e: e.dma_start(out=out, in_=in_), writes=w, dma_owner=owner)

    def st(out, in_, owner, r):
        o = P.add("sp", lambda e: e.dma_start(out=out, in_=in_), reads=r, dma_owner=owner)
        stores.append(o)
        return o

    ident = sb("ident", [128, 128]); bident = Buf("ident")
    ld(ident[:], identd, bident, [bident])
    ones16 = sb("ones16", [128, 128], BF16); bones = Buf("ones")
    memset("dve", ones16[:], 1.0, [bones])
    jr = sb("jr", [128, TS + 1]); bjr = Buf("jr")
    ld(jr[:], jrow, bjr, [bjr])
    mk = sb("mk", [128, 4, 512], BF16); bmk = Buf("mk")
    mkn = mk[:, 0:2, :]; bmkn = bmk
    ld(mk[:, 0:2, :], maskP.rearrange("a p n -> p a n"), bmk, [bmk], q="pool")
    ld(mk[:, 2:4, :], maskS.rearrange("a p n -> p a n"), bmk, [bmk], q="pool")
    ident16 = sb("ident16", [128, 128], BF16); bident16 = Buf("ident16")
    cp("dve", ident16[:], ident[:], [bident], [bident16])
    rps = sb("rps", [128, 2, TSM]); brps = Buf("rps")
    ld(rps[:], ropeS.rearrange("a p n -> p a n"), brps, [brps])
    gf = sb("gf", [128, KT]); bgf = Buf("gf")
    ld(gf[:], gfT, bgf, [bgf])
    pat = sb("pat", [128, NB, LS]); bpat = Buf("pat")
    memset("dve", pat[:], 1.0, [bpat])
    memset("dve", pat[:, :, 0:1], 0.0, [bpat])

    modT1 = sb("modT1", [128, 48, 17]); bmod = Buf("modT")
    modD = nc.dram_tensor("modD", [NS, 128, 48 * 17], F32).ap()
    bmodD = [Buf(f"modD{i}") for i in range(NS)]
    csb = sb("csb", [128, KT, 17]); bcs = Buf("csb")
    sgc = sb("sgc", [128, KT, 17]); bsgc = Buf("sgc")
    ld(csb[:], cT.rearrange("(k p) n -> p k n", p=128), bcs, [bcs])
    act(sgc[:], csb[:], AF.Sigmoid, [bcs], [bsgc])
    tt("dve", csb[:], csb[:], sgc[:], ALU.mult, [bcs, bsgc], [bcs])
    bmt = sb("bmt", [128, NS, 48]); bbmt = Buf("bmt")
    ld(bmt[:], b_modT.rearrange("l p m -> p l m"), bbmt, [bbmt])
    WMB = 512
    _g0 = nc.sbuf_tensor("wmr0", [128, KT, WMB], F32)
    _g1 = nc.sbuf_tensor("wmr1", [128, KT, WMB], F32)
    _g2 = nc.sbuf_tensor("modrow", [17, 6 * D], F32)
    wmr = [_g0.__enter__(), _g1.__enter__()]
    modrow = _g2.__enter__()
    bwmr = [Buf(f"wmr{i}") for i in range(2)]
    bmrow = Buf("modrow")
    lastmod = None
    it = 0
    for l in range(nl):
        for blk in range(6 * D // WMB):
            s = it % 2
            it += 1
            ld(wmr[s][:], w_mod[l, :, blk * WMB:(blk + 1) * WMB].rearrange("(k p) n -> p k n", p=128), bwmr[s], [bwmr[s]])
            pi = nps()
            for k in range(KT):
                mm(PS[pi][0:17, 0:WMB], csb[:, k, :], wmr[s][:, k, :], k == 0, k == KT - 1, [bwmr[s], bcs], [bPS[pi]])
            cp("act" if blk % 2 == 0 else "dve", modrow[:, blk * WMB:(blk + 1) * WMB], PS[pi][0:17, 0:WMB], [bPS[pi]], [bmrow])
        for m in range(48):
            pi = nps()
            P.add("pe", lambda e, pi=pi, m=m: e.transpose(out=PS[pi][:, 0:17], in_=modrow[:, m * 128:(m + 1) * 128], identity=ident[0:17, 0:17]),
                  reads=[bmrow, bident], writes=[bPS[pi]])
            lastmod = ts("dve", modT1[:, m, :], PS[pi][:, 0:17], bmt[:, l, m:m + 1], None, ALU.add, None, [bPS[pi], bbmt], [bmod])
        lastmod = P.add("sp", lambda e, l=l: e.dma_start(out=modD[l], in_=modT1[:].rearrange("p m c -> p (m c)")),
                        reads=[bmod], writes=[bmodD[l]], dma_owner=bmod)
    _g2.__exit__(None, None, None)
    _g1.__exit__(None, None, None)
    _g0.__exit__(None, None, None)
    for _e in ("pe", "act", "pool", "sp"):
        P.add(_e, lambda e: e.nop(), extra_deps=[lastmod])

    xs = sb("xs", [128, KT, T]); bx = [Buf(f"x{k}") for k in range(KT)]
    xsm = sb("xsm", [128, KT, TSM]); bxsm = Buf("xsm")
    ld(xsm[:], xsT.rearrange("(k p) n -> p k n", p=128), bxsm, [bxsm])
    h16 = sb("h16", [128, KT, T], BF16); bh = Buf("h16")
    scr16 = sb("scr16", [128, JT, T], BF16); bscr = Buf("scr16")
    rstd = sb("rstd", [128, T]); brstd = Buf("rstd")
    tmpAB = sb("tmpAB", [128, 2, T])
    tmpA = tmpAB[:, 0, :]; btA = Buf("tmpA")
    tmpB = tmpAB[:, 1, :]; btB = Buf("tmpB")
    tmpC = sb("tmpC", [128, T]); btC = Buf("tmpC")
    tmpD = sb("tmpD", [128, T]); btD = Buf("tmpD")
    rp = sb("rp", [128, 2, T]); brp = Buf("rp")
    q16 = sb("q16", [128, 4, T], BF16); bq = Buf("q16")
    kb16 = sb("kb16", [128, 128 + T], BF16); bkb = Buf("kb16")
    k32 = sb("k32", [128, T]); bk32 = Buf("k32")
    v16 = sb("v16", [128, 1 + T // 128, 128], BF16); bv16 = Buf("v16")
    v32 = sb("v32", [128, 128]); bv32 = Buf("v32")
    pown = sb("pown", [128, 512], BF16); bpown = Buf("pown")
    pprev = sb("pprev", [128, 512], BF16); bpprev = Buf("pprev")
    den = tmpAB[0:64].rearrange("p a t -> p (a t)")
    att16 = sb("att16", [64, 8, T], BF16); batt = Buf("att16")
    u32 = sb("u32", [128, 2, T]); bu32 = Buf("u32")
    u16 = sb("u16", [128, 2, T], BF16); bu16 = Buf("u16")
    cb32 = sb("cb32", [128, 2, 30 + T]); bcb32 = Buf("cb32")
    cb16 = sb("cb16", [128, 2, 30 + T], BF16); bcb16 = Buf("cb16")
    cbs32 = sb("cbs32", [128, 2, NB, 34]); bcbs32 = Buf("cbs32")
    cbs16 = sb("cbs16", [128, 2, NB, 34], BF16); bcbs16 = Buf("cbs16")
    ycf = sb("ycf", [128, 2, T]); bycf = Buf("ycf")
    cva = sb("cva", [128, T]); bcva = Buf("cva")
    cvb = sb("cvb", [128, T]); bcvb = Buf("cvb")
    cvc = sb("cvc", [128, 2, T]); bcvc = Buf("cvc")
    yc16 = scr16[:, 8:12, :]; byc16 = bscr
    oc16 = sb("oc16", [128, 2, T], BF16); boc = Buf("oc16")
    os16 = sb("os16", [128, 2, T], BF16); bos = Buf("os16")
    yss = sb("yss", [128, 2, T]); byss = Buf("yss")
    z32 = sb("z32", [128, 2, T]); bz32 = Buf("z32")
    assert 2 * T == 4 * 8 * 16
    z16 = sb("z16", [128, 2, T], BF16); bz16 = Buf("z16")
    sx = [sb(f"sx{i}", [128, TS]) for i in range(8)]
    bsx = [Buf(f"sx{i}") for i in range(8)]
    sq = [[sb(f"sq{i}{j}", [128, TS]) for j in range(2)] for i in range(2)]
    bsq = [[Buf(f"sq{i}{j}") for j in range(2)] for i in range(2)]
    sh16 = [[sb(f"sh16{i}{j}", [128, TS], BF16) for j in range(2)] for i in range(2)]
    bsh16 = [[Buf(f"sh16{i}{j}") for j in range(2)] for i in range(2)]
    ssi = [0]
    hre16 = sh16[0][0]; bhre = bsh16[0][0]
    him16 = sh16[0][1]; bhim = bsh16[0][1]
    qlre = sb("qlre", [128, 8]); qlim = sb("qlim", [128, 8]); bql = Buf("ql")
    hlre = sb("hlre", [128, 8]); hlim = sb("hlim", [128, 8]); bhl = Buf("hl")
    inre = sb("inre", [128, 8]); inim = sb("inim", [128, 8]); binit = Buf("init")
    sm8 = [sb(f"sm8_{i}", [128, 8]) for i in range(4)]; bsm8 = Buf("sm8")
    h0re = sb("h0re", [128, 8, NB]); h0im = sb("h0im", [128, 8, NB]); bh0 = Buf("h0")
    ahre = sb("ahre", [128, 8, NB]); ahim = sb("ahim", [128, 8, NB]); bah = Buf("ah")
    hsre = sb("hsre", [128, 8, NB]); hsim = sb("hsim", [128, 8, NB]); bhs = Buf("hs")
    st16 = [sb(f"st16_{i}", [128, NB]) for i in range(4)]; bst16 = Buf("st16")
    r_kc = sb("r_kc", [128, 2048], BF16); bkc = Buf("kc16")
    r_vc = sb("r_vc", [128, 2048], BF16); bvc = Buf("vc16")
    kc16 = r_kc[:].rearrange("p (b k) -> p b k", k=128)
    vc16 = r_vc[:].rearrange("p (b k) -> p b k", k=128)
    vn16 = sb("vn16", [4, NB, 128], BF16); bvn16 = Buf("vn16")
    vn32 = sb("vn32", [4, NB, 128]); bvn32 = Buf("vn32")
    pn16 = sb("pn16", [4, 512], BF16); bpn = Buf("pn16")

    win16 = sb("win16", [128, KT, WIN], BF16); bwin = Buf("win16")
    wgl16 = sb("wgl16", [128, 2, 256], BF16); bwgl = Buf("wgl16")
    sp8 = sb("sp8", [128, 12, 8]); bsp8 = Buf("sp8")
    sp2 = sb("sp2", [128, 8, 2]); bsp2 = Buf("sp2")
    cw = sb("cw", [128, 2, 31]); bcw = Buf("cw")
    sk = sb("sk", [64, 8]); bsk = Buf("sk")
    bc32 = yss[:].rearrange("p a (b c) -> p (a b) c", c=16).rearrange("p (a b) c -> p a b c", a=4); bbc = byss
    bbt = z32[:].rearrange("p a (b c) -> p (a b) c", c=16).rearrange("p (a b) c -> p a b c", a=4); bbbt = bz32
    exq = sb("exq", [128, 128]); bexq = Buf("exq")
    LB = sb("LB", [128, 4, 8, 128], BF16); bLB = Buf("LB")
    cosT = sb("cosT", [128, 8, TS + 1]); sinT = sb("sinT", [128, 8, TS + 1]); btab = Buf("tab")
    r4 = sb("r4", [128, 8, NB, LS]); btab4 = Buf("tab4")
    diag16 = sb("diag16", [128, 2, 31, 128], BF16); bdiag = Buf("diag16")
    gsc = sb("gsc", [128, 2, KT, 17]); bgsc = Buf("gsc")
    NG = 8
    _raw = [sb("wr0", [128, 2048], BF16), sb("wr1", [128, 2048], BF16), r_kc, r_vc, sb("wr4", [128, 2048], BF16),
            sb("wr5", [128, 2048], BF16), sb("wr6", [128, 2048], BF16), sb("wr7", [128, 2048], BF16)]
    bwgu = [Buf("wr0"), Buf("wr1"), bkc, bvc, Buf("wr4"), Buf("wr5"), Buf("wr6"), Buf("wr7")]
    wfl = [r[:] for r in _raw]
    wgu = [r[:].rearrange("p (a k c) -> p a k c", a=2, k=KT) for r in _raw]
    wdn = [r[:, 0:1408].rearrange("p (j c) -> p j c", c=128) for r in _raw]
    bwdn = bwgu
    ffi = [0, 0]

    def S8(i):
        return sp8[:, i, :]

    angt = tmpC[:, 0:TS + 1]; angk = tmpD[:, 0:TS + 1]; bang = btC
    CM = 12582912.0

    def sin_of(out, x, shift, tmp, r, w, wt):
        xs_ = x
        if shift != 0.0:
            ts("dve", out, x, shift, None, ALU.add, None, r, w)
            xs_ = out
        ts("dve", tmp, xs_, 1.0 / TWO_PI, CM, ALU.mult, ALU.add, r + w, wt)
        ts("dve", tmp, tmp, -CM, None, ALU.add, None, r + wt, wt)
        stt("dve", tmp, tmp, -TWO_PI, xs_, ALU.mult, ALU.add, r + w + wt, wt)
        ts("dve", tmp, tmp, math.pi, -math.pi, ALU.min, ALU.max, r + wt, wt)
        act(out, tmp, AF.Sin, r + wt, w)

    def layer_params(l):
        for k in range(KT):
            ld(win16[:, k, :], w_in2[l, k * 128:(k + 1) * 128, :], bwin, [bwin], q="pool")
        ld(wgl16[:], wglu[l].rearrange("(k p) n -> p k n", p=128), bwgl, [bwgl], q="pool")
        ld(sp8[:, 0, :], lamre[l], bsp8, [bsp8])
        ld(sp8[:, 1, :], lamim[l], bsp8, [bsp8])
        ld(sp8[:, 2, :], logdt[l], bsp8, [bsp8])
        ld(sp2[:, 0, :], dskipT[l], bsp2, [bsp2])
        ld(sp2[:, 1, :], bgluT[l], bsp2, [bsp2])
        ld(sp2[:, 2, :], convbT[l], bsp2, [bsp2])
        ld(sp2[:, 3, :], lngT[l], bsp2, [bsp2])
        ld(sp2[:, 4, :], lnbT[l], bsp2, [bsp2])
        ld(cw[:], convwT[l], bcw, [bcw])
        ld(sk[:], sinkT[l], bsk, [bsk])
        act(sk[:], sk[:], AF.Exp, [bsk], [bsk])
        ld(bc32[:, 0], bre[l], bbc, [bbc])
        ld(bc32[:, 1], bim[l], bbc, [bbc])
        ld(bc32[:, 2], cre[l], bbc, [bbc])
        ld(bc32[:, 3], cim[l], bbc, [bbc])
        P.add("sp", lambda e, l=l: e.dma_start(out=modT1[:].rearrange("p m c -> p (m c)"), in_=modD[l]),
              reads=[bmodD[l]], writes=[bmod], dma_owner=bmod)
        for a, (gT, off) in enumerate(((g1T, 8), (g2T, 32))):
            ld(sp8[:, 3, :], gT[l], bsp8, [bsp8])
            ts("dve", gsc[:, a], modT1[:, off:off + 8, :], 1.0, None, ALU.add, None, [bmod], [bgsc])
            tt("dve", gsc[:, a], gsc[:, a], sp8[:, 3, :].unsqueeze(2).to_broadcast([128, KT, 17]), ALU.mult, [bgsc, bsp8], [bgsc])
        R = [bsp8]
        W = [bsp8]
        act(S8(2), S8(2), AF.Exp, R, W)
        tt("dve", S8(3), S8(0), S8(2), ALU.mult, R, W)
        tt("dve", S8(4), S8(1), S8(2), ALU.mult, R, W)
        act(S8(5), S8(3), AF.Exp, R, W)
        sin_of(S8(7), S8(4), 0.0, sm8[0][:], R + [bsm8], W, [bsm8])
        sin_of(S8(6), S8(4), 0.5 * math.pi, sm8[0][:], R + [bsm8], W, [bsm8])
        tt("dve", S8(8), S8(5), S8(6), ALU.mult, R, W)
        tt("dve", S8(9), S8(5), S8(7), ALU.mult, R, W)
        a0, a1, a2, a3 = (sm8[i][:] for i in range(4))
        R2 = [bsp8, bsm8]
        tt("dve", a0, S8(0), S8(0), ALU.mult, R2, [bsm8])
        tt("dve", a1, S8(1), S8(1), ALU.mult, R2, [bsm8])
        tt("dve", a0, a0, a1, ALU.add, R2, [bsm8])
        P.add("dve", lambda e: e.reciprocal(out=a0, in_=a0), reads=R2, writes=[bsm8])
        ts("dve", a1, S8(8), -1.0, None, ALU.add, None, R2, [bsm8])
        tt("dve", a2, a1, S8(0), ALU.mult, R2, [bsm8])
        tt("dve", a3, S8(9), S8(1), ALU.mult, R2, [bsm8])
        tt("dve", a2, a2, a3, ALU.add, R2, [bsm8])
        tt("dve", S8(10), a2, a0, ALU.mult, R2, W)
        tt("dve", a2, S8(9), S8(0), ALU.mult, R2, [bsm8])
        tt("dve", a3, a1, S8(1), ALU.mult, R2, [bsm8])
        tt("dve", a2, a2, a3, ALU.subtract, R2, [bsm8])
        tt("dve", S8(11), a2, a0, ALU.mult, R2, W)
        cre_b = sp8[:, 10, :].unsqueeze(2).to_broadcast([128, 8, 16])
        cim_b = sp8[:, 11, :].unsqueeze(2).to_broadcast([128, 8, 16])
        Rb = [bbc, bsp8, bbbt]
        tt("dve", bbt[:, 0], bc32[:, 0], cre_b, ALU.mult, Rb, [bbbt])
        tt("dve", bbt[:, 2], bc32[:, 1], cim_b, ALU.mult, Rb, [bbbt])
        tt("dve", bbt[:, 0], bbt[:, 0], bbt[:, 2], ALU.subtract, Rb, [bbbt])
        tt("dve", bbt[:, 1], bc32[:, 1], cre_b, ALU.mult, Rb, [bbbt])
        tt("dve", bbt[:, 2], bc32[:, 0], cim_b, ALU.mult, Rb, [bbbt])
        tt("dve", bbt[:, 1], bbt[:, 1], bbt[:, 2], ALU.add, Rb, [bbbt])
        cp("dve", bbt[:, 2], bc32[:, 2], Rb, [bbbt])
        ts("dve", bbt[:, 3], bc32[:, 3], -1.0, None, ALU.mult, None, Rb, [bbbt])
        xsf = xs[:].rearrange("p k t -> p (k t)")
        Eb = [xsf[:, 0:1024].rearrange("p (c n) -> p c n", n=128), xsf[:, 1024:2048].rearrange("p (c n) -> p c n", n=128)]
        bEb = [bx[0:4], bx[4:8]]
        for mi in range(4):
            E = Eb[mi % 2]
            bE = bEb[mi % 2]
            memset("dve", E, 0.0, bE)
            for ct in range(8):
                for gg in range(2):
                    gp = (2 * ct + gg) % 8
                    cp("dve", E[64 * gg:64 * gg + 64, ct, 16 * gp:16 * gp + 16], bbt[64 * gg:64 * gg + 64, mi, ct, :], [bbbt], bE)
            if mi < 2:
                for ct in range(8):
                    pi = nps()
                    P.add("pe", lambda e, pi=pi, E=E, ct=ct: e.transpose(out=PS[pi][:, 0:128], in_=E[:, ct, :], identity=ident[:]),
                          reads=bE + [bident], writes=[bPS[pi]])
                    cp("act", LB[:, mi, ct, :], PS[pi][:, 0:128], [bPS[pi]], [bLB])
            else:
                cp("act", LB[:, mi], E, bE, [bLB])
        angt3 = xsf[:, 0:8 * (TS + 1)].rearrange("p (c j) -> p c j", j=TS + 1)
        angk3 = scr16[:, 0:9, :].rearrange("p a b -> p (a b)").bitcast(F32)[:, 0:8 * (TS + 1)].rearrange("p (c j) -> p c j", j=TS + 1)
        tt("dve", angt3, jr[:].unsqueeze(1).to_broadcast([128, 8, TS + 1]), sp8[:, 4, :].unsqueeze(2).to_broadcast([128, 8, TS + 1]),
           ALU.mult, [bjr, bsp8], bx)
        sin_of(sinT[:], angt3, 0.0, angk3, bx, [btab], [bscr])
        sin_of(cosT[:], angt3, 0.5 * math.pi, angk3, bx, [btab], [bscr])
        for ct in range(8):
            ts("dve", r4[:, ct], pat[:], sp8[:, 5, ct:ct + 1], None, ALU.mult, None, [bpat, bsp8], [btab4])
        for m in range(2):
            for k in range(31):
                act(diag16[:, m, k, :], ident[:], AF.Identity, [bident, bcw], [bdiag], scale=cw[:, m, k:k + 1])

    def rmsnorm(xa, bxs, n, gcol, shm, a, l, sample):
        for k in range(KT):
            act(scr16[:, k, 0:n], xa[:, k, :], AF.Square, [bxs[k]], [bscr])
        pi = nps()
        for k in range(KT):
            mm(PS[pi][:, 0:n], ones16[:], scr16[:, k, 0:n], k == 0, k == KT - 1, [bones, bscr], [bPS[pi]])
        ts("dve", rstd[:, 0:n], PS[pi][:, 0:n], 1.0 / D, 1e-6, ALU.mult, ALU.add, [bPS[pi]], [brstd])
        act(rstd[:, 0:n], rstd[:, 0:n], AF.Sqrt, [brstd], [brstd])
        P.add("dve", lambda e: e.reciprocal(out=rstd[:, 0:n], in_=rstd[:, 0:n]), reads=[brstd], writes=[brstd])
        for k in range(KT):
            tA = tmpA if k % 2 == 0 else tmpB
            bA = btA if k % 2 == 0 else btB
            tt("dve", tA[:, 0:n], xa[:, k, :], rstd[:, 0:n], ALU.mult, [bxs[k], brstd], [bA])
            if gcol is None:
                ts("pool", xa[:, k, :], tA[:, 0:n], gf[:, k:k + 1], None, ALU.mult, None, [bA, bgf], [bxs[k]])
            elif not sample:
                if False:
                    pass
                else:
                    act(h16[:, k, 0:n], tA[:, 0:n], AF.Identity, [bA, bgsc, bmod], [bh],
                        bias=modT1[:, shm + k, 0:1], scale=gsc[:, a, k, 0:1])
            else:
                v3 = tA[:, 0:n].rearrange("p (b t) -> p b t", t=LS)
                if True:
                    tt("pool", v3, v3, gsc[:, a, k, 1:17].unsqueeze(2).to_broadcast([128, NB, LS]), ALU.mult, [bA, bgsc], [bA])
                    tt("pool", h16[:, k, 0:n].rearrange("p (b t) -> p b t", t=LS), v3,
                       modT1[:, shm + k, 1:17].unsqueeze(2).to_broadcast([128, NB, LS]), ALU.add, [bA, bmod], [bh])

    def resid(xa, bxs, n, mo, pi, l, gm, sample):
        if not sample:
            stt("dve", xa[:, mo, :], PS[pi][:, 0:n], modT1[:, gm + mo, 0:1], xa[:, mo, :], ALU.mult, ALU.add,
                [bPS[pi], bmod, bxs[mo]], [bxs[mo]])
        else:
            tt("dve", tmpC[:, 0:n].rearrange("p (b t) -> p b t", t=LS), PS[pi][:, 0:n].rearrange("p (b t) -> p b t", t=LS),
               modT1[:, gm + mo, 1:17].unsqueeze(2).to_broadcast([128, NB, LS]), ALU.mult, [bPS[pi], bmod], [btC])
            tt("dve", xa[:, mo, :], xa[:, mo, :], tmpC[:, 0:n], ALU.add, [btC, bxs[mo]], [bxs[mo]])

    def inproj_tile(m, n):
        pi = nps()
        for k in range(KT):
            mm(PS[pi][:, 0:n], win16[:, k, m * 128:(m + 1) * 128], h16[:, k, 0:n], k == 0, k == KT - 1, [bwin, bh], [bPS[pi]])
        return pi

    ri = [0]

    def rope_pair(m_a, m_b, cos_ap, sin_ap, brope, out_ap, bout, n, extra32=None):
        pa = inproj_tile(m_a, n)
        pb = inproj_tile(m_b, n)
        ri[0] += 1
        if ri[0] % 2 == 0:
            tX, bX, tY, bY = tmpA, btA, tmpB, btB
        else:
            tX, bX, tY, bY = tmpC, btC, tmpD, btD
        tt("dve", tX[:, 0:n], PS[pa][:, 0:n], cos_ap, ALU.mult, [bPS[pa], brope], [bX])
        tt("dve", tY[:, 0:n], PS[pb][:, 0:n], sin_ap, ALU.mult, [bPS[pb], brope], [bY])
        tt("pool", out_ap, tX[:, 0:n], tY[:, 0:n], ALU.add, [bX, bY], [bout])
        if extra32 is not None:
            tt("pool", extra32[0], tX[:, 0:n], tY[:, 0:n], ALU.add, [bX, bY], [extra32[1]])

    def gelu_glu(n):
        tmps = [(tmpC, btC), (tmpD, btD)]
        for step in range(3):
            for o in range(2):
                tq, btq = tmps[o]
                if step == 0:
                    act(tq[:, 0:n], yss[:, o, 0:n], AF.Square, [byss], [btq])
                elif step == 1:
                    ts("dve", tq[:, 0:n], tq[:, 0:n], 0.044715, 1.0, ALU.mult, ALU.add, [btq], [btq])
                else:
                    tt("dve", tq[:, 0:n], tq[:, 0:n], yss[:, o, 0:n], ALU.mult, [btq, byss], [btq])
        for o in range(2):
            tq, btq = tmps[o]
            act(tq[:, 0:n], tq[:, 0:n], AF.Sigmoid, [btq], [btq], scale=2.0 * math.sqrt(2.0 / math.pi))
        for o in range(2):
            tq, btq = tmps[o]
            tt("dve", z32[:, o, 0:n], yss[:, o, 0:n], tq[:, 0:n], ALU.mult, [byss, btq], [bz32])
            cp("act", z16[:, o, 0:n], z32[:, o, 0:n], [bz32], [bz16])
        pis = []
        for o in range(2):
            pi = nps()
            pis.append(pi)
            for k in range(2):
                mm(PS[pi][:, 0:n], wgl16[:, k, o * 128:(o + 1) * 128], z16[:, k, 0:n], k == 0, k == 1, [bwgl, bz16], [bPS[pi]])
        for o in range(2):
            tq, btq = tmps[o]
            act(tq[:, 0:n], PS[pis[o]][:, 0:n], AF.Sigmoid, [bPS[pis[o]], bsp2], [btq], bias=sp2[:, 1, o:o + 1])
        for o in range(2):
            tq, btq = tmps[o]
            tt("dve", os16[:, o, 0:n], z32[:, o, 0:n], tq[:, 0:n], ALU.mult, [bz32, btq], [bos])

    def conv_ln_stages(rhs_fn, n, view):
        pcs = {}

        def st_mm(m):
            def f():
                pi = nps()
                pcs[m] = pi
                for k in range(31):
                    mm(view(PS[pi][:, 0:n]), diag16[:, m, k, :], rhs_fn(m, k), k == 0, k == 30, [bdiag, bcb16, bcbs16], [bPS[pi]])
            return f

        def st_evac(m):
            def f():
                pi = pcs[m]
                act(ycf[:, m, 0:n], PS[pi][:, 0:n], AF.Identity, [bPS[pi], bsp2], [bycf], bias=sp2[:, 2, m:m + 1])
                act(yc16[:, 2 + m, 0:n], PS[pi][:, 0:n], AF.Square, [bPS[pi], bsp2], [byc16], bias=sp2[:, 2, m:m + 1])
            return f

        def st_cast(m):
            def f():
                cp("dve", yc16[:, m, 0:n], ycf[:, m, 0:n], [bycf], [byc16])
            return f

        def st_statmm():
            p1 = nps()
            for m in range(2):
                mm(PS[p1][:, 0:n], ones16[:], yc16[:, m, 0:n], m == 0, m == 1, [bones, byc16], [bPS[p1]])
            p2 = nps()
            for m in range(2):
                mm(PS[p2][:, 0:n], ones16[:], yc16[:, 2 + m, 0:n], m == 0, m == 1, [bones, byc16], [bPS[p2]])
            pcs["p1"] = p1
            pcs["p2"] = p2

        def st_stat1():
            p1, p2 = pcs["p1"], pcs["p2"]
            ts("dve", cva[:, 0:n], PS[p1][:, 0:n], 1.0 / 256, None, ALU.mult, None, [bPS[p1]], [bcva])
            tt("dve", cvb[:, 0:n], cva[:, 0:n], cva[:, 0:n], ALU.mult, [bcva], [bcvb])
            stt("dve", cvb[:, 0:n], PS[p2][:, 0:n], 1.0 / 256, cvb[:, 0:n], ALU.mult, ALU.subtract, [bPS[p2], bcvb], [bcvb])
            ts("dve", cvb[:, 0:n], cvb[:, 0:n], 1e-6, None, ALU.add, None, [bcvb], [bcvb])
            act(cvb[:, 0:n], cvb[:, 0:n], AF.Sqrt, [bcvb], [bcvb])

        def st_stat2():
            P.add("dve", lambda e: e.reciprocal(out=cvb[:, 0:n], in_=cvb[:, 0:n]), reads=[bcvb], writes=[bcvb])

        def st_apply1(m):
            def f():
                tt("dve", ycf[:, m, 0:n], ycf[:, m, 0:n], cva[:, 0:n], ALU.subtract, [bycf, bcva], [bycf])
                tt("dve", ycf[:, m, 0:n], ycf[:, m, 0:n], cvb[:, 0:n], ALU.mult, [bycf, bcvb], [bycf])
                act(ycf[:, m, 0:n], ycf[:, m, 0:n], AF.Identity, [bycf, bsp2], [bycf], bias=sp2[:, 4, m:m + 1], scale=sp2[:, 3, m:m + 1])
                act(cvc[:, m, 0:n], ycf[:, m, 0:n], AF.Sigmoid, [bycf], [bcvc])
            return f

        def st_apply2(m):
            def f():
                tt("dve", oc16[:, m, 0:n], ycf[:, m, 0:n], cvc[:, m, 0:n], ALU.mult, [bcvc, bycf], [boc])
            return f
        return [st_mm(0), st_mm(1), st_evac(0), st_evac(1), st_cast(0), st_cast(1), st_statmm, st_stat1, st_stat2,
                st_apply1(0), st_apply1(1), st_apply2(0), st_apply2(1)]

    def conv_ln(rhs_fn, n, view):
        for f in conv_ln_stages(rhs_fn, n, view):
            f()

    wo_pref = []

    def prefetch_wo(l):
        del wo_pref[:]
        for mo in range(8):
            s = ffi[0] % NG
            ffi[0] += 1
            wo_pref.append(s)
            wa_v = wfl[s][0:64, 0:1024].rearrange("p (h c) -> p h c", c=128)
            wr_v = wfl[s][:, 1024:1536].rearrange("p (h c) -> p h c", c=128)
            fa = wfl[s][0:64, 0:1024]
            fr = wfl[s][:, 1024:1536]
            bw = bwgu[s]
            ld(wa_v, w_out[l, 0:512, mo * 128:(mo + 1) * 128].rearrange("(h d) n -> d h n", d=64), bw, [bw], q="pool")
            ld(wr_v, w_out[l, 512:1024, mo * 128:(mo + 1) * 128].rearrange("(j p) n -> p j n", p=128), bw, [bw], q="pool")
            P.add("sp", lambda e, fa=fa, mo=mo: e.dma_start(out=woa_c[mo], in_=fa), reads=[bw], writes=[bwo_c[mo]], dma_owner=bw)
            P.add("sp", lambda e, fr=fr, mo=mo: e.dma_start(out=wor_c[mo], in_=fr), reads=[bw], writes=[bwo_c[mo]], dma_owner=bw)

    def outproj_ffn(xa, bxs, n, l, sample, first=False):
        for mo in range(8):
            if first:
                s = wo_pref[mo]
            else:
                s = ffi[0] % NG
                ffi[0] += 1
            wa_v = wfl[s][0:64, 0:1024].rearrange("p (h c) -> p h c", c=128)
            wr_v = wfl[s][:, 1024:1536].rearrange("p (h c) -> p h c", c=128)
            fa = wfl[s][0:64, 0:1024]
            fr = wfl[s][:, 1024:1536]
            bw = bwgu[s]
            if first:
                pass
            elif False:
                ld(wa_v, w_out[l, 0:512, mo * 128:(mo + 1) * 128].rearrange("(h d) n -> d h n", d=64), bw, [bw], q="pool")
                ld(wr_v, w_out[l, 512:1024, mo * 128:(mo + 1) * 128].rearrange("(j p) n -> p j n", p=128), bw, [bw], q="pool")
                P.add("sp", lambda e, fa=fa, mo=mo: e.dma_start(out=woa_c[mo], in_=fa), reads=[bw], writes=[bwo_c[mo]], dma_owner=bw)
                P.add("sp", lambda e, fr=fr, mo=mo: e.dma_start(out=wor_c[mo], in_=fr), reads=[bw], writes=[bwo_c[mo]], dma_owner=bw)
            else:
                P.add("sp", lambda e, fa=fa, mo=mo: e.dma_start(out=fa, in_=woa_c[mo]), reads=[bwo_c[mo]], writes=[bw], dma_owner=bw)
                P.add("sp", lambda e, fr=fr, mo=mo: e.dma_start(out=fr, in_=wor_c[mo]), reads=[bwo_c[mo]], writes=[bw], dma_owner=bw)
            pi = nps()
            for hq in range(8):
                mm(PS[pi][:, 0:n], wa_v[:, hq, :], att16[:, hq, 0:n], hq == 0, False, [bw, batt], [bPS[pi]])
            for j in range(2):
                mm(PS[pi][:, 0:n], wr_v[:, 2 + j, :], oc16[:, j, 0:n], False, False, [bw, boc], [bPS[pi]])
            for j in range(2):
                mm(PS[pi][:, 0:n], wr_v[:, j, :], os16[:, j, 0:n], False, j == 1, [bw, bos], [bPS[pi]])
            resid(xa, bxs, n, mo, pi, l, 16, sample)
        rmsnorm(xa, bxs, n, 1, 24, 1, l, sample)
        for j in range(JT):
            s = ffi[0] % NG
            ffi[0] += 1
            fg = wfl[s]
            if first:
                ld(wgu[s][:, 0], w_gate[l, :, j * 128:(j + 1) * 128].rearrange("(k p) n -> p k n", p=128), bwgu[s], [bwgu[s]], q="pool")
                ld(wgu[s][:, 1], w_up[l, :, j * 128:(j + 1) * 128].rearrange("(k p) n -> p k n", p=128), bwgu[s], [bwgu[s]], q="pool")
                P.add("sp", lambda e, fg=fg, j=j: e.dma_start(out=wgu_c[j], in_=fg), reads=[bwgu[s]], writes=[bwgu_c[j]], dma_owner=bwgu[s])
            else:
                P.add("sp", lambda e, fg=fg, j=j: e.dma_start(out=fg, in_=wgu_c[j]), reads=[bwgu_c[j]], writes=[bwgu[s]], dma_owner=bwgu[s])
            pg = nps()
            for k in range(KT):
                mm(PS[pg][:, 0:n], wgu[s][:, 0, k, :], h16[:, k, 0:n], k == 0, k == KT - 1, [bwgu[s], bh], [bPS[pg]])
            pu = nps()
            for k in range(KT):
                mm(PS[pu][:, 0:n], wgu[s][:, 1, k, :], h16[:, k, 0:n], k == 0, k == KT - 1, [bwgu[s], bh], [bPS[pu]])
            tA = tmpA if j % 2 == 0 else tmpB
            bA = btA if j % 2 == 0 else btB
            act(tA[:, 0:n], PS[pg][:, 0:n], AF.Silu, [bPS[pg]], [bA])
            tt("dve", scr16[:, j, 0:n], tA[:, 0:n], PS[pu][:, 0:n], ALU.mult, [bA, bPS[pu]], [bscr])
        for mo in range(8):
            pi = nps()
            for jh in range(2):
                s = ffi[0] % NG
                ffi[0] += 1
                ci = mo * 2 + jh
                fd = wfl[s][:, 0:1408]
                if first:
                    ld(wdn[s], w_down[l, jh * 1408:(jh + 1) * 1408, mo * 128:(mo + 1) * 128].rearrange("(j p) n -> p j n", p=128), bwdn[s], [bwdn[s]], q="pool")
                    P.add("sp", lambda e, fd=fd, ci=ci: e.dma_start(out=wdn_c[ci], in_=fd), reads=[bwdn[s]], writes=[bwdn_c[ci]], dma_owner=bwdn[s])
                else:
                    P.add("sp", lambda e, fd=fd, ci=ci: e.dma_start(out=fd, in_=wdn_c[ci]), reads=[bwdn_c[ci]], writes=[bwdn[s]], dma_owner=bwdn[s])
                for jj in range(11):
                    j = jh * 11 + jj
                    mm(PS[pi][:, 0:n], wdn[s][:, jj, :], scr16[:, j, 0:n], j == 0, j == JT - 1, [bwdn[s], bscr], [bPS[pi]])
            resid(xa, bxs, n, mo, pi, l, 40, sample)

    hp = sb("hp", [128, 2]); bhp = Buf("hp")
    ld(hp[:], hprev, bhp, [bhp])
    hst = ycf[:].rearrange("p a t -> p (a t)")[:, 0:HF]
    GROUPS = [[0, 4], [1, 5], [2, 6], [3, 7]]

    def slot_begin(l):
        if l == 0:
            memset("dve", kb16[:, 0:128], 0.0, [bkb])
            memset("dve", v16[:, 0, :], 0.0, [bv16])
            memset("dve", cb32[:, :, 0:30], 0.0, [bcb32])
            memset("dve", inre[:], 0.0, [binit])
            memset("dve", inim[:], 0.0, [binit])
            return
        P.add("sp", lambda e: e.dma_start(out=hst, in_=hall.ap()[0:128, :]), reads=[bhall], writes=[bycf], dma_owner=bycf)
        ts("dve", hst, hst, hp[:, 0:1], None, ALU.mult, None, [bycf, bhp], [bycf])
        cp("dve", kb16[:, 0:128], hst[:, 0:128], [bycf], [bkb])
        cp("dve", v16[:, 0, :], hst[:, 128:256], [bycf], [bv16])
        cp("dve", cb32[:, :, 0:30], hst[:, 256:316].rearrange("p (a r) -> p a r", a=2), [bycf], [bcb32])
        cp("dve", inre[:], hst[:, 316:324], [bycf], [binit])
        cp("dve", inim[:], hst[:, 324:332], [bycf], [binit])

    def slot_end(l):
        cp("dve", hst[:, 0:128], kb16[:, 0:128], [bkb], [bycf])
        cp("dve", hst[:, 128:256], v16[:, 0, :], [bv16], [bycf])
        cp("dve", hst[:, 256:316].rearrange("p (a r) -> p a r", a=2), cb32[:, :, 0:30], [bcb32], [bycf])
        cp("dve", hst[:, 316:324], inre[:], [binit], [bycf])
        cp("dve", hst[:, 324:332], inim[:], [binit], [bycf])
        P.add("sp", lambda e: e.dma_start(out=hin.ap(), in_=hst), reads=[bycf], writes=[bhin], dma_owner=bycf)
        P.add("pool", lambda e: e.collective_compute("AllGather", ALU.bypass, replica_groups=GROUPS,
                                                     ins=[hin.ap().opt()], outs=[hall.ap().opt()]),
              reads=[bhin], writes=[bhall], dma_owner=bhall, dinc=1)

    def prompt_chunk(l, c):
        n = T
        t0 = c * T
        src = xT if l == 0 else xscr
        xdr = src[:, t0:t0 + T].rearrange("(k p) t -> p k t", p=128)
        ld(xs[:], xdr, bx[0], bx)
        ld(rp[:], ropeP[:, :, t0:t0 + T].rearrange("a p t -> p a t"), brp, [brp])
        if c == 0:
            prefetch_wo(l)
        if KS2 < 1:
            return
        rmsnorm(xs, bx, n, 1, 0, 0, l, False)
        if KS2 < 2:
            return
        for j in range(4):
            rope_pair(j, 5 + j, rp[:, 0, :], rp[:, 1, :], brp, q16[:, j, :], bq, n)
        rope_pair(4, 9, rp[:, 0, :], rp[:, 1, :], brp, kb16[:, 128:128 + T], bkb, n, extra32=(k32[:, 0:n], bk32))
        if KS2 < 3:
            return
        for o in range(2):
            pi = inproj_tile(10 + o, n)
            cp("act", u32[:, o, :], PS[pi][:, 0:n], [bPS[pi]], [bu32])
            cp("dve", u16[:, o, :], PS[pi][:, 0:n], [bPS[pi]], [bu16])
        for o in range(2):
            pa = inproj_tile(12 + o, n)
            pg = inproj_tile(14 + o, n)
            tS, bS = (tmpA, btA) if o == 0 else (tmpB, btB)
            act(tS[:, 0:n], PS[pg][:, 0:n], AF.Sigmoid, [bPS[pg]], [bS])
            tt("dve", cb32[:, o, 30:30 + T], PS[pa][:, 0:n], tS[:, 0:n], ALU.mult, [bPS[pa], bS], [bcb32])
            cp("pool", cb16[:, o, :], cb32[:, o, :], [bcb32], [bcb16])
        if KS2 < 4:
            return
        for tb in range(T // 128):
            pi = nps()
            for k in range(KT):
                mm(PS[pi][:, 0:128], h16[:, k, tb * 128:(tb + 1) * 128], win16[:, k, 2048:2176], k == 0, k == KT - 1, [bh, bwin], [bPS[pi]])
            cp("act", v16[:, 1 + tb, :], PS[pi][:, 0:128], [bPS[pi]], [bv16])
            if c == NCH - 1 and tb == T // 128 - 1:
                cp("dve", v32[:], PS[pi][:, 0:128], [bPS[pi]], [bv32])
                st(nv[l], v32[:], bv32, [bv32])
        if c == NCH - 1:
            st(nkT[l], k32[:, T - 128:T], bk32, [bk32])
        if KSUB < 1:
            return
        blocks = [(qb_, hh_) for qb_ in range(T // 128) for hh_ in range(2)]

        def emit_scores(qb_, hh_):
            hs_ = slice(64 * hh_, 64 * hh_ + 64)
            qrhs_ = q16[hs_, :, qb_ * 128:(qb_ + 1) * 128]
            first_ = False
            po_ = nps()
            mm(PS[po_][:].rearrange("p (g q) -> p g q", g=4), kb16[hs_, 128 + qb_ * 128:128 + (qb_ + 1) * 128], qrhs_, True, False, [bkb, bq], [bPS[po_]])
            mm(PS[po_][:], ident16[:], mkn[:, 0, :], False, True, [bident16, bmkn], [bPS[po_]])
            pp_ = None
            if not first_:
                pp_ = nps()
                mm(PS[pp_][:].rearrange("p (g q) -> p g q", g=4), kb16[hs_, qb_ * 128:(qb_ + 1) * 128], qrhs_, True, False, [bkb, bq], [bPS[pp_]])
                mm(PS[pp_][:], ident16[:], mkn[:, 1, :], False, True, [bident16, bmkn], [bPS[pp_]])
            return po_, pp_
        pend = emit_scores(*blocks[0])
        for bi, (qb, hh) in enumerate(blocks):
            hs = slice(64 * hh, 64 * hh + 64)
            first = False
            po, pp = pend
            act(pown[:], PS[po][:], AF.Exp, [bPS[po]], [bpown], scale=0.125)
            if c == 0 and qb == 0:
                act(pprev[:], PS[pp][:], AF.Exp, [bPS[pp], bhp], [bpprev], scale=0.125, bias=hp[:, 1:2])
            else:
                act(pprev[:], PS[pp][:], AF.Exp, [bPS[pp]], [bpprev], scale=0.125)
            if bi + 1 < len(blocks):
                pend = emit_scores(*blocks[bi + 1])
            pO = 6
            pD = 7
            if not first:
                mm(PS[pO][0:64, :], v16[:, qb, hs], pprev[:], True, False, [bv16, bpprev], [bPS[pO]])
                mm(PS[pD][0:64, :], ones16[:, 0:64], pprev[:], True, False, [bones, bpprev], [bPS[pD]])
            mm(PS[pO][0:64, :], v16[:, qb + 1, hs], pown[:], first, True, [bv16, bpown], [bPS[pO]])
            mm(PS[pD][0:64, :], ones16[:, 0:64], pown[:], first, True, [bones, bpown], [bPS[pD]])
            tt("dve", den[:].rearrange("p (g q) -> p g q", g=4), PS[pD][0:64, :].rearrange("p (g q) -> p g q", g=4),
               sk[:, 4 * hh:4 * hh + 4].unsqueeze(2).to_broadcast([64, 4, 128]), ALU.add, [bPS[pD], bsk], [btA, btB])
            P.add("dve", lambda e: e.reciprocal(out=den[:], in_=den[:]), reads=[btA, btB], writes=[btA, btB])
            tt("dve", att16[:, 4 * hh:4 * hh + 4, qb * 128:(qb + 1) * 128], PS[pO][0:64, :].rearrange("p (g q) -> p g q", g=4),
               den[:].rearrange("p (g q) -> p g q", g=4), ALU.mult, [bPS[pO], btA, btB], [batt])
        cp("pool", kb16[:, 0:128], kb16[:, T:T + 128], [bkb], [bkb])
        cp("pool", v16[:, 0, :], v16[:, T // 128, :], [bv16], [bv16])
        if KSUB < 2:
            return
        pY = [6, 7]
        def emit_bu(sc_, ct_):
            pr_ = nps()
            pim_ = nps()
            mm(PS[pr_][:, 0:TS], LB[:, 0, ct_, :], u16[:, ct_ // 4, sc_ * TS:(sc_ + 1) * TS], True, True, [bLB, bu16], [bPS[pr_]])
            mm(PS[pim_][:, 0:TS], LB[:, 1, ct_, :], u16[:, ct_ // 4, sc_ * TS:(sc_ + 1) * TS], True, True, [bLB, bu16], [bPS[pim_]])
            return pr_, pim_
        iters = [(sc_, ct_) for sc_ in range(T // TS) for ct_ in range(8)]
        cstages = conv_ln_stages(lambda m, k: cb16[:, m, k:k + T], n, lambda ap: ap)
        csched = {0: [0], 1: [1], 2: [2], 3: [3], 4: [4], 5: [5], 6: [6], 8: [7], 9: [8], 10: [9], 11: [10], 13: [11], 14: [12]}
        pend = emit_bu(*iters[0])
        for sc in range(T // TS):
            c0 = sc * TS
            firstsub = (c == 0 and sc == 0)
            for ct in range(8):
                uh = ct // 4
                pr, pim = pend
                nxt = sc * 8 + ct + 1
                if nxt < len(iters):
                    pend = emit_bu(*iters[nxt])
                cs_ = cosT[:, ct, 0:TS]
                sn_ = sinT[:, ct, 0:TS]
                par = ssi[0] % 2
                ssi[0] += 1
                qA, bqA = sq[par][0], bsq[par][0]
                qB, bqB = sq[par][1], bsq[par][1]
                hA, bhA = sh16[par][0], bsh16[par][0]
                hB, bhB = sh16[par][1], bsh16[par][1]
                tt("dve", sx[0][:], PS[pr][:, 0:TS], cs_, ALU.mult, [bPS[pr], btab], [bsx[0]])
                tt("dve", sx[1][:], PS[pim][:, 0:TS], sn_, ALU.mult, [bPS[pim], btab], [bsx[1]])
                tt("dve", sx[2][:], PS[pim][:, 0:TS], cs_, ALU.mult, [bPS[pim], btab], [bsx[2]])
                tt("dve", sx[3][:], PS[pr][:, 0:TS], sn_, ALU.mult, [bPS[pr], btab], [bsx[3]])
                tt("dve", sx[0][:], sx[0][:], sx[1][:], ALU.add, [bsx[0], bsx[1]], [bsx[0]])
                tt("dve", sx[2][:], sx[2][:], sx[3][:], ALU.subtract, [bsx[2], bsx[3]], [bsx[2]])
                rbc = sp8[:, 5, ct:ct + 1].to_broadcast([128, TS])
                ire = inre[:, ct:ct + 1]
                iim = inim[:, ct:ct + 1]
                P.add("dve", lambda e, ire=ire, rbc=rbc, qA=qA: e.tensor_tensor_scan(out=qA[:], data0=rbc, data1=sx[0][:], initial=ire, op0=ALU.mult, op1=ALU.add),
                      reads=[bsp8, bsx[0], binit], writes=[bqA])
                P.add("dve", lambda e, iim=iim, rbc=rbc, qB=qB: e.tensor_tensor_scan(out=qB[:], data0=rbc, data1=sx[2][:], initial=iim, op0=ALU.mult, op1=ALU.add),
                      reads=[bsp8, bsx[2], binit], writes=[bqB])
                cp("pool", qlre[:, ct:ct + 1], qA[:, TS - 1:TS], [bqA], [bql])
                cp("pool", qlim[:, ct:ct + 1], qB[:, TS - 1:TS], [bqB], [bql])
                tt("pool", sx[6][:], qA[:], cs_, ALU.mult, [bqA, btab], [bsx[6]])
                tt("pool", sx[7][:], qB[:], sn_, ALU.mult, [bqB, btab], [bsx[7]])
                tt("pool", sx[4][:], qA[:], sn_, ALU.mult, [bqA, btab], [bsx[4]])
                tt("pool", sx[5][:], qB[:], cs_, ALU.mult, [bqB, btab], [bsx[5]])
                tt("pool", hA[:], sx[6][:], sx[7][:], ALU.subtract, [bsx[6], bsx[7]], [bhA])
                tt("pool", hB[:], sx[4][:], sx[5][:], ALU.add, [bsx[4], bsx[5]], [bhB])
                ot = ct // 4
                mm(PS[pY[ot]][:, c0:c0 + TS], LB[:, 2, ct, :], hA[:], ct % 4 == 0, False, [bLB, bhA], [bPS[pY[ot]]])
                mm(PS[pY[ot]][:, c0:c0 + TS], LB[:, 3, ct, :], hB[:], False, ct % 4 == 3, [bLB, bhB], [bPS[pY[ot]]])
                for si_ in csched.get(sc * 8 + ct, []):
                    cstages[si_]()
            a0, a1, a2, a3 = (sm8[i_][:] for i_ in range(4))
            cT_ = cosT[:, :, TS]
            sT_ = sinT[:, :, TS]
            tt("dve", a0, qlre[:], cT_, ALU.mult, [bql, btab], [bsm8])
            tt("dve", a1, qlim[:], sT_, ALU.mult, [bql, btab], [bsm8])
            tt("dve", a2, qlre[:], sT_, ALU.mult, [bql, btab], [bsm8])
            tt("dve", a3, qlim[:], cT_, ALU.mult, [bql, btab], [bsm8])
            tt("dve", inre[:], a0, a1, ALU.subtract, [bsm8], [binit])
            tt("dve", inim[:], a2, a3, ALU.add, [bsm8], [binit])
            if c == NCH - 1 and sc == T // TS - 1:
                cl = cosT[:, :, TS - 1]
                sl = sinT[:, :, TS - 1]
                tt("dve", a0, qlre[:], cl, ALU.mult, [bql, btab, binit], [bsm8])
                tt("dve", a1, qlim[:], sl, ALU.mult, [bql, btab], [bsm8])
                tt("dve", a2, qlre[:], sl, ALU.mult, [bql, btab], [bsm8])
                tt("dve", a3, qlim[:], cl, ALU.mult, [bql, btab], [bsm8])
                tt("dve", hlre[:], a0, a1, ALU.subtract, [bsm8], [bhl])
                tt("dve", hlim[:], a2, a3, ALU.add, [bsm8], [bhl])
                st(nre[l], hlre[:], bhl, [bhl])
                st(nim[l], hlim[:], bhl, [bhl])
        for o in range(2):
            stt("dve", yss[:, o, 0:n], u32[:, o, 0:n], sp2[:, 0, o:o + 1], PS[pY[o]][:, 0:n], ALU.mult, ALU.add, [bu32, bsp2, bPS[pY[o]]], [byss])
        gelu_glu(n)
        if KSUB < 3:
            return
        if c == NCH - 1:
            st(ncv[l], cb32[:, :, T:T + 30], bcb32, [bcb32])
        for o in range(2):
            cp("pool", cb32[:, o, 0:30], cb32[:, o, T:T + 30], [bcb32], [bcb32])
        if KSUB < 4:
            return
        outproj_ffn(xs, bx, n, l, False, first=(c == 0))
        if l < nl - 1:
            st(xscr[:, t0:t0 + T].rearrange("(k p) t -> p k t", p=128), xs[:], bx[0], bx)
        else:
            rmsnorm(xs, bx, n, None, 0, 0, l, False)
            st(yT[:, t0:t0 + T].rearrange("(k p) t -> p k t", p=128), xs[:], bx[0], bx)

    def sample_layer(l):
        n = TSM
        bxs = [bxsm] * KT
        ld(kc16, kcT[l].rearrange("b f k -> f b k"), bkc, [bkc], q="pool")
        ld(vc16, vc[l].rearrange("b k f -> k b f"), bvc, [bvc], q="pool")
        ld(h0re[:], ssmre_in[l], bh0, [bh0])
        ld(h0im[:], ssmim_in[l], bh0, [bh0])
        ld(cbs32[:, :, :, 0:30], sconv_in[l], bcbs32, [bcbs32])
        dd = Buf(f"dd{l}")
        o1 = P.add("sp", lambda e: e.dma_start(out=nks_c[l], in_=kcn[l, :, 4:128, :]), dma_owner=dd)
        stores.append(o1)
        vcn_src = vc[l, :, 4:128, :]
        o2 = P.add("sp", lambda e: e.dma_start(out=nvs_c[l], in_=vcn_src), dma_owner=dd)
        stores.append(o2)
        rmsnorm(xsm, bxs, n, 1, 0, 0, l, True)
        for j in range(4):
            rope_pair(j, 5 + j, rps[:, 0, :], rps[:, 1, :], brps, q16[:, j, 0:n], bq, n)
        rope_pair(4, 9, rps[:, 0, :], rps[:, 1, :], brps, kb16[:, 128:128 + n], bkb, n, extra32=(k32[:, 0:n], bk32))
        st(skT[l], k32[:, 0:n], bk32, [bk32])
        for o in range(2):
            pi = inproj_tile(10 + o, n)
            cp("act", u32[:, o, 0:n], PS[pi][:, 0:n], [bPS[pi]], [bu32])
            cp("dve", u16[:, o, 0:n], PS[pi][:, 0:n], [bPS[pi]], [bu16])
        for o in range(2):
            pa = inproj_tile(12 + o, n)
            pg = inproj_tile(14 + o, n)
            act(tmpC[:, 0:n], PS[pg][:, 0:n], AF.Sigmoid, [bPS[pg]], [btC])
            tt("dve", cbs32[:, o, :, 30:34], PS[pa][:, 0:n].rearrange("p (b t) -> p b t", t=LS),
               tmpC[:, 0:n].rearrange("p (b t) -> p b t", t=LS), ALU.mult, [bPS[pa], btC], [bcbs32])
            cp("pool", cbs16[:, o], cbs32[:, o], [bcbs32], [bcbs16])
        st(scv[l], cbs32[:, :, :, 4:34], bcbs32, [bcbs32])
        pv = nps()
        for b in range(NB):
            for k in range(KT):
                mm(PS[pv][0:4, :].rearrange("p (b f) -> p b f", b=4)[:, b % 4, :] if False else PS[pv][0:4, (b % 4) * 128:(b % 4 + 1) * 128],
                   h16[:, k, 4 * b:4 * b + 4], win16[:, k, 2048:2176], k == 0, k == KT - 1, [bh, bwin], [bPS[pv]])
            if b % 4 == 3:
                g0 = b - 3
                cp("act", vn16[:, g0:g0 + 4, :], PS[pv][0:4, :].rearrange("p (b f) -> p b f", b=4), [bPS[pv]], [bvn16])
                cp("dve", vn32[:, g0:g0 + 4, :], PS[pv][0:4, :].rearrange("p (b f) -> p b f", b=4), [bPS[pv]], [bvn32])
                if b < NB - 1:
                    pv = nps()
        st(svn[l], vn32[:], bvn32, [bvn32])
        pC = nps()
        pN = nps()
        for b in range(NB):
            for hh in range(2):
                hs = slice(64 * hh, 64 * hh + 64)
                col = (b * 2 + hh) * 16
                qrhs = q16[hs, :, 4 * b:4 * b + 4]
                mm(PS[pC][:, col:col + 16].rearrange("p (g t) -> p g t", g=4), kc16[hs, b, :], qrhs, True, True, [bkc, bq], [bPS[pC]])
                mm(PS[pN][0:4, col:col + 16].rearrange("p (g t) -> p g t", g=4), kb16[hs, 128 + 4 * b:128 + 4 * b + 4], qrhs, True, True, [bkb, bq], [bPS[pN]])
        act(pown[:], PS[pC][:], AF.Exp, [bPS[pC]], [bpown], scale=0.125)
        tt("pool", pown[:], pown[:], mk[:, 2, :], ALU.mult, [bpown, bmk], [bpown])
        act(pn16[:], PS[pN][0:4, :], AF.Exp, [bPS[pN]], [bpn], scale=0.125)
        tt("pool", pn16[:], pn16[:], mk[0:4, 3, :], ALU.mult, [bpn, bmk], [bpn])
        pO = nps()
        pD = nps()
        for b in range(NB):
            for hh in range(2):
                hs = slice(64 * hh, 64 * hh + 64)
                col = (b * 2 + hh) * 16
                mm(PS[pO][0:64, col:col + 16], vc16[:, b, hs], pown[:, col:col + 16], True, False, [bvc, bpown], [bPS[pO]])
                mm(PS[pO][0:64, col:col + 16], vn16[0:4, b, hs], pn16[0:4, col:col + 16], False, True, [bvn16, bpn], [bPS[pO]])
                mm(PS[pD][0:64, col:col + 16], ones16[:, 0:64], pown[:, col:col + 16], True, False, [bones, bpown], [bPS[pD]])
                mm(PS[pD][0:64, col:col + 16], ones16[0:4, 0:64], pn16[0:4, col:col + 16], False, True, [bones, bpn], [bPS[pD]])
        tt("dve", den[:].rearrange("p (b h t) -> p b h t", b=NB, t=LS), PS[pD][0:64, :].rearrange("p (b h t) -> p b h t", b=NB, t=LS),
           sk[:, :].unsqueeze(1).unsqueeze(3).to_broadcast([64, NB, 8, LS]), ALU.add, [bPS[pD], bsk], [btA, btB])
        P.add("dve", lambda e: e.reciprocal(out=den[:], in_=den[:]), reads=[btA, btB], writes=[btA, btB])
        tt("dve", att16[:, :, 0:n].rearrange("p h (b t) -> p b h t", t=LS), PS[pO][0:64, :].rearrange("p (b h t) -> p b h t", b=NB, t=LS),
           den[:].rearrange("p (b h t) -> p b h t", b=NB, t=LS), ALU.mult, [bPS[pO], btA, btB], [batt])
        for (dst, x1, y1, x2, y2, op) in ((ahre, 8, h0re, 9, h0im, ALU.subtract), (ahim, 8, h0im, 9, h0re, ALU.add)):
            tt("dve", dst[:], y1[:], sp8[:, x1, :].unsqueeze(2).to_broadcast([128, 8, NB]), ALU.mult, [bh0, bsp8], [bah])
            tt("dve", hsre[:], y2[:], sp8[:, x2, :].unsqueeze(2).to_broadcast([128, 8, NB]), ALU.mult, [bh0, bsp8], [bhs])
            tt("dve", dst[:], dst[:], hsre[:], op, [bah, bhs], [bah])
        pY = [6, 7]
        for ct in range(8):
            uh = ct // 4
            pr = nps()
            pim = nps()
            mm(PS[pr][:, 0:n], LB[:, 0, ct, :], u16[:, uh, 0:n], True, True, [bLB, bu16], [bPS[pr]])
            mm(PS[pim][:, 0:n], LB[:, 1, ct, :], u16[:, uh, 0:n], True, True, [bLB, bu16], [bPS[pim]])
            cs_ = cosT[:, ct, 0:LS].unsqueeze(1).to_broadcast([128, NB, LS])
            sn_ = sinT[:, ct, 0:LS].unsqueeze(1).to_broadcast([128, NB, LS])
            V3 = lambda ap: ap.rearrange("p (b t) -> p b t", t=LS)
            X = [s_[:, 0:n] for s_ in sx]
            tt("dve", V3(X[0]), V3(PS[pr][:, 0:n]), cs_, ALU.mult, [bPS[pr], btab], [bsx[0]])
            tt("dve", V3(X[1]), V3(PS[pim][:, 0:n]), sn_, ALU.mult, [bPS[pim], btab], [bsx[1]])
            tt("pool", X[0], X[0], X[1], ALU.add, [bsx[0], bsx[1]], [bsx[0]])
            tt("dve", V3(X[2]), V3(PS[pim][:, 0:n]), cs_, ALU.mult, [bPS[pim], btab], [bsx[2]])
            tt("dve", V3(X[3]), V3(PS[pr][:, 0:n]), sn_, ALU.mult, [bPS[pr], btab], [bsx[3]])
            tt("pool", X[2], X[2], X[3], ALU.subtract, [bsx[2], bsx[3]], [bsx[2]])
            tt("dve", sx[0][:, 0:n:LS], sx[0][:, 0:n:LS], ahre[:, ct, :], ALU.add, [bsx[0], bah], [bsx[0]])
            tt("dve", sx[2][:, 0:n:LS], sx[2][:, 0:n:LS], ahim[:, ct, :], ALU.add, [bsx[2], bah], [bsx[2]])
            r4v = r4[:, ct].rearrange("p b t -> p (b t)")
            P.add("dve", lambda e, r4v=r4v, X=X: e.tensor_tensor_scan(out=X[4], data0=r4v, data1=X[0], initial=0.0, op0=ALU.mult, op1=ALU.add),
                  reads=[btab4, bsx[0]], writes=[bsx[4]])
            P.add("dve", lambda e, r4v=r4v, X=X: e.tensor_tensor_scan(out=X[5], data0=r4v, data1=X[2], initial=0.0, op0=ALU.mult, op1=ALU.add),
                  reads=[btab4, bsx[2]], writes=[bsx[5]])
            tt("pool", V3(X[6]), V3(X[4]), cs_, ALU.mult, [bsx[4], btab], [bsx[6]])
            tt("pool", V3(X[7]), V3(X[5]), sn_, ALU.mult, [bsx[5], btab], [bsx[7]])
            tt("pool", X[6], X[6], X[7], ALU.subtract, [bsx[6], bsx[7]], [bsx[6]])
            cp("act", hre16[:, 0:n], X[6], [bsx[6]], [bhre])
            cp("act", hsre[:, ct, :], sx[6][:, LS - 1:n:LS], [bsx[6]], [bhs])
            tt("dve", V3(X[1]), V3(X[4]), sn_, ALU.mult, [bsx[4], btab], [bsx[1]])
            tt("dve", V3(X[3]), V3(X[5]), cs_, ALU.mult, [bsx[5], btab], [bsx[3]])
            tt("dve", X[1], X[1], X[3], ALU.add, [bsx[1], bsx[3]], [bsx[1]])
            cp("act", him16[:, 0:n], X[1], [bsx[1]], [bhim])
            cp("act", hsim[:, ct, :], sx[1][:, LS - 1:n:LS], [bsx[1]], [bhs])
            ot = ct // 4
            mm(PS[pY[ot]][:, 0:n], LB[:, 2, ct, :], hre16[:, 0:n], ct % 4 == 0, False, [bLB, bhre], [bPS[pY[ot]]])
            mm(PS[pY[ot]][:, 0:n], LB[:, 3, ct, :], him16[:, 0:n], False, ct % 4 == 3, [bLB, bhim], [bPS[pY[ot]]])
        st(sre[l], hsre[:], bhs, [bhs])
        st(sim_o[l], hsim[:], bhs, [bhs])
        for o in range(2):
            stt("dve", yss[:, o, 0:n], u32[:, o, 0:n], sp2[:, 0, o:o + 1], PS[pY[o]][:, 0:n], ALU.mult, ALU.add, [bu32, bsp2, bPS[pY[o]]], [byss])
        gelu_glu(n)
        conv_ln(lambda m, k: cbs16[:, m, :, k:k + LS], n, lambda ap: ap.rearrange("p (b t) -> p b t", t=LS))
        if DBG and l == 0:
            dsb = xs[:, 0:6, :].rearrange("p (a k) (h t) -> p a (k h) t", k=2, t=TSM); bdsb = bx[0]
            memset("dve", dsb, 0.0, bx)
            cp("dve", dsb[0:64, 0, :, :], att16[:, :, 0:n], [batt, bdsb], [bdsb])
            cp("dve", dsb[:, 1, 0:2, :], os16[:, :, 0:n], [bos, bdsb], [bdsb])
            cp("dve", dsb[:, 2, 0:2, :], oc16[:, :, 0:n], [boc, bdsb], [bdsb])
            st(dbg.rearrange("a p h t -> p a h t"), dsb, bdsb, [bdsb])
        outproj_ffn(xsm, bxs, n, l, True)
        if l == nl - 1:
            rmsnorm(xsm, bxs, n, None, 0, 0, l, True)
            st(ysT.rearrange("(k p) t -> p k t", p=128), xsm[:], bxsm, [bxsm])

    STG = int(os.environ.get("KSTAGE", "9"))
    for l in range(nl):
        if STG >= 1:
            layer_params(l)
        slot_begin(l)
        for c in range(NCH):
            if STG >= 3 or (STG == 2 and c == 0):
                prompt_chunk(l, c)
        if l < nl - 1:
            slot_end(l)
        if STG >= 4:
            sample_layer(l)

    P.add("sp", lambda e: e.nop(), extra_deps=stores)
    P.emit()
    return nc


def _perm_win(w):
    q = w[:, 0:512].reshape(D, 8, 64)
    k = w[:, 512:640].reshape(D, 2, 64)
    v = w[:, 640:768]
    u = w[:, 768:1024]
    a = w[:, 1024:1280]
    g = w[:, 1280:1536]

    def swap(t):
        return np.concatenate([t[..., 32:], t[..., :32]], axis=-1)
    order = [0, 4, 1, 5, 2, 6, 3, 7]
    qt = q[:, order, :].reshape(D, 512)
    qs = swap(q)[:, order, :].reshape(D, 512)
    kt = k.reshape(D, 128)
    ks = swap(k).reshape(D, 128)
    return np.ascontiguousarray(np.concatenate([qt, kt, qs, ks, u, a, g, v], axis=1))


def _rope_tab(pos):
    half = 32
    inv = (np.float32(10000.0) ** (-(np.arange(half, dtype=np.float32) / np.float32(half)))).astype(np.float32)
    ang = (pos.astype(np.float32)[None, :] * inv[:, None]).astype(np.float32)
    c = np.cos(ang.astype(np.float64)).astype(np.float32)
    s = np.sin(ang.astype(np.float64)).astype(np.float32)
    cos = np.concatenate([c, c, c, c], axis=0)
    sins = np.concatenate([-s, s, -s, s], axis=0)
    return np.ascontiguousarray(np.stack([cos, sins], axis=0))


_NC_CACHE = {}


def kernel(**inp):
    f = lambda a: np.ascontiguousarray(np.asarray(a, dtype=np.float32))
    I = {k: np.asarray(v) for k, v in inp.items()}
    nlr = _NC_CACHE.get("nl", NS)
    if "nc" not in _NC_CACHE:
        _NC_CACHE["nc"] = build(nlr)
    nc = _NC_CACHE["nc"]
    import os
    ncr = int(os.environ.get("KCORES", "8"))

    SLOT = {0: [0, 1, 2, 3, 0], 1: [0, 0, 1, 2, 3]}
    DUMMY = {0: 4, 1: 0}

    def pk(a):
        return a.reshape(NL, 8, 128).transpose(0, 2, 1)

    def p2(a):
        return a.reshape(NL, 2, 128).transpose(0, 2, 1)

    per_layer = {
        "w_mod": I["w_mod"],
        "b_modT": I["b_mod"].reshape(NL, 48, 128).transpose(0, 2, 1),
        "g1T": pk(I["norm1_g"]), "g2T": pk(I["norm2_g"]),
        "w_in2": np.stack([_perm_win(I["w_in"][l]) for l in range(NL)]),
        "sinkT": np.broadcast_to(I["attn_sinks"][:, None, :], (NL, 64, 8)),
        "lamre": I["ssm_lam_re"].reshape(NL, 8, 128).transpose(0, 2, 1),
        "lamim": I["ssm_lam_im"].reshape(NL, 8, 128).transpose(0, 2, 1),
        "logdt": np.repeat(I["ssm_log_dt"], 64, axis=1).reshape(NL, 8, 128).transpose(0, 2, 1),
        "bre": I["ssm_b_re"].reshape(NL, 8, 128, 16).transpose(0, 2, 1, 3),
        "bim": I["ssm_b_im"].reshape(NL, 8, 128, 16).transpose(0, 2, 1, 3),
        "cre": I["ssm_c_re"].reshape(NL, 8, 2, 16, 64).transpose(0, 2, 4, 1, 3).reshape(NL, 128, 8, 16),
        "cim": I["ssm_c_im"].reshape(NL, 8, 2, 16, 64).transpose(0, 2, 4, 1, 3).reshape(NL, 128, 8, 16),
        "dskipT": p2(I["ssm_d"]), "wglu": I["ssm_w_glu"], "bgluT": p2(I["ssm_b_glu"]),
        "convwT": I["conv_w"].reshape(NL, 31, 2, 128).transpose(0, 3, 2, 1),
        "convbT": p2(I["conv_b"]), "lngT": p2(I["conv_ln_g"]), "lnbT": p2(I["conv_ln_b"]),
        "w_out": I["w_out"], "w_gate": I["w_gate"], "w_up": I["w_up"], "w_down": I["w_down"],
    }
    role_w = {}
    for role in (0, 1):
        d = {}
        for k, a in per_layer.items():
            arr = f(np.asarray(a)[SLOT[role]])
            if k in ("w_out", "w_down"):
                arr[DUMMY[role]] = 0.0
            d[k] = arr
        role_w[role] = d
    const = {
        "gfT": f(I["final_norm_g"].reshape(8, 128).T),
        "ropeS": _rope_tab(PAST + np.tile(np.arange(LS), NB)),
        "identd": np.eye(128, dtype=np.float32),
        "jrow": f(np.broadcast_to(np.arange(TS + 1, dtype=np.float32)[None, :], (128, TS + 1))),
    }
    kk = np.arange(128)[:, None]
    qq = np.tile(np.arange(128), 4)[None, :]
    m_own = np.where(qq >= kk, 0.0, -30000.0).astype(np.float32)
    m_prev = np.where(kk > qq, 0.0, -30000.0).astype(np.float32)
    const["maskP"] = f(np.stack([m_own, m_prev]))
    tq = np.tile(np.arange(LS), 128)[None, :]
    m_c = (kk > tq).astype(np.float32)
    m_n = (kk <= tq).astype(np.float32)
    const["maskS"] = f(np.stack([m_c, m_n]))
    rope_role = {0: _rope_tab(np.arange(NTOK)), 1: _rope_tab(NTOK + np.arange(NTOK))}
    hp_role = {0: f(np.tile(np.array([[0.0, -1.0e4]], np.float32), (128, 1))),
               1: f(np.tile(np.array([[1.0, 0.0]], np.float32), (128, 1)))}

    in_maps = []
    for c in range(ncr):
        role = c // 4
        b = c % 4
        sbs = slice(NB * c, NB * (c + 1))
        sl = SLOT[role]
        m = dict(const)
        m.update(role_w[role])
        m["ropeP"] = rope_role[role]
        m["hprev"] = hp_role[role]
        m["xT"] = f(I["x_prompt"][b, role * NTOK:(role + 1) * NTOK].T)
        m["xsT"] = f(I["x_sample"][sbs].reshape(TSM, D).T)
        m["cT"] = f(np.concatenate([I["c_prompt"][b][None, :], I["c_sample"][sbs]], axis=0).T)
        ck = I["cache_k"][:, sbs].reshape(NL, NB, 128, 128)[sl]
        cv = I["cache_v"][:, sbs].reshape(NL, NB, 128, 128)[sl]
        m["kcT"] = f(ck.transpose(0, 1, 3, 2))
        m["kcn"] = f(ck)
        m["vc"] = f(cv)
        m["ssmre_in"] = f(I["state_ssm_re"][:, sbs].reshape(NL, NB, 8, 128).transpose(0, 3, 2, 1)[sl])
        m["ssmim_in"] = f(I["state_ssm_im"][:, sbs].reshape(NL, NB, 8, 128).transpose(0, 3, 2, 1)[sl])
        m["sconv_in"] = f(I["state_conv"][:, sbs].reshape(NL, NB, 30, 2, 128).transpose(0, 4, 3, 1, 2)[sl])
        in_maps.append(m)

    res = run_bass_kernel_spmd(nc, in_maps, core_ids=list(range(ncr)))
    R = list(res.results)
    while len(R) < 8:
        R.append(R[0])
    if "dbg" in R[0]:
        _NC_CACHE["dbg"] = R[0]["dbg"]
    SA = slice(0, 4)
    SB = slice(1, 5)

    y_prompt = np.stack([np.concatenate([R[b]["yT"].T, R[4 + b]["yT"].T], axis=0) for b in range(4)])
    y_sample = np.concatenate([R[c]["ysT"].T.reshape(NB, LS, D) for c in range(8)], axis=0)
    nk_p = np.stack([R[4 + b]["nkT"][SB].transpose(0, 2, 1).reshape(NL, 128, 2, 64) for b in range(4)], axis=1)
    nv_p = np.stack([R[4 + b]["nv"][SB].reshape(NL, 128, 2, 64) for b in range(4)], axis=1)

    def unst(a):
        return a.transpose(0, 2, 1).reshape(NL, 16, 64)
    re_p = np.stack([unst(R[4 + b]["nre"][SB]) for b in range(4)], axis=1)
    im_p = np.stack([unst(R[4 + b]["nim"][SB]) for b in range(4)], axis=1)
    cv_p = np.stack([R[4 + b]["ncv"][SB].transpose(0, 3, 2, 1).reshape(NL, 30, 256) for b in range(4)], axis=1)
    nk_s, nv_s, re_s, im_s, cv_s = [], [], [], [], []
    for c in range(8):
        r = R[c]
        S_ = SA if c < 4 else SB
        knew = r["skT"][S_].transpose(0, 2, 1).reshape(NL, NB, LS, 128)
        nk_s.append(np.concatenate([r["nks_c"][S_], knew], axis=2).reshape(NL, NB, 128, 2, 64))
        vnew = r["svn"][S_].transpose(0, 2, 1, 3)
        nv_s.append(np.concatenate([r["nvs_c"][S_], vnew], axis=2).reshape(NL, NB, 128, 2, 64))
        re_s.append(r["sre"][S_].transpose(0, 3, 2, 1).reshape(NL, NB, 16, 64))
        im_s.append(r["sim_o"][S_].transpose(0, 3, 2, 1).reshape(NL, NB, 16, 64))
        cv_s.append(r["scv"][S_].transpose(0, 3, 4, 2, 1).reshape(NL, NB, 30, 256))
    cat = lambda xs_: np.ascontiguousarray(np.concatenate(xs_, axis=1).astype(np.float32))
    outs = (y_prompt, y_sample, nk_p, nv_p, re_p, im_p, cv_p, cat(nk_s), cat(nv_s), cat(re_s), cat(im_s), cat(cv_s))
    return tuple(np.ascontiguousarray(o.astype(np.float32)) for o in outs)
```
